# Optimizing a Trainium2 kernel written in Bass

```python
import math
import jax, jax.numpy as jnp
from jax import lax
import numpy as np

D_MODEL = 1024
BATCH = 8
SEQ = 2048
DEPTH = 2

LRU_WIDTH = 512
LRU_BLOCKS = 8
LRU_BLOCK = LRU_WIDTH // LRU_BLOCKS
LRU_CONV = 4
LRU_C = 8.0
S5_WIDTH = 512
S5_GROUP = 16
S5_GROUPS = S5_WIDTH // S5_GROUP
S5_STATE = 64
RET_HEADS = 4
RET_HEAD_DIM = 128
RET_WIDTH = RET_HEADS * RET_HEAD_DIM
RET_CHUNK = 128
ROPE_BASE = 10000.0
MIX_WIDTH = LRU_WIDTH + S5_WIDTH + RET_WIDTH
IN_WIDTH = 2 * LRU_WIDTH + S5_WIDTH + 4 * RET_WIDTH
IN_SPLITS = (LRU_WIDTH, 2 * LRU_WIDTH, 2 * LRU_WIDTH + S5_WIDTH,
             2 * LRU_WIDTH + S5_WIDTH + RET_WIDTH,
             2 * LRU_WIDTH + S5_WIDTH + 2 * RET_WIDTH,
             2 * LRU_WIDTH + S5_WIDTH + 3 * RET_WIDTH)
D_FF = 3 * D_MODEL
FFN_CONV = 3
NORM_EPS = 1e-6

kernel_name = 'hymba_style_lru_s5_retention_convffn'


def rmsnorm(x, gain):
    xf = x.astype(jnp.float32)
    y = xf * lax.rsqrt(jnp.mean(xf * xf, axis=-1, keepdims=True) + NORM_EPS)
    return (y * gain.astype(jnp.float32)).astype(x.dtype)


def causal_dwconv(x, w, b):
    k, c = w.shape
    y = lax.conv_general_dilated(x, w[:, None, :].astype(x.dtype), window_strides=(1,),
                                 padding=[(k - 1, 0)],
                                 dimension_numbers=('NWC', 'WIO', 'NWC'),
                                 feature_group_count=c)
    return y + b.astype(x.dtype)


def _lin_combine(e1, e2):
    a1, b1 = e1
    a2, b2 = e2
    return a1 * a2, a2 * b1 + b2


def _clin_combine(e1, e2):
    ar1, ai1, br1, bi1 = e1
    ar2, ai2, br2, bi2 = e2
    return (ar1 * ar2 - ai1 * ai2,
            ar1 * ai2 + ai1 * ar2,
            ar2 * br1 - ai2 * bi1 + br2,
            ar2 * bi1 + ai2 * br1 + bi2)


def rglru_mixer(xb, yb, conv_w, conv_b, wa, ba, wx, bx, lam):
    bsz, L, _ = xb.shape
    xc = causal_dwconv(xb, conv_w, conv_b)
    xh = xc.reshape(bsz, L, LRU_BLOCKS, LRU_BLOCK)
    r = jax.nn.sigmoid(jnp.einsum('blhi,hij->blhj', xh, wa) + ba).reshape(bsz, L, LRU_WIDTH)
    i = jax.nn.sigmoid(jnp.einsum('blhi,hij->blhj', xh, wx) + bx).reshape(bsz, L, LRU_WIDTH)
    log_a = -LRU_C * r.astype(jnp.float32) * jax.nn.softplus(-lam.astype(jnp.float32))
    a = jnp.exp(log_a)
    b = jnp.sqrt(-jnp.expm1(2.0 * log_a)) * (i * xc).astype(jnp.float32)
    _, h = lax.associative_scan(_lin_combine, (a.transpose(1, 0, 2), b.transpose(1, 0, 2)), axis=0)
    return h.transpose(1, 0, 2).astype(xb.dtype) * jax.nn.gelu(yb)


def s5_mixer(u, lam_re, lam_im, log_dt, b_re, b_im, c_re, c_im, d, w_glu, b_glu):
    bsz, L, _ = u.shape
    f32 = jnp.float32
    uf = u.astype(f32).reshape(bsz, L, S5_GROUPS, S5_GROUP).transpose(1, 0, 2, 3)
    dt = jnp.exp(log_dt.astype(f32))[:, None]
    lr, li = lam_re.astype(f32), lam_im.astype(f32)
    mag = jnp.exp(lr * dt)
    abar_re, abar_im = mag * jnp.cos(li * dt), mag * jnp.sin(li * dt)
    den = lr * lr + li * li
    nr, ni = abar_re - 1.0, abar_im
    coef_re = ((nr * lr + ni * li) / den)[..., None]
    coef_im = ((ni * lr - nr * li) / den)[..., None]
    br, bi = b_re.astype(f32), b_im.astype(f32)
    bbar_re = coef_re * br - coef_im * bi
    bbar_im = coef_re * bi + coef_im * br
    bu_re = jnp.einsum('lbgc,gpc->lbgp', uf, bbar_re)
    bu_im = jnp.einsum('lbgc,gpc->lbgp', uf, bbar_im)
    shape_a = (L, 1, S5_GROUPS, S5_STATE)
    ar = jnp.broadcast_to(abar_re[None, None], shape_a)
    ai = jnp.broadcast_to(abar_im[None, None], shape_a)
    _, _, xr, xi = lax.associative_scan(_clin_combine, (ar, ai, bu_re, bu_im), axis=0)
    y = (jnp.einsum('lbgp,gcp->lbgc', xr, c_re.astype(f32))
         - jnp.einsum('lbgp,gcp->lbgc', xi, c_im.astype(f32))
         + d.astype(f32).reshape(S5_GROUPS, S5_GROUP) * uf)
    y = y.transpose(1, 0, 2, 3).reshape(bsz, L, S5_WIDTH)
    z = jax.nn.gelu(y)
    out = z * jax.nn.sigmoid(jnp.einsum('ble,ef->blf', z, w_glu.astype(f32)) + b_glu.astype(f32))
    return out.astype(u.dtype)


def rotary_tables(positions):
    half = RET_HEAD_DIM // 2
    inv = ROPE_BASE ** (-jnp.arange(half, dtype=jnp.float32) * 2.0 / RET_HEAD_DIM)
    ang = positions.astype(jnp.float32)[..., None] * inv
    return jnp.cos(ang)[:, None], jnp.sin(ang)[:, None]


def apply_rotary(t, cos, sin):
    half = RET_HEAD_DIM // 2
    t1, t2 = t[..., :half], t[..., half:]
    return jnp.concatenate([t1 * cos - t2 * sin, t1 * sin + t2 * cos], axis=-1)


def retention_mixer(q, k, v, g, cos, sin, gn_gain):
    bsz, L, _ = q.shape
    H, Dh, C = RET_HEADS, RET_HEAD_DIM, RET_CHUNK
    nc = L // C
    f32 = jnp.float32

    def heads(t):
        return t.astype(f32).reshape(bsz, L, H, Dh).transpose(0, 2, 1, 3)

    qh = apply_rotary(heads(q), cos, sin)
    kh = apply_rotary(heads(k), cos, sin) * (Dh ** -0.5)
    qc = qh.reshape(bsz, H, nc, C, Dh)
    kc = kh.reshape(bsz, H, nc, C, Dh)
    vc = heads(v).reshape(bsz, H, nc, C, Dh)
    log_gamma = jnp.log1p(-(2.0 ** (-5.0 - jnp.arange(H, dtype=f32))))
    idx = jnp.arange(C, dtype=f32)
    rel = idx[:, None] - idx[None, :]
    intra_decay = jnp.where(rel >= 0, jnp.exp(log_gamma[:, None, None] * jnp.maximum(rel, 0.0)), 0.0)
    scores = jnp.einsum('bhncd,bhnmd->bhncm', qc, kc) * intra_decay[None, :, None]
    intra = jnp.einsum('bhncm,bhnme->bhnce', scores, vc)
    k_decay = jnp.exp(log_gamma[:, None] * (C - 1.0 - idx))
    kv = jnp.einsum('bhnmd,bhnme->nbhde', kc * k_decay[None, :, None, :, None], vc)
    chunk_decay = jnp.exp(log_gamma * C)[None, :, None, None]

    def step(state, kv_n):
        return chunk_decay * state + kv_n, state

    _, prev = lax.scan(step, jnp.zeros((bsz, H, Dh, Dh), f32), kv)
    q_decay = jnp.exp(log_gamma[:, None] * (idx + 1.0))
    cross = jnp.einsum('bhncd,nbhde->bhnce', qc * q_decay[None, :, None, :, None], prev)
    o = (intra + cross).reshape(bsz, H, L, Dh)
    mu = jnp.mean(o, axis=-1, keepdims=True)
    var = jnp.mean(jnp.square(o - mu), axis=-1, keepdims=True)
    o = (o - mu) * lax.rsqrt(var + NORM_EPS)
    o = o.transpose(0, 2, 1, 3).reshape(bsz, L, RET_WIDTH) * gn_gain.astype(f32)
    return (o * jax.nn.silu(g.astype(f32))).astype(q.dtype)


def gated_conv_ffn(xn, w_up, conv_w, conv_b, w_down):
    up = causal_dwconv(jnp.einsum('bld,de->ble', xn, w_up), conv_w, conv_b)
    val, gate = jnp.split(up, 2, axis=-1)
    return jnp.einsum('blf,fd->bld', jax.nn.gelu(gate) * val, w_down)


def setup_inputs(seed: int = 0) -> dict:
    key = jax.random.key(seed)
    ks = iter(jax.random.split(key, 48))
    f32 = jnp.float32
    L = DEPTH
    G, P = S5_GROUPS, S5_STATE

    def nrm(shape, scale):
        return jax.random.normal(next(ks), shape, f32) * scale

    def gain(shape):
        return 1.0 + nrm(shape, 0.02)

    x = jax.random.normal(next(ks), (BATCH, SEQ, D_MODEL), f32)
    offset = jax.random.randint(next(ks), (BATCH, 1), 0, 4096, dtype=jnp.int32)
    positions = offset + jnp.arange(SEQ, dtype=jnp.int32)[None, :]
    u = jax.random.uniform(next(ks), (L, LRU_WIDTH), f32, 0.9, 0.999)
    a = u ** (1.0 / LRU_C)
    lru_lambda = jnp.log(a) - jnp.log1p(-a)
    s5_log_dt = jax.random.uniform(next(ks), (L, G), f32, math.log(1e-3), math.log(1e-1))
    return {
        'x': x,
        'positions': positions,
        'norm_mix': gain((L, D_MODEL)),
        'w_in': nrm((L, D_MODEL, IN_WIDTH), D_MODEL ** -0.5),
        'lru_conv_w': nrm((L, LRU_CONV, LRU_WIDTH), LRU_CONV ** -0.5),
        'lru_conv_b': nrm((L, LRU_WIDTH), 0.01),
        'lru_wa': nrm((L, LRU_BLOCKS, LRU_BLOCK, LRU_BLOCK), LRU_BLOCK ** -0.5),
        'lru_ba': nrm((L, LRU_BLOCKS, LRU_BLOCK), 0.01),
        'lru_wx': nrm((L, LRU_BLOCKS, LRU_BLOCK, LRU_BLOCK), LRU_BLOCK ** -0.5),
        'lru_bx': nrm((L, LRU_BLOCKS, LRU_BLOCK), 0.01),
        'lru_lambda': lru_lambda,
        'lru_norm': gain((L, LRU_WIDTH)),
        's5_lambda_re': -0.5 + nrm((L, G, P), 0.01),
        's5_lambda_im': jnp.pi * jnp.arange(P, dtype=f32) + nrm((L, G, P), 0.01),
        's5_log_dt': s5_log_dt,
        's5_b_re': nrm((L, G, P, S5_GROUP), S5_GROUP ** -0.5),
        's5_b_im': nrm((L, G, P, S5_GROUP), S5_GROUP ** -0.5),
        's5_c_re': nrm((L, G, S5_GROUP, P), P ** -0.5),
        's5_c_im': nrm((L, G, S5_GROUP, P), P ** -0.5),
        's5_d': nrm((L, S5_WIDTH), 0.5),
        's5_w_glu': nrm((L, S5_WIDTH, S5_WIDTH), S5_WIDTH ** -0.5),
        's5_b_glu': nrm((L, S5_WIDTH), 0.01),
        's5_norm': gain((L, S5_WIDTH)),
        'ret_norm': gain((L, RET_WIDTH)),
        'w_out': nrm((L, MIX_WIDTH, D_MODEL), MIX_WIDTH ** -0.5),
        'norm_ffn': gain((L, D_MODEL)),
        'w_up': nrm((L, D_MODEL, 2 * D_FF), D_MODEL ** -0.5),
        'ffn_conv_w': nrm((L, FFN_CONV, 2 * D_FF), FFN_CONV ** -0.5),
        'ffn_conv_b': nrm((L, 2 * D_FF), 0.01),
        'w_down': nrm((L, D_FF, D_MODEL), D_FF ** -0.5),
        'norm_final': gain((D_MODEL,)),
    }


def reference(x, positions, norm_mix, w_in, lru_conv_w, lru_conv_b, lru_wa, lru_ba, lru_wx, lru_bx,
              lru_lambda, lru_norm, s5_lambda_re, s5_lambda_im, s5_log_dt, s5_b_re, s5_b_im,
              s5_c_re, s5_c_im, s5_d, s5_w_glu, s5_b_glu, s5_norm, ret_norm, w_out, norm_ffn,
              w_up, ffn_conv_w, ffn_conv_b, w_down, norm_final):
    cos, sin = rotary_tables(positions)
    h = x
    for l in range(DEPTH):
        xn = rmsnorm(h, norm_mix[l])
        proj = jnp.einsum('bld,de->ble', xn, w_in[l])
        lru_x, lru_gate, s5_u, q, k, v, g = jnp.split(proj, IN_SPLITS, axis=-1)
        y_lru = rmsnorm(rglru_mixer(lru_x, lru_gate, lru_conv_w[l], lru_conv_b[l], lru_wa[l],
                                    lru_ba[l], lru_wx[l], lru_bx[l], lru_lambda[l]), lru_norm[l])
        y_s5 = rmsnorm(s5_mixer(s5_u, s5_lambda_re[l], s5_lambda_im[l], s5_log_dt[l], s5_b_re[l],
                                s5_b_im[l], s5_c_re[l], s5_c_im[l], s5_d[l], s5_w_glu[l],
                                s5_b_glu[l]), s5_norm[l])
        y_ret = retention_mixer(q, k, v, g, cos, sin, ret_norm[l])
        mixed = jnp.concatenate([y_lru, y_s5, y_ret], axis=-1)
        h = h + jnp.einsum('ble,ed->bld', mixed, w_out[l])
        xn = rmsnorm(h, norm_ffn[l])
        h = h + gated_conv_ffn(xn, w_up[l], ffn_conv_w[l], ffn_conv_b[l], w_down[l])
    return rmsnorm(h, norm_final)
```

```python
import math
from collections import deque
import numpy as np
import concourse.bass as bass
import concourse.mybir as mybir
from concourse.bass_utils import run_bass_kernel_spmd

F32 = mybir.dt.float32
BF16 = mybir.dt.bfloat16
I32 = mybir.dt.int32
AF = mybir.ActivationFunctionType
ALU = mybir.AluOpType

ENGS = ("pe", "act", "dve", "pool", "sp")
N_DMA_SEMS = 24

D_MODEL = 1024
SEQ = 2048
DEPTH = 2
TT = 512
NT = SEQ // TT
IN_WIDTH = 3584
D_FF = 3072
NORM_EPS = 1e-6
MAGIC = 12582912.0
TWO_PI_S = 6.2831850
GAMMAS = [1.0 - 2.0 ** (-5.0 - h) for h in range(4)]


class Reg:
    __slots__ = ("name", "w", "rd")

    def __init__(self, name):
        self.name = name
        self.w = None
        self.rd = {}


class T:
    def __init__(self, h, name, nslots=1):
        self.h = h
        self.name = name
        self.regs = [Reg(f"{name}.{i}") for i in range(nslots)]

    def __getitem__(self, k):
        return self.h[k]

    def r(self, i=None, j=None):
        if i is None:
            return list(self.regs)
        if j is None:
            return [self.regs[i]]
        return self.regs[i:j]


class Ins:
    __slots__ = ("eng", "idx", "fn", "waits", "needs_inc", "count", "dma_sem", "dma_use")

    def __init__(self, eng, idx, fn):
        self.eng = eng
        self.idx = idx
        self.fn = fn
        self.waits = []
        self.needs_inc = False
        self.count = None
        self.dma_sem = None
        self.dma_use = None


class FW:
    def __init__(self, nc):
        self.nc = nc
        self.streams = {e: [] for e in ENGS}
        self.seen = {e: {} for e in ENGS}
        self.dma_rr = 0
        self.dma_uses = [0] * N_DMA_SEMS
        self._stack = []

    def sb(self, name, shape, dtype, nslots=1):
        cm = self.nc.sbuf_tensor(name, list(shape), dtype)
        h = cm.__enter__()
        self._stack.append(cm)
        return T(h, name, nslots)

    def ps(self, name, shape, dtype, nslots=1):
        cm = self.nc.psum_tensor(name, list(shape), dtype)
        h = cm.__enter__()
        self._stack.append(cm)
        return T(h, name, nslots)

    def _need(self, ins, stream, idx):
        e = ins.eng
        if stream == e and e == "pe":
            return
        s = self.seen[e]
        if s.get(stream, -1) >= idx:
            return
        s[stream] = idx
        ins.waits.append((stream, idx))
        if stream in ENGS:
            self.streams[stream][idx].needs_inc = True

    def op(self, eng, fn, reads=(), writes=(), dma=False):
        st = self.streams[eng]
        ins = Ins(eng, len(st), fn)
        st.append(ins)
        if dma:
            k = self.dma_rr
            self.dma_rr = (k + 1) % N_DMA_SEMS
            use = self.dma_uses[k]
            if use > 0:
                self._need(ins, f"d{k}", use)
            self.dma_uses[k] = use + 1
            ins.dma_sem = k
            ins.dma_use = use + 1
            tok = (f"d{k}", use + 1)
        else:
            tok = (eng, ins.idx)
        for r in reads:
            if r.w is not None and not (r.w[0] == eng and eng == "pe"):
                self._need(ins, r.w[0], r.w[1])
        for r in writes:
            if r.w is not None and r.w[0] != eng:
                self._need(ins, r.w[0], r.w[1])
            for (re_, ri) in r.rd.items():
                if re_ != eng:
                    self._need(ins, re_, ri)
        for r in reads:
            if r.rd.get(tok[0], -1) < tok[1]:
                r.rd[tok[0]] = tok[1]
        for r in writes:
            r.w = tok
            r.rd = {}
        return ins

    def barrier(self):
        lasts = {}
        for e in ENGS:
            for ins in reversed(self.streams[e]):
                if ins.fn is not None and ins.dma_sem is None:
                    lasts[e] = ins.idx
                    break
        for e in ENGS:
            ins = self.op(e, None)
            for e2, idx in lasts.items():
                if e2 != e:
                    self._need(ins, e2, idx)
            for k in range(N_DMA_SEMS):
                if self.dma_uses[k] > 0:
                    self._need(ins, f"d{k}", self.dma_uses[k])

    def emit(self):
        nc = self.nc
        sem_cms = []
        sems = {}
        for e in list(ENGS) + [f"d{k}" for k in range(N_DMA_SEMS)]:
            cm = nc.semaphore(f"s_{e}")
            sems[e] = cm.__enter__()
            sem_cms.append(cm)
        for e in ENGS:
            c = 0
            for ins in self.streams[e]:
                if ins.needs_inc:
                    c += 1
                    ins.count = c

        def val(stream, idx):
            if stream in ENGS:
                return self.streams[stream][idx].count
            return 16 * idx

        def run(eng_name, eng):
            for ins in self.streams[eng_name]:
                for (s, i) in ins.waits:
                    eng.wait_ge(sems[s], val(s, i))
                if ins.fn is None:
                    continue
                name, args, kw = ins.fn
                bi = getattr(eng, name)(*args, **kw)
                if ins.dma_sem is not None:
                    bi.then_inc(sems[f"d{ins.dma_sem}"], 16)
                elif ins.needs_inc:
                    bi.then_inc(sems[eng_name], 1)

        with nc.Block() as block:
            @block.tensor
            def _(eng):
                run("pe", eng)

            @block.scalar
            def _(eng):
                run("act", eng)

            @block.vector
            def _(eng):
                run("dve", eng)

            @block.gpsimd
            def _(eng):
                run("pool", eng)

            @block.sync
            def _(eng):
                run("sp", eng)
        for cm in reversed(sem_cms):
            cm.__exit__(None, None, None)

    def close(self):
        for cm in reversed(self._stack):
            cm.__exit__(None, None, None)
        self._stack = []


def OP(name, *args, **kw):
    return (name, args, kw)


class Pool:
    def __init__(self, items):
        self.free_ = deque(items)
        self.n = len(items)

    def get(self):
        assert self.free_, "scratch pool exhausted"
        return self.free_.popleft()

    def free(self, *items):
        for it in items:
            self.free_.append(it)


def _chunked(v, nchunk):
    return np.ascontiguousarray(np.asarray(v, np.float32).reshape(nchunk, 128).T)


class PVLayout:
    def __init__(self):
        self.idx = {}
        self.n = 0

    def add(self, name, k):
        self.idx[name] = (self.n, k)
        self.n += k


def pv_layout():
    pl = PVLayout()
    for l in range(DEPTH):
        for name, k in (("gmix", 8), ("gffn", 8), ("lcw", 16), ("lcb", 4), ("lba", 4), ("lbx", 4),
                        ("llam", 4), ("lnorm", 4), ("s5d", 4), ("s5bg", 4), ("s5n", 4), ("rnorm", 4),
                        ("fcw", 144), ("fcb", 48), ("s5lr", 32), ("s5li", 32), ("s5ldt", 32)):
            pl.add(f"{name}{l}", k)
    for name, k in (("gfin", 8), ("invf", 1), ("sgn", 1), ("nsgn", 1), ("vdec", 4), ("rowmask", 8)):
        pl.add(name, k)
    return pl


PVL = pv_layout()
C_ID = 0
C_IOTA = 128
C_MASK = C_IOTA + 512
C_QDEC = C_MASK + 128
NCST = C_QDEC + 512


def host_consts():
    cst = np.zeros((128, NCST), np.float32)
    cst[:, C_ID:C_ID + 128] = np.eye(128, dtype=np.float32)
    cst[:, C_IOTA:C_IOTA + 512] = np.arange(512, dtype=np.float32)[None, :]
    m = np.arange(128)[:, None]
    c = np.arange(128)[None, :]
    cst[:, C_MASK:C_MASK + 128] = np.where(c >= m, np.float32(128.0 ** -0.5), np.float32(0.0))
    for h in range(4):
        lg = np.log1p(-np.float32(2.0) ** np.float32(-5.0 - h)).astype(np.float32)
        cst[:, C_QDEC + h * 128:C_QDEC + (h + 1) * 128] = np.exp(lg * (np.arange(128, dtype=np.float32) + 1.0))[None, :]
    return cst


def host_pv(inp):
    pv = np.zeros((128, PVL.n), np.float32)

    def put(name, arr):
        o, k = PVL.idx[name]
        assert arr.shape == (128, k), (name, arr.shape, k)
        pv[:, o:o + k] = arr

    for l in range(DEPTH):
        put(f"gmix{l}", _chunked(inp["norm_mix"][l], 8))
        put(f"gffn{l}", _chunked(inp["norm_ffn"][l], 8))
        cw = np.asarray(inp["lru_conv_w"][l], np.float32)
        put(f"lcw{l}", np.ascontiguousarray(cw.reshape(4, 4, 128).transpose(2, 1, 0).reshape(128, 16)))
        put(f"lcb{l}", _chunked(inp["lru_conv_b"][l], 4))
        put(f"lba{l}", _chunked(np.asarray(inp["lru_ba"][l]).reshape(512), 4))
        put(f"lbx{l}", _chunked(np.asarray(inp["lru_bx"][l]).reshape(512), 4))
        put(f"llam{l}", _chunked(inp["lru_lambda"][l], 4))
        put(f"lnorm{l}", _chunked(inp["lru_norm"][l], 4))
        put(f"s5d{l}", _chunked(inp["s5_d"][l], 4))
        put(f"s5bg{l}", _chunked(inp["s5_b_glu"][l], 4))
        put(f"s5n{l}", _chunked(inp["s5_norm"][l], 4))
        put(f"rnorm{l}", _chunked(inp["ret_norm"][l], 4))
        fw_ = np.asarray(inp["ffn_conv_w"][l], np.float32)
        put(f"fcw{l}", np.ascontiguousarray(fw_.reshape(3, 48, 128).transpose(2, 1, 0).reshape(128, 144)))
        put(f"fcb{l}", _chunked(inp["ffn_conv_b"][l], 48))
        lr = np.asarray(inp["s5_lambda_re"][l], np.float32).T
        li = np.asarray(inp["s5_lambda_im"][l], np.float32).T
        put(f"s5lr{l}", np.concatenate([lr, lr], 0))
        put(f"s5li{l}", np.concatenate([li, li], 0))
        put(f"s5ldt{l}", np.broadcast_to(np.asarray(inp["s5_log_dt"][l], np.float32)[None, :], (128, 32)).copy())
    put("gfin", _chunked(inp["norm_final"], 8))
    half = 64
    inv = (np.float32(10000.0) ** (-np.arange(half, dtype=np.float32) * np.float32(2.0) / np.float32(128.0))).astype(np.float32)
    put("invf", np.concatenate([inv, inv])[:, None])
    sgn = np.concatenate([-np.ones(64, np.float32), np.ones(64, np.float32)])[:, None]
    put("sgn", sgn)
    put("nsgn", -sgn)
    vd = np.zeros((128, 4), np.float32)
    for h in range(4):
        lg = np.log1p(-np.float32(2.0) ** np.float32(-5.0 - h)).astype(np.float32)
        vd[:, h] = np.exp(-lg * (np.arange(128, dtype=np.float32) + 1.0))
    put("vdec", vd)
    rm = np.zeros((128, 8), np.float32)
    for g in range(8):
        rm[16 * g:16 * g + 16, g] = 1.0
    put("rowmask", rm)
    return pv


def host_struct(inp):
    wbd = np.zeros((DEPTH, 128, 2, 4, 128), np.float32)
    for l in range(DEPTH):
        for which, nm in enumerate(("lru_wa", "lru_wx")):
            w = np.asarray(inp[nm][l], np.float32)
            for c in range(4):
                for hb in range(2):
                    wbd[l, hb * 64:(hb + 1) * 64, which, c, hb * 64:(hb + 1) * 64] = w[2 * c + hb]
    x1 = np.zeros((DEPTH, 128, 512), np.float32)
    x2 = np.zeros((DEPTH, 128, 512), np.float32)
    c1 = np.zeros((DEPTH, 128, 4, 128), np.float32)
    c2 = np.zeros((DEPTH, 128, 4, 128), np.float32)
    for l in range(DEPTH):
        br = np.asarray(inp["s5_b_re"][l], np.float32).transpose(1, 0, 2).reshape(64, 512)
        bi = np.asarray(inp["s5_b_im"][l], np.float32).transpose(1, 0, 2).reshape(64, 512)
        x1[l] = np.concatenate([br, bi], 0)
        x2[l] = np.concatenate([bi, br], 0)
        cr = np.asarray(inp["s5_c_re"][l], np.float32).reshape(4, 128, 64)
        ci = np.asarray(inp["s5_c_im"][l], np.float32).reshape(4, 128, 64)
        c1[l] = np.concatenate([cr, ci], 2).transpose(1, 0, 2)
        c2[l] = np.concatenate([ci, cr], 2).transpose(1, 0, 2)
    return wbd, x1, x2, c1, c2


def build(debug=(), upto="all", nlayers=DEPTH):
    nc = bass.Bass("TRN2", target_bir_lowering=False)
    fw = FW(nc)
    dram = {}

    def din(name, shape, dtype=F32):
        dram[name] = nc.dram_tensor(name, list(shape), dtype, kind="ExternalInput").ap()
        return dram[name]

    xT_d = din("xT", [D_MODEL, SEQ])
    pos_d = din("pos", [1, SEQ], I32)
    pv_d = din("pv", [128, PVL.n])
    cst_d = din("cst", [128, NCST])
    wbd_d = din("wbd", [DEPTH, 128, 2 * 4 * 128])
    x1_d = din("s5x1", [DEPTH, 128, 512])
    x2_d = din("s5x2", [DEPTH, 128, 512])
    c1_d = din("s5c1", [DEPTH, 128, 512])
    c2_d = din("s5c2", [DEPTH, 128, 512])
    w_in_d = din("w_in", [DEPTH, D_MODEL, IN_WIDTH])
    w_glu_d = din("s5_w_glu", [DEPTH, 512, 512])
    w_out_d = din("w_out", [DEPTH, 1536, D_MODEL])
    w_up_d = din("w_up", [DEPTH, D_MODEL, 2 * D_FF])
    w_down_d = din("w_down", [DEPTH, D_FF, D_MODEL])
    outT_d = nc.dram_tensor("outT", [D_MODEL, SEQ], F32, kind="ExternalOutput").ap()
    dbg_out = {}
    out_dmas = []

    A = fw.op

    hT = fw.sb("hT", [128, 8, SEQ], F32, nslots=32)
    xn = fw.sb("xn", [128, 8, SEQ], BF16, nslots=32)
    rcos = fw.sb("rcos", [128, SEQ], F32, nslots=4)
    rsin = fw.sb("rsin", [128, SEQ], F32, nslots=4)
    pv = fw.sb("pvs", [128, PVL.n], F32)
    cst = fw.sb("csts", [128, NCST], F32)
    ident_b = fw.sb("ident_b", [128, 128], BF16)
    ones_b = fw.sb("ones_b", [128, 128], BF16)
    o128_b = fw.sb("o128_b", [128, 128], BF16)
    NRING = 10
    ring = Pool([fw.sb(f"ring{i}", [128, 520], F32) for i in range(NRING)])
    psum = fw.ps("psum", [128, 8, 512], F32, nslots=8)
    banks = Pool(list(range(8)))

    def PV(name, c=0, n=1):
        o, k = PVL.idx[name]
        return pv[:, o + c:o + c + n]

    def hs(c, t):
        return hT.r(c * 4 + t)

    def xs(c, t):
        return xn.r(c * 4 + t)

    def tsl(t):
        return slice(t * TT, (t + 1) * TT)

    def bf(tile, n=TT, off=0):
        return tile[:].bitcast(BF16)[:, off:off + n]

    def dbg(name, tile, ap, shape, dtype=F32):
        if name not in debug:
            return
        d = nc.dram_tensor("dbg_" + name, list(shape), dtype, kind="ExternalOutput").ap()
        dbg_out[name] = d
        ins = A("sp", OP("dma_start", out=d, in_=ap), reads=tile.r(), dma=True)
        out_dmas.append(ins)

    A("sp", OP("dma_start", out=pv[:], in_=pv_d), writes=pv.r(), dma=True)
    A("sp", OP("dma_start", out=cst[:], in_=cst_d), writes=cst.r(), dma=True)
    for c in range(8):
        for t in range(NT):
            A("sp", OP("dma_start", out=hT[:, c, tsl(t)], in_=xT_d[c * 128:(c + 1) * 128, tsl(t)]),
              writes=hs(c, t), dma=True)
    A("dve", OP("memset", ones_b[:], 1.0), writes=ones_b.r())
    A("dve", OP("memset", o128_b[:], 1.0 / 128.0), writes=o128_b.r())
    A("act", OP("activation", out=ident_b[:], in_=cst[:, C_ID:C_ID + 128], func=AF.Copy),
      reads=cst.r(), writes=ident_b.r())
    ident_f = cst[:, C_ID:C_ID + 128]
    iota = cst[:, C_IOTA:C_IOTA + 512]
    maskT = cst[:, C_MASK:C_MASK + 128]

    for t in range(NT):
        pi_ = ring.get(); pf = ring.get(); k_ = ring.get()
        A("sp", OP("dma_start", out=pi_[:].bitcast(I32)[:, 0:TT],
                                                      in_=pos_d[0:1, tsl(t)].to_broadcast([128, TT])),
          writes=pi_.r(), dma=True)
        A("dve", OP("tensor_copy", out=pf[:, 0:TT], in_=pi_[:].bitcast(I32)[:, 0:TT]),
          reads=pi_.r(), writes=pf.r())
        A("dve", OP("tensor_scalar", out=pf[:, 0:TT], in0=pf[:, 0:TT], scalar1=PV("invf"), scalar2=None,
                                                  op0=ALU.mult), reads=pf.r() + pv.r(), writes=pf.r())
        A("dve", OP("tensor_scalar", out=k_[:, 0:TT], in0=pf[:, 0:TT], scalar1=1.0 / (2.0 * math.pi),
                                                         scalar2=MAGIC, op0=ALU.mult, op1=ALU.add),
          reads=pf.r(), writes=k_.r())
        A("dve", OP("tensor_scalar", out=k_[:, 0:TT], in0=k_[:, 0:TT], scalar1=MAGIC, scalar2=None,
                                                  op0=ALU.subtract), reads=k_.r(), writes=k_.r())
        C1 = 6.28125
        C2 = 2.0 * math.pi - 6.28125
        A("dve", OP("scalar_tensor_tensor", out=pf[:, 0:TT], in0=k_[:, 0:TT], scalar=-C1, in1=pf[:, 0:TT],
                                                                op0=ALU.mult, op1=ALU.add), reads=pf.r() + k_.r(), writes=pf.r())
        A("dve", OP("scalar_tensor_tensor", out=pf[:, 0:TT], in0=k_[:, 0:TT], scalar=-C2, in1=pf[:, 0:TT],
                                                                op0=ALU.mult, op1=ALU.add), reads=pf.r() + k_.r(), writes=pf.r())
        A("dve", OP("tensor_scalar", out=pf[:, 0:TT], in0=pf[:, 0:TT], scalar1=3.1415925, scalar2=-3.1415925,
                                                  op0=ALU.min, op1=ALU.max), reads=pf.r(), writes=pf.r())
        A("act", OP("activation", out=rsin[:, tsl(t)], in_=pf[:, 0:TT], func=AF.Sin, scale=PV("sgn")),
          reads=pf.r() + pv.r(), writes=rsin.r(t))
        A("act", OP("activation", out=k_[:, 0:TT], in_=pf[:, 0:TT], func=AF.Abs),
          reads=pf.r(), writes=k_.r())
        A("act", OP("activation", out=rcos[:, tsl(t)], in_=k_[:, 0:TT], func=AF.Sin, scale=-1.0,
                                                    bias=math.pi / 2.0 - 1e-6), reads=k_.r(), writes=rcos.r(t))
        ring.free(pi_, pf, k_)
    dbg("rcos", rcos, rcos[:], [128, SEQ])
    dbg("rsin", rsin, rsin[:], [128, SEQ])

    def load_w(dst_tile, dst_ap, src_ap):
        return A("pool", OP("dma_start", out=dst_ap, in_=src_ap), writes=dst_tile.r(), dma=True)

    def rmsnorm_to_xn(gname):
        for t in range(NT):
            b = banks.get()
            for c in range(8):
                sq = ring.get()
                A("act", OP("activation", out=bf(sq), in_=hT[:, c, tsl(t)], func=AF.Square),
                  reads=hs(c, t), writes=sq.r())
                A("pe", OP("matmul", out=psum[:, b, :], lhsT=ones_b[:], rhs=bf(sq), start=(c == 0), stop=(c == 7)),
                  reads=sq.r() + ones_b.r(), writes=psum.r(b))
                ring.free(sq)
            rs = ring.get()
            A("act", OP("activation", out=rs[:, 0:TT], in_=psum[:, b, :], func=AF.Sqrt, scale=1.0 / D_MODEL, bias=NORM_EPS),
              reads=psum.r(b), writes=rs.r())
            banks.free(b)
            A("dve", OP("reciprocal", out=rs[:, 0:TT], in_=rs[:, 0:TT]), reads=rs.r(), writes=rs.r())
            for c in range(8):
                A("dve", OP("scalar_tensor_tensor", out=xn[:, c, tsl(t)], in0=hT[:, c, tsl(t)], scalar=PV(gname, c),
                                                                      in1=rs[:, 0:TT], op0=ALU.mult, op1=ALU.mult),
                  reads=hs(c, t) + rs.r() + pv.r(), writes=xs(c, t))
            ring.free(rs)

    def proj(b, wtile, w_ap_fn, t):
        for k in range(8):
            A("pe", OP("matmul", out=psum[:, b, :], lhsT=w_ap_fn(k), rhs=xn[:, k, tsl(t)], start=(k == 0), stop=(k == 7)),
              reads=wtile.r() + xs(k, t), writes=psum.r(b))

    def group_post(l, mixed, gname, grp, wbufs_pool, eps_div):
        wo = wbufs_pool.get()
        wo_v = wo[:].rearrange("p (k n) -> p k n", k=4)
        load_w(wo, wo_v, w_out_d[l, grp * 512:(grp + 1) * 512, :].rearrange("(k p) n -> p k n", p=128))
        for t in range(NT):
            if gname is not None:
                b = banks.get()
                for c in range(4):
                    sq = ring.get()
                    A("act", OP("activation", out=bf(sq), in_=mixed[:, c, tsl(t)], func=AF.Square),
                      reads=mixed.r(c * 4 + t), writes=sq.r())
                    A("pe", OP("matmul", out=psum[:, b, :], lhsT=ones_b[:], rhs=bf(sq), start=(c == 0), stop=(c == 3)),
                      reads=sq.r() + ones_b.r(), writes=psum.r(b))
                    ring.free(sq)
                rs = ring.get()
                A("act", OP("activation", out=rs[:, 0:TT], in_=psum[:, b, :], func=AF.Sqrt, scale=1.0 / 512.0, bias=NORM_EPS),
                  reads=psum.r(b), writes=rs.r())
                banks.free(b)
                A("dve", OP("reciprocal", out=rs[:, 0:TT], in_=rs[:, 0:TT]), reads=rs.r(), writes=rs.r())
                for c in range(4):
                    A("dve", OP("scalar_tensor_tensor", out=mixed[:, c, tsl(t)], in0=mixed[:, c, tsl(t)],
                                                                          scalar=PV(gname + str(l), c), in1=rs[:, 0:TT],
                                                                          op0=ALU.mult, op1=ALU.mult),
                      reads=mixed.r(c * 4 + t) + rs.r() + pv.r(), writes=mixed.r(c * 4 + t))
                ring.free(rs)
            for dc in range(8):
                b = banks.get()
                for k in range(4):
                    A("pe", OP("matmul", out=psum[:, b, :], lhsT=wo_v[:, k, dc * 128:(dc + 1) * 128],
                                                           rhs=mixed[:, k, tsl(t)], start=(k == 0), stop=(k == 3)),
                      reads=wo.r() + mixed.r(k * 4 + t), writes=psum.r(b))
                A("dve", OP("tensor_tensor", out=hT[:, dc, tsl(t)], in0=hT[:, dc, tsl(t)], in1=psum[:, b, :], op=ALU.add),
                  reads=hs(dc, t) + psum.r(b), writes=hs(dc, t))
                banks.free(b)
        wbufs_pool.free(wo)

    phase = []
    for l in range(nlayers):

        def psb(name, shape, dtype, nslots=1):
            cm = nc.sbuf_tensor(f"{name}_{l}", list(shape), dtype)
            h = cm.__enter__()
            phase.append(cm)
            return T(h, name, nslots)

        mixed = psb("mixed", [128, 4, SEQ], BF16, nslots=16)
        aux = psb("aux", [128, 8192], BF16, nslots=16)
        wbufs = Pool([psb(f"wbuf{i}", [128, 4096], BF16) for i in range(2)])
        wbd = psb("wbd", [128, 2, 4, 128], BF16)
        lpar = psb("lpar", [128, 16], F32)
        lcar = psb("lcar", [128, 4, 4], F32)
        s5p = psb("s5p", [128, 32, 8], F32)
        s5off = psb("s5off", [128, 32, 4], F32)
        bst1 = psb("bst1", [128, 512], F32)
        bst2 = psb("bst2", [128, 512], F32)
        lhs_b = psb("lhs_b", [128, 2, 8, 128], BF16)
        lhs_c = psb("lhs_c", [128, 2, 8, 128], BF16)
        zcar = psb("zcar", [128, 32], F32)
        ust = psb("ust", [128, 128], F32)
        pbs = [psb(f"pb{i}", [128, 128], BF16) for i in range(2)]

        load_w(wbd, wbd[:].rearrange("p a c n -> p (a c n)"), wbd_d[l])
        A("act", OP("activation", out=lpar[:, 0:4], in_=PV(f"llam{l}", 0, 4), func=AF.Exp, scale=-1.0), reads=pv.r(), writes=lpar.r())
        A("act", OP("activation", out=lpar[:, 0:4], in_=lpar[:, 0:4], func=AF.Ln, bias=1.0), reads=lpar.r(), writes=lpar.r())
        A("dve", OP("tensor_scalar", out=lpar[:, 4:8], in0=lpar[:, 0:4], scalar1=-16.0, scalar2=None, op0=ALU.mult), reads=lpar.r(), writes=lpar.r())
        A("dve", OP("tensor_scalar", out=lpar[:, 0:4], in0=lpar[:, 0:4], scalar1=-8.0, scalar2=None, op0=ALU.mult), reads=lpar.r(), writes=lpar.r())
        A("dve", OP("memset", lcar[:], 0.0), writes=lcar.r())
        A("dve", OP("memset", zcar[:], 0.0), writes=zcar.r())

        S = lambda j: s5p[:, :, j]
        LR = PV(f"s5lr{l}", 0, 32)
        LI = PV(f"s5li{l}", 0, 32)
        R_, W_ = s5p.r(), s5p.r()
        A("act", OP("activation", out=S(6), in_=PV(f"s5ldt{l}", 0, 32), func=AF.Exp), reads=pv.r(), writes=W_)
        A("dve", OP("tensor_tensor", out=S(0), in0=LR, in1=S(6), op=ALU.mult), reads=R_ + pv.r(), writes=W_)
        A("act", OP("activation", out=S(0), in_=S(0), func=AF.Exp), reads=R_, writes=W_)
        A("dve", OP("tensor_tensor", out=S(7), in0=LI, in1=S(6), op=ALU.mult), reads=R_ + pv.r(), writes=W_)
        A("dve", OP("tensor_scalar", out=S(6), in0=S(7), scalar1=1.0 / (2.0 * math.pi), scalar2=MAGIC, op0=ALU.mult, op1=ALU.add), reads=R_, writes=W_)
        A("dve", OP("tensor_scalar", out=S(6), in0=S(6), scalar1=MAGIC, scalar2=None, op0=ALU.subtract), reads=R_, writes=W_)
        A("dve", OP("scalar_tensor_tensor", out=S(1), in0=S(7), scalar=1.0 / (2.0 * math.pi), in1=S(6), op0=ALU.mult, op1=ALU.subtract), reads=R_, writes=W_)
        A("act", OP("activation", out=S(7), in_=S(1), func=AF.Sin, scale=TWO_PI_S), reads=R_, writes=W_)
        A("act", OP("activation", out=S(6), in_=S(1), func=AF.Abs), reads=R_, writes=W_)
        A("act", OP("activation", out=S(6), in_=S(6), func=AF.Sin, scale=-TWO_PI_S, bias=math.pi / 2.0 - 1e-6), reads=R_, writes=W_)
        A("dve", OP("tensor_tensor", out=S(6), in0=S(6), in1=S(0), op=ALU.mult), reads=R_, writes=W_)
        A("dve", OP("tensor_scalar", out=S(6), in0=S(6), scalar1=-1.0, scalar2=None, op0=ALU.add), reads=R_, writes=W_)
        A("dve", OP("tensor_tensor", out=S(7), in0=S(7), in1=S(0), op=ALU.mult), reads=R_, writes=W_)
        A("dve", OP("tensor_tensor", out=S(5), in0=LR, in1=LR, op=ALU.mult), reads=R_ + pv.r(), writes=W_)
        A("dve", OP("tensor_tensor", out=S(4), in0=LI, in1=LI, op=ALU.mult), reads=R_ + pv.r(), writes=W_)
        A("dve", OP("tensor_tensor", out=S(5), in0=S(5), in1=S(4), op=ALU.add), reads=R_, writes=W_)
        A("dve", OP("reciprocal", out=S(5), in_=S(5)), reads=R_, writes=W_)
        A("dve", OP("tensor_tensor", out=S(2), in0=S(6), in1=LR, op=ALU.mult), reads=R_ + pv.r(), writes=W_)
        A("dve", OP("tensor_tensor", out=S(4), in0=S(7), in1=LI, op=ALU.mult), reads=R_ + pv.r(), writes=W_)
        A("dve", OP("tensor_tensor", out=S(2), in0=S(2), in1=S(4), op=ALU.add), reads=R_, writes=W_)
        A("dve", OP("tensor_tensor", out=S(2), in0=S(2), in1=S(5), op=ALU.mult), reads=R_, writes=W_)
        A("dve", OP("tensor_tensor", out=S(3), in0=S(7), in1=LR, op=ALU.mult), reads=R_ + pv.r(), writes=W_)
        A("dve", OP("tensor_tensor", out=S(4), in0=S(6), in1=LI, op=ALU.mult), reads=R_ + pv.r(), writes=W_)
        A("dve", OP("tensor_tensor", out=S(3), in0=S(3), in1=S(4), op=ALU.subtract), reads=R_, writes=W_)
        A("dve", OP("tensor_tensor", out=S(3), in0=S(3), in1=S(5), op=ALU.mult), reads=R_, writes=W_)
        A("dve", OP("tensor_copy", out=S(5), in_=S(3)), reads=R_, writes=W_)
        A("dve", OP("tensor_scalar", out=S(3), in0=S(3), scalar1=PV("sgn"), scalar2=None, op0=ALU.mult), reads=R_ + pv.r(), writes=W_)
        A("dve", OP("tensor_scalar", out=S(4), in0=S(2), scalar1=PV("nsgn"), scalar2=None, op0=ALU.mult), reads=R_ + pv.r(), writes=W_)
        for t in range(NT):
            A("dve", OP("tensor_scalar", out=s5off[:, :, t], in0=S(1), scalar1=float(TT * t), scalar2=MAGIC, op0=ALU.mult, op1=ALU.add),
              reads=R_, writes=s5off.r())
            A("dve", OP("tensor_scalar", out=s5off[:, :, t], in0=s5off[:, :, t], scalar1=MAGIC, scalar2=None, op0=ALU.subtract),
              reads=s5off.r(), writes=s5off.r())
            A("dve", OP("scalar_tensor_tensor", out=s5off[:, :, t], in0=S(1), scalar=float(TT * t), in1=s5off[:, :, t],
                                                           op0=ALU.mult, op1=ALU.subtract), reads=R_ + s5off.r(), writes=s5off.r())
        cn1 = ring.get(); cn2 = ring.get()
        A("sp", OP("dma_start", out=cn1[:, 0:512], in_=x1_d[l]), writes=cn1.r(), dma=True)
        A("sp", OP("dma_start", out=cn2[:, 0:512], in_=x2_d[l]), writes=cn2.r(), dma=True)
        v3 = lambda tl: tl[:, 0:512].rearrange("p (g c) -> p g c", c=16)
        bc = lambda j: s5p[:, :, j:j + 1].to_broadcast([128, 32, 16])
        t1 = ring.get(); t2 = ring.get()
        t1v = t1[:, 0:512].rearrange("p (g c) -> p g c", c=16)
        t2v = t2[:, 0:512].rearrange("p (g c) -> p g c", c=16)
        A("dve", OP("tensor_tensor", out=t1v, in0=v3(cn1), in1=bc(2), op=ALU.mult), reads=cn1.r() + R_, writes=t1.r())
        A("dve", OP("tensor_tensor", out=t2v, in0=v3(cn2), in1=bc(3), op=ALU.mult), reads=cn2.r() + R_, writes=t2.r())
        A("dve", OP("tensor_tensor", out=bst1[:], in0=t1[:, 0:512], in1=t2[:, 0:512], op=ALU.add), reads=t1.r() + t2.r(), writes=bst1.r())
        A("dve", OP("tensor_tensor", out=t1v, in0=v3(cn2), in1=bc(4), op=ALU.mult), reads=cn2.r() + R_, writes=t1.r())
        A("dve", OP("tensor_tensor", out=t2v, in0=v3(cn1), in1=bc(5), op=ALU.mult), reads=cn1.r() + R_, writes=t2.r())
        A("dve", OP("tensor_tensor", out=bst2[:], in0=t1[:, 0:512], in1=t2[:, 0:512], op=ALU.add), reads=t1.r() + t2.r(), writes=bst2.r())
        ring.free(t1, t2, cn1, cn2)
        A("pool", OP("memset", lhs_c[:].rearrange("p a g n -> p (a g n)"), 0.0), writes=lhs_c.r())
        dbg(f"bst1_{l}", bst1, bst1[:], [128, 512])
        dbg(f"s5p_{l}", s5p, s5p[:].rearrange("p g j -> p (g j)"), [128, 256])

        rmsnorm_to_xn(f"gmix{l}")
        dbg(f"xn{l}", xn, xn[:].rearrange("p c t -> p (c t)"), [128, 8 * SEQ], BF16)

        for c in range(4):
            wb = wbufs.get()
            wv_ = wb[:].rearrange("p (k n) -> p k n", k=8)
            load_w(wb, wv_[:, :, 0:128], w_in_d[l, :, c * 128:(c + 1) * 128].rearrange("(k p) n -> p k n", p=128))
            load_w(wb, wv_[:, :, 128:256], w_in_d[l, :, 512 + c * 128:512 + (c + 1) * 128].rearrange("(k p) n -> p k n", p=128))
            for t in range(NT):
                bx = banks.get(); bg = banks.get()
                proj(bx, wb, lambda k: wv_[:, k, 0:128], t)
                proj(bg, wb, lambda k: wv_[:, k, 128:256], t)
                xp = ring.get(); gy = ring.get(); xc = ring.get(); xcb = ring.get()
                A("act", OP("activation", out=xp[:, 0:3], in_=lcar[:, c, 0:3], func=AF.Copy), reads=lcar.r(), writes=xp.r())
                A("act", OP("activation", out=xp[:, 3:3 + TT], in_=psum[:, bx, :], func=AF.Copy), reads=psum.r(bx), writes=xp.r())
                A("act", OP("activation", out=lcar[:, c, 0:3], in_=xp[:, TT:TT + 3], func=AF.Copy), reads=xp.r(), writes=lcar.r())
                A("act", OP("activation", out=gy[:, 0:TT], in_=psum[:, bg, :], func=AF.Gelu_apprx_tanh), reads=psum.r(bg), writes=gy.r())
                banks.free(bx, bg)
                A("act", OP("activation", out=xc[:, 0:TT], in_=xp[:, 3:3 + TT], func=AF.Identity,
                                                              scale=PV(f"lcw{l}", c * 4 + 3), bias=PV(f"lcb{l}", c)),
                  reads=xp.r() + pv.r(), writes=xc.r())
                for k in range(3):
                    A("dve", OP("scalar_tensor_tensor", out=xc[:, 0:TT], in0=xp[:, k:k + TT], scalar=PV(f"lcw{l}", c * 4 + k),
                                                                                in1=xc[:, 0:TT], op0=ALU.mult, op1=ALU.add),
                      reads=xp.r() + xc.r() + pv.r(), writes=xc.r())
                A("act", OP("activation", out=bf(xcb), in_=xc[:, 0:TT], func=AF.Copy), reads=xc.r(), writes=xcb.r())
                ring.free(xp)
                br_ = banks.get(); bi_ = banks.get()
                A("pe", OP("matmul", out=psum[:, br_, :], lhsT=wbd[:, 0, c, :], rhs=bf(xcb), start=True, stop=True),
                  reads=wbd.r() + xcb.r(), writes=psum.r(br_))
                A("pe", OP("matmul", out=psum[:, bi_, :], lhsT=wbd[:, 1, c, :], rhs=bf(xcb), start=True, stop=True),
                  reads=wbd.r() + xcb.r(), writes=psum.r(bi_))
                rr = ring.get(); ii = ring.get(); a2 = ring.get()
                A("act", OP("activation", out=rr[:, 0:TT], in_=psum[:, br_, :], func=AF.Sigmoid, bias=PV(f"lba{l}", c)),
                  reads=psum.r(br_) + pv.r(), writes=rr.r())
                A("act", OP("activation", out=ii[:, 0:TT], in_=psum[:, bi_, :], func=AF.Sigmoid, bias=PV(f"lbx{l}", c)),
                  reads=psum.r(bi_) + pv.r(), writes=ii.r())
                banks.free(br_, bi_)
                ring.free(xcb)
                A("act", OP("activation", out=a2[:, 0:TT], in_=rr[:, 0:TT], func=AF.Exp, scale=lpar[:, 4 + c:5 + c]),
                  reads=rr.r() + lpar.r(), writes=a2.r())
                A("act", OP("activation", out=rr[:, 0:TT], in_=rr[:, 0:TT], func=AF.Exp, scale=lpar[:, c:c + 1]),
                  reads=rr.r() + lpar.r(), writes=rr.r())
                A("act", OP("activation", out=a2[:, 0:TT], in_=a2[:, 0:TT], func=AF.Sqrt, scale=-1.0, bias=1.0),
                  reads=a2.r(), writes=a2.r())
                A("dve", OP("tensor_tensor", out=ii[:, 0:TT], in0=ii[:, 0:TT], in1=xc[:, 0:TT], op=ALU.mult),
                  reads=ii.r() + xc.r(), writes=ii.r())
                A("dve", OP("tensor_tensor", out=ii[:, 0:TT], in0=ii[:, 0:TT], in1=a2[:, 0:TT], op=ALU.mult),
                  reads=ii.r() + a2.r(), writes=ii.r())
                hh = xc
                A("dve", OP("tensor_tensor_scan", out=hh[:, 0:TT], data0=rr[:, 0:TT], data1=ii[:, 0:TT],
                                                                            initial=lcar[:, c, 3:4], op0=ALU.mult, op1=ALU.add),
                  reads=rr.r() + ii.r() + lcar.r(), writes=hh.r())
                A("act", OP("activation", out=lcar[:, c, 3:4], in_=hh[:, TT - 1:TT], func=AF.Copy), reads=hh.r(), writes=lcar.r())
                A("dve", OP("tensor_tensor", out=mixed[:, c, tsl(t)], in0=hh[:, 0:TT], in1=gy[:, 0:TT], op=ALU.mult),
                  reads=hh.r() + gy.r(), writes=mixed.r(c * 4 + t))
                ring.free(rr, ii, a2, xc, gy)
            wbufs.free(wb)
        dbg(f"ylru_raw{l}", mixed, mixed[:].rearrange("p c t -> p (c t)"), [128, 4 * SEQ], BF16)
        if upto == "lru_raw":
            break
        group_post(l, mixed, "lnorm", 0, wbufs, 512.0)
        dbg(f"ylru{l}", mixed, mixed[:].rearrange("p c t -> p (c t)"), [128, 4 * SEQ], BF16)
        if upto == "lru":
            break

        zbf = aux[:].rearrange("p (c t) -> p c t", c=4)
        for c in range(4):
            wb = wbufs.get()
            wv_ = wb[:].rearrange("p (k n) -> p k n", k=8)
            load_w(wb, wv_[:, :, 0:128], w_in_d[l, :, 1024 + c * 128:1024 + (c + 1) * 128].rearrange("(k p) n -> p k n", p=128))
            for which, bst in enumerate((bst1, bst2)):
                b = banks.get()
                A("pe", OP("transpose", out=psum[:, b, 0:128], in_=bst[:, c * 128:(c + 1) * 128], identity=ident_f),
                  reads=bst.r() + cst.r(), writes=psum.r(b))
                tb = ring.get()
                A("act", OP("activation", out=tb[:, 0:128], in_=psum[:, b, 0:128], func=AF.Copy), reads=psum.r(b), writes=tb.r())
                banks.free(b)
                for g in range(8):
                    A("dve", OP("tensor_scalar", out=lhs_b[:, which, g, :], in0=tb[:, 0:128], scalar1=PV("rowmask", g),
                                                                                scalar2=None, op0=ALU.mult),
                      reads=tb.r() + pv.r(), writes=lhs_b.r())
                ring.free(tb)
            for which, (cd, sname) in enumerate(((c1_d, "nsgn"), (c2_d, None))):
                cn = ring.get()
                A("sp", OP("dma_start", out=cn[:, 0:128], in_=cd[l, :, c * 128:(c + 1) * 128]), writes=cn.r(), dma=True)
                b = banks.get()
                A("pe", OP("transpose", out=psum[:, b, 0:128], in_=cn[:, 0:128], identity=ident_f),
                  reads=cn.r() + cst.r(), writes=psum.r(b))
                ring.free(cn)
                for g in range(8):
                    if sname is not None:
                        A("dve", OP("tensor_scalar", out=lhs_c[:, which, g, g * 16:(g + 1) * 16], in0=psum[:, b, g * 16:(g + 1) * 16],
                                                                                  scalar1=PV("nsgn"), scalar2=None, op0=ALU.mult),
                          reads=psum.r(b) + pv.r(), writes=lhs_c.r())
                    else:
                        A("dve", OP("tensor_scalar", out=lhs_c[:, which, g, g * 16:(g + 1) * 16], in0=psum[:, b, g * 16:(g + 1) * 16],
                                                                                  scalar1=-1.0, scalar2=None, op0=ALU.mult),
                          reads=psum.r(b), writes=lhs_c.r())
                banks.free(b)
            for t in range(NT):
                bu = banks.get()
                proj(bu, wb, lambda k: wv_[:, k, 0:128], t)
                uf = ring.get(); ub = ring.get()
                A("act", OP("activation", out=uf[:, 0:TT], in_=psum[:, bu, :], func=AF.Copy), reads=psum.r(bu), writes=uf.r())
                A("act", OP("activation", out=bf(ub), in_=psum[:, bu, :], func=AF.Copy), reads=psum.r(bu), writes=ub.r())
                banks.free(bu)
                by = banks.get()
                for g in range(8):
                    G = c * 8 + g
                    b1 = banks.get(); b2 = banks.get()
                    A("pe", OP("matmul", out=psum[:, b1, :], lhsT=lhs_b[:, 0, g, :], rhs=bf(ub), start=True, stop=True),
                      reads=lhs_b.r() + ub.r(), writes=psum.r(b1))
                    A("pe", OP("matmul", out=psum[:, b2, :], lhsT=lhs_b[:, 1, g, :], rhs=bf(ub), start=True, stop=True),
                      reads=lhs_b.r() + ub.r(), writes=psum.r(b2))
                    uu = ring.get(); kk = ring.get(); tc_ = ring.get(); ts_ = ring.get()
                    A("pool", OP("tensor_scalar", out=uu[:, 0:TT], in0=iota, scalar1=s5p[:, G, 1:2], scalar2=s5off[:, G, t:t + 1],
                                                                    op0=ALU.mult, op1=ALU.add), reads=cst.r() + s5p.r() + s5off.r(), writes=uu.r())
                    A("dve", OP("tensor_scalar", out=kk[:, 0:TT], in0=uu[:, 0:TT], scalar1=MAGIC, scalar2=MAGIC,
                                                                      op0=ALU.add, op1=ALU.subtract), reads=uu.r(), writes=kk.r())
                    A("pool", OP("tensor_tensor", out=uu[:, 0:TT], in0=uu[:, 0:TT], in1=kk[:, 0:TT], op=ALU.subtract),
                      reads=uu.r() + kk.r(), writes=uu.r())
                    A("act", OP("activation", out=ts_[:, 0:TT], in_=uu[:, 0:TT], func=AF.Sin, scale=TWO_PI_S), reads=uu.r(), writes=ts_.r())
                    A("act", OP("activation", out=kk[:, 0:TT], in_=uu[:, 0:TT], func=AF.Abs), reads=uu.r(), writes=kk.r())
                    A("act", OP("activation", out=tc_[:, 0:TT], in_=kk[:, 0:TT], func=AF.Sin, scale=-TWO_PI_S, bias=math.pi / 2.0 - 1e-6),
                      reads=kk.r(), writes=tc_.r())
                    m1 = uu; m2 = kk
                    A("dve", OP("tensor_tensor", out=m1[:, 0:TT], in0=psum[:, b1, :], in1=tc_[:, 0:TT], op=ALU.mult),
                      reads=psum.r(b1) + tc_.r(), writes=m1.r())
                    A("dve", OP("tensor_tensor", out=m2[:, 0:TT], in0=psum[:, b2, :], in1=ts_[:, 0:TT], op=ALU.mult),
                      reads=psum.r(b2) + ts_.r(), writes=m2.r())
                    banks.free(b1, b2)
                    A("pool", OP("tensor_tensor", out=m1[:, 0:TT], in0=m1[:, 0:TT], in1=m2[:, 0:TT], op=ALU.add),
                      reads=m1.r() + m2.r(), writes=m1.r())
                    zz = m2
                    A("dve", OP("tensor_tensor_scan", out=zz[:, 0:TT], data0=s5p[:, G, 0:1].to_broadcast([128, TT]), data1=m1[:, 0:TT],
                                                                              initial=zcar[:, G:G + 1], op0=ALU.mult, op1=ALU.add),
                      reads=m1.r() + s5p.r() + zcar.r(), writes=zz.r())
                    A("act", OP("activation", out=zcar[:, G:G + 1], in_=zz[:, TT - 1:TT], func=AF.Copy), reads=zz.r(), writes=zcar.r())
                    w12 = m1
                    A("pool", OP("tensor_tensor", out=bf(w12, TT, 0), in0=zz[:, 0:TT], in1=tc_[:, 0:TT], op=ALU.mult),
                      reads=zz.r() + tc_.r(), writes=w12.r())
                    A("pool", OP("tensor_tensor", out=bf(w12, TT, TT), in0=zz[:, 0:TT], in1=ts_[:, 0:TT], op=ALU.mult),
                      reads=zz.r() + ts_.r(), writes=w12.r())
                    A("pe", OP("matmul", out=psum[:, by, :], lhsT=lhs_c[:, 0, g, :], rhs=bf(w12, TT, 0), start=(g == 0), stop=False),
                      reads=lhs_c.r() + w12.r(), writes=psum.r(by))
                    A("pe", OP("matmul", out=psum[:, by, :], lhsT=lhs_c[:, 1, g, :], rhs=bf(w12, TT, TT), start=False, stop=(g == 7)),
                      reads=lhs_c.r() + w12.r(), writes=psum.r(by))
                    ring.free(uu, kk, tc_, ts_)
                A("dve", OP("scalar_tensor_tensor", out=uf[:, 0:TT], in0=uf[:, 0:TT], scalar=PV(f"s5d{l}", c), in1=psum[:, by, :],
                                                                 op0=ALU.mult, op1=ALU.add), reads=uf.r() + psum.r(by) + pv.r(), writes=uf.r())
                banks.free(by)
                A("act", OP("activation", out=zbf[:, c, tsl(t)], in_=uf[:, 0:TT], func=AF.Gelu_apprx_tanh), reads=uf.r(), writes=aux.r(c * 4 + t))
                ring.free(uf, ub)
            wbufs.free(wb)
        dbg(f"s5z{l}", aux, aux[:], [128, 4 * SEQ], BF16)
        wg_t = wbufs.get()
        wgl = wg_t[:, 0:2048].rearrange("p (k n) -> p k n", k=4)
        load_w(wg_t, wgl, w_glu_d[l].rearrange("(k p) n -> p k n", p=128))
        for t in range(NT):
            for oc in range(4):
                b = banks.get()
                for k in range(4):
                    A("pe", OP("matmul", out=psum[:, b, :], lhsT=wgl[:, k, oc * 128:(oc + 1) * 128], rhs=zbf[:, k, tsl(t)],
                                                                start=(k == 0), stop=(k == 3)), reads=wg_t.r() + aux.r(k * 4 + t), writes=psum.r(b))
                sg = ring.get()
                A("act", OP("activation", out=sg[:, 0:TT], in_=psum[:, b, :], func=AF.Sigmoid, bias=PV(f"s5bg{l}", oc)),
                  reads=psum.r(b) + pv.r(), writes=sg.r())
                banks.free(b)
                A("dve", OP("tensor_tensor", out=mixed[:, oc, tsl(t)], in0=zbf[:, oc, tsl(t)], in1=sg[:, 0:TT], op=ALU.mult),
                  reads=sg.r() + aux.r(oc * 4 + t), writes=mixed.r(oc * 4 + t))
                ring.free(sg)
        wbufs.free(wg_t)
        group_post(l, mixed, "s5n", 1, wbufs, 512.0)
        dbg(f"ys5{l}", mixed, mixed[:].rearrange("p c t -> p (c t)"), [128, 4 * SEQ], BF16)
        if upto == "s5":
            break

        vh = aux[:].rearrange("p (n e) -> p n e", n=16)
        wv_t = wbufs.get()
        wvv = wv_t[:].rearrange("p (k n) -> p k n", k=8)
        load_w(wv_t, wvv, w_in_d[l, :, 2560:3072].rearrange("(k p) n -> p k n", p=128))
        for n in range(16):
            b = banks.get()
            t = n // 4
            for k in range(8):
                A("pe", OP("matmul", out=psum[:, b, :], lhsT=xn[:, k, n * 128:(n + 1) * 128], rhs=wvv[:, k, :],
                                                          start=(k == 0), stop=(k == 7)), reads=wv_t.r() + xs(k, t), writes=psum.r(b))
            for hd in range(4):
                A("act", OP("activation", out=vh[:, n, hd * 128:(hd + 1) * 128], in_=psum[:, b, hd * 128:(hd + 1) * 128],
                                                                 func=AF.Identity, scale=PV("vdec", hd)), reads=psum.r(b) + pv.r(), writes=aux.r(n))
            banks.free(b)
        wbufs.free(wv_t)
        wg_t = wbufs.get()
        wgv = wg_t[:].rearrange("p (k n) -> p k n", k=8)
        load_w(wg_t, wgv, w_in_d[l, :, 3072:3584].rearrange("(k p) n -> p k n", p=128))
        for hd in range(4):
            gam = GAMMAS[hd]
            gC = float(np.float32(np.exp(np.float32(np.log1p(-np.float32(2.0) ** np.float32(-5.0 - hd))) * np.float32(128.0))))
            wb = wbufs.get()
            wq = wb[:].rearrange("p (k n) -> p k n", k=8)
            qb = 1536 + hd * 128
            kb = 2048 + hd * 128
            src = lambda c0, c1: w_in_d[l, :, c0:c1].rearrange("(k p) n -> p k n", p=128)
            load_w(wb, wq[:, :, 0:128], src(qb, qb + 128))
            load_w(wb, wq[:, :, 128:192], src(qb + 64, qb + 128))
            load_w(wb, wq[:, :, 192:256], src(qb, qb + 64))
            load_w(wb, wq[:, :, 256:384], src(kb, kb + 128))
            load_w(wb, wq[:, :, 384:448], src(kb + 64, kb + 128))
            load_w(wb, wq[:, :, 448:512], src(kb, kb + 64))
            for t in range(NT):
                rot = []
                for which in range(2):
                    ba = banks.get(); bb = banks.get()
                    proj(ba, wb, lambda k, o=which * 256: wq[:, k, o:o + 128], t)
                    proj(bb, wb, lambda k, o=which * 256 + 128: wq[:, k, o:o + 128], t)
                    t1 = ring.get(); t2 = ring.get(); qr = ring.get()
                    A("dve", OP("tensor_tensor", out=t1[:, 0:TT], in0=psum[:, ba, :], in1=rcos[:, tsl(t)], op=ALU.mult),
                      reads=psum.r(ba) + rcos.r(t), writes=t1.r())
                    A("dve", OP("tensor_tensor", out=t2[:, 0:TT], in0=psum[:, bb, :], in1=rsin[:, tsl(t)], op=ALU.mult),
                      reads=psum.r(bb) + rsin.r(t), writes=t2.r())
                    banks.free(ba, bb)
                    A("pool", OP("tensor_tensor", out=bf(qr), in0=t1[:, 0:TT], in1=t2[:, 0:TT], op=ALU.add),
                      reads=t1.r() + t2.r(), writes=qr.r())
                    ring.free(t1, t2)
                    rot.append(qr)
                qr, kr = rot
                if hd == 0 and t == 0:
                    dbg(f"qr{l}", qr, bf(qr), [128, TT], BF16)
                    dbg(f"kr{l}", kr, bf(kr), [128, TT], BF16)
                bt = banks.get()
                ktp = psum[:, bt, :].bitcast(BF16)
                for j in range(4):
                    A("pe", OP("transpose", out=ktp[:, j * 128:(j + 1) * 128], in_=bf(kr, 128, j * 128), identity=ident_b[:]),
                      reads=kr.r() + ident_b.r(), writes=psum.r(bt))
                ktm = ring.get()
                A("act", OP("activation", out=bf(ktm), in_=ktp[:, 0:TT], func=AF.Copy), reads=psum.r(bt), writes=ktm.r())
                banks.free(bt)
                bs = banks.get()
                for j in range(4):
                    A("pe", OP("matmul", out=psum[:, bs, j * 128:(j + 1) * 128], lhsT=bf(kr, 128, j * 128), rhs=bf(qr, 128, j * 128),
                                                                  start=True, stop=True), reads=kr.r() + qr.r(), writes=psum.r(bs))
                pT = ring.get()
                A("dve", OP("tensor_tensor", out=bf(pT).rearrange("p (j c) -> p j c", j=4), in0=psum[:, bs, :].rearrange("p (j c) -> p j c", j=4),
                                                          in1=maskT.unsqueeze(1).to_broadcast([128, 4, 128]), op=ALU.mult),
                  reads=psum.r(bs) + cst.r(), writes=pT.r())
                banks.free(bs)
                bkv = banks.get()
                for j in range(4):
                    n = t * 4 + j
                    A("pe", OP("matmul", out=psum[:, bkv, j * 128:(j + 1) * 128], lhsT=bf(ktm, 128, j * 128),
                                                                  rhs=vh[:, n, hd * 128:(hd + 1) * 128], start=True, stop=True),
                      reads=ktm.r() + aux.r(n), writes=psum.r(bkv))
                ring.free(ktm)
                bxo = banks.get()
                for j in range(4):
                    n = t * 4 + j
                    pb = pbs[n % 2]
                    if n > 0:
                        A("act", OP("activation", out=pb[:], in_=ust[:], func=AF.Identity, scale=float(gC * (128.0 ** -0.5))),
                          reads=ust.r(), writes=pb.r())
                    A("pe", OP("matmul", out=psum[:, bxo, j * 128:(j + 1) * 128], lhsT=vh[:, n, hd * 128:(hd + 1) * 128],
                                                                rhs=bf(pT, 128, j * 128), start=True, stop=(n == 0)),
                      reads=aux.r(n) + pT.r(), writes=psum.r(bxo))
                    if n > 0:
                        A("pe", OP("matmul", out=psum[:, bxo, j * 128:(j + 1) * 128], lhsT=pb[:], rhs=bf(qr, 128, j * 128),
                                                                      start=False, stop=True), reads=pb.r() + qr.r(), writes=psum.r(bxo))
                        A("dve", OP("scalar_tensor_tensor", out=ust[:], in0=ust[:], scalar=gC, in1=psum[:, bkv, j * 128:(j + 1) * 128],
                                                                       op0=ALU.mult, op1=ALU.add), reads=ust.r() + psum.r(bkv), writes=ust.r())
                    else:
                        A("dve", OP("tensor_copy", out=ust[:], in_=psum[:, bkv, j * 128:(j + 1) * 128]), reads=psum.r(bkv), writes=ust.r())
                banks.free(bkv)
                ring.free(pT, kr)
                oT = ring.get()
                A("dve", OP("tensor_tensor", out=oT[:, 0:TT].rearrange("p (j c) -> p j c", j=4), in0=psum[:, bxo, :].rearrange("p (j c) -> p j c", j=4),
                                                          in1=cst[:, C_QDEC + hd * 128:C_QDEC + (hd + 1) * 128].unsqueeze(1).to_broadcast([128, 4, 128]), op=ALU.mult),
                  reads=psum.r(bxo) + cst.r(), writes=oT.r())
                banks.free(bxo)
                ring.free(qr)
                ob = ring.get()
                A("act", OP("activation", out=bf(ob, TT, 0), in_=oT[:, 0:TT], func=AF.Copy), reads=oT.r(), writes=ob.r())
                A("act", OP("activation", out=bf(ob, TT, TT), in_=oT[:, 0:TT], func=AF.Square), reads=oT.r(), writes=ob.r())
                bm = banks.get(); bq = banks.get()
                A("pe", OP("matmul", out=psum[:, bm, :], lhsT=o128_b[:], rhs=bf(ob, TT, 0), start=True, stop=True),
                  reads=ob.r() + o128_b.r(), writes=psum.r(bm))
                A("pe", OP("matmul", out=psum[:, bq, :], lhsT=o128_b[:], rhs=bf(ob, TT, TT), start=True, stop=True),
                  reads=ob.r() + o128_b.r(), writes=psum.r(bq))
                ring.free(ob)
                m2 = ring.get()
                A("act", OP("activation", out=m2[:, 0:TT], in_=psum[:, bm, :], func=AF.Square), reads=psum.r(bm), writes=m2.r())
                A("dve", OP("tensor_tensor", out=m2[:, 0:TT], in0=psum[:, bq, :], in1=m2[:, 0:TT], op=ALU.subtract),
                  reads=psum.r(bq) + m2.r(), writes=m2.r())
                banks.free(bq)
                A("dve", OP("tensor_scalar", out=m2[:, 0:TT], in0=m2[:, 0:TT], scalar1=0.0, scalar2=None, op0=ALU.max), reads=m2.r(), writes=m2.r())
                A("act", OP("activation", out=m2[:, 0:TT], in_=m2[:, 0:TT], func=AF.Sqrt, bias=NORM_EPS), reads=m2.r(), writes=m2.r())
                A("dve", OP("reciprocal", out=m2[:, 0:TT], in_=m2[:, 0:TT]), reads=m2.r(), writes=m2.r())
                A("dve", OP("tensor_tensor", out=oT[:, 0:TT], in0=oT[:, 0:TT], in1=psum[:, bm, :], op=ALU.subtract),
                  reads=oT.r() + psum.r(bm), writes=oT.r())
                banks.free(bm)
                A("dve", OP("tensor_tensor", out=oT[:, 0:TT], in0=oT[:, 0:TT], in1=m2[:, 0:TT], op=ALU.mult),
                  reads=oT.r() + m2.r(), writes=oT.r())
                ring.free(m2)
                bg = banks.get()
                proj(bg, wg_t, lambda k: wgv[:, k, hd * 128:(hd + 1) * 128], t)
                sg = ring.get()
                A("act", OP("activation", out=sg[:, 0:TT], in_=psum[:, bg, :], func=AF.Silu), reads=psum.r(bg), writes=sg.r())
                banks.free(bg)
                A("dve", OP("scalar_tensor_tensor", out=mixed[:, hd, tsl(t)], in0=oT[:, 0:TT], scalar=PV(f"rnorm{l}", hd), in1=sg[:, 0:TT],
                                                                        op0=ALU.mult, op1=ALU.mult), reads=oT.r() + sg.r() + pv.r(), writes=mixed.r(hd * 4 + t))
                ring.free(oT, sg)
            wbufs.free(wb)
        wbufs.free(wg_t)
        dbg(f"yret{l}", mixed, mixed[:].rearrange("p c t -> p (c t)"), [128, 4 * SEQ], BF16)
        group_post(l, mixed, None, 2, wbufs, 512.0)
        dbg(f"hmix{l}", hT, hT[:].rearrange("p c t -> p (c t)"), [128, 8 * SEQ])
        for cm in reversed(phase):
            cm.__exit__(None, None, None)
        phase = []
        fw.barrier()
        if upto == "mix":
            break

        rmsnorm_to_xn(f"gffn{l}")
        actb = psb("actb", [128, 4, SEQ], BF16, nslots=16)
        wus = Pool([psb(f"wu{i}", [128, 8, 1024], BF16) for i in range(2)])
        wds = Pool([psb(f"wd{i}", [128, 4, 1024], BF16) for i in range(2)])
        fcar = psb("fcar", [128, 2, 2], F32)
        for grp in range(6):
            j0 = grp * 4
            wu = wus.get(); wd = wds.get()
            load_w(wu, wu[:, :, 0:512], w_up_d[l, :, j0 * 128:(j0 + 4) * 128].rearrange("(k p) n -> p k n", p=128))
            load_w(wu, wu[:, :, 512:1024], w_up_d[l, :, D_FF + j0 * 128:D_FF + (j0 + 4) * 128].rearrange("(k p) n -> p k n", p=128))
            load_w(wd, wd[:], w_down_d[l, j0 * 128:(j0 + 4) * 128, :].rearrange("(k p) n -> p k n", p=128))
            for jj in range(4):
                j = j0 + jj
                for t in range(NT):
                    outs = []
                    for which in range(2):
                        ch = j + 24 * which
                        b = banks.get()
                        proj(b, wu, lambda k, o=which * 512 + jj * 128: wu[:, k, o:o + 128], t)
                        vc = ring.get()
                        w0 = PV(f"fcw{l}", ch * 3 + 0); w1 = PV(f"fcw{l}", ch * 3 + 1); w2 = PV(f"fcw{l}", ch * 3 + 2)
                        A("act", OP("activation", out=vc[:, 0:TT], in_=psum[:, b, :], func=AF.Identity, scale=w2,
                                                                                  bias=PV(f"fcb{l}", ch)), reads=psum.r(b) + pv.r(), writes=vc.r())
                        A("dve", OP("scalar_tensor_tensor", out=vc[:, 1:TT], in0=psum[:, b, 0:TT - 1], scalar=w1, in1=vc[:, 1:TT],
                                                                                    op0=ALU.mult, op1=ALU.add), reads=psum.r(b) + vc.r() + pv.r(), writes=vc.r())
                        A("dve", OP("scalar_tensor_tensor", out=vc[:, 2:TT], in0=psum[:, b, 0:TT - 2], scalar=w0, in1=vc[:, 2:TT],
                                                                                    op0=ALU.mult, op1=ALU.add), reads=psum.r(b) + vc.r() + pv.r(), writes=vc.r())
                        if t > 0:
                            A("dve", OP("scalar_tensor_tensor", out=vc[:, 0:1], in0=fcar[:, which, 1:2], scalar=w1, in1=vc[:, 0:1],
                                                                                                op0=ALU.mult, op1=ALU.add), reads=fcar.r() + vc.r() + pv.r(), writes=vc.r())
                            A("dve", OP("scalar_tensor_tensor", out=vc[:, 0:2], in0=fcar[:, which, 0:2], scalar=w0, in1=vc[:, 0:2],
                                                                                                op0=ALU.mult, op1=ALU.add), reads=fcar.r() + vc.r() + pv.r(), writes=vc.r())
                        if t < NT - 1:
                            A("act", OP("activation", out=fcar[:, which, :], in_=psum[:, b, TT - 2:TT], func=AF.Copy),
                              reads=psum.r(b), writes=fcar.r())
                        banks.free(b)
                        outs.append(vc)
                    vc, gc = outs
                    A("act", OP("activation", out=gc[:, 0:TT], in_=gc[:, 0:TT], func=AF.Gelu_apprx_tanh), reads=gc.r(), writes=gc.r())
                    A("dve", OP("tensor_tensor", out=actb[:, jj, tsl(t)], in0=gc[:, 0:TT], in1=vc[:, 0:TT], op=ALU.mult),
                      reads=vc.r() + gc.r(), writes=actb.r(jj * 4 + t))
                    ring.free(vc, gc)
            for t in range(NT):
                for dc in range(8):
                    b = banks.get()
                    for k in range(4):
                        A("pe", OP("matmul", out=psum[:, b, :], lhsT=wd[:, k, dc * 128:(dc + 1) * 128], rhs=actb[:, k, tsl(t)],
                                                                    start=(k == 0), stop=(k == 3)), reads=wd.r() + actb.r(k * 4 + t), writes=psum.r(b))
                    A("dve", OP("tensor_tensor", out=hT[:, dc, tsl(t)], in0=hT[:, dc, tsl(t)], in1=psum[:, b, :], op=ALU.add),
                      reads=hs(dc, t) + psum.r(b), writes=hs(dc, t))
                    banks.free(b)
            wus.free(wu); wds.free(wd)
        dbg(f"hffn{l}", hT, hT[:].rearrange("p c t -> p (c t)"), [128, 8 * SEQ])
        for cm in reversed(phase):
            cm.__exit__(None, None, None)
        phase = []
        fw.barrier()

    for t in range(NT):
        b = banks.get()
        for c in range(8):
            sq = ring.get()
            A("act", OP("activation", out=bf(sq), in_=hT[:, c, tsl(t)], func=AF.Square), reads=hs(c, t), writes=sq.r())
            A("pe", OP("matmul", out=psum[:, b, :], lhsT=ones_b[:], rhs=bf(sq), start=(c == 0), stop=(c == 7)),
              reads=sq.r() + ones_b.r(), writes=psum.r(b))
            ring.free(sq)
        rs = ring.get()
        A("act", OP("activation", out=rs[:, 0:TT], in_=psum[:, b, :], func=AF.Sqrt, scale=1.0 / D_MODEL, bias=NORM_EPS), reads=psum.r(b), writes=rs.r())
        banks.free(b)
        A("dve", OP("reciprocal", out=rs[:, 0:TT], in_=rs[:, 0:TT]), reads=rs.r(), writes=rs.r())
        for c in range(8):
            ot = ring.get()
            A("dve", OP("scalar_tensor_tensor", out=ot[:, 0:TT], in0=hT[:, c, tsl(t)], scalar=PV("gfin", c), in1=rs[:, 0:TT],
                                                                         op0=ALU.mult, op1=ALU.mult), reads=hs(c, t) + rs.r() + pv.r(), writes=ot.r())
            ins = A("sp", OP("dma_start", out=outT_d[c * 128:(c + 1) * 128, tsl(t)], in_=ot[:, 0:TT]), reads=ot.r(), dma=True)
            out_dmas.append(ins)
            ring.free(ot)
        ring.free(rs)

    for cm in reversed(phase):
        cm.__exit__(None, None, None)
    fin = A("sp", None)
    for ins in out_dmas:
        fw._need(fin, f"d{ins.dma_sem}", ins.dma_use)
    fw.emit()
    fw.close()
    return nc, dbg_out


def make_in_maps(inputs):
    inp = {k: np.asarray(v) for k, v in inputs.items()}
    pv = host_pv(inp)
    cst = host_consts()
    wbd, x1, x2, c1, c2 = host_struct(inp)
    shared = {
        "pv": pv, "cst": cst,
        "wbd": np.ascontiguousarray(wbd.reshape(DEPTH, 128, 1024)),
        "s5x1": x1, "s5x2": x2,
        "s5c1": np.ascontiguousarray(c1.reshape(DEPTH, 128, 512)),
        "s5c2": np.ascontiguousarray(c2.reshape(DEPTH, 128, 512)),
        "w_in": np.ascontiguousarray(inp["w_in"], dtype=np.float32),
        "s5_w_glu": np.ascontiguousarray(inp["s5_w_glu"], dtype=np.float32),
        "w_out": np.ascontiguousarray(inp["w_out"], dtype=np.float32),
        "w_up": np.ascontiguousarray(inp["w_up"], dtype=np.float32),
        "w_down": np.ascontiguousarray(inp["w_down"], dtype=np.float32),
    }
    maps = []
    for b in range(inp["x"].shape[0]):
        m = dict(shared)
        m["xT"] = np.ascontiguousarray(inp["x"][b].T.astype(np.float32))
        m["pos"] = np.ascontiguousarray(inp["positions"][b].astype(np.int32).reshape(1, SEQ))
        maps.append(m)
    return maps


_NC_CACHE = {}


def kernel(**inputs):
    if "nc" not in _NC_CACHE:
        _NC_CACHE["nc"] = build()[0]
    nc = _NC_CACHE["nc"]
    maps = make_in_maps(inputs)
    res = run_bass_kernel_spmd(nc, maps, core_ids=list(range(len(maps))))
    out = np.stack([np.ascontiguousarray(r["outT"].T) for r in res.results], axis=0)
    return out.astype(np.float32)
```

```python
import math
import os
from collections import deque
import numpy as np
import concourse.bass as bass
import concourse.mybir as mybir
from concourse.bass_utils import run_bass_kernel_spmd

F32 = mybir.dt.float32
BF16 = mybir.dt.bfloat16
I32 = mybir.dt.int32
AF = mybir.ActivationFunctionType
ALU = mybir.AluOpType

ENGS = ("pe", "act", "dve", "pool", "sp")
N_DMA_SEMS = 24

D_MODEL = 1024
SEQ = 2048
DEPTH = 2
TT = 512
NT = SEQ // TT
IN_WIDTH = 3584
D_FF = 3072
NORM_EPS = 1e-6
MAGIC = 12582912.0
TWO_PI_S = 6.2831850
GAMMAS = [1.0 - 2.0 ** (-5.0 - h) for h in range(4)]


class Reg:
    __slots__ = ("name", "w", "rds")

    def __init__(self, name):
        self.name = name
        self.w = None
        self.rds = []


class T:
    def __init__(self, h, name, nslots=1):
        self.h = h
        self.name = name
        self.regs = [Reg(f"{name}.{i}") for i in range(nslots)]

    def __getitem__(self, k):
        return self.h[k]

    def r(self, i=None, j=None):
        if i is None:
            return list(self.regs)
        if j is None:
            return [self.regs[i]]
        return self.regs[i:j]


class Ins:
    __slots__ = ("eng", "rec", "fn", "preds", "is_dma", "dma_sem", "dma_use", "cost", "lat", "seg",
                 "tset", "sched", "done", "rt", "pos", "waits", "needs_inc", "count", "waits_dma", "pin")

    def __init__(self, eng, rec, fn):
        self.eng = eng
        self.rec = rec
        self.fn = fn
        self.preds = {}
        self.is_dma = False
        self.dma_sem = None
        self.dma_use = None
        self.cost = 100.0
        self.lat = 0.0
        self.seg = 0
        self.tset = None
        self.sched = False
        self.done = 0.0
        self.rt = None
        self.pos = None
        self.waits = []
        self.needs_inc = False
        self.count = None
        self.pin = False


_ACT_GROUP = {}


def _act_group(func):
    if not _ACT_GROUP:
        _ACT_GROUP.update({AF.Exp: 1, AF.Ln: 1, AF.Gelu_apprx_tanh: 2, AF.Silu: 3, AF.Sin: 3, AF.Sigmoid: 4, AF.Sqrt: 5})
    return _ACT_GROUP.get(func)


def _free(ap):
    n = 1
    for d in ap.shape[1:]:
        n *= int(d)
    return n


def est_cost(eng, fn, is_dma):
    if fn is None:
        return 0.0, 0.0
    name, args, kw = fn
    if is_dma:
        out = kw["out"]
        nbytes = _free(out) * int(out.shape[0]) * 4
        return (1000.0 if eng == "pool" else 120.0), 2000.0 + nbytes / 150.0
    if eng == "pe":
        if name == "transpose":
            return 110.0, 0.0
        n = _free(kw["rhs"])
        return max(n, 64) / 1.9 + 10.0, 0.0
    out = kw.get("out")
    if out is None:
        out = args[0]
    n = _free(out)
    if eng == "act":
        return 150.0 + n / 1.2, 0.0
    if eng == "dve":
        if name == "tensor_tensor_scan":
            return 120.0 + 2.0 * n / 0.96, 0.0
        if name == "reciprocal":
            return 120.0 + 6.1 * n, 0.0
        if name in ("tensor_tensor", "scalar_tensor_tensor"):
            return 120.0 + n / 0.96, 0.0
        return 120.0 + n / 1.5, 0.0
    if eng == "pool":
        if name == "tensor_tensor":
            return 150.0 + 2.15 * n, 0.0
        return 150.0 + 1.2 * n, 0.0
    return 100.0, 0.0


class FW:
    SEM_LAT = 80.0
    WINDOW = int(os.environ.get('KW', '40'))
    WIN_ENG = {e: int(os.environ.get('KW_' + e.upper(), '0')) for e in ('pe', 'act', 'dve', 'pool', 'sp')}

    def __init__(self, nc):
        self.nc = nc
        self.all = []
        self.dma_rr = 0
        self.dma_rr_pool = 0
        self.dma_uses = [0] * N_DMA_SEMS
        self.dma_last = [None] * N_DMA_SEMS
        self.seg = 0
        self.seg_dma_uses = []
        self.pool_dmas = []
        self.pin = set()
        self._stack = []

    def sb(self, name, shape, dtype, nslots=1):
        cm = self.nc.sbuf_tensor(name, list(shape), dtype)
        h = cm.__enter__()
        self._stack.append(cm)
        return T(h, name, nslots)

    def ps(self, name, shape, dtype, nslots=1):
        cm = self.nc.psum_tensor(name, list(shape), dtype)
        h = cm.__enter__()
        self._stack.append(cm)
        return T(h, name, nslots)

    @staticmethod
    def _add_pred(ins, p, kind):
        if p is None or p is ins:
            return
        if p.is_dma or ins.is_dma or p.eng != ins.eng:
            needs = True
        else:
            needs = (ins.eng != "pe")
        ins.preds[p] = ins.preds.get(p, False) or needs

    def op(self, eng, fn, reads=(), writes=(), dma=False):
        ins = Ins(eng, len(self.all), fn)
        ins.seg = self.seg
        ins.is_dma = dma
        ins.pin = eng in self.pin
        self.all.append(ins)
        ins.cost, ins.lat = est_cost(eng, fn, dma)
        if eng == "act" and fn is not None and fn[0] == "activation":
            ins.tset = _act_group(fn[2].get("func"))
        if dma:
            half = N_DMA_SEMS // 2
            if eng == "pool":
                k = half + self.dma_rr_pool
                self.dma_rr_pool = (self.dma_rr_pool + 1) % half
            else:
                k = self.dma_rr
                self.dma_rr = (k + 1) % half
            self._add_pred(ins, self.dma_last[k], "SEM")
            self.dma_last[k] = ins
            self.dma_uses[k] += 1
            ins.dma_sem = k
            ins.dma_use = self.dma_uses[k]
            if eng == "pool":
                self.pool_dmas.append(ins)
                if len(self.pool_dmas) > 4:
                    self._add_pred(ins, self.pool_dmas[-5], "SEM")
        for r in reads:
            self._add_pred(ins, r.w, "RAW")
        for r in writes:
            self._add_pred(ins, r.w, "WAW")
            for p in r.rds:
                self._add_pred(ins, p, "WAR")
        for r in reads:
            r.rds.append(ins)
        for r in writes:
            r.w = ins
            r.rds = []
        return ins

    def barrier(self):
        self.seg_dma_uses.append(list(self.dma_uses))
        self.seg += 1

    def schedule(self):
        nseg = self.seg + 1
        final = {e: [] for e in ENGS}
        eng_free = {e: 0.0 for e in ENGS}
        cur_set = None
        bar_marks = []
        for sg in range(nseg):
            pending = {e: [i for i in self.all if i.seg == sg and i.eng == e] for e in ENGS}
            remaining = sum(len(v) for v in pending.values())
            while remaining:
                best = None
                best_t = None
                for e in ENGS:
                    lst = pending[e]
                    ef = eng_free[e]
                    lim = min(len(lst), self.WIN_ENG[e] or self.WINDOW)
                    if lim and lst[0].pin:
                        lim = 1
                    for wi in range(lim):
                        c = lst[wi]
                        if wi > 0 and c.pin:
                            break
                        if c.rt is None:
                            rt = 0.0
                            ok = True
                            for p, needs in c.preds.items():
                                if not p.sched:
                                    ok = False
                                    break
                                d = p.done + (self.SEM_LAT if needs else 0.0)
                                if d > rt:
                                    rt = d
                            if not ok:
                                continue
                            c.rt = rt
                        t = c.rt if c.rt > ef else ef
                        if e == "act" and c.tset is not None and c.tset != cur_set:
                            t += 1300.0
                        if best is None or t < best_t or (t == best_t and c.rec < best[1].rec):
                            best = (e, c, wi)
                            best_t = t
                        if t <= ef:
                            break
                assert best is not None, "scheduler deadlock"
                e, c, wi = best
                pending[e].pop(wi)
                remaining -= 1
                if e == "act" and c.tset is not None:
                    cur_set = c.tset
                end = best_t + c.cost
                eng_free[e] = end
                c.done = end + c.lat
                c.sched = True
                c.pos = len(final[e])
                final[e].append(c)
            if sg < nseg - 1:
                tmax = max(eng_free.values())
                dmax = max([i.done for i in self.all if i.seg == sg and i.is_dma] + [0.0])
                tmax = max(tmax, dmax)
                marks = {}
                for e in ENGS:
                    b = Ins(e, -1, None)
                    b.sched = True
                    b.pos = len(final[e])
                    final[e].append(b)
                    marks[e] = b
                    eng_free[e] = tmax
                bar_marks.append(marks)
        self.final = final
        self.bar_marks = bar_marks
        self.est_total = max(eng_free.values())

    def emit(self):
        nc = self.nc
        self.schedule()
        final = self.final
        for bi, marks in enumerate(self.bar_marks):
            for e, b in marks.items():
                for e2 in ENGS:
                    if e2 == e:
                        continue
                    pos2 = marks[e2].pos
                    for j in range(pos2 - 1, -1, -1):
                        p = final[e2][j]
                        if p.fn is not None and not p.is_dma:
                            b.preds[p] = True
                            break
                b.waits_dma = self.seg_dma_uses[bi]
        for e in ENGS:
            seen = {}
            for ins in final[e]:
                for p, needs in ins.preds.items():
                    if not needs:
                        assert p.eng == e and p.pos < ins.pos, "ordering edge violated"
                        continue
                    if p.is_dma:
                        key = f"d{p.dma_sem}"
                        v = p.dma_use
                    else:
                        key = p.eng
                        v = p.pos
                        if p.eng == e:
                            assert p.pos < ins.pos
                    if seen.get(key, -1) >= v:
                        continue
                    seen[key] = v
                    ins.waits.append((key, p))
                    if not p.is_dma:
                        p.needs_inc = True
                wd = getattr(ins, "waits_dma", None) if ins.fn is None else None
                if wd is not None:
                    for k, u in enumerate(wd):
                        if u > 0 and seen.get(f"d{k}", -1) < u:
                            seen[f"d{k}"] = u
                            ins.waits.append((f"d{k}", u))
        sem_cms = []
        sems = {}
        for e in list(ENGS) + [f"d{k}" for k in range(N_DMA_SEMS)]:
            cm = nc.semaphore(f"s_{e}")
            sems[e] = cm.__enter__()
            sem_cms.append(cm)
        for e in ENGS:
            c = 0
            for ins in final[e]:
                if ins.needs_inc:
                    c += 1
                    ins.count = c

        def run(eng_name, eng):
            for ins in final[eng_name]:
                for (key, p) in ins.waits:
                    if isinstance(p, int):
                        eng.wait_ge(sems[key], 16 * p)
                    elif p.is_dma:
                        eng.wait_ge(sems[key], 16 * p.dma_use)
                    else:
                        eng.wait_ge(sems[key], p.count)
                if ins.fn is None:
                    continue
                name, args, kw = ins.fn
                bi = getattr(eng, name)(*args, **kw)
                if ins.dma_sem is not None:
                    bi.then_inc(sems[f"d{ins.dma_sem}"], 16)
                elif ins.needs_inc:
                    bi.then_inc(sems[eng_name], 1)

        with nc.Block() as block:
            @block.tensor
            def _(eng):
                run("pe", eng)

            @block.scalar
            def _(eng):
                run("act", eng)

            @block.vector
            def _(eng):
                run("dve", eng)

            @block.gpsimd
            def _(eng):
                run("pool", eng)

            @block.sync
            def _(eng):
                run("sp", eng)
        for cm in reversed(sem_cms):
            cm.__exit__(None, None, None)

    def close(self):
        for cm in reversed(self._stack):
            cm.__exit__(None, None, None)
        self._stack = []


def OP(name, *args, **kw):
    return (name, args, kw)


class Pool:
    def __init__(self, items):
        self.free_ = deque(items)
        self.n = len(items)

    def get(self):
        assert self.free_, "scratch pool exhausted"
        return self.free_.popleft()

    def free(self, *items):
        for it in items:
            self.free_.append(it)


def _chunked(v, nchunk):
    return np.ascontiguousarray(np.asarray(v, np.float32).reshape(nchunk, 128).T)


class PVLayout:
    def __init__(self):
        self.idx = {}
        self.n = 0

    def add(self, name, k):
        self.idx[name] = (self.n, k)
        self.n += k


def pv_layout():
    pl = PVLayout()
    for l in range(DEPTH):
        for name, k in (("gmix", 8), ("gffn", 8), ("lcw", 16), ("lcb", 4), ("lba", 4), ("lbx", 4),
                        ("llam", 4), ("lnorm", 4), ("s5d", 4), ("s5bg", 4), ("s5n", 4), ("rnorm", 4),
                        ("fcw", 144), ("fcb", 48), ("s5lr", 32), ("s5li", 32), ("s5ldt", 32)):
            pl.add(f"{name}{l}", k)
    for name, k in (("gfin", 8), ("invf", 1), ("sgn", 1), ("nsgn", 1), ("vdec", 4), ("rowmask", 8)):
        pl.add(name, k)
    return pl


PVL = pv_layout()
C_ID = 0
C_IOTA = 128
C_MASK = C_IOTA + 512
C_QDEC = C_MASK + 128
NCST = C_QDEC + 512


def host_consts():
    cst = np.zeros((128, NCST), np.float32)
    cst[:, C_ID:C_ID + 128] = np.eye(128, dtype=np.float32)
    cst[:, C_IOTA:C_IOTA + 512] = np.arange(512, dtype=np.float32)[None, :]
    m = np.arange(128)[:, None]
    c = np.arange(128)[None, :]
    cst[:, C_MASK:C_MASK + 128] = np.where(c >= m, np.float32(128.0 ** -0.5), np.float32(0.0))
    for h in range(4):
        lg = np.log1p(-np.float32(2.0) ** np.float32(-5.0 - h)).astype(np.float32)
        cst[:, C_QDEC + h * 128:C_QDEC + (h + 1) * 128] = np.exp(lg * (np.arange(128, dtype=np.float32) + 1.0))[None, :]
    return cst


def host_pv(inp):
    pv = np.zeros((128, PVL.n), np.float32)

    def put(name, arr):
        o, k = PVL.idx[name]
        assert arr.shape == (128, k), (name, arr.shape, k)
        pv[:, o:o + k] = arr

    for l in range(DEPTH):
        put(f"gmix{l}", _chunked(inp["norm_mix"][l], 8))
        put(f"gffn{l}", _chunked(inp["norm_ffn"][l], 8))
        cw = np.asarray(inp["lru_conv_w"][l], np.float32)
        put(f"lcw{l}", np.ascontiguousarray(cw.reshape(4, 4, 128).transpose(2, 1, 0).reshape(128, 16)))
        put(f"lcb{l}", _chunked(inp["lru_conv_b"][l], 4))
        put(f"lba{l}", _chunked(np.asarray(inp["lru_ba"][l]).reshape(512), 4))
        put(f"lbx{l}", _chunked(np.asarray(inp["lru_bx"][l]).reshape(512), 4))
        put(f"llam{l}", _chunked(inp["lru_lambda"][l], 4))
        put(f"lnorm{l}", _chunked(inp["lru_norm"][l], 4))
        put(f"s5d{l}", _chunked(inp["s5_d"][l], 4))
        put(f"s5bg{l}", _chunked(inp["s5_b_glu"][l], 4))
        put(f"s5n{l}", _chunked(inp["s5_norm"][l], 4))
        put(f"rnorm{l}", _chunked(inp["ret_norm"][l], 4))
        fw_ = np.asarray(inp["ffn_conv_w"][l], np.float32)
        put(f"fcw{l}", np.ascontiguousarray(fw_.reshape(3, 48, 128).transpose(2, 1, 0).reshape(128, 144)))
        put(f"fcb{l}", _chunked(inp["ffn_conv_b"][l], 48))
        lr = np.asarray(inp["s5_lambda_re"][l], np.float32).T
        li = np.asarray(inp["s5_lambda_im"][l], np.float32).T
        put(f"s5lr{l}", np.concatenate([lr, lr], 0))
        put(f"s5li{l}", np.concatenate([li, li], 0))
        put(f"s5ldt{l}", np.broadcast_to(np.asarray(inp["s5_log_dt"][l], np.float32)[None, :], (128, 32)).copy())
    put("gfin", _chunked(inp["norm_final"], 8))
    half = 64
    inv = (np.float32(10000.0) ** (-np.arange(half, dtype=np.float32) * np.float32(2.0) / np.float32(128.0))).astype(np.float32)
    put("invf", np.concatenate([inv, inv])[:, None])
    sgn = np.concatenate([-np.ones(64, np.float32), np.ones(64, np.float32)])[:, None]
    put("sgn", sgn)
    put("nsgn", -sgn)
    vd = np.zeros((128, 4), np.float32)
    for h in range(4):
        lg = np.log1p(-np.float32(2.0) ** np.float32(-5.0 - h)).astype(np.float32)
        vd[:, h] = np.exp(-lg * (np.arange(128, dtype=np.float32) + 1.0))
    put("vdec", vd)
    rm = np.zeros((128, 8), np.float32)
    for g in range(8):
        rm[16 * g:16 * g + 16, g] = 1.0
    put("rowmask", rm)
    return pv


def host_struct(inp):
    wbd = np.zeros((DEPTH, 128, 2, 4, 128), np.float32)
    for l in range(DEPTH):
        for which, nm in enumerate(("lru_wa", "lru_wx")):
            w = np.asarray(inp[nm][l], np.float32)
            for c in range(4):
                for hb in range(2):
                    wbd[l, hb * 64:(hb + 1) * 64, which, c, hb * 64:(hb + 1) * 64] = w[2 * c + hb]
    x1 = np.zeros((DEPTH, 128, 512), np.float32)
    x2 = np.zeros((DEPTH, 128, 512), np.float32)
    c1 = np.zeros((DEPTH, 128, 4, 128), np.float32)
    c2 = np.zeros((DEPTH, 128, 4, 128), np.float32)
    for l in range(DEPTH):
        br = np.asarray(inp["s5_b_re"][l], np.float32).transpose(1, 0, 2).reshape(64, 512)
        bi = np.asarray(inp["s5_b_im"][l], np.float32).transpose(1, 0, 2).reshape(64, 512)
        x1[l] = np.concatenate([br, bi], 0)
        x2[l] = np.concatenate([bi, br], 0)
        cr = np.asarray(inp["s5_c_re"][l], np.float32).reshape(4, 128, 64)
        ci = np.asarray(inp["s5_c_im"][l], np.float32).reshape(4, 128, 64)
        c1[l] = np.concatenate([cr, ci], 2).transpose(1, 0, 2)
        c2[l] = np.concatenate([ci, cr], 2).transpose(1, 0, 2)
    return wbd, x1, x2, c1, c2


def build(debug=(), upto="all", nlayers=DEPTH):
    nc = bass.Bass("TRN2", target_bir_lowering=False)
    fw = FW(nc)
    dram = {}

    def din(name, shape, dtype=F32):
        dram[name] = nc.dram_tensor(name, list(shape), dtype, kind="ExternalInput").ap()
        return dram[name]

    xT_d = din("xT", [D_MODEL, SEQ])
    pos_d = din("pos", [1, SEQ], I32)
    pv_d = din("pv", [128, PVL.n])
    cst_d = din("cst", [128, NCST])
    wbd_d = din("wbd", [DEPTH, 128, 2 * 4 * 128])
    x1_d = din("s5x1", [DEPTH, 128, 512])
    x2_d = din("s5x2", [DEPTH, 128, 512])
    c1_d = din("s5c1", [DEPTH, 128, 512])
    c2_d = din("s5c2", [DEPTH, 128, 512])
    w_in_d = din("w_in", [DEPTH, D_MODEL, IN_WIDTH])
    w_glu_d = din("s5_w_glu", [DEPTH, 512, 512])
    w_out_d = din("w_out", [DEPTH, 1536, D_MODEL])
    w_up_d = din("w_up", [DEPTH, D_MODEL, 2 * D_FF])
    w_down_d = din("w_down", [DEPTH, D_FF, D_MODEL])
    outT_d = nc.dram_tensor("outT", [D_MODEL, SEQ], F32, kind="ExternalOutput").ap()
    dbg_out = {}
    out_dmas = []

    A = fw.op

    hT = fw.sb("hT", [128, 8, SEQ], F32, nslots=32)
    xn = fw.sb("xn", [128, 8, SEQ], BF16, nslots=32)
    rcos = fw.sb("rcos", [128, SEQ], F32, nslots=4)
    rsin = fw.sb("rsin", [128, SEQ], F32, nslots=4)
    pv = fw.sb("pvs", [128, PVL.n], F32)
    cst = fw.sb("csts", [128, NCST], F32)
    ident_b = fw.sb("ident_b", [128, 128], BF16)
    ones_b = fw.sb("ones_b", [128, 128], BF16)
    o128_b = fw.sb("o128_b", [128, 128], BF16)
    NRING = 10
    ring = Pool([fw.sb(f"ring{i}", [128, 520], F32) for i in range(NRING)])
    psum = fw.ps("psum", [128, 8, 512], F32, nslots=8)
    banks = Pool(list(range(8)))

    def PV(name, c=0, n=1):
        o, k = PVL.idx[name]
        return pv[:, o + c:o + c + n]

    def hs(c, t):
        return hT.r(c * 4 + t)

    def xs(c, t):
        return xn.r(c * 4 + t)

    def tsl(t):
        return slice(t * TT, (t + 1) * TT)

    def bf(tile, n=TT, off=0):
        return tile[:].bitcast(BF16)[:, off:off + n]

    def dbg(name, tile, ap, shape, dtype=F32):
        if name not in debug:
            return
        d = nc.dram_tensor("dbg_" + name, list(shape), dtype, kind="ExternalOutput").ap()
        dbg_out[name] = d
        ins = A("sp", OP("dma_start", out=d, in_=ap), reads=tile.r(), dma=True)
        out_dmas.append(ins)

    A("sp", OP("dma_start", out=pv[:], in_=pv_d), writes=pv.r(), dma=True)
    A("sp", OP("dma_start", out=cst[:], in_=cst_d), writes=cst.r(), dma=True)
    for c in range(8):
        for t in range(NT):
            A("sp", OP("dma_start", out=hT[:, c, tsl(t)], in_=xT_d[c * 128:(c + 1) * 128, tsl(t)]),
              writes=hs(c, t), dma=True)
    A("dve", OP("memset", ones_b[:], 1.0), writes=ones_b.r())
    A("dve", OP("memset", o128_b[:], 1.0 / 128.0), writes=o128_b.r())
    A("act", OP("activation", out=ident_b[:], in_=cst[:, C_ID:C_ID + 128], func=AF.Copy),
      reads=cst.r(), writes=ident_b.r())
    ident_f = cst[:, C_ID:C_ID + 128]
    iota = cst[:, C_IOTA:C_IOTA + 512]
    maskT = cst[:, C_MASK:C_MASK + 128]

    for t in range(NT):
        pi_ = ring.get(); pf = ring.get(); k_ = ring.get()
        A("sp", OP("dma_start", out=pi_[:].bitcast(I32)[:, 0:TT],
                                                      in_=pos_d[0:1, tsl(t)].to_broadcast([128, TT])),
          writes=pi_.r(), dma=True)
        A("dve", OP("tensor_copy", out=pf[:, 0:TT], in_=pi_[:].bitcast(I32)[:, 0:TT]),
          reads=pi_.r(), writes=pf.r())
        A("dve", OP("tensor_scalar", out=pf[:, 0:TT], in0=pf[:, 0:TT], scalar1=PV("invf"), scalar2=None,
                                                  op0=ALU.mult), reads=pf.r() + pv.r(), writes=pf.r())
        A("dve", OP("tensor_scalar", out=k_[:, 0:TT], in0=pf[:, 0:TT], scalar1=1.0 / (2.0 * math.pi),
                                                         scalar2=MAGIC, op0=ALU.mult, op1=ALU.add),
          reads=pf.r(), writes=k_.r())
        A("dve", OP("tensor_scalar", out=k_[:, 0:TT], in0=k_[:, 0:TT], scalar1=MAGIC, scalar2=None,
                                                  op0=ALU.subtract), reads=k_.r(), writes=k_.r())
        C1 = 6.28125
        C2 = 2.0 * math.pi - 6.28125
        A("dve", OP("scalar_tensor_tensor", out=pf[:, 0:TT], in0=k_[:, 0:TT], scalar=-C1, in1=pf[:, 0:TT],
                                                                op0=ALU.mult, op1=ALU.add), reads=pf.r() + k_.r(), writes=pf.r())
        A("dve", OP("scalar_tensor_tensor", out=pf[:, 0:TT], in0=k_[:, 0:TT], scalar=-C2, in1=pf[:, 0:TT],
                                                                op0=ALU.mult, op1=ALU.add), reads=pf.r() + k_.r(), writes=pf.r())
        A("dve", OP("tensor_scalar", out=pf[:, 0:TT], in0=pf[:, 0:TT], scalar1=3.1415925, scalar2=-3.1415925,
                                                  op0=ALU.min, op1=ALU.max), reads=pf.r(), writes=pf.r())
        A("act", OP("activation", out=rsin[:, tsl(t)], in_=pf[:, 0:TT], func=AF.Sin, scale=PV("sgn")),
          reads=pf.r() + pv.r(), writes=rsin.r(t))
        A("act", OP("activation", out=k_[:, 0:TT], in_=pf[:, 0:TT], func=AF.Abs),
          reads=pf.r(), writes=k_.r())
        A("act", OP("activation", out=rcos[:, tsl(t)], in_=k_[:, 0:TT], func=AF.Sin, scale=-1.0,
                                                    bias=math.pi / 2.0 - 1e-6), reads=k_.r(), writes=rcos.r(t))
        ring.free(pi_, pf, k_)
    dbg("rcos", rcos, rcos[:], [128, SEQ])
    dbg("rsin", rsin, rsin[:], [128, SEQ])

    def load_w(dst_tile, dst_ap, src_ap):
        return A("pool", OP("dma_start", out=dst_ap, in_=src_ap), writes=dst_tile.r(), dma=True)

    def rmsnorm_to_xn(gname):
        for t in range(NT):
            b = banks.get()
            for c in range(8):
                sq = ring.get()
                A("act", OP("activation", out=bf(sq), in_=hT[:, c, tsl(t)], func=AF.Square),
                  reads=hs(c, t), writes=sq.r())
                A("pe", OP("matmul", out=psum[:, b, :], lhsT=ones_b[:], rhs=bf(sq), start=(c == 0), stop=(c == 7)),
                  reads=sq.r() + ones_b.r(), writes=psum.r(b))
                ring.free(sq)
            rs = ring.get()
            A("act", OP("activation", out=rs[:, 0:TT], in_=psum[:, b, :], func=AF.Ln, scale=1.0 / D_MODEL, bias=NORM_EPS), reads=psum.r(b), writes=rs.r())
            banks.free(b)
            A("act", OP("activation", out=rs[:, 0:TT], in_=rs[:, 0:TT], func=AF.Exp, scale=-0.5), reads=rs.r(), writes=rs.r())
            for c in range(8):
                A("dve", OP("scalar_tensor_tensor", out=xn[:, c, tsl(t)], in0=hT[:, c, tsl(t)], scalar=PV(gname, c),
                                                                      in1=rs[:, 0:TT], op0=ALU.mult, op1=ALU.mult),
                  reads=hs(c, t) + rs.r() + pv.r(), writes=xs(c, t))
            ring.free(rs)

    def proj(b, wtile, w_ap_fn, t):
        for k in range(8):
            A("pe", OP("matmul", out=psum[:, b, :], lhsT=w_ap_fn(k), rhs=xn[:, k, tsl(t)], start=(k == 0), stop=(k == 7)),
              reads=wtile.r() + xs(k, t), writes=psum.r(b))

    def group_post(l, mixed, gname, grp, wbufs_pool, eps_div):
        wo = wbufs_pool.get()
        wo_v = wo[:].rearrange("p (k n) -> p k n", k=4)
        load_w(wo, wo_v, w_out_d[l, grp * 512:(grp + 1) * 512, :].rearrange("(k p) n -> p k n", p=128))
        for t in range(NT):
            if gname is not None:
                b = banks.get()
                for c in range(4):
                    sq = ring.get()
                    A("act", OP("activation", out=bf(sq), in_=mixed[:, c, tsl(t)], func=AF.Square),
                      reads=mixed.r(c * 4 + t), writes=sq.r())
                    A("pe", OP("matmul", out=psum[:, b, :], lhsT=ones_b[:], rhs=bf(sq), start=(c == 0), stop=(c == 3)),
                      reads=sq.r() + ones_b.r(), writes=psum.r(b))
                    ring.free(sq)
                rs = ring.get()
                A("act", OP("activation", out=rs[:, 0:TT], in_=psum[:, b, :], func=AF.Ln, scale=1.0 / 512.0, bias=NORM_EPS), reads=psum.r(b), writes=rs.r())
                banks.free(b)
                A("act", OP("activation", out=rs[:, 0:TT], in_=rs[:, 0:TT], func=AF.Exp, scale=-0.5), reads=rs.r(), writes=rs.r())
                for c in range(4):
                    A("dve", OP("scalar_tensor_tensor", out=mixed[:, c, tsl(t)], in0=mixed[:, c, tsl(t)],
                                                                          scalar=PV(gname + str(l), c), in1=rs[:, 0:TT],
                                                                          op0=ALU.mult, op1=ALU.mult),
                      reads=mixed.r(c * 4 + t) + rs.r() + pv.r(), writes=mixed.r(c * 4 + t))
                ring.free(rs)
            for dc in range(8):
                b = banks.get()
                for k in range(4):
                    A("pe", OP("matmul", out=psum[:, b, :], lhsT=wo_v[:, k, dc * 128:(dc + 1) * 128],
                                                           rhs=mixed[:, k, tsl(t)], start=(k == 0), stop=(k == 3)),
                      reads=wo.r() + mixed.r(k * 4 + t), writes=psum.r(b))
                A("dve", OP("tensor_tensor", out=hT[:, dc, tsl(t)], in0=hT[:, dc, tsl(t)], in1=psum[:, b, :], op=ALU.add),
                  reads=hs(dc, t) + psum.r(b), writes=hs(dc, t))
                banks.free(b)
        wbufs_pool.free(wo)

    phase = []
    for l in range(nlayers):

        def psb(name, shape, dtype, nslots=1):
            cm = nc.sbuf_tensor(f"{name}_{l}", list(shape), dtype)
            h = cm.__enter__()
            phase.append(cm)
            return T(h, name, nslots)

        mixed = psb("mixed", [128, 4, SEQ], BF16, nslots=16)
        aux = psb("aux", [128, 8192], BF16, nslots=16)
        wbufs = Pool([psb(f"wbuf{i}", [128, 4096], BF16) for i in range(2)])
        wbd = psb("wbd", [128, 2, 4, 128], BF16)
        lpar = psb("lpar", [128, 16], F32)
        lcar = psb("lcar", [128, 4, 4], F32)
        s5p = psb("s5p", [128, 32, 8], F32)
        s5off = psb("s5off", [128, 32, 4], F32)
        bst1 = psb("bst1", [128, 512], F32)
        bst2 = psb("bst2", [128, 512], F32)
        lhs_b = psb("lhs_b", [128, 2, 8, 128], BF16)
        lhs_c = psb("lhs_c", [128, 2, 8, 128], BF16)
        zcar = psb("zcar", [128, 32], F32)
        ust = psb("ust", [128, 128], F32)
        pbs = [psb(f"pb{i}", [128, 128], BF16) for i in range(2)]

        load_w(wbd, wbd[:].rearrange("p a c n -> p (a c n)"), wbd_d[l])
        A("act", OP("activation", out=lpar[:, 0:4], in_=PV(f"llam{l}", 0, 4), func=AF.Exp, scale=-1.0), reads=pv.r(), writes=lpar.r())
        A("act", OP("activation", out=lpar[:, 0:4], in_=lpar[:, 0:4], func=AF.Ln, bias=1.0), reads=lpar.r(), writes=lpar.r())
        A("dve", OP("tensor_scalar", out=lpar[:, 4:8], in0=lpar[:, 0:4], scalar1=-4.0, scalar2=None, op0=ALU.mult), reads=lpar.r(), writes=lpar.r())
        A("dve", OP("tensor_scalar", out=lpar[:, 0:4], in0=lpar[:, 0:4], scalar1=-8.0, scalar2=None, op0=ALU.mult), reads=lpar.r(), writes=lpar.r())
        A("dve", OP("tensor_scalar", out=lpar[:, 8:12], in0=PV(f"lba{l}", 0, 4), scalar1=0.5, scalar2=None, op0=ALU.mult), reads=pv.r(), writes=lpar.r())
        A("dve", OP("tensor_scalar", out=lpar[:, 12:16], in0=PV(f"lbx{l}", 0, 4), scalar1=0.5, scalar2=None, op0=ALU.mult), reads=pv.r(), writes=lpar.r())
        A("dve", OP("memset", lcar[:], 0.0), writes=lcar.r())
        A("dve", OP("memset", zcar[:], 0.0), writes=zcar.r())

        S = lambda j: s5p[:, :, j]
        LR = PV(f"s5lr{l}", 0, 32)
        LI = PV(f"s5li{l}", 0, 32)
        R_, W_ = s5p.r(), s5p.r()
        A("act", OP("activation", out=S(6), in_=PV(f"s5ldt{l}", 0, 32), func=AF.Exp), reads=pv.r(), writes=W_)
        A("dve", OP("tensor_tensor", out=S(0), in0=LR, in1=S(6), op=ALU.mult), reads=R_ + pv.r(), writes=W_)
        A("act", OP("activation", out=S(0), in_=S(0), func=AF.Exp), reads=R_, writes=W_)
        A("dve", OP("tensor_tensor", out=S(7), in0=LI, in1=S(6), op=ALU.mult), reads=R_ + pv.r(), writes=W_)
        A("dve", OP("tensor_scalar", out=S(6), in0=S(7), scalar1=1.0 / (2.0 * math.pi), scalar2=MAGIC, op0=ALU.mult, op1=ALU.add), reads=R_, writes=W_)
        A("dve", OP("tensor_scalar", out=S(6), in0=S(6), scalar1=MAGIC, scalar2=None, op0=ALU.subtract), reads=R_, writes=W_)
        A("dve", OP("scalar_tensor_tensor", out=S(1), in0=S(7), scalar=1.0 / (2.0 * math.pi), in1=S(6), op0=ALU.mult, op1=ALU.subtract), reads=R_, writes=W_)
        A("act", OP("activation", out=S(7), in_=S(1), func=AF.Sin, scale=TWO_PI_S), reads=R_, writes=W_)
        A("act", OP("activation", out=S(6), in_=S(1), func=AF.Abs), reads=R_, writes=W_)
        A("act", OP("activation", out=S(6), in_=S(6), func=AF.Sin, scale=-TWO_PI_S, bias=math.pi / 2.0 - 1e-6), reads=R_, writes=W_)
        A("dve", OP("tensor_tensor", out=S(6), in0=S(6), in1=S(0), op=ALU.mult), reads=R_, writes=W_)
        A("dve", OP("tensor_scalar", out=S(6), in0=S(6), scalar1=-1.0, scalar2=None, op0=ALU.add), reads=R_, writes=W_)
        A("dve", OP("tensor_tensor", out=S(7), in0=S(7), in1=S(0), op=ALU.mult), reads=R_, writes=W_)
        A("dve", OP("tensor_tensor", out=S(5), in0=LR, in1=LR, op=ALU.mult), reads=R_ + pv.r(), writes=W_)
        A("dve", OP("tensor_tensor", out=S(4), in0=LI, in1=LI, op=ALU.mult), reads=R_ + pv.r(), writes=W_)
        A("dve", OP("tensor_tensor", out=S(5), in0=S(5), in1=S(4), op=ALU.add), reads=R_, writes=W_)
        A("dve", OP("reciprocal", out=S(5), in_=S(5)), reads=R_, writes=W_)
        A("dve", OP("tensor_tensor", out=S(2), in0=S(6), in1=LR, op=ALU.mult), reads=R_ + pv.r(), writes=W_)
        A("dve", OP("tensor_tensor", out=S(4), in0=S(7), in1=LI, op=ALU.mult), reads=R_ + pv.r(), writes=W_)
        A("dve", OP("tensor_tensor", out=S(2), in0=S(2), in1=S(4), op=ALU.add), reads=R_, writes=W_)
        A("dve", OP("tensor_tensor", out=S(2), in0=S(2), in1=S(5), op=ALU.mult), reads=R_, writes=W_)
        A("dve", OP("tensor_tensor", out=S(3), in0=S(7), in1=LR, op=ALU.mult), reads=R_ + pv.r(), writes=W_)
        A("dve", OP("tensor_tensor", out=S(4), in0=S(6), in1=LI, op=ALU.mult), reads=R_ + pv.r(), writes=W_)
        A("dve", OP("tensor_tensor", out=S(3), in0=S(3), in1=S(4), op=ALU.subtract), reads=R_, writes=W_)
        A("dve", OP("tensor_tensor", out=S(3), in0=S(3), in1=S(5), op=ALU.mult), reads=R_, writes=W_)
        A("dve", OP("tensor_copy", out=S(5), in_=S(3)), reads=R_, writes=W_)
        A("dve", OP("tensor_scalar", out=S(3), in0=S(3), scalar1=PV("sgn"), scalar2=None, op0=ALU.mult), reads=R_ + pv.r(), writes=W_)
        A("dve", OP("tensor_scalar", out=S(4), in0=S(2), scalar1=PV("nsgn"), scalar2=None, op0=ALU.mult), reads=R_ + pv.r(), writes=W_)
        for t in range(NT):
            A("dve", OP("tensor_scalar", out=s5off[:, :, t], in0=S(1), scalar1=float(TT * t), scalar2=MAGIC, op0=ALU.mult, op1=ALU.add),
              reads=R_, writes=s5off.r())
            A("dve", OP("tensor_scalar", out=s5off[:, :, t], in0=s5off[:, :, t], scalar1=MAGIC, scalar2=None, op0=ALU.subtract),
              reads=s5off.r(), writes=s5off.r())
            A("dve", OP("scalar_tensor_tensor", out=s5off[:, :, t], in0=S(1), scalar=float(TT * t), in1=s5off[:, :, t],
                                                           op0=ALU.mult, op1=ALU.subtract), reads=R_ + s5off.r(), writes=s5off.r())
        cn1 = ring.get(); cn2 = ring.get()
        A("sp", OP("dma_start", out=cn1[:, 0:512], in_=x1_d[l]), writes=cn1.r(), dma=True)
        A("sp", OP("dma_start", out=cn2[:, 0:512], in_=x2_d[l]), writes=cn2.r(), dma=True)
        v3 = lambda tl: tl[:, 0:512].rearrange("p (g c) -> p g c", c=16)
        bc = lambda j: s5p[:, :, j:j + 1].to_broadcast([128, 32, 16])
        t1 = ring.get(); t2 = ring.get()
        t1v = t1[:, 0:512].rearrange("p (g c) -> p g c", c=16)
        t2v = t2[:, 0:512].rearrange("p (g c) -> p g c", c=16)
        A("dve", OP("tensor_tensor", out=t1v, in0=v3(cn1), in1=bc(2), op=ALU.mult), reads=cn1.r() + R_, writes=t1.r())
        A("dve", OP("tensor_tensor", out=t2v, in0=v3(cn2), in1=bc(3), op=ALU.mult), reads=cn2.r() + R_, writes=t2.r())
        A("dve", OP("tensor_tensor", out=bst1[:], in0=t1[:, 0:512], in1=t2[:, 0:512], op=ALU.add), reads=t1.r() + t2.r(), writes=bst1.r())
        A("dve", OP("tensor_tensor", out=t1v, in0=v3(cn2), in1=bc(4), op=ALU.mult), reads=cn2.r() + R_, writes=t1.r())
        A("dve", OP("tensor_tensor", out=t2v, in0=v3(cn1), in1=bc(5), op=ALU.mult), reads=cn1.r() + R_, writes=t2.r())
        A("dve", OP("tensor_tensor", out=bst2[:], in0=t1[:, 0:512], in1=t2[:, 0:512], op=ALU.add), reads=t1.r() + t2.r(), writes=bst2.r())
        ring.free(t1, t2, cn1, cn2)
        A("pool", OP("memset", lhs_c[:].rearrange("p a g n -> p (a g n)"), 0.0), writes=lhs_c.r())
        dbg(f"bst1_{l}", bst1, bst1[:], [128, 512])
        dbg(f"s5p_{l}", s5p, s5p[:].rearrange("p g j -> p (g j)"), [128, 256])

        rmsnorm_to_xn(f"gmix{l}")
        dbg(f"xn{l}", xn, xn[:].rearrange("p c t -> p (c t)"), [128, 8 * SEQ], BF16)

        for c in range(4):
            wb = wbufs.get()
            wv_ = wb[:].rearrange("p (k n) -> p k n", k=8)
            load_w(wb, wv_[:, :, 0:128], w_in_d[l, :, c * 128:(c + 1) * 128].rearrange("(k p) n -> p k n", p=128))
            load_w(wb, wv_[:, :, 128:256], w_in_d[l, :, 512 + c * 128:512 + (c + 1) * 128].rearrange("(k p) n -> p k n", p=128))
            for t in range(NT):
                bx = banks.get(); bg = banks.get()
                proj(bx, wb, lambda k: wv_[:, k, 0:128], t)
                proj(bg, wb, lambda k: wv_[:, k, 128:256], t)
                xp = ring.get(); gy = ring.get(); xc = ring.get(); xcb = ring.get()
                A("act", OP("activation", out=xp[:, 0:3], in_=lcar[:, c, 0:3], func=AF.Copy), reads=lcar.r(), writes=xp.r())
                A("act", OP("activation", out=xp[:, 3:3 + TT], in_=psum[:, bx, :], func=AF.Copy), reads=psum.r(bx), writes=xp.r())
                A("act", OP("activation", out=lcar[:, c, 0:3], in_=xp[:, TT:TT + 3], func=AF.Copy), reads=xp.r(), writes=lcar.r())
                A("act", OP("activation", out=gy[:, 0:TT], in_=psum[:, bg, :], func=AF.Gelu_apprx_tanh), reads=psum.r(bg), writes=gy.r())
                banks.free(bx, bg)
                A("act", OP("activation", out=xc[:, 0:TT], in_=xp[:, 3:3 + TT], func=AF.Identity,
                                                              scale=PV(f"lcw{l}", c * 4 + 3), bias=PV(f"lcb{l}", c)),
                  reads=xp.r() + pv.r(), writes=xc.r())
                for k in range(3):
                    A("dve", OP("scalar_tensor_tensor", out=xc[:, 0:TT], in0=xp[:, k:k + TT], scalar=PV(f"lcw{l}", c * 4 + k),
                                                                                in1=xc[:, 0:TT], op0=ALU.mult, op1=ALU.add),
                      reads=xp.r() + xc.r() + pv.r(), writes=xc.r())
                A("act", OP("activation", out=bf(xcb), in_=xc[:, 0:TT], func=AF.Copy), reads=xc.r(), writes=xcb.r())
                ring.free(xp)
                br_ = banks.get(); bi_ = banks.get()
                A("pe", OP("matmul", out=psum[:, br_, :], lhsT=wbd[:, 0, c, :], rhs=bf(xcb), start=True, stop=True),
                  reads=wbd.r() + xcb.r(), writes=psum.r(br_))
                A("pe", OP("matmul", out=psum[:, bi_, :], lhsT=wbd[:, 1, c, :], rhs=bf(xcb), start=True, stop=True),
                  reads=wbd.r() + xcb.r(), writes=psum.r(bi_))
                rr = ring.get(); ii = ring.get(); a2 = ring.get()
                A("act", OP("activation", out=rr[:, 0:TT], in_=psum[:, br_, :], func=AF.Tanh, scale=0.5, bias=lpar[:, 8 + c:9 + c]),
                  reads=psum.r(br_) + lpar.r(), writes=rr.r())
                A("act", OP("activation", out=ii[:, 0:TT], in_=psum[:, bi_, :], func=AF.Tanh, scale=0.5, bias=lpar[:, 12 + c:13 + c]),
                  reads=psum.r(bi_) + lpar.r(), writes=ii.r())
                banks.free(br_, bi_)
                ring.free(xcb)
                A("act", OP("activation", out=a2[:, 0:TT], in_=rr[:, 0:TT], func=AF.Exp, scale=lpar[:, c:c + 1], bias=lpar[:, c:c + 1]),
                  reads=rr.r() + lpar.r(), writes=a2.r())
                A("act", OP("activation", out=rr[:, 0:TT], in_=rr[:, 0:TT], func=AF.Exp, scale=lpar[:, 4 + c:5 + c], bias=lpar[:, 4 + c:5 + c]),
                  reads=rr.r() + lpar.r(), writes=rr.r())
                A("dve", OP("tensor_scalar", out=a2[:, 0:TT], in0=a2[:, 0:TT], scalar1=1.0, scalar2=-1e-30, op0=ALU.subtract, op1=ALU.min),
                  reads=a2.r(), writes=a2.r())
                A("act", OP("activation", out=a2[:, 0:TT], in_=a2[:, 0:TT], func=AF.Ln, scale=-1.0), reads=a2.r(), writes=a2.r())
                A("act", OP("activation", out=a2[:, 0:TT], in_=a2[:, 0:TT], func=AF.Exp, scale=0.5), reads=a2.r(), writes=a2.r())
                A("dve", OP("scalar_tensor_tensor", out=ii[:, 0:TT], in0=ii[:, 0:TT], scalar=1.0, in1=xc[:, 0:TT], op0=ALU.add, op1=ALU.mult),
                  reads=ii.r() + xc.r(), writes=ii.r())
                A("dve", OP("scalar_tensor_tensor", out=ii[:, 0:TT], in0=ii[:, 0:TT], scalar=0.5, in1=a2[:, 0:TT], op0=ALU.mult, op1=ALU.mult),
                  reads=ii.r() + a2.r(), writes=ii.r())
                hh = xc
                A("dve", OP("tensor_tensor_scan", out=hh[:, 0:TT], data0=rr[:, 0:TT], data1=ii[:, 0:TT],
                                                                            initial=lcar[:, c, 3:4], op0=ALU.mult, op1=ALU.add),
                  reads=rr.r() + ii.r() + lcar.r(), writes=hh.r())
                A("act", OP("activation", out=lcar[:, c, 3:4], in_=hh[:, TT - 1:TT], func=AF.Copy), reads=hh.r(), writes=lcar.r())
                A("dve", OP("tensor_tensor", out=mixed[:, c, tsl(t)], in0=hh[:, 0:TT], in1=gy[:, 0:TT], op=ALU.mult),
                  reads=hh.r() + gy.r(), writes=mixed.r(c * 4 + t))
                ring.free(rr, ii, a2, xc, gy)
            wbufs.free(wb)
        dbg(f"ylru_raw{l}", mixed, mixed[:].rearrange("p c t -> p (c t)"), [128, 4 * SEQ], BF16)
        if upto == "lru_raw":
            break
        group_post(l, mixed, "lnorm", 0, wbufs, 512.0)
        dbg(f"ylru{l}", mixed, mixed[:].rearrange("p c t -> p (c t)"), [128, 4 * SEQ], BF16)
        if upto == "lru":
            break

        zbf = aux[:].rearrange("p (c t) -> p c t", c=4)
        for c in range(4):
            wb = wbufs.get()
            wv_ = wb[:].rearrange("p (k n) -> p k n", k=8)
            load_w(wb, wv_[:, :, 0:128], w_in_d[l, :, 1024 + c * 128:1024 + (c + 1) * 128].rearrange("(k p) n -> p k n", p=128))
            for which, bst in enumerate((bst1, bst2)):
                b = banks.get()
                A("pe", OP("transpose", out=psum[:, b, 0:128], in_=bst[:, c * 128:(c + 1) * 128], identity=ident_f),
                  reads=bst.r() + cst.r(), writes=psum.r(b))
                tb = ring.get()
                A("act", OP("activation", out=tb[:, 0:128], in_=psum[:, b, 0:128], func=AF.Copy), reads=psum.r(b), writes=tb.r())
                banks.free(b)
                for g in range(8):
                    A("dve", OP("tensor_scalar", out=lhs_b[:, which, g, :], in0=tb[:, 0:128], scalar1=PV("rowmask", g),
                                                                                scalar2=None, op0=ALU.mult),
                      reads=tb.r() + pv.r(), writes=lhs_b.r())
                ring.free(tb)
            for which, (cd, sname) in enumerate(((c1_d, "nsgn"), (c2_d, None))):
                cn = ring.get()
                A("sp", OP("dma_start", out=cn[:, 0:128], in_=cd[l, :, c * 128:(c + 1) * 128]), writes=cn.r(), dma=True)
                b = banks.get()
                A("pe", OP("transpose", out=psum[:, b, 0:128], in_=cn[:, 0:128], identity=ident_f),
                  reads=cn.r() + cst.r(), writes=psum.r(b))
                ring.free(cn)
                for g in range(8):
                    if sname is not None:
                        A("dve", OP("tensor_scalar", out=lhs_c[:, which, g, g * 16:(g + 1) * 16], in0=psum[:, b, g * 16:(g + 1) * 16],
                                                                                  scalar1=PV("nsgn"), scalar2=None, op0=ALU.mult),
                          reads=psum.r(b) + pv.r(), writes=lhs_c.r())
                    else:
                        A("dve", OP("tensor_scalar", out=lhs_c[:, which, g, g * 16:(g + 1) * 16], in0=psum[:, b, g * 16:(g + 1) * 16],
                                                                                  scalar1=-1.0, scalar2=None, op0=ALU.mult),
                          reads=psum.r(b), writes=lhs_c.r())
                banks.free(b)
            for t in range(NT):
                bu = banks.get()
                proj(bu, wb, lambda k: wv_[:, k, 0:128], t)
                uf = ring.get(); ub = ring.get()
                A("act", OP("activation", out=uf[:, 0:TT], in_=psum[:, bu, :], func=AF.Copy), reads=psum.r(bu), writes=uf.r())
                A("act", OP("activation", out=bf(ub), in_=psum[:, bu, :], func=AF.Copy), reads=psum.r(bu), writes=ub.r())
                banks.free(bu)
                by = banks.get()
                for g in range(8):
                    G = c * 8 + g
                    b1 = banks.get(); b2 = banks.get()
                    A("pe", OP("matmul", out=psum[:, b1, :], lhsT=lhs_b[:, 0, g, :], rhs=bf(ub), start=True, stop=True),
                      reads=lhs_b.r() + ub.r(), writes=psum.r(b1))
                    A("pe", OP("matmul", out=psum[:, b2, :], lhsT=lhs_b[:, 1, g, :], rhs=bf(ub), start=True, stop=True),
                      reads=lhs_b.r() + ub.r(), writes=psum.r(b2))
                    uu = ring.get(); kk = ring.get(); tc_ = ring.get(); ts_ = ring.get()
                    A("pool", OP("tensor_scalar", out=uu[:, 0:TT], in0=iota, scalar1=s5p[:, G, 1:2], scalar2=s5off[:, G, t:t + 1],
                                                                    op0=ALU.mult, op1=ALU.add), reads=cst.r() + s5p.r() + s5off.r(), writes=uu.r())
                    A("dve", OP("tensor_scalar", out=kk[:, 0:TT], in0=uu[:, 0:TT], scalar1=MAGIC, scalar2=MAGIC,
                                                                      op0=ALU.add, op1=ALU.subtract), reads=uu.r(), writes=kk.r())
                    A("pool", OP("tensor_tensor", out=uu[:, 0:TT], in0=uu[:, 0:TT], in1=kk[:, 0:TT], op=ALU.subtract),
                      reads=uu.r() + kk.r(), writes=uu.r())
                    A("act", OP("activation", out=ts_[:, 0:TT], in_=uu[:, 0:TT], func=AF.Sin, scale=TWO_PI_S), reads=uu.r(), writes=ts_.r())
                    A("act", OP("activation", out=kk[:, 0:TT], in_=uu[:, 0:TT], func=AF.Abs), reads=uu.r(), writes=kk.r())
                    A("act", OP("activation", out=tc_[:, 0:TT], in_=kk[:, 0:TT], func=AF.Sin, scale=-TWO_PI_S, bias=math.pi / 2.0 - 1e-6),
                      reads=kk.r(), writes=tc_.r())
                    m1 = uu; m2 = kk
                    A("dve", OP("tensor_tensor", out=m1[:, 0:TT], in0=psum[:, b1, :], in1=tc_[:, 0:TT], op=ALU.mult),
                      reads=psum.r(b1) + tc_.r(), writes=m1.r())
                    A("dve", OP("tensor_tensor", out=m2[:, 0:TT], in0=psum[:, b2, :], in1=ts_[:, 0:TT], op=ALU.mult),
                      reads=psum.r(b2) + ts_.r(), writes=m2.r())
                    banks.free(b1, b2)
                    A("pool", OP("tensor_tensor", out=m1[:, 0:TT], in0=m1[:, 0:TT], in1=m2[:, 0:TT], op=ALU.add),
                      reads=m1.r() + m2.r(), writes=m1.r())
                    zz = m2
                    A("dve", OP("tensor_tensor_scan", out=zz[:, 0:TT], data0=s5p[:, G, 0:1].to_broadcast([128, TT]), data1=m1[:, 0:TT],
                                                                              initial=zcar[:, G:G + 1], op0=ALU.mult, op1=ALU.add),
                      reads=m1.r() + s5p.r() + zcar.r(), writes=zz.r())
                    A("act", OP("activation", out=zcar[:, G:G + 1], in_=zz[:, TT - 1:TT], func=AF.Copy), reads=zz.r(), writes=zcar.r())
                    w12 = m1
                    A("pool", OP("tensor_tensor", out=bf(w12, TT, 0), in0=zz[:, 0:TT], in1=tc_[:, 0:TT], op=ALU.mult),
                      reads=zz.r() + tc_.r(), writes=w12.r())
                    A("pool", OP("tensor_tensor", out=bf(w12, TT, TT), in0=zz[:, 0:TT], in1=ts_[:, 0:TT], op=ALU.mult),
                      reads=zz.r() + ts_.r(), writes=w12.r())
                    A("pe", OP("matmul", out=psum[:, by, :], lhsT=lhs_c[:, 0, g, :], rhs=bf(w12, TT, 0), start=(g == 0), stop=False),
                      reads=lhs_c.r() + w12.r(), writes=psum.r(by))
                    A("pe", OP("matmul", out=psum[:, by, :], lhsT=lhs_c[:, 1, g, :], rhs=bf(w12, TT, TT), start=False, stop=(g == 7)),
                      reads=lhs_c.r() + w12.r(), writes=psum.r(by))
                    ring.free(uu, kk, tc_, ts_)
                A("dve", OP("scalar_tensor_tensor", out=uf[:, 0:TT], in0=uf[:, 0:TT], scalar=PV(f"s5d{l}", c), in1=psum[:, by, :],
                                                                 op0=ALU.mult, op1=ALU.add), reads=uf.r() + psum.r(by) + pv.r(), writes=uf.r())
                banks.free(by)
                A("act", OP("activation", out=zbf[:, c, tsl(t)], in_=uf[:, 0:TT], func=AF.Gelu_apprx_tanh), reads=uf.r(), writes=aux.r(c * 4 + t))
                ring.free(uf, ub)
            wbufs.free(wb)
        dbg(f"s5z{l}", aux, aux[:], [128, 4 * SEQ], BF16)
        wg_t = wbufs.get()
        wgl = wg_t[:, 0:2048].rearrange("p (k n) -> p k n", k=4)
        load_w(wg_t, wgl, w_glu_d[l].rearrange("(k p) n -> p k n", p=128))
        for t in range(NT):
            for oc in range(4):
                b = banks.get()
                for k in range(4):
                    A("pe", OP("matmul", out=psum[:, b, :], lhsT=wgl[:, k, oc * 128:(oc + 1) * 128], rhs=zbf[:, k, tsl(t)],
                                                                start=(k == 0), stop=(k == 3)), reads=wg_t.r() + aux.r(k * 4 + t), writes=psum.r(b))
                sg = ring.get()
                A("act", OP("activation", out=sg[:, 0:TT], in_=psum[:, b, :], func=AF.Sigmoid, bias=PV(f"s5bg{l}", oc)),
                  reads=psum.r(b) + pv.r(), writes=sg.r())
                banks.free(b)
                A("dve", OP("tensor_tensor", out=mixed[:, oc, tsl(t)], in0=zbf[:, oc, tsl(t)], in1=sg[:, 0:TT], op=ALU.mult),
                  reads=sg.r() + aux.r(oc * 4 + t), writes=mixed.r(oc * 4 + t))
                ring.free(sg)
        wbufs.free(wg_t)
        group_post(l, mixed, "s5n", 1, wbufs, 512.0)
        dbg(f"ys5{l}", mixed, mixed[:].rearrange("p c t -> p (c t)"), [128, 4 * SEQ], BF16)
        if upto == "s5":
            break

        fw.pin = set(os.environ.get('KPIN', 'dve').split(',')) - {''}
        vh = aux[:].rearrange("p (n e) -> p n e", n=16)
        wv_t = wbufs.get()
        wvv = wv_t[:].rearrange("p (k n) -> p k n", k=8)
        load_w(wv_t, wvv, w_in_d[l, :, 2560:3072].rearrange("(k p) n -> p k n", p=128))
        for n in range(16):
            b = banks.get()
            t = n // 4
            for k in range(8):
                A("pe", OP("matmul", out=psum[:, b, :], lhsT=xn[:, k, n * 128:(n + 1) * 128], rhs=wvv[:, k, :],
                                                          start=(k == 0), stop=(k == 7)), reads=wv_t.r() + xs(k, t), writes=psum.r(b))
            for hd in range(4):
                A("act", OP("activation", out=vh[:, n, hd * 128:(hd + 1) * 128], in_=psum[:, b, hd * 128:(hd + 1) * 128],
                                                                 func=AF.Identity, scale=PV("vdec", hd)), reads=psum.r(b) + pv.r(), writes=aux.r(n))
            banks.free(b)
        wbufs.free(wv_t)
        wg_t = wbufs.get()
        wgv = wg_t[:].rearrange("p (k n) -> p k n", k=8)
        load_w(wg_t, wgv, w_in_d[l, :, 3072:3584].rearrange("(k p) n -> p k n", p=128))
        for hd in range(4):
            gam = GAMMAS[hd]
            gC = float(np.float32(np.exp(np.float32(np.log1p(-np.float32(2.0) ** np.float32(-5.0 - hd))) * np.float32(128.0))))
            wb = wbufs.get()
            wq = wb[:].rearrange("p (k n) -> p k n", k=8)
            qb = 1536 + hd * 128
            kb = 2048 + hd * 128
            src = lambda c0, c1: w_in_d[l, :, c0:c1].rearrange("(k p) n -> p k n", p=128)
            load_w(wb, wq[:, :, 0:128], src(qb, qb + 128))
            load_w(wb, wq[:, :, 128:192], src(qb + 64, qb + 128))
            load_w(wb, wq[:, :, 192:256], src(qb, qb + 64))
            load_w(wb, wq[:, :, 256:384], src(kb, kb + 128))
            load_w(wb, wq[:, :, 384:448], src(kb + 64, kb + 128))
            load_w(wb, wq[:, :, 448:512], src(kb, kb + 64))
            for t in range(NT):
                rot = []
                for which in range(2):
                    ba = banks.get(); bb = banks.get()
                    proj(ba, wb, lambda k, o=which * 256: wq[:, k, o:o + 128], t)
                    proj(bb, wb, lambda k, o=which * 256 + 128: wq[:, k, o:o + 128], t)
                    t1 = ring.get(); t2 = ring.get(); qr = ring.get()
                    A("dve", OP("tensor_tensor", out=t1[:, 0:TT], in0=psum[:, ba, :], in1=rcos[:, tsl(t)], op=ALU.mult),
                      reads=psum.r(ba) + rcos.r(t), writes=t1.r())
                    A("dve", OP("tensor_tensor", out=t2[:, 0:TT], in0=psum[:, bb, :], in1=rsin[:, tsl(t)], op=ALU.mult),
                      reads=psum.r(bb) + rsin.r(t), writes=t2.r())
                    banks.free(ba, bb)
                    A("pool", OP("tensor_tensor", out=bf(qr), in0=t1[:, 0:TT], in1=t2[:, 0:TT], op=ALU.add),
                      reads=t1.r() + t2.r(), writes=qr.r())
                    ring.free(t1, t2)
                    rot.append(qr)
                qr, kr = rot
                if hd == 0 and t == 0:
                    dbg(f"qr{l}", qr, bf(qr), [128, TT], BF16)
                    dbg(f"kr{l}", kr, bf(kr), [128, TT], BF16)
                bt = banks.get()
                ktp = psum[:, bt, :].bitcast(BF16)
                for j in range(4):
                    A("pe", OP("transpose", out=ktp[:, j * 128:(j + 1) * 128], in_=bf(kr, 128, j * 128), identity=ident_b[:]),
                      reads=kr.r() + ident_b.r(), writes=psum.r(bt))
                ktm = ring.get()
                A("act", OP("activation", out=bf(ktm), in_=ktp[:, 0:TT], func=AF.Copy), reads=psum.r(bt), writes=ktm.r())
                banks.free(bt)
                bs = banks.get()
                for j in range(4):
                    A("pe", OP("matmul", out=psum[:, bs, j * 128:(j + 1) * 128], lhsT=bf(kr, 128, j * 128), rhs=bf(qr, 128, j * 128),
                                                                  start=True, stop=True), reads=kr.r() + qr.r(), writes=psum.r(bs))
                pT = ring.get()
                A("dve", OP("tensor_tensor", out=bf(pT).rearrange("p (j c) -> p j c", j=4), in0=psum[:, bs, :].rearrange("p (j c) -> p j c", j=4),
                                                          in1=maskT.unsqueeze(1).to_broadcast([128, 4, 128]), op=ALU.mult),
                  reads=psum.r(bs) + cst.r(), writes=pT.r())
                banks.free(bs)
                bkv = banks.get()
                for j in range(4):
                    n = t * 4 + j
                    A("pe", OP("matmul", out=psum[:, bkv, j * 128:(j + 1) * 128], lhsT=bf(ktm, 128, j * 128),
                                                                  rhs=vh[:, n, hd * 128:(hd + 1) * 128], start=True, stop=True),
                      reads=ktm.r() + aux.r(n), writes=psum.r(bkv))
                ring.free(ktm)
                bxo = banks.get()
                for j in range(4):
                    n = t * 4 + j
                    pb = pbs[n % 2]
                    if n > 0:
                        A("act", OP("activation", out=pb[:], in_=ust[:], func=AF.Identity, scale=float(gC * (128.0 ** -0.5))),
                          reads=ust.r(), writes=pb.r())
                    A("pe", OP("matmul", out=psum[:, bxo, j * 128:(j + 1) * 128], lhsT=vh[:, n, hd * 128:(hd + 1) * 128],
                                                                rhs=bf(pT, 128, j * 128), start=True, stop=(n == 0)),
                      reads=aux.r(n) + pT.r(), writes=psum.r(bxo))
                    if n > 0:
                        A("pe", OP("matmul", out=psum[:, bxo, j * 128:(j + 1) * 128], lhsT=pb[:], rhs=bf(qr, 128, j * 128),
                                                                      start=False, stop=True), reads=pb.r() + qr.r(), writes=psum.r(bxo))
                        A("dve", OP("scalar_tensor_tensor", out=ust[:], in0=ust[:], scalar=gC, in1=psum[:, bkv, j * 128:(j + 1) * 128],
                                                                       op0=ALU.mult, op1=ALU.add), reads=ust.r() + psum.r(bkv), writes=ust.r())
                    else:
                        A("dve", OP("tensor_copy", out=ust[:], in_=psum[:, bkv, j * 128:(j + 1) * 128]), reads=psum.r(bkv), writes=ust.r())
                banks.free(bkv)
                ring.free(pT, kr)
                oT = ring.get()
                A("dve", OP("tensor_tensor", out=oT[:, 0:TT].rearrange("p (j c) -> p j c", j=4), in0=psum[:, bxo, :].rearrange("p (j c) -> p j c", j=4),
                                                          in1=cst[:, C_QDEC + hd * 128:C_QDEC + (hd + 1) * 128].unsqueeze(1).to_broadcast([128, 4, 128]), op=ALU.mult),
                  reads=psum.r(bxo) + cst.r(), writes=oT.r())
                banks.free(bxo)
                ring.free(qr)
                ob = ring.get()
                A("act", OP("activation", out=bf(ob, TT, 0), in_=oT[:, 0:TT], func=AF.Copy), reads=oT.r(), writes=ob.r())
                A("act", OP("activation", out=bf(ob, TT, TT), in_=oT[:, 0:TT], func=AF.Square), reads=oT.r(), writes=ob.r())
                bm = banks.get(); bq = banks.get()
                A("pe", OP("matmul", out=psum[:, bm, :], lhsT=o128_b[:], rhs=bf(ob, TT, 0), start=True, stop=True),
                  reads=ob.r() + o128_b.r(), writes=psum.r(bm))
                A("pe", OP("matmul", out=psum[:, bq, :], lhsT=o128_b[:], rhs=bf(ob, TT, TT), start=True, stop=True),
                  reads=ob.r() + o128_b.r(), writes=psum.r(bq))
                ring.free(ob)
                m2 = ring.get()
                A("act", OP("activation", out=m2[:, 0:TT], in_=psum[:, bm, :], func=AF.Square), reads=psum.r(bm), writes=m2.r())
                A("dve", OP("tensor_tensor", out=m2[:, 0:TT], in0=psum[:, bq, :], in1=m2[:, 0:TT], op=ALU.subtract),
                  reads=psum.r(bq) + m2.r(), writes=m2.r())
                banks.free(bq)
                A("dve", OP("tensor_scalar", out=m2[:, 0:TT], in0=m2[:, 0:TT], scalar1=0.0, scalar2=None, op0=ALU.max), reads=m2.r(), writes=m2.r())
                A("act", OP("activation", out=m2[:, 0:TT], in_=m2[:, 0:TT], func=AF.Ln, bias=NORM_EPS), reads=m2.r(), writes=m2.r())
                A("act", OP("activation", out=m2[:, 0:TT], in_=m2[:, 0:TT], func=AF.Exp, scale=-0.5), reads=m2.r(), writes=m2.r())
                A("dve", OP("tensor_tensor", out=oT[:, 0:TT], in0=oT[:, 0:TT], in1=psum[:, bm, :], op=ALU.subtract),
                  reads=oT.r() + psum.r(bm), writes=oT.r())
                banks.free(bm)
                A("dve", OP("tensor_tensor", out=oT[:, 0:TT], in0=oT[:, 0:TT], in1=m2[:, 0:TT], op=ALU.mult),
                  reads=oT.r() + m2.r(), writes=oT.r())
                ring.free(m2)
                bg = banks.get()
                proj(bg, wg_t, lambda k: wgv[:, k, hd * 128:(hd + 1) * 128], t)
                sg = ring.get()
                A("act", OP("activation", out=sg[:, 0:TT], in_=psum[:, bg, :], func=AF.Silu), reads=psum.r(bg), writes=sg.r())
                banks.free(bg)
                A("dve", OP("scalar_tensor_tensor", out=mixed[:, hd, tsl(t)], in0=oT[:, 0:TT], scalar=PV(f"rnorm{l}", hd), in1=sg[:, 0:TT],
                                                                        op0=ALU.mult, op1=ALU.mult), reads=oT.r() + sg.r() + pv.r(), writes=mixed.r(hd * 4 + t))
                ring.free(oT, sg)
            wbufs.free(wb)
        wbufs.free(wg_t)
        dbg(f"yret{l}", mixed, mixed[:].rearrange("p c t -> p (c t)"), [128, 4 * SEQ], BF16)
        fw.pin = set()
        group_post(l, mixed, None, 2, wbufs, 512.0)
        dbg(f"hmix{l}", hT, hT[:].rearrange("p c t -> p (c t)"), [128, 8 * SEQ])
        for cm in reversed(phase):
            cm.__exit__(None, None, None)
        phase = []
        fw.barrier()
        if upto == "mix":
            break

        rmsnorm_to_xn(f"gffn{l}")
        actb = psb("actb", [128, 4, SEQ], BF16, nslots=16)
        wus = Pool([psb(f"wu{i}", [128, 8, 1024], BF16) for i in range(2)])
        wds = Pool([psb(f"wd{i}", [128, 4, 1024], BF16) for i in range(2)])
        fcar = psb("fcar", [128, 2, 2], F32)
        for grp in range(6):
            j0 = grp * 4
            wu = wus.get(); wd = wds.get()
            load_w(wu, wu[:, :, 0:512], w_up_d[l, :, j0 * 128:(j0 + 4) * 128].rearrange("(k p) n -> p k n", p=128))
            load_w(wu, wu[:, :, 512:1024], w_up_d[l, :, D_FF + j0 * 128:D_FF + (j0 + 4) * 128].rearrange("(k p) n -> p k n", p=128))
            load_w(wd, wd[:], w_down_d[l, j0 * 128:(j0 + 4) * 128, :].rearrange("(k p) n -> p k n", p=128))
            for jj in range(4):
                j = j0 + jj
                for t in range(NT):
                    outs = []
                    for which in range(2):
                        ch = j + 24 * which
                        b = banks.get()
                        proj(b, wu, lambda k, o=which * 512 + jj * 128: wu[:, k, o:o + 128], t)
                        vc = ring.get()
                        w0 = PV(f"fcw{l}", ch * 3 + 0); w1 = PV(f"fcw{l}", ch * 3 + 1); w2 = PV(f"fcw{l}", ch * 3 + 2)
                        A("act", OP("activation", out=vc[:, 0:TT], in_=psum[:, b, :], func=AF.Identity, scale=w2,
                                                                                  bias=PV(f"fcb{l}", ch)), reads=psum.r(b) + pv.r(), writes=vc.r())
                        A("dve", OP("scalar_tensor_tensor", out=vc[:, 1:TT], in0=psum[:, b, 0:TT - 1], scalar=w1, in1=vc[:, 1:TT],
                                                                                    op0=ALU.mult, op1=ALU.add), reads=psum.r(b) + vc.r() + pv.r(), writes=vc.r())
                        A("dve", OP("scalar_tensor_tensor", out=vc[:, 2:TT], in0=psum[:, b, 0:TT - 2], scalar=w0, in1=vc[:, 2:TT],
                                                                                    op0=ALU.mult, op1=ALU.add), reads=psum.r(b) + vc.r() + pv.r(), writes=vc.r())
                        if t > 0:
                            A("dve", OP("scalar_tensor_tensor", out=vc[:, 0:1], in0=fcar[:, which, 1:2], scalar=w1, in1=vc[:, 0:1],
                                                                                                op0=ALU.mult, op1=ALU.add), reads=fcar.r() + vc.r() + pv.r(), writes=vc.r())
                            A("dve", OP("scalar_tensor_tensor", out=vc[:, 0:2], in0=fcar[:, which, 0:2], scalar=w0, in1=vc[:, 0:2],
                                                                                                op0=ALU.mult, op1=ALU.add), reads=fcar.r() + vc.r() + pv.r(), writes=vc.r())
                        if t < NT - 1:
                            A("act", OP("activation", out=fcar[:, which, :], in_=psum[:, b, TT - 2:TT], func=AF.Copy),
                              reads=psum.r(b), writes=fcar.r())
                        banks.free(b)
                        outs.append(vc)
                    vc, gc = outs
                    A("act", OP("activation", out=gc[:, 0:TT], in_=gc[:, 0:TT], func=AF.Gelu_apprx_tanh), reads=gc.r(), writes=gc.r())
                    A("dve", OP("tensor_tensor", out=actb[:, jj, tsl(t)], in0=gc[:, 0:TT], in1=vc[:, 0:TT], op=ALU.mult),
                      reads=vc.r() + gc.r(), writes=actb.r(jj * 4 + t))
                    ring.free(vc, gc)
            for t in range(NT):
                for dc in range(8):
                    b = banks.get()
                    for k in range(4):
                        A("pe", OP("matmul", out=psum[:, b, :], lhsT=wd[:, k, dc * 128:(dc + 1) * 128], rhs=actb[:, k, tsl(t)],
                                                                    start=(k == 0), stop=(k == 3)), reads=wd.r() + actb.r(k * 4 + t), writes=psum.r(b))
                    A("dve", OP("tensor_tensor", out=hT[:, dc, tsl(t)], in0=hT[:, dc, tsl(t)], in1=psum[:, b, :], op=ALU.add),
                      reads=hs(dc, t) + psum.r(b), writes=hs(dc, t))
                    banks.free(b)
            wus.free(wu); wds.free(wd)
        dbg(f"hffn{l}", hT, hT[:].rearrange("p c t -> p (c t)"), [128, 8 * SEQ])
        for cm in reversed(phase):
            cm.__exit__(None, None, None)
        phase = []
        fw.barrier()

    for t in range(NT):
        b = banks.get()
        for c in range(8):
            sq = ring.get()
            A("act", OP("activation", out=bf(sq), in_=hT[:, c, tsl(t)], func=AF.Square), reads=hs(c, t), writes=sq.r())
            A("pe", OP("matmul", out=psum[:, b, :], lhsT=ones_b[:], rhs=bf(sq), start=(c == 0), stop=(c == 7)),
              reads=sq.r() + ones_b.r(), writes=psum.r(b))
            ring.free(sq)
        rs = ring.get()
        A("act", OP("activation", out=rs[:, 0:TT], in_=psum[:, b, :], func=AF.Ln, scale=1.0 / D_MODEL, bias=NORM_EPS), reads=psum.r(b), writes=rs.r())
        banks.free(b)
        A("act", OP("activation", out=rs[:, 0:TT], in_=rs[:, 0:TT], func=AF.Exp, scale=-0.5), reads=rs.r(), writes=rs.r())
        for c in range(8):
            ot = ring.get()
            A("dve", OP("scalar_tensor_tensor", out=ot[:, 0:TT], in0=hT[:, c, tsl(t)], scalar=PV("gfin", c), in1=rs[:, 0:TT],
                                                                         op0=ALU.mult, op1=ALU.mult), reads=hs(c, t) + rs.r() + pv.r(), writes=ot.r())
            ins = A("sp", OP("dma_start", out=outT_d[c * 128:(c + 1) * 128, tsl(t)], in_=ot[:, 0:TT]), reads=ot.r(), dma=True)
            out_dmas.append(ins)
            ring.free(ot)
        ring.free(rs)

    for cm in reversed(phase):
        cm.__exit__(None, None, None)
    fin = A("sp", None)
    for ins in out_dmas:
        fin.preds[ins] = True
    fw.emit()
    fw.close()
    return nc, dbg_out


def make_in_maps(inputs):
    inp = {k: np.asarray(v) for k, v in inputs.items()}
    pv = host_pv(inp)
    cst = host_consts()
    wbd, x1, x2, c1, c2 = host_struct(inp)
    shared = {
        "pv": pv, "cst": cst,
        "wbd": np.ascontiguousarray(wbd.reshape(DEPTH, 128, 1024)),
        "s5x1": x1, "s5x2": x2,
        "s5c1": np.ascontiguousarray(c1.reshape(DEPTH, 128, 512)),
        "s5c2": np.ascontiguousarray(c2.reshape(DEPTH, 128, 512)),
        "w_in": np.ascontiguousarray(inp["w_in"], dtype=np.float32),
        "s5_w_glu": np.ascontiguousarray(inp["s5_w_glu"], dtype=np.float32),
        "w_out": np.ascontiguousarray(inp["w_out"], dtype=np.float32),
        "w_up": np.ascontiguousarray(inp["w_up"], dtype=np.float32),
        "w_down": np.ascontiguousarray(inp["w_down"], dtype=np.float32),
    }
    maps = []
    for b in range(inp["x"].shape[0]):
        m = dict(shared)
        m["xT"] = np.ascontiguousarray(inp["x"][b].T.astype(np.float32))
        m["pos"] = np.ascontiguousarray(inp["positions"][b].astype(np.int32).reshape(1, SEQ))
        maps.append(m)
    return maps


_NC_CACHE = {}


def kernel(**inputs):
    if "nc" not in _NC_CACHE:
        _NC_CACHE["nc"] = build()[0]
    nc = _NC_CACHE["nc"]
    maps = make_in_maps(inputs)
    res = run_bass_kernel_spmd(nc, maps, core_ids=list(range(len(maps))))
    out = np.stack([np.ascontiguousarray(r["outT"].T) for r in res.results], axis=0)
    return out.astype(np.float32)
```

```python
import math
import os
from collections import deque
import numpy as np
import concourse.bass as bass
import concourse.mybir as mybir
from concourse.bass_utils import run_bass_kernel_spmd

F32 = mybir.dt.float32
BF16 = mybir.dt.bfloat16
I32 = mybir.dt.int32
AF = mybir.ActivationFunctionType
ALU = mybir.AluOpType

ENGS = ("pe", "act", "dve", "pool", "sp")
N_DMA_SEMS = 24

D_MODEL = 1024
SEQ = 2048
DEPTH = 2
TT = 512
NT = SEQ // TT
IN_WIDTH = 3584
D_FF = 3072
NORM_EPS = 1e-6
MAGIC = 12582912.0
TWO_PI_S = 6.2831850
GAMMAS = [1.0 - 2.0 ** (-5.0 - h) for h in range(4)]


class Reg:
    __slots__ = ("name", "w", "rds")

    def __init__(self, name):
        self.name = name
        self.w = None
        self.rds = []


class T:
    def __init__(self, h, name, nslots=1):
        self.h = h
        self.name = name
        self.regs = [Reg(f"{name}.{i}") for i in range(nslots)]

    def __getitem__(self, k):
        return self.h[k]

    def r(self, i=None, j=None):
        if i is None:
            return list(self.regs)
        if j is None:
            return [self.regs[i]]
        return self.regs[i:j]


class Ins:
    __slots__ = ("eng", "rec", "fn", "preds", "is_dma", "dma_sem", "dma_use", "cost", "lat", "seg",
                 "tset", "sched", "done", "rt", "pos", "waits", "needs_inc", "count", "waits_dma", "pin")

    def __init__(self, eng, rec, fn):
        self.eng = eng
        self.rec = rec
        self.fn = fn
        self.preds = {}
        self.is_dma = False
        self.dma_sem = None
        self.dma_use = None
        self.cost = 100.0
        self.lat = 0.0
        self.seg = 0
        self.tset = None
        self.sched = False
        self.done = 0.0
        self.rt = None
        self.pos = None
        self.waits = []
        self.needs_inc = False
        self.count = None
        self.pin = False


_ACT_GROUP = {}


def _act_group(func):
    if not _ACT_GROUP:
        _ACT_GROUP.update({AF.Exp: 1, AF.Ln: 1, AF.Gelu_apprx_tanh: 2, AF.Silu: 3, AF.Sin: 3, AF.Sigmoid: 4, AF.Sqrt: 5})
    return _ACT_GROUP.get(func)


def _free(ap):
    n = 1
    for d in ap.shape[1:]:
        n *= int(d)
    return n


def est_cost(eng, fn, is_dma):
    if fn is None:
        return 0.0, 0.0
    name, args, kw = fn
    if is_dma:
        out = kw["out"]
        nbytes = _free(out) * int(out.shape[0]) * 4
        return (1000.0 if eng == "pool" else 120.0), 2000.0 + nbytes / 150.0
    if eng == "pe":
        if name == "transpose":
            return 110.0, 0.0
        n = _free(kw["rhs"])
        return max(n, 64) / 1.9 + 10.0, 0.0
    out = kw.get("out")
    if out is None:
        out = args[0]
    n = _free(out)
    if eng == "act":
        return 150.0 + n / 1.2, 0.0
    if eng == "dve":
        if name == "tensor_tensor_scan":
            return 120.0 + 2.0 * n / 0.96, 0.0
        if name == "reciprocal":
            return 120.0 + 6.1 * n, 0.0
        if name in ("tensor_tensor", "scalar_tensor_tensor"):
            return 120.0 + n / 0.96, 0.0
        return 120.0 + n / 1.5, 0.0
    if eng == "pool":
        if name == "tensor_tensor":
            return 150.0 + 2.15 * n, 0.0
        return 150.0 + 1.2 * n, 0.0
    return 100.0, 0.0


class FW:
    SEM_LAT = 80.0
    WINDOW = int(os.environ.get('KW', '40'))
    WIN_ENG = {e: int(os.environ.get('KW_' + e.upper(), '0')) for e in ('pe', 'act', 'dve', 'pool', 'sp')}

    def __init__(self, nc):
        self.nc = nc
        self.all = []
        self.dma_rr = 0
        self.dma_rr_pool = 0
        self.dma_uses = [0] * N_DMA_SEMS
        self.dma_last = [None] * N_DMA_SEMS
        self.seg = 0
        self.seg_dma_uses = []
        self.pool_dmas = []
        self.pin = set()
        self._stack = []

    def sb(self, name, shape, dtype, nslots=1):
        cm = self.nc.sbuf_tensor(name, list(shape), dtype)
        h = cm.__enter__()
        self._stack.append(cm)
        return T(h, name, nslots)

    def ps(self, name, shape, dtype, nslots=1):
        cm = self.nc.psum_tensor(name, list(shape), dtype)
        h = cm.__enter__()
        self._stack.append(cm)
        return T(h, name, nslots)

    @staticmethod
    def _add_pred(ins, p, kind):
        if p is None or p is ins:
            return
        if p.is_dma or ins.is_dma or p.eng != ins.eng:
            needs = True
        else:
            needs = (ins.eng != "pe")
        ins.preds[p] = ins.preds.get(p, False) or needs

    def op(self, eng, fn, reads=(), writes=(), dma=False):
        ins = Ins(eng, len(self.all), fn)
        ins.seg = self.seg
        ins.is_dma = dma
        ins.pin = eng in self.pin
        self.all.append(ins)
        ins.cost, ins.lat = est_cost(eng, fn, dma)
        if eng == "act" and fn is not None and fn[0] == "activation":
            ins.tset = _act_group(fn[2].get("func"))
        if dma:
            half = N_DMA_SEMS // 2
            if eng == "pool":
                k = half + self.dma_rr_pool
                self.dma_rr_pool = (self.dma_rr_pool + 1) % half
            else:
                k = self.dma_rr
                self.dma_rr = (k + 1) % half
            self._add_pred(ins, self.dma_last[k], "SEM")
            self.dma_last[k] = ins
            self.dma_uses[k] += 1
            ins.dma_sem = k
            ins.dma_use = self.dma_uses[k]
            if eng == "pool":
                self.pool_dmas.append(ins)
                if len(self.pool_dmas) > 4:
                    self._add_pred(ins, self.pool_dmas[-5], "SEM")
        for r in reads:
            self._add_pred(ins, r.w, "RAW")
        for r in writes:
            self._add_pred(ins, r.w, "WAW")
            for p in r.rds:
                self._add_pred(ins, p, "WAR")
        for r in reads:
            r.rds.append(ins)
        for r in writes:
            r.w = ins
            r.rds = []
        return ins

    def barrier(self):
        self.seg_dma_uses.append(list(self.dma_uses))
        self.seg += 1

    def schedule(self):
        nseg = self.seg + 1
        final = {e: [] for e in ENGS}
        eng_free = {e: 0.0 for e in ENGS}
        cur_set = None
        bar_marks = []
        for sg in range(nseg):
            pending = {e: [i for i in self.all if i.seg == sg and i.eng == e] for e in ENGS}
            remaining = sum(len(v) for v in pending.values())
            while remaining:
                best = None
                best_t = None
                for e in ENGS:
                    lst = pending[e]
                    ef = eng_free[e]
                    lim = min(len(lst), self.WIN_ENG[e] or self.WINDOW)
                    if lim and lst[0].pin:
                        lim = 1
                    for wi in range(lim):
                        c = lst[wi]
                        if wi > 0 and c.pin:
                            break
                        if c.rt is None:
                            rt = 0.0
                            ok = True
                            for p, needs in c.preds.items():
                                if not p.sched:
                                    ok = False
                                    break
                                d = p.done + (self.SEM_LAT if needs else 0.0)
                                if d > rt:
                                    rt = d
                            if not ok:
                                continue
                            c.rt = rt
                        t = c.rt if c.rt > ef else ef
                        if e == "act" and c.tset is not None and c.tset != cur_set:
                            t += 1300.0
                        if best is None or t < best_t or (t == best_t and c.rec < best[1].rec):
                            best = (e, c, wi)
                            best_t = t
                        if t <= ef:
                            break
                assert best is not None, "scheduler deadlock"
                e, c, wi = best
                pending[e].pop(wi)
                remaining -= 1
                if e == "act" and c.tset is not None:
                    cur_set = c.tset
                end = best_t + c.cost
                eng_free[e] = end
                c.done = end + c.lat
                c.sched = True
                c.pos = len(final[e])
                final[e].append(c)
            if sg < nseg - 1:
                tmax = max(eng_free.values())
                dmax = max([i.done for i in self.all if i.seg == sg and i.is_dma] + [0.0])
                tmax = max(tmax, dmax)
                marks = {}
                for e in ENGS:
                    b = Ins(e, -1, None)
                    b.sched = True
                    b.pos = len(final[e])
                    final[e].append(b)
                    marks[e] = b
                    eng_free[e] = tmax
                bar_marks.append(marks)
        self.final = final
        self.bar_marks = bar_marks
        self.est_total = max(eng_free.values())

    def emit(self):
        nc = self.nc
        self.schedule()
        final = self.final
        for bi, marks in enumerate(self.bar_marks):
            for e, b in marks.items():
                for e2 in ENGS:
                    if e2 == e:
                        continue
                    pos2 = marks[e2].pos
                    for j in range(pos2 - 1, -1, -1):
                        p = final[e2][j]
                        if p.fn is not None and not p.is_dma:
                            b.preds[p] = True
                            break
                b.waits_dma = self.seg_dma_uses[bi]
        for e in ENGS:
            seen = {}
            for ins in final[e]:
                for p, needs in ins.preds.items():
                    if not needs:
                        assert p.eng == e and p.pos < ins.pos, "ordering edge violated"
                        continue
                    if p.is_dma:
                        key = f"d{p.dma_sem}"
                        v = p.dma_use
                    else:
                        key = p.eng
                        v = p.pos
                        if p.eng == e:
                            assert p.pos < ins.pos
                    if seen.get(key, -1) >= v:
                        continue
                    seen[key] = v
                    ins.waits.append((key, p))
                    if not p.is_dma:
                        p.needs_inc = True
                wd = getattr(ins, "waits_dma", None) if ins.fn is None else None
                if wd is not None:
                    for k, u in enumerate(wd):
                        if u > 0 and seen.get(f"d{k}", -1) < u:
                            seen[f"d{k}"] = u
                            ins.waits.append((f"d{k}", u))
        sem_cms = []
        sems = {}
        for e in list(ENGS) + [f"d{k}" for k in range(N_DMA_SEMS)]:
            cm = nc.semaphore(f"s_{e}")
            sems[e] = cm.__enter__()
            sem_cms.append(cm)
        for e in ENGS:
            c = 0
            for ins in final[e]:
                if ins.needs_inc:
                    c += 1
                    ins.count = c

        def run(eng_name, eng):
            for ins in final[eng_name]:
                for (key, p) in ins.waits:
                    if isinstance(p, int):
                        eng.wait_ge(sems[key], 16 * p)
                    elif p.is_dma:
                        eng.wait_ge(sems[key], 16 * p.dma_use)
                    else:
                        eng.wait_ge(sems[key], p.count)
                if ins.fn is None:
                    continue
                name, args, kw = ins.fn
                bi = getattr(eng, name)(*args, **kw)
                if ins.dma_sem is not None:
                    bi.then_inc(sems[f"d{ins.dma_sem}"], 16)
                elif ins.needs_inc:
                    bi.then_inc(sems[eng_name], 1)

        with nc.Block() as block:
            @block.tensor
            def _(eng):
                run("pe", eng)

            @block.scalar
            def _(eng):
                run("act", eng)

            @block.vector
            def _(eng):
                run("dve", eng)

            @block.gpsimd
            def _(eng):
                run("pool", eng)

            @block.sync
            def _(eng):
                run("sp", eng)
        for cm in reversed(sem_cms):
            cm.__exit__(None, None, None)

    def close(self):
        for cm in reversed(self._stack):
            cm.__exit__(None, None, None)
        self._stack = []


def OP(name, *args, **kw):
    return (name, args, kw)


class TV:
    def __init__(self, parent, slot, ap):
        self.parent = parent
        self.slot = slot
        self.ap = ap
        self.is_view = True

    def __getitem__(self, k):
        return self.ap[k]

    def r(self):
        return self.parent.r(self.slot)


class Pool:
    def __init__(self, items):
        self.free_ = deque(items)
        self.n = len(items)
        self.extra = []

    def get(self, halo=False):
        assert self.free_, "scratch pool exhausted"
        if not halo:
            return self.free_.popleft()
        for i, it in enumerate(self.free_):
            if not getattr(it, "is_view", False):
                del self.free_[i]
                return it
        raise AssertionError("no halo-capable scratch tile free")

    def free(self, *items):
        for it in items:
            self.free_.append(it)

    def enable_extra(self, views):
        self.extra = list(views)
        for v in views:
            self.free_.appendleft(v)

    def disable_extra(self):
        for v in self.extra:
            assert v in self.free_, "extra scratch view still in use"
            self.free_.remove(v)
        self.extra = []


def _chunked(v, nchunk):
    return np.ascontiguousarray(np.asarray(v, np.float32).reshape(nchunk, 128).T)


class PVLayout:
    def __init__(self):
        self.idx = {}
        self.n = 0

    def add(self, name, k):
        self.idx[name] = (self.n, k)
        self.n += k


def pv_layout():
    pl = PVLayout()
    for l in range(DEPTH):
        for name, k in (("gmix", 8), ("gffn", 8), ("lcw", 16), ("lcb", 4), ("lba", 4), ("lbx", 4),
                        ("llam", 4), ("lnorm", 4), ("s5d", 4), ("s5bg", 4), ("s5n", 4), ("rnorm", 4),
                        ("fcw", 144), ("fcb", 48), ("s5lr", 32), ("s5li", 32), ("s5ldt", 32)):
            pl.add(f"{name}{l}", k)
    for name, k in (("gfin", 8), ("invf", 1), ("sgn", 1), ("nsgn", 1), ("vdec", 4), ("rowmask", 8)):
        pl.add(name, k)
    return pl


PVL = pv_layout()
C_ID = 0
C_IOTA = 128
C_MASK = C_IOTA + 512
C_QDEC = C_MASK + 128
NCST = C_QDEC + 512


def host_consts():
    cst = np.zeros((128, NCST), np.float32)
    cst[:, C_ID:C_ID + 128] = np.eye(128, dtype=np.float32)
    cst[:, C_IOTA:C_IOTA + 512] = np.arange(512, dtype=np.float32)[None, :]
    m = np.arange(128)[:, None]
    c = np.arange(128)[None, :]
    cst[:, C_MASK:C_MASK + 128] = np.where(c >= m, np.float32(128.0 ** -0.5), np.float32(0.0))
    for h in range(4):
        lg = np.log1p(-np.float32(2.0) ** np.float32(-5.0 - h)).astype(np.float32)
        cst[:, C_QDEC + h * 128:C_QDEC + (h + 1) * 128] = np.exp(lg * (np.arange(128, dtype=np.float32) + 1.0))[None, :]
    return cst


def host_pv(inp):
    pv = np.zeros((128, PVL.n), np.float32)

    def put(name, arr):
        o, k = PVL.idx[name]
        assert arr.shape == (128, k), (name, arr.shape, k)
        pv[:, o:o + k] = arr

    for l in range(DEPTH):
        put(f"gmix{l}", _chunked(inp["norm_mix"][l], 8))
        put(f"gffn{l}", _chunked(inp["norm_ffn"][l], 8))
        cw = np.asarray(inp["lru_conv_w"][l], np.float32)
        put(f"lcw{l}", np.ascontiguousarray(cw.reshape(4, 4, 128).transpose(2, 1, 0).reshape(128, 16)))
        put(f"lcb{l}", _chunked(inp["lru_conv_b"][l], 4))
        put(f"lba{l}", _chunked(np.asarray(inp["lru_ba"][l]).reshape(512), 4))
        put(f"lbx{l}", _chunked(np.asarray(inp["lru_bx"][l]).reshape(512), 4))
        put(f"llam{l}", _chunked(inp["lru_lambda"][l], 4))
        put(f"lnorm{l}", _chunked(inp["lru_norm"][l], 4))
        put(f"s5d{l}", _chunked(inp["s5_d"][l], 4))
        put(f"s5bg{l}", _chunked(inp["s5_b_glu"][l], 4))
        put(f"s5n{l}", _chunked(inp["s5_norm"][l], 4))
        put(f"rnorm{l}", _chunked(inp["ret_norm"][l], 4))
        fw_ = np.asarray(inp["ffn_conv_w"][l], np.float32)
        put(f"fcw{l}", np.ascontiguousarray(fw_.reshape(3, 48, 128).transpose(2, 1, 0).reshape(128, 144)))
        put(f"fcb{l}", _chunked(inp["ffn_conv_b"][l], 48))
        lr = np.asarray(inp["s5_lambda_re"][l], np.float32).T
        li = np.asarray(inp["s5_lambda_im"][l], np.float32).T
        put(f"s5lr{l}", np.concatenate([lr, lr], 0))
        put(f"s5li{l}", np.concatenate([li, li], 0))
        put(f"s5ldt{l}", np.broadcast_to(np.asarray(inp["s5_log_dt"][l], np.float32)[None, :], (128, 32)).copy())
    put("gfin", _chunked(inp["norm_final"], 8))
    half = 64
    inv = (np.float32(10000.0) ** (-np.arange(half, dtype=np.float32) * np.float32(2.0) / np.float32(128.0))).astype(np.float32)
    put("invf", np.concatenate([inv, inv])[:, None])
    sgn = np.concatenate([-np.ones(64, np.float32), np.ones(64, np.float32)])[:, None]
    put("sgn", sgn)
    put("nsgn", -sgn)
    vd = np.zeros((128, 4), np.float32)
    for h in range(4):
        lg = np.log1p(-np.float32(2.0) ** np.float32(-5.0 - h)).astype(np.float32)
        vd[:, h] = np.exp(-lg * (np.arange(128, dtype=np.float32) + 1.0))
    put("vdec", vd)
    rm = np.zeros((128, 8), np.float32)
    for g in range(8):
        rm[16 * g:16 * g + 16, g] = 1.0
    put("rowmask", rm)
    return pv


def host_struct(inp):
    wbd = np.zeros((DEPTH, 128, 2, 4, 128), np.float32)
    for l in range(DEPTH):
        for which, nm in enumerate(("lru_wa", "lru_wx")):
            w = np.asarray(inp[nm][l], np.float32)
            for c in range(4):
                for hb in range(2):
                    wbd[l, hb * 64:(hb + 1) * 64, which, c, hb * 64:(hb + 1) * 64] = w[2 * c + hb]
    x1 = np.zeros((DEPTH, 128, 512), np.float32)
    x2 = np.zeros((DEPTH, 128, 512), np.float32)
    c1 = np.zeros((DEPTH, 128, 4, 128), np.float32)
    c2 = np.zeros((DEPTH, 128, 4, 128), np.float32)
    for l in range(DEPTH):
        br = np.asarray(inp["s5_b_re"][l], np.float32).transpose(1, 0, 2).reshape(64, 512)
        bi = np.asarray(inp["s5_b_im"][l], np.float32).transpose(1, 0, 2).reshape(64, 512)
        x1[l] = np.concatenate([br, bi], 0)
        x2[l] = np.concatenate([bi, br], 0)
        cr = np.asarray(inp["s5_c_re"][l], np.float32).reshape(4, 128, 64)
        ci = np.asarray(inp["s5_c_im"][l], np.float32).reshape(4, 128, 64)
        c1[l] = np.concatenate([cr, ci], 2).transpose(1, 0, 2)
        c2[l] = np.concatenate([ci, cr], 2).transpose(1, 0, 2)
    return wbd, x1, x2, c1, c2


def build(debug=(), upto="all", nlayers=DEPTH):
    nc = bass.Bass("TRN2", target_bir_lowering=False)
    fw = FW(nc)
    dram = {}

    def din(name, shape, dtype=F32):
        dram[name] = nc.dram_tensor(name, list(shape), dtype, kind="ExternalInput").ap()
        return dram[name]

    xT_d = din("xT", [D_MODEL, SEQ])
    pos_d = din("pos", [1, SEQ], I32)
    pv_d = din("pv", [128, PVL.n])
    cst_d = din("cst", [128, NCST])
    wbd_d = din("wbd", [DEPTH, 128, 2 * 4 * 128])
    x1_d = din("s5x1", [DEPTH, 128, 512])
    x2_d = din("s5x2", [DEPTH, 128, 512])
    c1_d = din("s5c1", [DEPTH, 128, 512])
    c2_d = din("s5c2", [DEPTH, 128, 512])
    w_in_d = din("w_in", [DEPTH, D_MODEL, IN_WIDTH])
    w_glu_d = din("s5_w_glu", [DEPTH, 512, 512])
    w_out_d = din("w_out", [DEPTH, 1536, D_MODEL])
    w_up_d = din("w_up", [DEPTH, D_MODEL, 2 * D_FF])
    w_down_d = din("w_down", [DEPTH, D_FF, D_MODEL])
    outT_d = nc.dram_tensor("outT", [D_MODEL, SEQ], F32, kind="ExternalOutput").ap()
    dbg_out = {}
    out_dmas = []

    A = fw.op

    hT = fw.sb("hT", [128, 8, SEQ], F32, nslots=32)
    xn = fw.sb("xn", [128, 8, SEQ], BF16, nslots=32)
    rcos = fw.sb("rcos", [128, SEQ], F32, nslots=4)
    rsin = fw.sb("rsin", [128, SEQ], F32, nslots=4)
    pv = fw.sb("pvs", [128, PVL.n], F32)
    cst = fw.sb("csts", [128, NCST], F32)
    ident_b = fw.sb("ident_b", [128, 128], BF16)
    ones_b = fw.sb("ones_b", [128, 128], BF16)
    o128_b = fw.sb("o128_b", [128, 128], BF16)
    NRING = 9
    ring = Pool([fw.sb(f"ring{i}", [128, 520], F32) for i in range(NRING)])
    psum = fw.ps("psum", [128, 8, 512], F32, nslots=8)
    banks = Pool(list(range(8)))
    rot_views = [TV(rcos, i, rcos[:, i * TT:(i + 1) * TT]) for i in range(4)] + [TV(rsin, i, rsin[:, i * TT:(i + 1) * TT]) for i in range(4)]

    def PV(name, c=0, n=1):
        o, k = PVL.idx[name]
        return pv[:, o + c:o + c + n]

    def hs(c, t):
        return hT.r(c * 4 + t)

    def xs(c, t):
        return xn.r(c * 4 + t)

    def tsl(t):
        return slice(t * TT, (t + 1) * TT)

    def bf(tile, n=TT, off=0):
        return tile[:].bitcast(BF16)[:, off:off + n]

    def dbg(name, tile, ap, shape, dtype=F32):
        if name not in debug:
            return
        d = nc.dram_tensor("dbg_" + name, list(shape), dtype, kind="ExternalOutput").ap()
        dbg_out[name] = d
        ins = A("sp", OP("dma_start", out=d, in_=ap), reads=tile.r(), dma=True)
        out_dmas.append(ins)

    A("sp", OP("dma_start", out=pv[:], in_=pv_d), writes=pv.r(), dma=True)
    A("sp", OP("dma_start", out=cst[:], in_=cst_d), writes=cst.r(), dma=True)
    for c in range(8):
        for t in range(NT):
            A("sp", OP("dma_start", out=hT[:, c, tsl(t)], in_=xT_d[c * 128:(c + 1) * 128, tsl(t)]),
              writes=hs(c, t), dma=True)
    A("dve", OP("memset", ones_b[:], 1.0), writes=ones_b.r())
    A("dve", OP("memset", o128_b[:], 1.0 / 128.0), writes=o128_b.r())
    A("act", OP("activation", out=ident_b[:], in_=cst[:, C_ID:C_ID + 128], func=AF.Copy),
      reads=cst.r(), writes=ident_b.r())
    ident_f = cst[:, C_ID:C_ID + 128]
    iota = cst[:, C_IOTA:C_IOTA + 512]
    maskT = cst[:, C_MASK:C_MASK + 128]

    def load_w(dst_tile, dst_ap, src_ap):
        return A("pool", OP("dma_start", out=dst_ap, in_=src_ap), writes=dst_tile.r(), dma=True)

    def rmsnorm_to_xn(gname):
        for t in range(NT):
            b = banks.get()
            for c in range(8):
                sq = ring.get()
                A("act", OP("activation", out=bf(sq), in_=hT[:, c, tsl(t)], func=AF.Square),
                  reads=hs(c, t), writes=sq.r())
                A("pe", OP("matmul", out=psum[:, b, :], lhsT=ones_b[:], rhs=bf(sq), start=(c == 0), stop=(c == 7)),
                  reads=sq.r() + ones_b.r(), writes=psum.r(b))
                ring.free(sq)
            rs = ring.get()
            A("act", OP("activation", out=rs[:, 0:TT], in_=psum[:, b, :], func=AF.Ln, scale=1.0 / D_MODEL, bias=NORM_EPS), reads=psum.r(b), writes=rs.r())
            banks.free(b)
            A("act", OP("activation", out=rs[:, 0:TT], in_=rs[:, 0:TT], func=AF.Exp, scale=-0.5), reads=rs.r(), writes=rs.r())
            for c in range(8):
                A("dve", OP("scalar_tensor_tensor", out=xn[:, c, tsl(t)], in0=hT[:, c, tsl(t)], scalar=PV(gname, c),
                                                                      in1=rs[:, 0:TT], op0=ALU.mult, op1=ALU.mult),
                  reads=hs(c, t) + rs.r() + pv.r(), writes=xs(c, t))
            ring.free(rs)

    def proj(b, wtile, w_ap_fn, t):
        for k in range(8):
            A("pe", OP("matmul", out=psum[:, b, :], lhsT=w_ap_fn(k), rhs=xn[:, k, tsl(t)], start=(k == 0), stop=(k == 7)),
              reads=wtile.r() + xs(k, t), writes=psum.r(b))

    def group_post(l, mixed, gname, grp, wbufs_pool, eps_div):
        wo = wbufs_pool.get()
        wo_v = wo[:].rearrange("p (k n) -> p k n", k=4)
        load_w(wo, wo_v, w_out_d[l, grp * 512:(grp + 1) * 512, :].rearrange("(k p) n -> p k n", p=128))
        for t in range(NT):
            if gname is not None:
                b = banks.get()
                for c in range(4):
                    sq = ring.get()
                    A("act", OP("activation", out=bf(sq), in_=mixed[:, c, tsl(t)], func=AF.Square),
                      reads=mixed.r(c * 4 + t), writes=sq.r())
                    A("pe", OP("matmul", out=psum[:, b, :], lhsT=ones_b[:], rhs=bf(sq), start=(c == 0), stop=(c == 3)),
                      reads=sq.r() + ones_b.r(), writes=psum.r(b))
                    ring.free(sq)
                rs = ring.get()
                A("act", OP("activation", out=rs[:, 0:TT], in_=psum[:, b, :], func=AF.Ln, scale=1.0 / 512.0, bias=NORM_EPS), reads=psum.r(b), writes=rs.r())
                banks.free(b)
                A("act", OP("activation", out=rs[:, 0:TT], in_=rs[:, 0:TT], func=AF.Exp, scale=-0.5), reads=rs.r(), writes=rs.r())
                for c in range(4):
                    A("dve", OP("scalar_tensor_tensor", out=mixed[:, c, tsl(t)], in0=mixed[:, c, tsl(t)],
                                                                          scalar=PV(gname + str(l), c), in1=rs[:, 0:TT],
                                                                          op0=ALU.mult, op1=ALU.mult),
                      reads=mixed.r(c * 4 + t) + rs.r() + pv.r(), writes=mixed.r(c * 4 + t))
                ring.free(rs)
            for dc in range(8):
                b = banks.get()
                for k in range(4):
                    A("pe", OP("matmul", out=psum[:, b, :], lhsT=wo_v[:, k, dc * 128:(dc + 1) * 128],
                                                           rhs=mixed[:, k, tsl(t)], start=(k == 0), stop=(k == 3)),
                      reads=wo.r() + mixed.r(k * 4 + t), writes=psum.r(b))
                A("dve", OP("tensor_tensor", out=hT[:, dc, tsl(t)], in0=hT[:, dc, tsl(t)], in1=psum[:, b, :], op=ALU.add),
                  reads=hs(dc, t) + psum.r(b), writes=hs(dc, t))
                banks.free(b)
        wbufs_pool.free(wo)

    def build_rot():
        for t in range(NT):
            pi_ = ring.get(); pf = ring.get(); k_ = ring.get()
            A("sp", OP("dma_start", out=pi_[:].bitcast(I32)[:, 0:TT],
                                                          in_=pos_d[0:1, tsl(t)].to_broadcast([128, TT])),
              writes=pi_.r(), dma=True)
            A("dve", OP("tensor_copy", out=pf[:, 0:TT], in_=pi_[:].bitcast(I32)[:, 0:TT]),
              reads=pi_.r(), writes=pf.r())
            A("dve", OP("tensor_scalar", out=pf[:, 0:TT], in0=pf[:, 0:TT], scalar1=PV("invf"), scalar2=None,
                                                      op0=ALU.mult), reads=pf.r() + pv.r(), writes=pf.r())
            A("dve", OP("tensor_scalar", out=k_[:, 0:TT], in0=pf[:, 0:TT], scalar1=1.0 / (2.0 * math.pi),
                                                             scalar2=MAGIC, op0=ALU.mult, op1=ALU.add),
              reads=pf.r(), writes=k_.r())
            A("dve", OP("tensor_scalar", out=k_[:, 0:TT], in0=k_[:, 0:TT], scalar1=MAGIC, scalar2=None,
                                                      op0=ALU.subtract), reads=k_.r(), writes=k_.r())
            C1 = 6.28125
            C2 = 2.0 * math.pi - 6.28125
            A("dve", OP("scalar_tensor_tensor", out=pf[:, 0:TT], in0=k_[:, 0:TT], scalar=-C1, in1=pf[:, 0:TT],
                                                                    op0=ALU.mult, op1=ALU.add), reads=pf.r() + k_.r(), writes=pf.r())
            A("dve", OP("scalar_tensor_tensor", out=pf[:, 0:TT], in0=k_[:, 0:TT], scalar=-C2, in1=pf[:, 0:TT],
                                                                    op0=ALU.mult, op1=ALU.add), reads=pf.r() + k_.r(), writes=pf.r())
            A("dve", OP("tensor_scalar", out=pf[:, 0:TT], in0=pf[:, 0:TT], scalar1=3.1415925, scalar2=-3.1415925,
                                                      op0=ALU.min, op1=ALU.max), reads=pf.r(), writes=pf.r())
            A("act", OP("activation", out=rsin[:, tsl(t)], in_=pf[:, 0:TT], func=AF.Sin, scale=PV("sgn")),
              reads=pf.r() + pv.r(), writes=rsin.r(t))
            A("act", OP("activation", out=k_[:, 0:TT], in_=pf[:, 0:TT], func=AF.Abs),
              reads=pf.r(), writes=k_.r())
            A("act", OP("activation", out=rcos[:, tsl(t)], in_=k_[:, 0:TT], func=AF.Sin, scale=-1.0,
                                                        bias=math.pi / 2.0 - 1e-6), reads=k_.r(), writes=rcos.r(t))
            ring.free(pi_, pf, k_)


    phase = []
    for l in range(nlayers):

        def psb(name, shape, dtype, nslots=1):
            cm = nc.sbuf_tensor(f"{name}_{l}", list(shape), dtype)
            h = cm.__enter__()
            phase.append(cm)
            return T(h, name, nslots)

        mixed = psb("mixed", [128, 4, SEQ], BF16, nslots=16)
        aux = psb("aux", [128, 8192], BF16, nslots=16)
        wbufs = Pool([psb(f"wbuf{i}", [128, 4096], BF16) for i in range(2)])
        wbd = psb("wbd", [128, 2, 4, 128], BF16)
        lpar = psb("lpar", [128, 16], F32)
        lcar = psb("lcar", [128, 4, 4], F32)
        s5p = psb("s5p", [128, 32, 8], F32)
        s5off = psb("s5off", [128, 32, 4], F32)
        bst1 = psb("bst1", [128, 512], BF16)
        bst2 = psb("bst2", [128, 512], BF16)
        w5pool = Pool([psb(f"w5_{i}", [128, 8, 128], BF16) for i in range(2)])
        lhs_b = psb("lhs_b", [128, 2, 8, 128], BF16)
        lhs_c = psb("lhs_c", [128, 2, 8, 128], BF16)
        zcar = psb("zcar", [128, 32], F32)
        ust = psb("ust", [128, 128], F32)
        pbs = [psb(f"pb{i}", [128, 128], BF16) for i in range(2)]

        load_w(wbd, wbd[:].rearrange("p a c n -> p (a c n)"), wbd_d[l])
        A("act", OP("activation", out=lpar[:, 0:4], in_=PV(f"llam{l}", 0, 4), func=AF.Exp, scale=-1.0), reads=pv.r(), writes=lpar.r())
        A("act", OP("activation", out=lpar[:, 0:4], in_=lpar[:, 0:4], func=AF.Ln, bias=1.0), reads=lpar.r(), writes=lpar.r())
        A("dve", OP("tensor_scalar", out=lpar[:, 4:8], in0=lpar[:, 0:4], scalar1=-4.0, scalar2=None, op0=ALU.mult), reads=lpar.r(), writes=lpar.r())
        A("dve", OP("tensor_scalar", out=lpar[:, 0:4], in0=lpar[:, 0:4], scalar1=-8.0, scalar2=None, op0=ALU.mult), reads=lpar.r(), writes=lpar.r())
        A("dve", OP("tensor_scalar", out=lpar[:, 8:12], in0=PV(f"lba{l}", 0, 4), scalar1=0.5, scalar2=None, op0=ALU.mult), reads=pv.r(), writes=lpar.r())
        A("dve", OP("tensor_scalar", out=lpar[:, 12:16], in0=PV(f"lbx{l}", 0, 4), scalar1=0.5, scalar2=None, op0=ALU.mult), reads=pv.r(), writes=lpar.r())
        A("dve", OP("memset", lcar[:], 0.0), writes=lcar.r())
        A("dve", OP("memset", zcar[:], 0.0), writes=zcar.r())

        S = lambda j: s5p[:, :, j]
        LR = PV(f"s5lr{l}", 0, 32)
        LI = PV(f"s5li{l}", 0, 32)
        R_, W_ = s5p.r(), s5p.r()
        A("act", OP("activation", out=S(6), in_=PV(f"s5ldt{l}", 0, 32), func=AF.Exp), reads=pv.r(), writes=W_)
        A("dve", OP("tensor_tensor", out=S(0), in0=LR, in1=S(6), op=ALU.mult), reads=R_ + pv.r(), writes=W_)
        A("act", OP("activation", out=S(0), in_=S(0), func=AF.Exp), reads=R_, writes=W_)
        A("dve", OP("tensor_tensor", out=S(7), in0=LI, in1=S(6), op=ALU.mult), reads=R_ + pv.r(), writes=W_)
        A("dve", OP("tensor_scalar", out=S(6), in0=S(7), scalar1=1.0 / (2.0 * math.pi), scalar2=MAGIC, op0=ALU.mult, op1=ALU.add), reads=R_, writes=W_)
        A("dve", OP("tensor_scalar", out=S(6), in0=S(6), scalar1=MAGIC, scalar2=None, op0=ALU.subtract), reads=R_, writes=W_)
        A("dve", OP("scalar_tensor_tensor", out=S(1), in0=S(7), scalar=1.0 / (2.0 * math.pi), in1=S(6), op0=ALU.mult, op1=ALU.subtract), reads=R_, writes=W_)
        A("act", OP("activation", out=S(7), in_=S(1), func=AF.Sin, scale=TWO_PI_S), reads=R_, writes=W_)
        A("act", OP("activation", out=S(6), in_=S(1), func=AF.Abs), reads=R_, writes=W_)
        A("act", OP("activation", out=S(6), in_=S(6), func=AF.Sin, scale=-TWO_PI_S, bias=math.pi / 2.0 - 1e-6), reads=R_, writes=W_)
        A("dve", OP("tensor_tensor", out=S(6), in0=S(6), in1=S(0), op=ALU.mult), reads=R_, writes=W_)
        A("dve", OP("tensor_scalar", out=S(6), in0=S(6), scalar1=-1.0, scalar2=None, op0=ALU.add), reads=R_, writes=W_)
        A("dve", OP("tensor_tensor", out=S(7), in0=S(7), in1=S(0), op=ALU.mult), reads=R_, writes=W_)
        A("dve", OP("tensor_tensor", out=S(5), in0=LR, in1=LR, op=ALU.mult), reads=R_ + pv.r(), writes=W_)
        A("dve", OP("tensor_tensor", out=S(4), in0=LI, in1=LI, op=ALU.mult), reads=R_ + pv.r(), writes=W_)
        A("dve", OP("tensor_tensor", out=S(5), in0=S(5), in1=S(4), op=ALU.add), reads=R_, writes=W_)
        A("dve", OP("reciprocal", out=S(5), in_=S(5)), reads=R_, writes=W_)
        A("dve", OP("tensor_tensor", out=S(2), in0=S(6), in1=LR, op=ALU.mult), reads=R_ + pv.r(), writes=W_)
        A("dve", OP("tensor_tensor", out=S(4), in0=S(7), in1=LI, op=ALU.mult), reads=R_ + pv.r(), writes=W_)
        A("dve", OP("tensor_tensor", out=S(2), in0=S(2), in1=S(4), op=ALU.add), reads=R_, writes=W_)
        A("dve", OP("tensor_tensor", out=S(2), in0=S(2), in1=S(5), op=ALU.mult), reads=R_, writes=W_)
        A("dve", OP("tensor_tensor", out=S(3), in0=S(7), in1=LR, op=ALU.mult), reads=R_ + pv.r(), writes=W_)
        A("dve", OP("tensor_tensor", out=S(4), in0=S(6), in1=LI, op=ALU.mult), reads=R_ + pv.r(), writes=W_)
        A("dve", OP("tensor_tensor", out=S(3), in0=S(3), in1=S(4), op=ALU.subtract), reads=R_, writes=W_)
        A("dve", OP("tensor_tensor", out=S(3), in0=S(3), in1=S(5), op=ALU.mult), reads=R_, writes=W_)
        A("dve", OP("tensor_copy", out=S(5), in_=S(3)), reads=R_, writes=W_)
        A("dve", OP("tensor_scalar", out=S(3), in0=S(3), scalar1=PV("sgn"), scalar2=None, op0=ALU.mult), reads=R_ + pv.r(), writes=W_)
        A("dve", OP("tensor_scalar", out=S(4), in0=S(2), scalar1=PV("nsgn"), scalar2=None, op0=ALU.mult), reads=R_ + pv.r(), writes=W_)
        for t in range(NT):
            A("dve", OP("tensor_scalar", out=s5off[:, :, t], in0=S(1), scalar1=float(TT * t), scalar2=MAGIC, op0=ALU.mult, op1=ALU.add),
              reads=R_, writes=s5off.r())
            A("dve", OP("tensor_scalar", out=s5off[:, :, t], in0=s5off[:, :, t], scalar1=MAGIC, scalar2=None, op0=ALU.subtract),
              reads=s5off.r(), writes=s5off.r())
            A("dve", OP("scalar_tensor_tensor", out=s5off[:, :, t], in0=S(1), scalar=float(TT * t), in1=s5off[:, :, t],
                                                           op0=ALU.mult, op1=ALU.subtract), reads=R_ + s5off.r(), writes=s5off.r())
        cn1 = ring.get(); cn2 = ring.get()
        A("sp", OP("dma_start", out=cn1[:, 0:512], in_=x1_d[l]), writes=cn1.r(), dma=True)
        A("sp", OP("dma_start", out=cn2[:, 0:512], in_=x2_d[l]), writes=cn2.r(), dma=True)
        v3 = lambda tl: tl[:, 0:512].rearrange("p (g c) -> p g c", c=16)
        bc = lambda j: s5p[:, :, j:j + 1].to_broadcast([128, 32, 16])
        t1 = ring.get(); t2 = ring.get()
        t1v = t1[:, 0:512].rearrange("p (g c) -> p g c", c=16)
        t2v = t2[:, 0:512].rearrange("p (g c) -> p g c", c=16)
        A("dve", OP("tensor_tensor", out=t1v, in0=v3(cn1), in1=bc(2), op=ALU.mult), reads=cn1.r() + R_, writes=t1.r())
        A("dve", OP("tensor_tensor", out=t2v, in0=v3(cn2), in1=bc(3), op=ALU.mult), reads=cn2.r() + R_, writes=t2.r())
        A("dve", OP("tensor_tensor", out=bst1[:], in0=t1[:, 0:512], in1=t2[:, 0:512], op=ALU.add), reads=t1.r() + t2.r(), writes=bst1.r())
        A("dve", OP("tensor_tensor", out=t1v, in0=v3(cn2), in1=bc(4), op=ALU.mult), reads=cn2.r() + R_, writes=t1.r())
        A("dve", OP("tensor_tensor", out=t2v, in0=v3(cn1), in1=bc(5), op=ALU.mult), reads=cn1.r() + R_, writes=t2.r())
        A("dve", OP("tensor_tensor", out=bst2[:], in0=t1[:, 0:512], in1=t2[:, 0:512], op=ALU.add), reads=t1.r() + t2.r(), writes=bst2.r())
        ring.free(t1, t2, cn1, cn2)
        A("pool", OP("memset", lhs_c[:].rearrange("p a g n -> p (a g n)"), 0.0), writes=lhs_c.r())
        dbg(f"bst1_{l}", bst1, bst1[:], [128, 512], BF16)
        dbg(f"s5p_{l}", s5p, s5p[:].rearrange("p g j -> p (g j)"), [128, 256])

        rmsnorm_to_xn(f"gmix{l}")
        dbg(f"xn{l}", xn, xn[:].rearrange("p c t -> p (c t)"), [128, 8 * SEQ], BF16)

        zbf = aux[:].rearrange("p (c t) -> p c t", c=4)
        def lru_chunk(c):
            wb = wbufs.get()
            wv_ = wb[:].rearrange("p (k n) -> p k n", k=8)
            load_w(wb, wv_[:, :, 0:128], w_in_d[l, :, c * 128:(c + 1) * 128].rearrange("(k p) n -> p k n", p=128))
            load_w(wb, wv_[:, :, 128:256], w_in_d[l, :, 512 + c * 128:512 + (c + 1) * 128].rearrange("(k p) n -> p k n", p=128))
            return wb, wv_

        def lru_unit(c, t, wb, wv_):
            bx = banks.get(); bg = banks.get()
            proj(bx, wb, lambda k: wv_[:, k, 0:128], t)
            proj(bg, wb, lambda k: wv_[:, k, 128:256], t)
            xp = ring.get(halo=True); gy = ring.get(); xc = ring.get(); xcb = ring.get()
            A("act", OP("activation", out=xp[:, 0:3], in_=lcar[:, c, 0:3], func=AF.Copy), reads=lcar.r(), writes=xp.r())
            A("act", OP("activation", out=xp[:, 3:3 + TT], in_=psum[:, bx, :], func=AF.Copy), reads=psum.r(bx), writes=xp.r())
            A("act", OP("activation", out=lcar[:, c, 0:3], in_=xp[:, TT:TT + 3], func=AF.Copy), reads=xp.r(), writes=lcar.r())
            A("act", OP("activation", out=gy[:, 0:TT], in_=psum[:, bg, :], func=AF.Gelu_apprx_tanh), reads=psum.r(bg), writes=gy.r())
            banks.free(bx, bg)
            A("act", OP("activation", out=xc[:, 0:TT], in_=xp[:, 3:3 + TT], func=AF.Identity,
                                                          scale=PV(f"lcw{l}", c * 4 + 3), bias=PV(f"lcb{l}", c)),
              reads=xp.r() + pv.r(), writes=xc.r())
            for k in range(3):
                A("dve", OP("scalar_tensor_tensor", out=xc[:, 0:TT], in0=xp[:, k:k + TT], scalar=PV(f"lcw{l}", c * 4 + k),
                                                                            in1=xc[:, 0:TT], op0=ALU.mult, op1=ALU.add),
                  reads=xp.r() + xc.r() + pv.r(), writes=xc.r())
            A("act", OP("activation", out=bf(xcb), in_=xc[:, 0:TT], func=AF.Copy), reads=xc.r(), writes=xcb.r())
            ring.free(xp)
            br_ = banks.get(); bi_ = banks.get()
            A("pe", OP("matmul", out=psum[:, br_, :], lhsT=wbd[:, 0, c, :], rhs=bf(xcb), start=True, stop=True),
              reads=wbd.r() + xcb.r(), writes=psum.r(br_))
            A("pe", OP("matmul", out=psum[:, bi_, :], lhsT=wbd[:, 1, c, :], rhs=bf(xcb), start=True, stop=True),
              reads=wbd.r() + xcb.r(), writes=psum.r(bi_))
            rr = ring.get(); ii = ring.get(); a2 = ring.get()
            A("act", OP("activation", out=rr[:, 0:TT], in_=psum[:, br_, :], func=AF.Tanh, scale=0.5, bias=lpar[:, 8 + c:9 + c]),
              reads=psum.r(br_) + lpar.r(), writes=rr.r())
            A("act", OP("activation", out=ii[:, 0:TT], in_=psum[:, bi_, :], func=AF.Tanh, scale=0.5, bias=lpar[:, 12 + c:13 + c]),
              reads=psum.r(bi_) + lpar.r(), writes=ii.r())
            banks.free(br_, bi_)
            ring.free(xcb)
            A("act", OP("activation", out=a2[:, 0:TT], in_=rr[:, 0:TT], func=AF.Exp, scale=lpar[:, c:c + 1], bias=lpar[:, c:c + 1]),
              reads=rr.r() + lpar.r(), writes=a2.r())
            A("act", OP("activation", out=rr[:, 0:TT], in_=rr[:, 0:TT], func=AF.Exp, scale=lpar[:, 4 + c:5 + c], bias=lpar[:, 4 + c:5 + c]),
              reads=rr.r() + lpar.r(), writes=rr.r())
            A("dve", OP("tensor_scalar", out=a2[:, 0:TT], in0=a2[:, 0:TT], scalar1=1.0, scalar2=-1e-30, op0=ALU.subtract, op1=ALU.min),
              reads=a2.r(), writes=a2.r())
            A("act", OP("activation", out=a2[:, 0:TT], in_=a2[:, 0:TT], func=AF.Ln, scale=-1.0), reads=a2.r(), writes=a2.r())
            A("act", OP("activation", out=a2[:, 0:TT], in_=a2[:, 0:TT], func=AF.Exp, scale=0.5), reads=a2.r(), writes=a2.r())
            A("dve", OP("scalar_tensor_tensor", out=ii[:, 0:TT], in0=ii[:, 0:TT], scalar=1.0, in1=xc[:, 0:TT], op0=ALU.add, op1=ALU.mult),
              reads=ii.r() + xc.r(), writes=ii.r())
            A("dve", OP("scalar_tensor_tensor", out=ii[:, 0:TT], in0=ii[:, 0:TT], scalar=0.5, in1=a2[:, 0:TT], op0=ALU.mult, op1=ALU.mult),
              reads=ii.r() + a2.r(), writes=ii.r())
            hh = xc
            A("dve", OP("tensor_tensor_scan", out=hh[:, 0:TT], data0=rr[:, 0:TT], data1=ii[:, 0:TT],
                                                                        initial=lcar[:, c, 3:4], op0=ALU.mult, op1=ALU.add),
              reads=rr.r() + ii.r() + lcar.r(), writes=hh.r())
            A("act", OP("activation", out=lcar[:, c, 3:4], in_=hh[:, TT - 1:TT], func=AF.Copy), reads=hh.r(), writes=lcar.r())
            A("dve", OP("tensor_tensor", out=mixed[:, c, tsl(t)], in0=hh[:, 0:TT], in1=gy[:, 0:TT], op=ALU.mult),
              reads=hh.r() + gy.r(), writes=mixed.r(c * 4 + t))
            ring.free(rr, ii, a2, xc, gy)

        def s5_chunk(c):
            wb = w5pool.get()
            wv_ = wb
            load_w(wb, wv_[:, :, 0:128], w_in_d[l, :, 1024 + c * 128:1024 + (c + 1) * 128].rearrange("(k p) n -> p k n", p=128))
            for which, bst in enumerate((bst1, bst2)):
                b = banks.get()
                tpv = psum[:, b, :].bitcast(BF16)
                A("pe", OP("transpose", out=tpv[:, 0:128], in_=bst[:, c * 128:(c + 1) * 128], identity=ident_b[:]),
                  reads=bst.r() + ident_b.r(), writes=psum.r(b))
                tb = ring.get()
                A("act", OP("activation", out=bf(tb, 128), in_=tpv[:, 0:128], func=AF.Copy), reads=psum.r(b), writes=tb.r())
                banks.free(b)
                for g in range(8):
                    A("dve", OP("tensor_scalar", out=lhs_b[:, which, g, :], in0=bf(tb, 128), scalar1=PV("rowmask", g),
                                                                                scalar2=None, op0=ALU.mult),
                      reads=tb.r() + pv.r(), writes=lhs_b.r())
                ring.free(tb)
            for which, (cd, sname) in enumerate(((c1_d, "nsgn"), (c2_d, None))):
                cn = ring.get()
                A("sp", OP("dma_start", out=cn[:, 0:128], in_=cd[l, :, c * 128:(c + 1) * 128]), writes=cn.r(), dma=True)
                b = banks.get()
                A("pe", OP("transpose", out=psum[:, b, 0:128], in_=cn[:, 0:128], identity=ident_f),
                  reads=cn.r() + cst.r(), writes=psum.r(b))
                ring.free(cn)
                for g in range(8):
                    if sname is not None:
                        A("dve", OP("tensor_scalar", out=lhs_c[:, which, g, g * 16:(g + 1) * 16], in0=psum[:, b, g * 16:(g + 1) * 16],
                                                                                  scalar1=PV("nsgn"), scalar2=None, op0=ALU.mult),
                          reads=psum.r(b) + pv.r(), writes=lhs_c.r())
                    else:
                        A("dve", OP("tensor_scalar", out=lhs_c[:, which, g, g * 16:(g + 1) * 16], in0=psum[:, b, g * 16:(g + 1) * 16],
                                                                                  scalar1=-1.0, scalar2=None, op0=ALU.mult),
                          reads=psum.r(b), writes=lhs_c.r())
                banks.free(b)
            return wb, wv_

        def s5_unit(c, t, wb, wv_):
            bu = banks.get()
            proj(bu, wb, lambda k: wv_[:, k, 0:128], t)
            uf = ring.get(); ub = ring.get()
            A("act", OP("activation", out=uf[:, 0:TT], in_=psum[:, bu, :], func=AF.Copy), reads=psum.r(bu), writes=uf.r())
            A("act", OP("activation", out=bf(ub), in_=psum[:, bu, :], func=AF.Copy), reads=psum.r(bu), writes=ub.r())
            banks.free(bu)
            by = banks.get()
            for g in range(8):
                G = c * 8 + g
                b1 = banks.get(); b2 = banks.get()
                A("pe", OP("matmul", out=psum[:, b1, :], lhsT=lhs_b[:, 0, g, :], rhs=bf(ub), start=True, stop=True),
                  reads=lhs_b.r() + ub.r(), writes=psum.r(b1))
                A("pe", OP("matmul", out=psum[:, b2, :], lhsT=lhs_b[:, 1, g, :], rhs=bf(ub), start=True, stop=True),
                  reads=lhs_b.r() + ub.r(), writes=psum.r(b2))
                uu = ring.get(); kk = ring.get(); tc_ = ring.get(); ts_ = ring.get()
                A("act", OP("activation", out=uu[:, 0:TT], in_=iota, func=AF.Identity, scale=s5p[:, G, 1:2], bias=s5off[:, G, t:t + 1]), reads=cst.r() + s5p.r() + s5off.r(), writes=uu.r())
                A("dve", OP("tensor_scalar", out=kk[:, 0:TT], in0=uu[:, 0:TT], scalar1=MAGIC, scalar2=MAGIC,
                                                                  op0=ALU.add, op1=ALU.subtract), reads=uu.r(), writes=kk.r())
                A("pool", OP("tensor_tensor", out=uu[:, 0:TT], in0=uu[:, 0:TT], in1=kk[:, 0:TT], op=ALU.subtract),
                  reads=uu.r() + kk.r(), writes=uu.r())
                A("act", OP("activation", out=ts_[:, 0:TT], in_=uu[:, 0:TT], func=AF.Sin, scale=TWO_PI_S), reads=uu.r(), writes=ts_.r())
                A("act", OP("activation", out=kk[:, 0:TT], in_=uu[:, 0:TT], func=AF.Abs), reads=uu.r(), writes=kk.r())
                A("act", OP("activation", out=tc_[:, 0:TT], in_=kk[:, 0:TT], func=AF.Sin, scale=-TWO_PI_S, bias=math.pi / 2.0 - 1e-6),
                  reads=kk.r(), writes=tc_.r())
                m1 = uu; m2 = kk
                A("dve", OP("tensor_tensor", out=m1[:, 0:TT], in0=psum[:, b1, :], in1=tc_[:, 0:TT], op=ALU.mult),
                  reads=psum.r(b1) + tc_.r(), writes=m1.r())
                A("dve", OP("tensor_tensor", out=m2[:, 0:TT], in0=psum[:, b2, :], in1=ts_[:, 0:TT], op=ALU.mult),
                  reads=psum.r(b2) + ts_.r(), writes=m2.r())
                banks.free(b1, b2)
                A("dve", OP("tensor_tensor", out=m1[:, 0:TT], in0=m1[:, 0:TT], in1=m2[:, 0:TT], op=ALU.add), reads=m1.r() + m2.r(), writes=m1.r())
                zz = m2
                A("dve", OP("tensor_tensor_scan", out=zz[:, 0:TT], data0=s5p[:, G, 0:1].to_broadcast([128, TT]), data1=m1[:, 0:TT],
                                                                          initial=zcar[:, G:G + 1], op0=ALU.mult, op1=ALU.add),
                  reads=m1.r() + s5p.r() + zcar.r(), writes=zz.r())
                A("act", OP("activation", out=zcar[:, G:G + 1], in_=zz[:, TT - 1:TT], func=AF.Copy), reads=zz.r(), writes=zcar.r())
                w12 = m1
                A("pool", OP("tensor_tensor", out=bf(w12, TT, 0), in0=zz[:, 0:TT], in1=tc_[:, 0:TT], op=ALU.mult),
                  reads=zz.r() + tc_.r(), writes=w12.r())
                A("pool", OP("tensor_tensor", out=bf(w12, TT, TT), in0=zz[:, 0:TT], in1=ts_[:, 0:TT], op=ALU.mult),
                  reads=zz.r() + ts_.r(), writes=w12.r())
                A("pe", OP("matmul", out=psum[:, by, :], lhsT=lhs_c[:, 0, g, :], rhs=bf(w12, TT, 0), start=(g == 0), stop=False),
                  reads=lhs_c.r() + w12.r(), writes=psum.r(by))
                A("pe", OP("matmul", out=psum[:, by, :], lhsT=lhs_c[:, 1, g, :], rhs=bf(w12, TT, TT), start=False, stop=(g == 7)),
                  reads=lhs_c.r() + w12.r(), writes=psum.r(by))
                ring.free(uu, kk, tc_, ts_)
            A("dve", OP("scalar_tensor_tensor", out=uf[:, 0:TT], in0=uf[:, 0:TT], scalar=PV(f"s5d{l}", c), in1=psum[:, by, :],
                                                             op0=ALU.mult, op1=ALU.add), reads=uf.r() + psum.r(by) + pv.r(), writes=uf.r())
            banks.free(by)
            A("act", OP("activation", out=zbf[:, c, tsl(t)], in_=uf[:, 0:TT], func=AF.Gelu_apprx_tanh), reads=uf.r(), writes=aux.r(c * 4 + t))
            ring.free(uf, ub)

        ring.enable_extra(rot_views)
        for c in range(4):
            lwb = lru_chunk(c)
            swb = s5_chunk(c)
            for t in range(NT):
                s5_unit(c, t, *swb)
                lru_unit(c, t, *lwb)
            wbufs.free(lwb[0])
            w5pool.free(swb[0])
        dbg(f"ylru_raw{l}", mixed, mixed[:].rearrange("p c t -> p (c t)"), [128, 4 * SEQ], BF16)
        if upto == "lru_raw":
            break
        group_post(l, mixed, "lnorm", 0, wbufs, 512.0)
        dbg(f"ylru{l}", mixed, mixed[:].rearrange("p c t -> p (c t)"), [128, 4 * SEQ], BF16)
        if upto == "lru":
            break

        dbg(f"s5z{l}", aux, aux[:], [128, 4 * SEQ], BF16)
        wg_t = wbufs.get()
        wgl = wg_t[:, 0:2048].rearrange("p (k n) -> p k n", k=4)
        load_w(wg_t, wgl, w_glu_d[l].rearrange("(k p) n -> p k n", p=128))
        for t in range(NT):
            for oc in range(4):
                b = banks.get()
                for k in range(4):
                    A("pe", OP("matmul", out=psum[:, b, :], lhsT=wgl[:, k, oc * 128:(oc + 1) * 128], rhs=zbf[:, k, tsl(t)],
                                                                start=(k == 0), stop=(k == 3)), reads=wg_t.r() + aux.r(k * 4 + t), writes=psum.r(b))
                sg = ring.get()
                A("act", OP("activation", out=sg[:, 0:TT], in_=psum[:, b, :], func=AF.Sigmoid, bias=PV(f"s5bg{l}", oc)),
                  reads=psum.r(b) + pv.r(), writes=sg.r())
                banks.free(b)
                A("dve", OP("tensor_tensor", out=mixed[:, oc, tsl(t)], in0=zbf[:, oc, tsl(t)], in1=sg[:, 0:TT], op=ALU.mult),
                  reads=sg.r() + aux.r(oc * 4 + t), writes=mixed.r(oc * 4 + t))
                ring.free(sg)
        wbufs.free(wg_t)
        group_post(l, mixed, "s5n", 1, wbufs, 512.0)
        dbg(f"ys5{l}", mixed, mixed[:].rearrange("p c t -> p (c t)"), [128, 4 * SEQ], BF16)
        if upto == "s5":
            break

        ring.disable_extra()
        build_rot()
        if l == 0:
            dbg("rcos", rcos, rcos[:], [128, SEQ])
            dbg("rsin", rsin, rsin[:], [128, SEQ])
        fw.pin = set(os.environ.get('KPIN', 'dve').split(',')) - {''}
        vh = aux[:].rearrange("p (n e) -> p n e", n=16)
        wv_t = wbufs.get()
        wvv = wv_t[:].rearrange("p (k n) -> p k n", k=8)
        load_w(wv_t, wvv, w_in_d[l, :, 2560:3072].rearrange("(k p) n -> p k n", p=128))
        for n in range(16):
            b = banks.get()
            t = n // 4
            for k in range(8):
                A("pe", OP("matmul", out=psum[:, b, :], lhsT=xn[:, k, n * 128:(n + 1) * 128], rhs=wvv[:, k, :],
                                                          start=(k == 0), stop=(k == 7)), reads=wv_t.r() + xs(k, t), writes=psum.r(b))
            for hd in range(4):
                A("act", OP("activation", out=vh[:, n, hd * 128:(hd + 1) * 128], in_=psum[:, b, hd * 128:(hd + 1) * 128],
                                                                 func=AF.Identity, scale=PV("vdec", hd)), reads=psum.r(b) + pv.r(), writes=aux.r(n))
            banks.free(b)
        wbufs.free(wv_t)
        wg_t = wbufs.get()
        wgv = wg_t[:].rearrange("p (k n) -> p k n", k=8)
        load_w(wg_t, wgv, w_in_d[l, :, 3072:3584].rearrange("(k p) n -> p k n", p=128))
        for hd in range(4):
            gam = GAMMAS[hd]
            gC = float(np.float32(np.exp(np.float32(np.log1p(-np.float32(2.0) ** np.float32(-5.0 - hd))) * np.float32(128.0))))
            wb = wbufs.get()
            wq = wb[:].rearrange("p (k n) -> p k n", k=8)
            qb = 1536 + hd * 128
            kb = 2048 + hd * 128
            src = lambda c0, c1: w_in_d[l, :, c0:c1].rearrange("(k p) n -> p k n", p=128)
            load_w(wb, wq[:, :, 0:128], src(qb, qb + 128))
            load_w(wb, wq[:, :, 128:192], src(qb + 64, qb + 128))
            load_w(wb, wq[:, :, 192:256], src(qb, qb + 64))
            load_w(wb, wq[:, :, 256:384], src(kb, kb + 128))
            load_w(wb, wq[:, :, 384:448], src(kb + 64, kb + 128))
            load_w(wb, wq[:, :, 448:512], src(kb, kb + 64))
            for t in range(NT):
                rot = []
                for which in range(2):
                    ba = banks.get(); bb = banks.get()
                    proj(ba, wb, lambda k, o=which * 256: wq[:, k, o:o + 128], t)
                    proj(bb, wb, lambda k, o=which * 256 + 128: wq[:, k, o:o + 128], t)
                    t1 = ring.get(); t2 = ring.get(); qr = ring.get()
                    A("dve", OP("tensor_tensor", out=t1[:, 0:TT], in0=psum[:, ba, :], in1=rcos[:, tsl(t)], op=ALU.mult),
                      reads=psum.r(ba) + rcos.r(t), writes=t1.r())
                    A("dve", OP("tensor_tensor", out=t2[:, 0:TT], in0=psum[:, bb, :], in1=rsin[:, tsl(t)], op=ALU.mult),
                      reads=psum.r(bb) + rsin.r(t), writes=t2.r())
                    banks.free(ba, bb)
                    A("pool", OP("tensor_tensor", out=bf(qr), in0=t1[:, 0:TT], in1=t2[:, 0:TT], op=ALU.add),
                      reads=t1.r() + t2.r(), writes=qr.r())
                    ring.free(t1, t2)
                    rot.append(qr)
                qr, kr = rot
                if hd == 0 and t == 0:
                    dbg(f"qr{l}", qr, bf(qr), [128, TT], BF16)
                    dbg(f"kr{l}", kr, bf(kr), [128, TT], BF16)
                bt = banks.get()
                ktp = psum[:, bt, :].bitcast(BF16)
                for j in range(4):
                    A("pe", OP("transpose", out=ktp[:, j * 128:(j + 1) * 128], in_=bf(kr, 128, j * 128), identity=ident_b[:]),
                      reads=kr.r() + ident_b.r(), writes=psum.r(bt))
                ktm = ring.get()
                A("act", OP("activation", out=bf(ktm), in_=ktp[:, 0:TT], func=AF.Copy), reads=psum.r(bt), writes=ktm.r())
                banks.free(bt)
                bs = banks.get()
                for j in range(4):
                    A("pe", OP("matmul", out=psum[:, bs, j * 128:(j + 1) * 128], lhsT=bf(kr, 128, j * 128), rhs=bf(qr, 128, j * 128),
                                                                  start=True, stop=True), reads=kr.r() + qr.r(), writes=psum.r(bs))
                pT = ring.get()
                A("dve", OP("tensor_tensor", out=bf(pT).rearrange("p (j c) -> p j c", j=4), in0=psum[:, bs, :].rearrange("p (j c) -> p j c", j=4),
                                                          in1=maskT.unsqueeze(1).to_broadcast([128, 4, 128]), op=ALU.mult),
                  reads=psum.r(bs) + cst.r(), writes=pT.r())
                banks.free(bs)
                bkv = banks.get()
                for j in range(4):
                    n = t * 4 + j
                    A("pe", OP("matmul", out=psum[:, bkv, j * 128:(j + 1) * 128], lhsT=bf(ktm, 128, j * 128),
                                                                  rhs=vh[:, n, hd * 128:(hd + 1) * 128], start=True, stop=True),
                      reads=ktm.r() + aux.r(n), writes=psum.r(bkv))
                ring.free(ktm)
                bxo = banks.get()
                for j in range(4):
                    n = t * 4 + j
                    pb = pbs[n % 2]
                    if n > 0:
                        A("act", OP("activation", out=pb[:], in_=ust[:], func=AF.Identity, scale=float(gC * (128.0 ** -0.5))),
                          reads=ust.r(), writes=pb.r())
                    A("pe", OP("matmul", out=psum[:, bxo, j * 128:(j + 1) * 128], lhsT=vh[:, n, hd * 128:(hd + 1) * 128],
                                                                rhs=bf(pT, 128, j * 128), start=True, stop=(n == 0)),
                      reads=aux.r(n) + pT.r(), writes=psum.r(bxo))
                    if n > 0:
                        A("pe", OP("matmul", out=psum[:, bxo, j * 128:(j + 1) * 128], lhsT=pb[:], rhs=bf(qr, 128, j * 128),
                                                                      start=False, stop=True), reads=pb.r() + qr.r(), writes=psum.r(bxo))
                        A("dve", OP("scalar_tensor_tensor", out=ust[:], in0=ust[:], scalar=gC, in1=psum[:, bkv, j * 128:(j + 1) * 128],
                                                                       op0=ALU.mult, op1=ALU.add), reads=ust.r() + psum.r(bkv), writes=ust.r())
                    else:
                        A("dve", OP("tensor_copy", out=ust[:], in_=psum[:, bkv, j * 128:(j + 1) * 128]), reads=psum.r(bkv), writes=ust.r())
                banks.free(bkv)
                ring.free(pT, kr)
                oT = ring.get()
                A("dve", OP("tensor_tensor", out=oT[:, 0:TT].rearrange("p (j c) -> p j c", j=4), in0=psum[:, bxo, :].rearrange("p (j c) -> p j c", j=4),
                                                          in1=cst[:, C_QDEC + hd * 128:C_QDEC + (hd + 1) * 128].unsqueeze(1).to_broadcast([128, 4, 128]), op=ALU.mult),
                  reads=psum.r(bxo) + cst.r(), writes=oT.r())
                banks.free(bxo)
                ring.free(qr)
                ob = ring.get()
                A("act", OP("activation", out=bf(ob, TT, 0), in_=oT[:, 0:TT], func=AF.Copy), reads=oT.r(), writes=ob.r())
                A("act", OP("activation", out=bf(ob, TT, TT), in_=oT[:, 0:TT], func=AF.Square), reads=oT.r(), writes=ob.r())
                bm = banks.get(); bq = banks.get()
                A("pe", OP("matmul", out=psum[:, bm, :], lhsT=o128_b[:], rhs=bf(ob, TT, 0), start=True, stop=True),
                  reads=ob.r() + o128_b.r(), writes=psum.r(bm))
                A("pe", OP("matmul", out=psum[:, bq, :], lhsT=o128_b[:], rhs=bf(ob, TT, TT), start=True, stop=True),
                  reads=ob.r() + o128_b.r(), writes=psum.r(bq))
                ring.free(ob)
                m2 = ring.get()
                A("act", OP("activation", out=m2[:, 0:TT], in_=psum[:, bm, :], func=AF.Square), reads=psum.r(bm), writes=m2.r())
                A("dve", OP("tensor_tensor", out=m2[:, 0:TT], in0=psum[:, bq, :], in1=m2[:, 0:TT], op=ALU.subtract),
                  reads=psum.r(bq) + m2.r(), writes=m2.r())
                banks.free(bq)
                A("dve", OP("tensor_scalar", out=m2[:, 0:TT], in0=m2[:, 0:TT], scalar1=0.0, scalar2=None, op0=ALU.max), reads=m2.r(), writes=m2.r())
                A("act", OP("activation", out=m2[:, 0:TT], in_=m2[:, 0:TT], func=AF.Ln, bias=NORM_EPS), reads=m2.r(), writes=m2.r())
                A("act", OP("activation", out=m2[:, 0:TT], in_=m2[:, 0:TT], func=AF.Exp, scale=-0.5), reads=m2.r(), writes=m2.r())
                A("dve", OP("tensor_tensor", out=oT[:, 0:TT], in0=oT[:, 0:TT], in1=psum[:, bm, :], op=ALU.subtract),
                  reads=oT.r() + psum.r(bm), writes=oT.r())
                banks.free(bm)
                A("dve", OP("tensor_tensor", out=oT[:, 0:TT], in0=oT[:, 0:TT], in1=m2[:, 0:TT], op=ALU.mult),
                  reads=oT.r() + m2.r(), writes=oT.r())
                ring.free(m2)
                bg = banks.get()
                proj(bg, wg_t, lambda k: wgv[:, k, hd * 128:(hd + 1) * 128], t)
                sg = ring.get()
                A("act", OP("activation", out=sg[:, 0:TT], in_=psum[:, bg, :], func=AF.Silu), reads=psum.r(bg), writes=sg.r())
                banks.free(bg)
                A("dve", OP("scalar_tensor_tensor", out=mixed[:, hd, tsl(t)], in0=oT[:, 0:TT], scalar=PV(f"rnorm{l}", hd), in1=sg[:, 0:TT],
                                                                        op0=ALU.mult, op1=ALU.mult), reads=oT.r() + sg.r() + pv.r(), writes=mixed.r(hd * 4 + t))
                ring.free(oT, sg)
            wbufs.free(wb)
        wbufs.free(wg_t)
        dbg(f"yret{l}", mixed, mixed[:].rearrange("p c t -> p (c t)"), [128, 4 * SEQ], BF16)
        fw.pin = set()
        group_post(l, mixed, None, 2, wbufs, 512.0)
        dbg(f"hmix{l}", hT, hT[:].rearrange("p c t -> p (c t)"), [128, 8 * SEQ])
        for cm in reversed(phase):
            cm.__exit__(None, None, None)
        phase = []
        fw.barrier()
        if upto == "mix":
            break

        rmsnorm_to_xn(f"gffn{l}")
        actb = psb("actb", [128, 4, SEQ], BF16, nslots=16)
        wus = Pool([psb(f"wu{i}", [128, 8, 1024], BF16) for i in range(2)])
        wds = Pool([psb(f"wd{i}", [128, 4, 1024], BF16) for i in range(2)])
        fcar = psb("fcar", [128, 2, 2], F32)
        for grp in range(6):
            j0 = grp * 4
            wu = wus.get(); wd = wds.get()
            load_w(wu, wu[:, :, 0:512], w_up_d[l, :, j0 * 128:(j0 + 4) * 128].rearrange("(k p) n -> p k n", p=128))
            load_w(wu, wu[:, :, 512:1024], w_up_d[l, :, D_FF + j0 * 128:D_FF + (j0 + 4) * 128].rearrange("(k p) n -> p k n", p=128))
            load_w(wd, wd[:], w_down_d[l, j0 * 128:(j0 + 4) * 128, :].rearrange("(k p) n -> p k n", p=128))
            for jj in range(4):
                j = j0 + jj
                for t in range(NT):
                    outs = []
                    for which in range(2):
                        ch = j + 24 * which
                        b = banks.get()
                        proj(b, wu, lambda k, o=which * 512 + jj * 128: wu[:, k, o:o + 128], t)
                        vc = ring.get()
                        w0 = PV(f"fcw{l}", ch * 3 + 0); w1 = PV(f"fcw{l}", ch * 3 + 1); w2 = PV(f"fcw{l}", ch * 3 + 2)
                        A("act", OP("activation", out=vc[:, 0:TT], in_=psum[:, b, :], func=AF.Identity, scale=w2,
                                                                                  bias=PV(f"fcb{l}", ch)), reads=psum.r(b) + pv.r(), writes=vc.r())
                        A("dve", OP("scalar_tensor_tensor", out=vc[:, 1:TT], in0=psum[:, b, 0:TT - 1], scalar=w1, in1=vc[:, 1:TT],
                                                                                    op0=ALU.mult, op1=ALU.add), reads=psum.r(b) + vc.r() + pv.r(), writes=vc.r())
                        A("dve", OP("scalar_tensor_tensor", out=vc[:, 2:TT], in0=psum[:, b, 0:TT - 2], scalar=w0, in1=vc[:, 2:TT],
                                                                                    op0=ALU.mult, op1=ALU.add), reads=psum.r(b) + vc.r() + pv.r(), writes=vc.r())
                        if t > 0:
                            A("dve", OP("scalar_tensor_tensor", out=vc[:, 0:1], in0=fcar[:, which, 1:2], scalar=w1, in1=vc[:, 0:1],
                                                                                                op0=ALU.mult, op1=ALU.add), reads=fcar.r() + vc.r() + pv.r(), writes=vc.r())
                            A("dve", OP("scalar_tensor_tensor", out=vc[:, 0:2], in0=fcar[:, which, 0:2], scalar=w0, in1=vc[:, 0:2],
                                                                                                op0=ALU.mult, op1=ALU.add), reads=fcar.r() + vc.r() + pv.r(), writes=vc.r())
                        if t < NT - 1:
                            A("act", OP("activation", out=fcar[:, which, :], in_=psum[:, b, TT - 2:TT], func=AF.Copy),
                              reads=psum.r(b), writes=fcar.r())
                        banks.free(b)
                        outs.append(vc)
                    vc, gc = outs
                    A("act", OP("activation", out=gc[:, 0:TT], in_=gc[:, 0:TT], func=AF.Gelu_apprx_tanh), reads=gc.r(), writes=gc.r())
                    A("dve", OP("tensor_tensor", out=actb[:, jj, tsl(t)], in0=gc[:, 0:TT], in1=vc[:, 0:TT], op=ALU.mult),
                      reads=vc.r() + gc.r(), writes=actb.r(jj * 4 + t))
                    ring.free(vc, gc)
            for t in range(NT):
                for dc in range(8):
                    b = banks.get()
                    for k in range(4):
                        A("pe", OP("matmul", out=psum[:, b, :], lhsT=wd[:, k, dc * 128:(dc + 1) * 128], rhs=actb[:, k, tsl(t)],
                                                                    start=(k == 0), stop=(k == 3)), reads=wd.r() + actb.r(k * 4 + t), writes=psum.r(b))
                    A("dve", OP("tensor_tensor", out=hT[:, dc, tsl(t)], in0=hT[:, dc, tsl(t)], in1=psum[:, b, :], op=ALU.add),
                      reads=hs(dc, t) + psum.r(b), writes=hs(dc, t))
                    banks.free(b)
            wus.free(wu); wds.free(wd)
        dbg(f"hffn{l}", hT, hT[:].rearrange("p c t -> p (c t)"), [128, 8 * SEQ])
        for cm in reversed(phase):
            cm.__exit__(None, None, None)
        phase = []
        fw.barrier()

    for t in range(NT):
        b = banks.get()
        for c in range(8):
            sq = ring.get()
            A("act", OP("activation", out=bf(sq), in_=hT[:, c, tsl(t)], func=AF.Square), reads=hs(c, t), writes=sq.r())
            A("pe", OP("matmul", out=psum[:, b, :], lhsT=ones_b[:], rhs=bf(sq), start=(c == 0), stop=(c == 7)),
              reads=sq.r() + ones_b.r(), writes=psum.r(b))
            ring.free(sq)
        rs = ring.get()
        A("act", OP("activation", out=rs[:, 0:TT], in_=psum[:, b, :], func=AF.Ln, scale=1.0 / D_MODEL, bias=NORM_EPS), reads=psum.r(b), writes=rs.r())
        banks.free(b)
        A("act", OP("activation", out=rs[:, 0:TT], in_=rs[:, 0:TT], func=AF.Exp, scale=-0.5), reads=rs.r(), writes=rs.r())
        for c in range(8):
            ot = ring.get()
            A("dve", OP("scalar_tensor_tensor", out=ot[:, 0:TT], in0=hT[:, c, tsl(t)], scalar=PV("gfin", c), in1=rs[:, 0:TT],
                                                                         op0=ALU.mult, op1=ALU.mult), reads=hs(c, t) + rs.r() + pv.r(), writes=ot.r())
            ins = A("sp", OP("dma_start", out=outT_d[c * 128:(c + 1) * 128, tsl(t)], in_=ot[:, 0:TT]), reads=ot.r(), dma=True)
            out_dmas.append(ins)
            ring.free(ot)
        ring.free(rs)

    for cm in reversed(phase):
        cm.__exit__(None, None, None)
    fin = A("sp", None)
    for ins in out_dmas:
        fin.preds[ins] = True
    fw.emit()
    fw.close()
    return nc, dbg_out


def make_in_maps(inputs):
    inp = {k: np.asarray(v) for k, v in inputs.items()}
    pv = host_pv(inp)
    cst = host_consts()
    wbd, x1, x2, c1, c2 = host_struct(inp)
    shared = {
        "pv": pv, "cst": cst,
        "wbd": np.ascontiguousarray(wbd.reshape(DEPTH, 128, 1024)),
        "s5x1": x1, "s5x2": x2,
        "s5c1": np.ascontiguousarray(c1.reshape(DEPTH, 128, 512)),
        "s5c2": np.ascontiguousarray(c2.reshape(DEPTH, 128, 512)),
        "w_in": np.ascontiguousarray(inp["w_in"], dtype=np.float32),
        "s5_w_glu": np.ascontiguousarray(inp["s5_w_glu"], dtype=np.float32),
        "w_out": np.ascontiguousarray(inp["w_out"], dtype=np.float32),
        "w_up": np.ascontiguousarray(inp["w_up"], dtype=np.float32),
        "w_down": np.ascontiguousarray(inp["w_down"], dtype=np.float32),
    }
    maps = []
    for b in range(inp["x"].shape[0]):
        m = dict(shared)
        m["xT"] = np.ascontiguousarray(inp["x"][b].T.astype(np.float32))
        m["pos"] = np.ascontiguousarray(inp["positions"][b].astype(np.int32).reshape(1, SEQ))
        maps.append(m)
    return maps


_NC_CACHE = {}


def kernel(**inputs):
    if "nc" not in _NC_CACHE:
        _NC_CACHE["nc"] = build()[0]
    nc = _NC_CACHE["nc"]
    maps = make_in_maps(inputs)
    res = run_bass_kernel_spmd(nc, maps, core_ids=list(range(len(maps))))
    out = np.stack([np.ascontiguousarray(r["outT"].T) for r in res.results], axis=0)
    return out.astype(np.float32)
```

```python
import math
import os
from collections import deque
import numpy as np
import concourse.bass as bass
import concourse.mybir as mybir
from concourse.bass_utils import run_bass_kernel_spmd

F32 = mybir.dt.float32
BF16 = mybir.dt.bfloat16
I32 = mybir.dt.int32
AF = mybir.ActivationFunctionType
ALU = mybir.AluOpType

ENGS = ("pe", "act", "dve", "pool", "sp")
N_DMA_SEMS = 24

D_MODEL = 1024
SEQ = 2048
DEPTH = 2
TT = 512
NT = SEQ // TT
IN_WIDTH = 3584
D_FF = 3072
NORM_EPS = 1e-6
MAGIC = 12582912.0
TWO_PI_S = 6.2831850
GAMMAS = [1.0 - 2.0 ** (-5.0 - h) for h in range(4)]


class Reg:
    __slots__ = ("name", "w", "rds")

    def __init__(self, name):
        self.name = name
        self.w = None
        self.rds = []


class T:
    def __init__(self, h, name, nslots=1):
        self.h = h
        self.name = name
        self.regs = [Reg(f"{name}.{i}") for i in range(nslots)]

    def __getitem__(self, k):
        return self.h[k]

    def r(self, i=None, j=None):
        if i is None:
            return list(self.regs)
        if j is None:
            return [self.regs[i]]
        return self.regs[i:j]


class Ins:
    __slots__ = ("eng", "rec", "fn", "preds", "is_dma", "dma_sem", "dma_use", "cost", "lat", "seg",
                 "tset", "sched", "done", "rt", "pos", "waits", "needs_inc", "count", "waits_dma", "pin")

    def __init__(self, eng, rec, fn):
        self.eng = eng
        self.rec = rec
        self.fn = fn
        self.preds = {}
        self.is_dma = False
        self.dma_sem = None
        self.dma_use = None
        self.cost = 100.0
        self.lat = 0.0
        self.seg = 0
        self.tset = None
        self.sched = False
        self.done = 0.0
        self.rt = None
        self.pos = None
        self.waits = []
        self.needs_inc = False
        self.count = None
        self.pin = False


_ACT_GROUP = {}


def _act_group(func):
    if not _ACT_GROUP:
        _ACT_GROUP.update({AF.Exp: 1, AF.Ln: 1, AF.Gelu_apprx_tanh: 2, AF.Silu: 3, AF.Sin: 3, AF.Sigmoid: 4, AF.Sqrt: 5})
    return _ACT_GROUP.get(func)


def _free(ap):
    n = 1
    for d in ap.shape[1:]:
        n *= int(d)
    return n


def est_cost(eng, fn, is_dma):
    if fn is None:
        return 0.0, 0.0
    name, args, kw = fn
    if is_dma:
        out = kw["out"]
        nbytes = _free(out) * int(out.shape[0]) * 4
        return (1000.0 if eng == "pool" else 120.0), 2000.0 + nbytes / 150.0
    if eng == "pe":
        if name == "transpose":
            return 110.0, 0.0
        n = _free(kw["rhs"])
        return max(n, 64) / 1.9 + 10.0, 0.0
    out = kw.get("out")
    if out is None:
        out = args[0]
    n = _free(out)
    if eng == "act":
        return 150.0 + n / 1.2, 0.0
    if eng == "dve":
        if name == "tensor_tensor_scan":
            return 120.0 + 2.0 * n / 0.96, 0.0
        if name == "reciprocal":
            return 120.0 + 6.1 * n, 0.0
        if name in ("tensor_tensor", "scalar_tensor_tensor"):
            return 120.0 + n / 0.96, 0.0
        return 120.0 + n / 1.5, 0.0
    if eng == "pool":
        if name == "tensor_tensor":
            return 150.0 + 2.15 * n, 0.0
        return 150.0 + 1.2 * n, 0.0
    return 100.0, 0.0


class FW:
    SEM_LAT = 80.0
    WINDOW = int(os.environ.get('KW', '80'))
    WIN_ENG = {e: int(os.environ.get('KW_' + e.upper(), '0')) for e in ('pe', 'act', 'dve', 'pool', 'sp')}

    def __init__(self, nc):
        self.nc = nc
        self.all = []
        self.dma_rr = 0
        self.dma_rr_pool = 0
        self.dma_uses = [0] * N_DMA_SEMS
        self.dma_last = [None] * N_DMA_SEMS
        self.seg = 0
        self.seg_dma_uses = []
        self.pool_dmas = []
        self.pin = set()
        self.unpin_names = {'scalar_tensor_tensor', 'tensor_tensor'}
        self._stack = []

    def sb(self, name, shape, dtype, nslots=1):
        cm = self.nc.sbuf_tensor(name, list(shape), dtype)
        h = cm.__enter__()
        self._stack.append(cm)
        return T(h, name, nslots)

    def ps(self, name, shape, dtype, nslots=1):
        cm = self.nc.psum_tensor(name, list(shape), dtype)
        h = cm.__enter__()
        self._stack.append(cm)
        return T(h, name, nslots)

    @staticmethod
    def _add_pred(ins, p, kind):
        if p is None or p is ins:
            return
        if p.is_dma or ins.is_dma or p.eng != ins.eng:
            needs = True
        else:
            needs = (ins.eng != "pe")
        ins.preds[p] = ins.preds.get(p, False) or needs

    def op(self, eng, fn, reads=(), writes=(), dma=False):
        ins = Ins(eng, len(self.all), fn)
        ins.seg = self.seg
        ins.is_dma = dma
        ins.pin = (eng in self.pin) and not (fn is not None and fn[0] in self.unpin_names)
        self.all.append(ins)
        ins.cost, ins.lat = est_cost(eng, fn, dma)
        if eng == "act" and fn is not None and fn[0] == "activation":
            ins.tset = _act_group(fn[2].get("func"))
        if dma:
            half = N_DMA_SEMS // 2
            if eng == "pool":
                k = half + self.dma_rr_pool
                self.dma_rr_pool = (self.dma_rr_pool + 1) % half
            else:
                k = self.dma_rr
                self.dma_rr = (k + 1) % half
            self._add_pred(ins, self.dma_last[k], "SEM")
            self.dma_last[k] = ins
            self.dma_uses[k] += 1
            ins.dma_sem = k
            ins.dma_use = self.dma_uses[k]
            if eng == "pool":
                self.pool_dmas.append(ins)
                if len(self.pool_dmas) > 4:
                    self._add_pred(ins, self.pool_dmas[-5], "SEM")
        for r in reads:
            self._add_pred(ins, r.w, "RAW")
        for r in writes:
            self._add_pred(ins, r.w, "WAW")
            for p in r.rds:
                self._add_pred(ins, p, "WAR")
        for r in reads:
            r.rds.append(ins)
        for r in writes:
            r.w = ins
            r.rds = []
        return ins

    def barrier(self):
        self.seg_dma_uses.append(list(self.dma_uses))
        self.seg += 1

    def schedule(self):
        nseg = self.seg + 1
        final = {e: [] for e in ENGS}
        eng_free = {e: 0.0 for e in ENGS}
        cur_set = None
        bar_marks = []
        for sg in range(nseg):
            pending = {e: [i for i in self.all if i.seg == sg and i.eng == e] for e in ENGS}
            remaining = sum(len(v) for v in pending.values())
            while remaining:
                best = None
                best_t = None
                for e in ENGS:
                    lst = pending[e]
                    ef = eng_free[e]
                    lim = min(len(lst), self.WIN_ENG[e] or self.WINDOW)
                    if lim and lst[0].pin:
                        lim = 1
                    for wi in range(lim):
                        c = lst[wi]
                        if wi > 0 and c.pin:
                            break
                        if c.rt is None:
                            rt = 0.0
                            ok = True
                            for p, needs in c.preds.items():
                                if not p.sched:
                                    ok = False
                                    break
                                d = p.done + (self.SEM_LAT if needs else 0.0)
                                if d > rt:
                                    rt = d
                            if not ok:
                                continue
                            c.rt = rt
                        t = c.rt if c.rt > ef else ef
                        if e == "act" and c.tset is not None and c.tset != cur_set:
                            t += 1300.0
                        if best is None or t < best_t or (t == best_t and c.rec < best[1].rec):
                            best = (e, c, wi)
                            best_t = t
                        if t <= ef:
                            break
                assert best is not None, "scheduler deadlock"
                e, c, wi = best
                pending[e].pop(wi)
                remaining -= 1
                if e == "act" and c.tset is not None:
                    cur_set = c.tset
                end = best_t + c.cost
                eng_free[e] = end
                c.done = end + c.lat
                c.sched = True
                c.pos = len(final[e])
                final[e].append(c)
            if sg < nseg - 1:
                tmax = max(eng_free.values())
                dmax = max([i.done for i in self.all if i.seg == sg and i.is_dma] + [0.0])
                tmax = max(tmax, dmax)
                marks = {}
                for e in ENGS:
                    b = Ins(e, -1, None)
                    b.sched = True
                    b.pos = len(final[e])
                    final[e].append(b)
                    marks[e] = b
                    eng_free[e] = tmax
                bar_marks.append(marks)
        self.final = final
        self.bar_marks = bar_marks
        self.est_total = max(eng_free.values())

    def emit(self):
        nc = self.nc
        self.schedule()
        final = self.final
        for bi, marks in enumerate(self.bar_marks):
            for e, b in marks.items():
                for e2 in ENGS:
                    if e2 == e:
                        continue
                    pos2 = marks[e2].pos
                    for j in range(pos2 - 1, -1, -1):
                        p = final[e2][j]
                        if p.fn is not None and not p.is_dma:
                            b.preds[p] = True
                            break
                b.waits_dma = self.seg_dma_uses[bi]
        for e in ENGS:
            seen = {}
            for ins in final[e]:
                for p, needs in ins.preds.items():
                    if not needs:
                        assert p.eng == e and p.pos < ins.pos, "ordering edge violated"
                        continue
                    if p.is_dma:
                        key = f"d{p.dma_sem}"
                        v = p.dma_use
                    else:
                        key = p.eng
                        v = p.pos
                        if p.eng == e:
                            assert p.pos < ins.pos
                    if seen.get(key, -1) >= v:
                        continue
                    seen[key] = v
                    ins.waits.append((key, p))
                    if not p.is_dma:
                        p.needs_inc = True
                wd = getattr(ins, "waits_dma", None) if ins.fn is None else None
                if wd is not None:
                    for k, u in enumerate(wd):
                        if u > 0 and seen.get(f"d{k}", -1) < u:
                            seen[f"d{k}"] = u
                            ins.waits.append((f"d{k}", u))
        sem_cms = []
        sems = {}
        for e in list(ENGS) + [f"d{k}" for k in range(N_DMA_SEMS)]:
            cm = nc.semaphore(f"s_{e}")
            sems[e] = cm.__enter__()
            sem_cms.append(cm)
        for e in ENGS:
            c = 0
            for ins in final[e]:
                if ins.needs_inc:
                    c += 1
                    ins.count = c

        def run(eng_name, eng):
            for ins in final[eng_name]:
                for (key, p) in ins.waits:
                    if isinstance(p, int):
                        eng.wait_ge(sems[key], 16 * p)
                    elif p.is_dma:
                        eng.wait_ge(sems[key], 16 * p.dma_use)
                    else:
                        eng.wait_ge(sems[key], p.count)
                if ins.fn is None:
                    continue
                name, args, kw = ins.fn
                bi = getattr(eng, name)(*args, **kw)
                if ins.dma_sem is not None:
                    bi.then_inc(sems[f"d{ins.dma_sem}"], 16)
                elif ins.needs_inc:
                    bi.then_inc(sems[eng_name], 1)

        with nc.Block() as block:
            @block.tensor
            def _(eng):
                run("pe", eng)

            @block.scalar
            def _(eng):
                run("act", eng)

            @block.vector
            def _(eng):
                run("dve", eng)

            @block.gpsimd
            def _(eng):
                run("pool", eng)

            @block.sync
            def _(eng):
                run("sp", eng)
        for cm in reversed(sem_cms):
            cm.__exit__(None, None, None)

    def close(self):
        for cm in reversed(self._stack):
            cm.__exit__(None, None, None)
        self._stack = []


def OP(name, *args, **kw):
    return (name, args, kw)


class TV:
    def __init__(self, parent, slot, ap):
        self.parent = parent
        self.slot = slot
        self.ap = ap
        self.is_view = True

    def __getitem__(self, k):
        return self.ap[k]

    def r(self):
        return self.parent.r(self.slot)


class Pool:
    def __init__(self, items):
        self.free_ = deque(items)
        self.n = len(items)
        self.extra = []

    def get(self, halo=False):
        assert self.free_, "scratch pool exhausted"
        if not halo:
            return self.free_.popleft()
        for i, it in enumerate(self.free_):
            if not getattr(it, "is_view", False):
                del self.free_[i]
                return it
        raise AssertionError("no halo-capable scratch tile free")

    def free(self, *items):
        for it in items:
            self.free_.append(it)

    def enable_extra(self, views):
        self.extra = list(views)
        for v in views:
            self.free_.appendleft(v)

    def disable_extra(self):
        for v in self.extra:
            assert v in self.free_, "extra scratch view still in use"
            self.free_.remove(v)
        self.extra = []


def _chunked(v, nchunk):
    return np.ascontiguousarray(np.asarray(v, np.float32).reshape(nchunk, 128).T)


class PVLayout:
    def __init__(self):
        self.idx = {}
        self.n = 0

    def add(self, name, k):
        self.idx[name] = (self.n, k)
        self.n += k


def pv_layout():
    pl = PVLayout()
    for l in range(DEPTH):
        for name, k in (("gmix", 8), ("gffn", 8), ("lcw", 16), ("lcb", 4), ("lba", 4), ("lbx", 4),
                        ("llam", 4), ("lnorm", 4), ("s5d", 4), ("s5bg", 4), ("s5n", 4), ("rnorm", 4),
                        ("fcw", 144), ("fcb", 48), ("s5lr", 32), ("s5li", 32), ("s5ldt", 32)):
            pl.add(f"{name}{l}", k)
    for name, k in (("gfin", 8), ("invf", 1), ("sgn", 1), ("nsgn", 1), ("vdec", 4), ("rowmask", 8)):
        pl.add(name, k)
    return pl


PVL = pv_layout()
C_ID = 0
C_IOTA = 128
C_MASK = C_IOTA + 512
C_QDEC = C_MASK + 128
NCST = C_QDEC + 512


def host_consts():
    cst = np.zeros((128, NCST), np.float32)
    cst[:, C_ID:C_ID + 128] = np.eye(128, dtype=np.float32)
    cst[:, C_IOTA:C_IOTA + 512] = np.arange(512, dtype=np.float32)[None, :]
    m = np.arange(128)[:, None]
    c = np.arange(128)[None, :]
    cst[:, C_MASK:C_MASK + 128] = np.where(c >= m, np.float32(128.0 ** -0.5), np.float32(0.0))
    for h in range(4):
        lg = np.log1p(-np.float32(2.0) ** np.float32(-5.0 - h)).astype(np.float32)
        cst[:, C_QDEC + h * 128:C_QDEC + (h + 1) * 128] = np.exp(lg * (np.arange(128, dtype=np.float32) + 1.0))[None, :]
    return cst


def host_pv(inp):
    pv = np.zeros((128, PVL.n), np.float32)

    def put(name, arr):
        o, k = PVL.idx[name]
        assert arr.shape == (128, k), (name, arr.shape, k)
        pv[:, o:o + k] = arr

    for l in range(DEPTH):
        put(f"gmix{l}", _chunked(inp["norm_mix"][l], 8))
        put(f"gffn{l}", _chunked(inp["norm_ffn"][l], 8))
        cw = np.asarray(inp["lru_conv_w"][l], np.float32)
        put(f"lcw{l}", np.ascontiguousarray(cw.reshape(4, 4, 128).transpose(2, 1, 0).reshape(128, 16)))
        put(f"lcb{l}", _chunked(inp["lru_conv_b"][l], 4))
        put(f"lba{l}", _chunked(np.asarray(inp["lru_ba"][l]).reshape(512), 4))
        put(f"lbx{l}", _chunked(np.asarray(inp["lru_bx"][l]).reshape(512), 4))
        put(f"llam{l}", _chunked(inp["lru_lambda"][l], 4))
        put(f"lnorm{l}", _chunked(inp["lru_norm"][l], 4))
        put(f"s5d{l}", _chunked(inp["s5_d"][l], 4))
        put(f"s5bg{l}", _chunked(inp["s5_b_glu"][l], 4))
        put(f"s5n{l}", _chunked(inp["s5_norm"][l], 4))
        put(f"rnorm{l}", _chunked(inp["ret_norm"][l], 4))
        fw_ = np.asarray(inp["ffn_conv_w"][l], np.float32)
        put(f"fcw{l}", np.ascontiguousarray(fw_.reshape(3, 48, 128).transpose(2, 1, 0).reshape(128, 144)))
        put(f"fcb{l}", _chunked(inp["ffn_conv_b"][l], 48))
        lr = np.asarray(inp["s5_lambda_re"][l], np.float32).T
        li = np.asarray(inp["s5_lambda_im"][l], np.float32).T
        put(f"s5lr{l}", np.concatenate([lr, lr], 0))
        put(f"s5li{l}", np.concatenate([li, li], 0))
        put(f"s5ldt{l}", np.broadcast_to(np.asarray(inp["s5_log_dt"][l], np.float32)[None, :], (128, 32)).copy())
    put("gfin", _chunked(inp["norm_final"], 8))
    half = 64
    inv = (np.float32(10000.0) ** (-np.arange(half, dtype=np.float32) * np.float32(2.0) / np.float32(128.0))).astype(np.float32)
    put("invf", np.concatenate([inv, inv])[:, None])
    sgn = np.concatenate([-np.ones(64, np.float32), np.ones(64, np.float32)])[:, None]
    put("sgn", sgn)
    put("nsgn", -sgn)
    vd = np.zeros((128, 4), np.float32)
    for h in range(4):
        lg = np.log1p(-np.float32(2.0) ** np.float32(-5.0 - h)).astype(np.float32)
        vd[:, h] = np.exp(-lg * (np.arange(128, dtype=np.float32) + 1.0))
    put("vdec", vd)
    rm = np.zeros((128, 8), np.float32)
    for g in range(8):
        rm[16 * g:16 * g + 16, g] = 1.0
    put("rowmask", rm)
    return pv


def host_struct(inp):
    wbd = np.zeros((DEPTH, 128, 2, 4, 128), np.float32)
    for l in range(DEPTH):
        for which, nm in enumerate(("lru_wa", "lru_wx")):
            w = np.asarray(inp[nm][l], np.float32)
            for c in range(4):
                for hb in range(2):
                    wbd[l, hb * 64:(hb + 1) * 64, which, c, hb * 64:(hb + 1) * 64] = w[2 * c + hb]
    x1 = np.zeros((DEPTH, 128, 512), np.float32)
    x2 = np.zeros((DEPTH, 128, 512), np.float32)
    c1 = np.zeros((DEPTH, 128, 4, 128), np.float32)
    c2 = np.zeros((DEPTH, 128, 4, 128), np.float32)
    for l in range(DEPTH):
        br = np.asarray(inp["s5_b_re"][l], np.float32).transpose(1, 0, 2).reshape(64, 512)
        bi = np.asarray(inp["s5_b_im"][l], np.float32).transpose(1, 0, 2).reshape(64, 512)
        x1[l] = np.concatenate([br, bi], 0)
        x2[l] = np.concatenate([bi, br], 0)
        cr = np.asarray(inp["s5_c_re"][l], np.float32).reshape(4, 128, 64)
        ci = np.asarray(inp["s5_c_im"][l], np.float32).reshape(4, 128, 64)
        c1[l] = np.concatenate([cr, ci], 2).transpose(1, 0, 2)
        c2[l] = np.concatenate([ci, cr], 2).transpose(1, 0, 2)
    return wbd, x1, x2, c1, c2


def build(debug=(), upto="all", nlayers=DEPTH):
    nc = bass.Bass("TRN2", target_bir_lowering=False)
    fw = FW(nc)
    dram = {}

    def din(name, shape, dtype=F32):
        dram[name] = nc.dram_tensor(name, list(shape), dtype, kind="ExternalInput").ap()
        return dram[name]

    xT_d = din("xT", [D_MODEL, SEQ])
    pos_d = din("pos", [1, SEQ], I32)
    pv_d = din("pv", [128, PVL.n])
    cst_d = din("cst", [128, NCST])
    wbd_d = din("wbd", [DEPTH, 128, 2 * 4 * 128])
    x1_d = din("s5x1", [DEPTH, 128, 512])
    x2_d = din("s5x2", [DEPTH, 128, 512])
    c1_d = din("s5c1", [DEPTH, 128, 512])
    c2_d = din("s5c2", [DEPTH, 128, 512])
    w_in_d = din("w_in", [DEPTH, D_MODEL, IN_WIDTH])
    w_glu_d = din("s5_w_glu", [DEPTH, 512, 512])
    w_out_d = din("w_out", [DEPTH, 1536, D_MODEL])
    w_up_d = din("w_up", [DEPTH, D_MODEL, 2 * D_FF])
    w_down_d = din("w_down", [DEPTH, D_FF, D_MODEL])
    outT_d = nc.dram_tensor("outT", [D_MODEL, SEQ], F32, kind="ExternalOutput").ap()
    dbg_out = {}
    out_dmas = []

    A = fw.op

    hT = fw.sb("hT", [128, 8, SEQ], F32, nslots=32)
    xn = fw.sb("xn", [128, 8, SEQ], BF16, nslots=32)
    rcos = fw.sb("rcos", [128, SEQ], F32, nslots=4)
    rsin = fw.sb("rsin", [128, SEQ], F32, nslots=4)
    pv = fw.sb("pvs", [128, PVL.n], F32)
    cst = fw.sb("csts", [128, NCST], F32)
    ident_b = fw.sb("ident_b", [128, 128], BF16)
    ones_b = fw.sb("ones_b", [128, 128], BF16)
    o128_b = fw.sb("o128_b", [128, 128], BF16)
    NRING = 9
    ring = Pool([fw.sb(f"ring{i}", [128, 520], F32) for i in range(NRING)])
    psum = fw.ps("psum", [128, 8, 512], F32, nslots=8)
    banks = Pool(list(range(8)))
    rot_views = [TV(rcos, i, rcos[:, i * TT:(i + 1) * TT]) for i in range(4)] + [TV(rsin, i, rsin[:, i * TT:(i + 1) * TT]) for i in range(4)]

    def PV(name, c=0, n=1):
        o, k = PVL.idx[name]
        return pv[:, o + c:o + c + n]

    def hs(c, t):
        return hT.r(c * 4 + t)

    def xs(c, t):
        return xn.r(c * 4 + t)

    def tsl(t):
        return slice(t * TT, (t + 1) * TT)

    def bf(tile, n=TT, off=0):
        return tile[:].bitcast(BF16)[:, off:off + n]

    def dbg(name, tile, ap, shape, dtype=F32):
        if name not in debug:
            return
        d = nc.dram_tensor("dbg_" + name, list(shape), dtype, kind="ExternalOutput").ap()
        dbg_out[name] = d
        ins = A("sp", OP("dma_start", out=d, in_=ap), reads=tile.r(), dma=True)
        out_dmas.append(ins)

    A("sp", OP("dma_start", out=pv[:], in_=pv_d), writes=pv.r(), dma=True)
    A("sp", OP("dma_start", out=cst[:], in_=cst_d), writes=cst.r(), dma=True)
    for c in range(8):
        for t in range(NT):
            A("sp", OP("dma_start", out=hT[:, c, tsl(t)], in_=xT_d[c * 128:(c + 1) * 128, tsl(t)]),
              writes=hs(c, t), dma=True)
    A("dve", OP("memset", ones_b[:], 1.0), writes=ones_b.r())
    A("dve", OP("memset", o128_b[:], 1.0 / 128.0), writes=o128_b.r())
    A("act", OP("activation", out=ident_b[:], in_=cst[:, C_ID:C_ID + 128], func=AF.Copy),
      reads=cst.r(), writes=ident_b.r())
    ident_f = cst[:, C_ID:C_ID + 128]
    iota = cst[:, C_IOTA:C_IOTA + 512]
    maskT = cst[:, C_MASK:C_MASK + 128]

    def load_w(dst_tile, dst_ap, src_ap):
        return A("pool", OP("dma_start", out=dst_ap, in_=src_ap), writes=dst_tile.r(), dma=True)

    def rmsnorm_to_xn(gname):
        for t in range(NT):
            b = banks.get()
            for c in range(8):
                sq = ring.get()
                A("act", OP("activation", out=bf(sq), in_=hT[:, c, tsl(t)], func=AF.Square),
                  reads=hs(c, t), writes=sq.r())
                A("pe", OP("matmul", out=psum[:, b, :], lhsT=ones_b[:], rhs=bf(sq), start=(c == 0), stop=(c == 7)),
                  reads=sq.r() + ones_b.r(), writes=psum.r(b))
                ring.free(sq)
            rs = ring.get()
            A("act", OP("activation", out=rs[:, 0:TT], in_=psum[:, b, :], func=AF.Ln, scale=1.0 / D_MODEL, bias=NORM_EPS), reads=psum.r(b), writes=rs.r())
            banks.free(b)
            A("act", OP("activation", out=rs[:, 0:TT], in_=rs[:, 0:TT], func=AF.Exp, scale=-0.5), reads=rs.r(), writes=rs.r())
            for c in range(8):
                A("dve", OP("scalar_tensor_tensor", out=xn[:, c, tsl(t)], in0=hT[:, c, tsl(t)], scalar=PV(gname, c),
                                                                      in1=rs[:, 0:TT], op0=ALU.mult, op1=ALU.mult),
                  reads=hs(c, t) + rs.r() + pv.r(), writes=xs(c, t))
            ring.free(rs)

    def proj(b, wtile, w_ap_fn, t):
        for k in range(8):
            A("pe", OP("matmul", out=psum[:, b, :], lhsT=w_ap_fn(k), rhs=xn[:, k, tsl(t)], start=(k == 0), stop=(k == 7)),
              reads=wtile.r() + xs(k, t), writes=psum.r(b))

    def group_post(l, mixed, gname, grp, wbufs_pool, eps_div):
        wo = wbufs_pool.get()
        wo_v = wo[:].rearrange("p (k n) -> p k n", k=4)
        load_w(wo, wo_v, w_out_d[l, grp * 512:(grp + 1) * 512, :].rearrange("(k p) n -> p k n", p=128))
        for t in range(NT):
            if gname is not None:
                b = banks.get()
                for c in range(4):
                    sq = ring.get()
                    A("act", OP("activation", out=bf(sq), in_=mixed[:, c, tsl(t)], func=AF.Square),
                      reads=mixed.r(c * 4 + t), writes=sq.r())
                    A("pe", OP("matmul", out=psum[:, b, :], lhsT=ones_b[:], rhs=bf(sq), start=(c == 0), stop=(c == 3)),
                      reads=sq.r() + ones_b.r(), writes=psum.r(b))
                    ring.free(sq)
                rs = ring.get()
                A("act", OP("activation", out=rs[:, 0:TT], in_=psum[:, b, :], func=AF.Ln, scale=1.0 / 512.0, bias=NORM_EPS), reads=psum.r(b), writes=rs.r())
                banks.free(b)
                A("act", OP("activation", out=rs[:, 0:TT], in_=rs[:, 0:TT], func=AF.Exp, scale=-0.5), reads=rs.r(), writes=rs.r())
                for c in range(4):
                    A("dve", OP("scalar_tensor_tensor", out=mixed[:, c, tsl(t)], in0=mixed[:, c, tsl(t)],
                                                                          scalar=PV(gname + str(l), c), in1=rs[:, 0:TT],
                                                                          op0=ALU.mult, op1=ALU.mult),
                      reads=mixed.r(c * 4 + t) + rs.r() + pv.r(), writes=mixed.r(c * 4 + t))
                ring.free(rs)
            for dc in range(8):
                b = banks.get()
                for k in range(4):
                    A("pe", OP("matmul", out=psum[:, b, :], lhsT=wo_v[:, k, dc * 128:(dc + 1) * 128],
                                                           rhs=mixed[:, k, tsl(t)], start=(k == 0), stop=(k == 3)),
                      reads=wo.r() + mixed.r(k * 4 + t), writes=psum.r(b))
                A("dve", OP("tensor_tensor", out=hT[:, dc, tsl(t)], in0=hT[:, dc, tsl(t)], in1=psum[:, b, :], op=ALU.add),
                  reads=hs(dc, t) + psum.r(b), writes=hs(dc, t))
                banks.free(b)
        wbufs_pool.free(wo)

    def build_rot():
        for t in range(NT):
            pi_ = ring.get(); pf = ring.get(); k_ = ring.get()
            A("sp", OP("dma_start", out=pi_[:].bitcast(I32)[:, 0:TT],
                                                          in_=pos_d[0:1, tsl(t)].to_broadcast([128, TT])),
              writes=pi_.r(), dma=True)
            A("dve", OP("tensor_copy", out=pf[:, 0:TT], in_=pi_[:].bitcast(I32)[:, 0:TT]),
              reads=pi_.r(), writes=pf.r())
            A("dve", OP("tensor_scalar", out=pf[:, 0:TT], in0=pf[:, 0:TT], scalar1=PV("invf"), scalar2=None,
                                                      op0=ALU.mult), reads=pf.r() + pv.r(), writes=pf.r())
            A("dve", OP("tensor_scalar", out=k_[:, 0:TT], in0=pf[:, 0:TT], scalar1=1.0 / (2.0 * math.pi),
                                                             scalar2=MAGIC, op0=ALU.mult, op1=ALU.add),
              reads=pf.r(), writes=k_.r())
            A("dve", OP("tensor_scalar", out=k_[:, 0:TT], in0=k_[:, 0:TT], scalar1=MAGIC, scalar2=None,
                                                      op0=ALU.subtract), reads=k_.r(), writes=k_.r())
            C1 = 6.28125
            C2 = 2.0 * math.pi - 6.28125
            A("dve", OP("scalar_tensor_tensor", out=pf[:, 0:TT], in0=k_[:, 0:TT], scalar=-C1, in1=pf[:, 0:TT],
                                                                    op0=ALU.mult, op1=ALU.add), reads=pf.r() + k_.r(), writes=pf.r())
            A("dve", OP("scalar_tensor_tensor", out=pf[:, 0:TT], in0=k_[:, 0:TT], scalar=-C2, in1=pf[:, 0:TT],
                                                                    op0=ALU.mult, op1=ALU.add), reads=pf.r() + k_.r(), writes=pf.r())
            A("dve", OP("tensor_scalar", out=pf[:, 0:TT], in0=pf[:, 0:TT], scalar1=3.1415925, scalar2=-3.1415925,
                                                      op0=ALU.min, op1=ALU.max), reads=pf.r(), writes=pf.r())
            A("act", OP("activation", out=rsin[:, tsl(t)], in_=pf[:, 0:TT], func=AF.Sin, scale=PV("sgn")),
              reads=pf.r() + pv.r(), writes=rsin.r(t))
            A("act", OP("activation", out=k_[:, 0:TT], in_=pf[:, 0:TT], func=AF.Abs),
              reads=pf.r(), writes=k_.r())
            A("act", OP("activation", out=rcos[:, tsl(t)], in_=k_[:, 0:TT], func=AF.Sin, scale=-1.0,
                                                        bias=math.pi / 2.0 - 1e-6), reads=k_.r(), writes=rcos.r(t))
            ring.free(pi_, pf, k_)


    phase = []
    for l in range(nlayers):

        def psb(name, shape, dtype, nslots=1):
            cm = nc.sbuf_tensor(f"{name}_{l}", list(shape), dtype)
            h = cm.__enter__()
            phase.append(cm)
            return T(h, name, nslots)

        mixed = psb("mixed", [128, 4, SEQ], BF16, nslots=16)
        aux = psb("aux", [128, 8192], BF16, nslots=16)
        wbufs = Pool([psb(f"wbuf{i}", [128, 4096], BF16) for i in range(2)])
        wbd = psb("wbd", [128, 2, 4, 128], BF16)
        lpar = psb("lpar", [128, 16], F32)
        lcar = psb("lcar", [128, 4, 4], F32)
        s5p = psb("s5p", [128, 32, 8], F32)
        s5off = psb("s5off", [128, 32, 4], F32)
        bst1 = psb("bst1", [128, 512], BF16)
        bst2 = psb("bst2", [128, 512], BF16)
        w5pool = Pool([psb(f"w5_{i}", [128, 8, 128], BF16) for i in range(2)])
        lhs_b = psb("lhs_b", [128, 2, 8, 128], BF16)
        lhs_c = psb("lhs_c", [128, 2, 8, 128], BF16)
        zcar = psb("zcar", [128, 32], F32)
        ust = psb("ust", [128, 128], F32)
        pbs = [psb(f"pb{i}", [128, 128], BF16) for i in range(2)]

        load_w(wbd, wbd[:].rearrange("p a c n -> p (a c n)"), wbd_d[l])
        A("act", OP("activation", out=lpar[:, 0:4], in_=PV(f"llam{l}", 0, 4), func=AF.Exp, scale=-1.0), reads=pv.r(), writes=lpar.r())
        A("act", OP("activation", out=lpar[:, 0:4], in_=lpar[:, 0:4], func=AF.Ln, bias=1.0), reads=lpar.r(), writes=lpar.r())
        A("dve", OP("tensor_scalar", out=lpar[:, 4:8], in0=lpar[:, 0:4], scalar1=-4.0, scalar2=None, op0=ALU.mult), reads=lpar.r(), writes=lpar.r())
        A("dve", OP("tensor_scalar", out=lpar[:, 0:4], in0=lpar[:, 0:4], scalar1=-8.0, scalar2=None, op0=ALU.mult), reads=lpar.r(), writes=lpar.r())
        A("dve", OP("tensor_scalar", out=lpar[:, 8:12], in0=PV(f"lba{l}", 0, 4), scalar1=0.5, scalar2=None, op0=ALU.mult), reads=pv.r(), writes=lpar.r())
        A("dve", OP("tensor_scalar", out=lpar[:, 12:16], in0=PV(f"lbx{l}", 0, 4), scalar1=0.5, scalar2=None, op0=ALU.mult), reads=pv.r(), writes=lpar.r())
        A("dve", OP("memset", lcar[:], 0.0), writes=lcar.r())
        A("dve", OP("memset", zcar[:], 0.0), writes=zcar.r())

        S = lambda j: s5p[:, :, j]
        LR = PV(f"s5lr{l}", 0, 32)
        LI = PV(f"s5li{l}", 0, 32)
        R_, W_ = s5p.r(), s5p.r()
        A("act", OP("activation", out=S(6), in_=PV(f"s5ldt{l}", 0, 32), func=AF.Exp), reads=pv.r(), writes=W_)
        A("dve", OP("tensor_tensor", out=S(0), in0=LR, in1=S(6), op=ALU.mult), reads=R_ + pv.r(), writes=W_)
        A("act", OP("activation", out=S(0), in_=S(0), func=AF.Exp), reads=R_, writes=W_)
        A("dve", OP("tensor_tensor", out=S(7), in0=LI, in1=S(6), op=ALU.mult), reads=R_ + pv.r(), writes=W_)
        A("dve", OP("tensor_scalar", out=S(6), in0=S(7), scalar1=1.0 / (2.0 * math.pi), scalar2=MAGIC, op0=ALU.mult, op1=ALU.add), reads=R_, writes=W_)
        A("dve", OP("tensor_scalar", out=S(6), in0=S(6), scalar1=MAGIC, scalar2=None, op0=ALU.subtract), reads=R_, writes=W_)
        A("dve", OP("scalar_tensor_tensor", out=S(1), in0=S(7), scalar=1.0 / (2.0 * math.pi), in1=S(6), op0=ALU.mult, op1=ALU.subtract), reads=R_, writes=W_)
        A("act", OP("activation", out=S(7), in_=S(1), func=AF.Sin, scale=TWO_PI_S), reads=R_, writes=W_)
        A("act", OP("activation", out=S(6), in_=S(1), func=AF.Abs), reads=R_, writes=W_)
        A("act", OP("activation", out=S(6), in_=S(6), func=AF.Sin, scale=-TWO_PI_S, bias=math.pi / 2.0 - 1e-6), reads=R_, writes=W_)
        A("dve", OP("tensor_tensor", out=S(6), in0=S(6), in1=S(0), op=ALU.mult), reads=R_, writes=W_)
        A("dve", OP("tensor_scalar", out=S(6), in0=S(6), scalar1=-1.0, scalar2=None, op0=ALU.add), reads=R_, writes=W_)
        A("dve", OP("tensor_tensor", out=S(7), in0=S(7), in1=S(0), op=ALU.mult), reads=R_, writes=W_)
        A("dve", OP("tensor_tensor", out=S(5), in0=LR, in1=LR, op=ALU.mult), reads=R_ + pv.r(), writes=W_)
        A("dve", OP("tensor_tensor", out=S(4), in0=LI, in1=LI, op=ALU.mult), reads=R_ + pv.r(), writes=W_)
        A("dve", OP("tensor_tensor", out=S(5), in0=S(5), in1=S(4), op=ALU.add), reads=R_, writes=W_)
        A("dve", OP("reciprocal", out=S(5), in_=S(5)), reads=R_, writes=W_)
        A("dve", OP("tensor_tensor", out=S(2), in0=S(6), in1=LR, op=ALU.mult), reads=R_ + pv.r(), writes=W_)
        A("dve", OP("tensor_tensor", out=S(4), in0=S(7), in1=LI, op=ALU.mult), reads=R_ + pv.r(), writes=W_)
        A("dve", OP("tensor_tensor", out=S(2), in0=S(2), in1=S(4), op=ALU.add), reads=R_, writes=W_)
        A("dve", OP("tensor_tensor", out=S(2), in0=S(2), in1=S(5), op=ALU.mult), reads=R_, writes=W_)
        A("dve", OP("tensor_tensor", out=S(3), in0=S(7), in1=LR, op=ALU.mult), reads=R_ + pv.r(), writes=W_)
        A("dve", OP("tensor_tensor", out=S(4), in0=S(6), in1=LI, op=ALU.mult), reads=R_ + pv.r(), writes=W_)
        A("dve", OP("tensor_tensor", out=S(3), in0=S(3), in1=S(4), op=ALU.subtract), reads=R_, writes=W_)
        A("dve", OP("tensor_tensor", out=S(3), in0=S(3), in1=S(5), op=ALU.mult), reads=R_, writes=W_)
        A("dve", OP("tensor_copy", out=S(5), in_=S(3)), reads=R_, writes=W_)
        A("dve", OP("tensor_scalar", out=S(3), in0=S(3), scalar1=PV("sgn"), scalar2=None, op0=ALU.mult), reads=R_ + pv.r(), writes=W_)
        A("dve", OP("tensor_scalar", out=S(4), in0=S(2), scalar1=PV("nsgn"), scalar2=None, op0=ALU.mult), reads=R_ + pv.r(), writes=W_)
        for t in range(NT):
            A("dve", OP("tensor_scalar", out=s5off[:, :, t], in0=S(1), scalar1=float(TT * t), scalar2=MAGIC, op0=ALU.mult, op1=ALU.add),
              reads=R_, writes=s5off.r())
            A("dve", OP("tensor_scalar", out=s5off[:, :, t], in0=s5off[:, :, t], scalar1=MAGIC, scalar2=None, op0=ALU.subtract),
              reads=s5off.r(), writes=s5off.r())
            A("dve", OP("scalar_tensor_tensor", out=s5off[:, :, t], in0=S(1), scalar=float(TT * t), in1=s5off[:, :, t],
                                                           op0=ALU.mult, op1=ALU.subtract), reads=R_ + s5off.r(), writes=s5off.r())
        cn1 = ring.get(); cn2 = ring.get()
        A("sp", OP("dma_start", out=cn1[:, 0:512], in_=x1_d[l]), writes=cn1.r(), dma=True)
        A("sp", OP("dma_start", out=cn2[:, 0:512], in_=x2_d[l]), writes=cn2.r(), dma=True)
        v3 = lambda tl: tl[:, 0:512].rearrange("p (g c) -> p g c", c=16)
        bc = lambda j: s5p[:, :, j:j + 1].to_broadcast([128, 32, 16])
        t1 = ring.get(); t2 = ring.get()
        t1v = t1[:, 0:512].rearrange("p (g c) -> p g c", c=16)
        t2v = t2[:, 0:512].rearrange("p (g c) -> p g c", c=16)
        A("dve", OP("tensor_tensor", out=t1v, in0=v3(cn1), in1=bc(2), op=ALU.mult), reads=cn1.r() + R_, writes=t1.r())
        A("dve", OP("tensor_tensor", out=t2v, in0=v3(cn2), in1=bc(3), op=ALU.mult), reads=cn2.r() + R_, writes=t2.r())
        A("dve", OP("tensor_tensor", out=bst1[:], in0=t1[:, 0:512], in1=t2[:, 0:512], op=ALU.add), reads=t1.r() + t2.r(), writes=bst1.r())
        A("dve", OP("tensor_tensor", out=t1v, in0=v3(cn2), in1=bc(4), op=ALU.mult), reads=cn2.r() + R_, writes=t1.r())
        A("dve", OP("tensor_tensor", out=t2v, in0=v3(cn1), in1=bc(5), op=ALU.mult), reads=cn1.r() + R_, writes=t2.r())
        A("dve", OP("tensor_tensor", out=bst2[:], in0=t1[:, 0:512], in1=t2[:, 0:512], op=ALU.add), reads=t1.r() + t2.r(), writes=bst2.r())
        ring.free(t1, t2, cn1, cn2)
        A("pool", OP("memset", lhs_c[:].rearrange("p a g n -> p (a g n)"), 0.0), writes=lhs_c.r())
        dbg(f"bst1_{l}", bst1, bst1[:], [128, 512], BF16)
        dbg(f"s5p_{l}", s5p, s5p[:].rearrange("p g j -> p (g j)"), [128, 256])

        rmsnorm_to_xn(f"gmix{l}")
        dbg(f"xn{l}", xn, xn[:].rearrange("p c t -> p (c t)"), [128, 8 * SEQ], BF16)

        zbf = aux[:].rearrange("p (c t) -> p c t", c=4)
        def lru_chunk(c):
            wb = wbufs.get()
            wv_ = wb[:].rearrange("p (k n) -> p k n", k=8)
            load_w(wb, wv_[:, :, 0:128], w_in_d[l, :, c * 128:(c + 1) * 128].rearrange("(k p) n -> p k n", p=128))
            load_w(wb, wv_[:, :, 128:256], w_in_d[l, :, 512 + c * 128:512 + (c + 1) * 128].rearrange("(k p) n -> p k n", p=128))
            return wb, wv_

        def lru_unit(c, t, wb, wv_):
            bx = banks.get(); bg = banks.get()
            proj(bx, wb, lambda k: wv_[:, k, 0:128], t)
            proj(bg, wb, lambda k: wv_[:, k, 128:256], t)
            xp = ring.get(halo=True); gy = ring.get(); xc = ring.get(); xcb = ring.get()
            A("act", OP("activation", out=xp[:, 0:3], in_=lcar[:, c, 0:3], func=AF.Copy), reads=lcar.r(), writes=xp.r())
            A("act", OP("activation", out=xp[:, 3:3 + TT], in_=psum[:, bx, :], func=AF.Copy), reads=psum.r(bx), writes=xp.r())
            A("act", OP("activation", out=lcar[:, c, 0:3], in_=xp[:, TT:TT + 3], func=AF.Copy), reads=xp.r(), writes=lcar.r())
            A("act", OP("activation", out=gy[:, 0:TT], in_=psum[:, bg, :], func=AF.Gelu_apprx_tanh), reads=psum.r(bg), writes=gy.r())
            banks.free(bx, bg)
            A("act", OP("activation", out=xc[:, 0:TT], in_=xp[:, 3:3 + TT], func=AF.Identity,
                                                          scale=PV(f"lcw{l}", c * 4 + 3), bias=PV(f"lcb{l}", c)),
              reads=xp.r() + pv.r(), writes=xc.r())
            for k in range(3):
                A("dve", OP("scalar_tensor_tensor", out=xc[:, 0:TT], in0=xp[:, k:k + TT], scalar=PV(f"lcw{l}", c * 4 + k),
                                                                            in1=xc[:, 0:TT], op0=ALU.mult, op1=ALU.add),
                  reads=xp.r() + xc.r() + pv.r(), writes=xc.r())
            A("act", OP("activation", out=bf(xcb), in_=xc[:, 0:TT], func=AF.Copy), reads=xc.r(), writes=xcb.r())
            ring.free(xp)
            br_ = banks.get(); bi_ = banks.get()
            A("pe", OP("matmul", out=psum[:, br_, :], lhsT=wbd[:, 0, c, :], rhs=bf(xcb), start=True, stop=True),
              reads=wbd.r() + xcb.r(), writes=psum.r(br_))
            A("pe", OP("matmul", out=psum[:, bi_, :], lhsT=wbd[:, 1, c, :], rhs=bf(xcb), start=True, stop=True),
              reads=wbd.r() + xcb.r(), writes=psum.r(bi_))
            rr = ring.get(); ii = ring.get(); a2 = ring.get()
            A("act", OP("activation", out=rr[:, 0:TT], in_=psum[:, br_, :], func=AF.Tanh, scale=0.5, bias=lpar[:, 8 + c:9 + c]),
              reads=psum.r(br_) + lpar.r(), writes=rr.r())
            A("act", OP("activation", out=ii[:, 0:TT], in_=psum[:, bi_, :], func=AF.Tanh, scale=0.5, bias=lpar[:, 12 + c:13 + c]),
              reads=psum.r(bi_) + lpar.r(), writes=ii.r())
            banks.free(br_, bi_)
            ring.free(xcb)
            A("act", OP("activation", out=a2[:, 0:TT], in_=rr[:, 0:TT], func=AF.Exp, scale=lpar[:, c:c + 1], bias=lpar[:, c:c + 1]),
              reads=rr.r() + lpar.r(), writes=a2.r())
            A("act", OP("activation", out=rr[:, 0:TT], in_=rr[:, 0:TT], func=AF.Exp, scale=lpar[:, 4 + c:5 + c], bias=lpar[:, 4 + c:5 + c]),
              reads=rr.r() + lpar.r(), writes=rr.r())
            A("dve", OP("tensor_scalar", out=a2[:, 0:TT], in0=a2[:, 0:TT], scalar1=1.0, scalar2=-1e-30, op0=ALU.subtract, op1=ALU.min),
              reads=a2.r(), writes=a2.r())
            A("act", OP("activation", out=a2[:, 0:TT], in_=a2[:, 0:TT], func=AF.Ln, scale=-1.0), reads=a2.r(), writes=a2.r())
            A("act", OP("activation", out=a2[:, 0:TT], in_=a2[:, 0:TT], func=AF.Exp, scale=0.5), reads=a2.r(), writes=a2.r())
            A("dve", OP("scalar_tensor_tensor", out=ii[:, 0:TT], in0=ii[:, 0:TT], scalar=1.0, in1=xc[:, 0:TT], op0=ALU.add, op1=ALU.mult),
              reads=ii.r() + xc.r(), writes=ii.r())
            A("dve", OP("scalar_tensor_tensor", out=ii[:, 0:TT], in0=ii[:, 0:TT], scalar=0.5, in1=a2[:, 0:TT], op0=ALU.mult, op1=ALU.mult),
              reads=ii.r() + a2.r(), writes=ii.r())
            hh = xc
            A("dve", OP("tensor_tensor_scan", out=hh[:, 0:TT], data0=rr[:, 0:TT], data1=ii[:, 0:TT],
                                                                        initial=lcar[:, c, 3:4], op0=ALU.mult, op1=ALU.add),
              reads=rr.r() + ii.r() + lcar.r(), writes=hh.r())
            A("act", OP("activation", out=lcar[:, c, 3:4], in_=hh[:, TT - 1:TT], func=AF.Copy), reads=hh.r(), writes=lcar.r())
            A("dve", OP("tensor_tensor", out=mixed[:, c, tsl(t)], in0=hh[:, 0:TT], in1=gy[:, 0:TT], op=ALU.mult),
              reads=hh.r() + gy.r(), writes=mixed.r(c * 4 + t))
            ring.free(rr, ii, a2, xc, gy)

        def s5_chunk(c):
            wb = w5pool.get()
            wv_ = wb
            load_w(wb, wv_[:, :, 0:128], w_in_d[l, :, 1024 + c * 128:1024 + (c + 1) * 128].rearrange("(k p) n -> p k n", p=128))
            for which, bst in enumerate((bst1, bst2)):
                b = banks.get()
                tpv = psum[:, b, :].bitcast(BF16)
                A("pe", OP("transpose", out=tpv[:, 0:128], in_=bst[:, c * 128:(c + 1) * 128], identity=ident_b[:]),
                  reads=bst.r() + ident_b.r(), writes=psum.r(b))
                tb = ring.get()
                A("act", OP("activation", out=bf(tb, 128), in_=tpv[:, 0:128], func=AF.Copy), reads=psum.r(b), writes=tb.r())
                banks.free(b)
                for g in range(8):
                    A("dve", OP("tensor_scalar", out=lhs_b[:, which, g, :], in0=bf(tb, 128), scalar1=PV("rowmask", g),
                                                                                scalar2=None, op0=ALU.mult),
                      reads=tb.r() + pv.r(), writes=lhs_b.r())
                ring.free(tb)
            for which, (cd, sname) in enumerate(((c1_d, "nsgn"), (c2_d, None))):
                cn = ring.get()
                A("sp", OP("dma_start", out=cn[:, 0:128], in_=cd[l, :, c * 128:(c + 1) * 128]), writes=cn.r(), dma=True)
                b = banks.get()
                A("pe", OP("transpose", out=psum[:, b, 0:128], in_=cn[:, 0:128], identity=ident_f),
                  reads=cn.r() + cst.r(), writes=psum.r(b))
                ring.free(cn)
                for g in range(8):
                    if sname is not None:
                        A("dve", OP("tensor_scalar", out=lhs_c[:, which, g, g * 16:(g + 1) * 16], in0=psum[:, b, g * 16:(g + 1) * 16],
                                                                                  scalar1=PV("nsgn"), scalar2=None, op0=ALU.mult),
                          reads=psum.r(b) + pv.r(), writes=lhs_c.r())
                    else:
                        A("dve", OP("tensor_scalar", out=lhs_c[:, which, g, g * 16:(g + 1) * 16], in0=psum[:, b, g * 16:(g + 1) * 16],
                                                                                  scalar1=-1.0, scalar2=None, op0=ALU.mult),
                          reads=psum.r(b), writes=lhs_c.r())
                banks.free(b)
            return wb, wv_

        def s5_unit(c, t, wb, wv_):
            bu = banks.get()
            proj(bu, wb, lambda k: wv_[:, k, 0:128], t)
            uf = ring.get(); ub = ring.get()
            A("act", OP("activation", out=uf[:, 0:TT], in_=psum[:, bu, :], func=AF.Copy), reads=psum.r(bu), writes=uf.r())
            A("act", OP("activation", out=bf(ub), in_=psum[:, bu, :], func=AF.Copy), reads=psum.r(bu), writes=ub.r())
            banks.free(bu)
            by = banks.get()
            for g in range(8):
                G = c * 8 + g
                b1 = banks.get(); b2 = banks.get()
                A("pe", OP("matmul", out=psum[:, b1, :], lhsT=lhs_b[:, 0, g, :], rhs=bf(ub), start=True, stop=True),
                  reads=lhs_b.r() + ub.r(), writes=psum.r(b1))
                A("pe", OP("matmul", out=psum[:, b2, :], lhsT=lhs_b[:, 1, g, :], rhs=bf(ub), start=True, stop=True),
                  reads=lhs_b.r() + ub.r(), writes=psum.r(b2))
                uu = ring.get(); kk = ring.get(); tc_ = ring.get(); ts_ = ring.get()
                A("act", OP("activation", out=uu[:, 0:TT], in_=iota, func=AF.Identity, scale=s5p[:, G, 1:2], bias=s5off[:, G, t:t + 1]), reads=cst.r() + s5p.r() + s5off.r(), writes=uu.r())
                A("dve", OP("tensor_scalar", out=kk[:, 0:TT], in0=uu[:, 0:TT], scalar1=MAGIC, scalar2=MAGIC,
                                                                  op0=ALU.add, op1=ALU.subtract), reads=uu.r(), writes=kk.r())
                A("pool", OP("tensor_tensor", out=uu[:, 0:TT], in0=uu[:, 0:TT], in1=kk[:, 0:TT], op=ALU.subtract),
                  reads=uu.r() + kk.r(), writes=uu.r())
                A("act", OP("activation", out=ts_[:, 0:TT], in_=uu[:, 0:TT], func=AF.Sin, scale=TWO_PI_S), reads=uu.r(), writes=ts_.r())
                A("act", OP("activation", out=kk[:, 0:TT], in_=uu[:, 0:TT], func=AF.Abs), reads=uu.r(), writes=kk.r())
                A("act", OP("activation", out=tc_[:, 0:TT], in_=kk[:, 0:TT], func=AF.Sin, scale=-TWO_PI_S, bias=math.pi / 2.0 - 1e-6),
                  reads=kk.r(), writes=tc_.r())
                m1 = uu; m2 = kk
                A("dve", OP("tensor_tensor", out=m1[:, 0:TT], in0=psum[:, b1, :], in1=tc_[:, 0:TT], op=ALU.mult),
                  reads=psum.r(b1) + tc_.r(), writes=m1.r())
                A("dve", OP("tensor_tensor", out=m2[:, 0:TT], in0=psum[:, b2, :], in1=ts_[:, 0:TT], op=ALU.mult),
                  reads=psum.r(b2) + ts_.r(), writes=m2.r())
                banks.free(b1, b2)
                A("dve", OP("tensor_tensor", out=m1[:, 0:TT], in0=m1[:, 0:TT], in1=m2[:, 0:TT], op=ALU.add), reads=m1.r() + m2.r(), writes=m1.r())
                zz = m2
                A("dve", OP("tensor_tensor_scan", out=zz[:, 0:TT], data0=s5p[:, G, 0:1].to_broadcast([128, TT]), data1=m1[:, 0:TT],
                                                                          initial=zcar[:, G:G + 1], op0=ALU.mult, op1=ALU.add),
                  reads=m1.r() + s5p.r() + zcar.r(), writes=zz.r())
                A("act", OP("activation", out=zcar[:, G:G + 1], in_=zz[:, TT - 1:TT], func=AF.Copy), reads=zz.r(), writes=zcar.r())
                w12 = m1
                A("pool", OP("tensor_tensor", out=bf(w12, TT, 0), in0=zz[:, 0:TT], in1=tc_[:, 0:TT], op=ALU.mult),
                  reads=zz.r() + tc_.r(), writes=w12.r())
                A("pool", OP("tensor_tensor", out=bf(w12, TT, TT), in0=zz[:, 0:TT], in1=ts_[:, 0:TT], op=ALU.mult),
                  reads=zz.r() + ts_.r(), writes=w12.r())
                A("pe", OP("matmul", out=psum[:, by, :], lhsT=lhs_c[:, 0, g, :], rhs=bf(w12, TT, 0), start=(g == 0), stop=False),
                  reads=lhs_c.r() + w12.r(), writes=psum.r(by))
                A("pe", OP("matmul", out=psum[:, by, :], lhsT=lhs_c[:, 1, g, :], rhs=bf(w12, TT, TT), start=False, stop=(g == 7)),
                  reads=lhs_c.r() + w12.r(), writes=psum.r(by))
                ring.free(uu, kk, tc_, ts_)
            A("dve", OP("scalar_tensor_tensor", out=uf[:, 0:TT], in0=uf[:, 0:TT], scalar=PV(f"s5d{l}", c), in1=psum[:, by, :],
                                                             op0=ALU.mult, op1=ALU.add), reads=uf.r() + psum.r(by) + pv.r(), writes=uf.r())
            banks.free(by)
            A("act", OP("activation", out=zbf[:, c, tsl(t)], in_=uf[:, 0:TT], func=AF.Gelu_apprx_tanh), reads=uf.r(), writes=aux.r(c * 4 + t))
            ring.free(uf, ub)

        ring.enable_extra(rot_views)
        for c in range(4):
            lwb = lru_chunk(c)
            swb = s5_chunk(c)
            for t in range(NT):
                s5_unit(c, t, *swb)
                lru_unit(c, t, *lwb)
            wbufs.free(lwb[0])
            w5pool.free(swb[0])
        dbg(f"ylru_raw{l}", mixed, mixed[:].rearrange("p c t -> p (c t)"), [128, 4 * SEQ], BF16)
        if upto == "lru_raw":
            break
        group_post(l, mixed, "lnorm", 0, wbufs, 512.0)
        dbg(f"ylru{l}", mixed, mixed[:].rearrange("p c t -> p (c t)"), [128, 4 * SEQ], BF16)
        if upto == "lru":
            break

        dbg(f"s5z{l}", aux, aux[:], [128, 4 * SEQ], BF16)
        wg_t = wbufs.get()
        wgl = wg_t[:, 0:2048].rearrange("p (k n) -> p k n", k=4)
        load_w(wg_t, wgl, w_glu_d[l].rearrange("(k p) n -> p k n", p=128))
        for t in range(NT):
            for oc in range(4):
                b = banks.get()
                for k in range(4):
                    A("pe", OP("matmul", out=psum[:, b, :], lhsT=wgl[:, k, oc * 128:(oc + 1) * 128], rhs=zbf[:, k, tsl(t)],
                                                                start=(k == 0), stop=(k == 3)), reads=wg_t.r() + aux.r(k * 4 + t), writes=psum.r(b))
                sg = ring.get()
                A("act", OP("activation", out=sg[:, 0:TT], in_=psum[:, b, :], func=AF.Sigmoid, bias=PV(f"s5bg{l}", oc)),
                  reads=psum.r(b) + pv.r(), writes=sg.r())
                banks.free(b)
                A("dve", OP("tensor_tensor", out=mixed[:, oc, tsl(t)], in0=zbf[:, oc, tsl(t)], in1=sg[:, 0:TT], op=ALU.mult),
                  reads=sg.r() + aux.r(oc * 4 + t), writes=mixed.r(oc * 4 + t))
                ring.free(sg)
        wbufs.free(wg_t)
        group_post(l, mixed, "s5n", 1, wbufs, 512.0)
        dbg(f"ys5{l}", mixed, mixed[:].rearrange("p c t -> p (c t)"), [128, 4 * SEQ], BF16)
        if upto == "s5":
            break

        ring.disable_extra()
        build_rot()
        if l == 0:
            dbg("rcos", rcos, rcos[:], [128, SEQ])
            dbg("rsin", rsin, rsin[:], [128, SEQ])
        fw.pin = set(os.environ.get('KPIN', 'dve').split(',')) - {''}
        vh = aux[:].rearrange("p (n e) -> p n e", n=16)
        wv_t = wbufs.get()
        wvv = wv_t[:].rearrange("p (k n) -> p k n", k=8)
        load_w(wv_t, wvv, w_in_d[l, :, 2560:3072].rearrange("(k p) n -> p k n", p=128))
        for n in range(16):
            b = banks.get()
            t = n // 4
            for k in range(8):
                A("pe", OP("matmul", out=psum[:, b, :], lhsT=xn[:, k, n * 128:(n + 1) * 128], rhs=wvv[:, k, :],
                                                          start=(k == 0), stop=(k == 7)), reads=wv_t.r() + xs(k, t), writes=psum.r(b))
            for hd in range(4):
                A("act", OP("activation", out=vh[:, n, hd * 128:(hd + 1) * 128], in_=psum[:, b, hd * 128:(hd + 1) * 128],
                                                                 func=AF.Identity, scale=PV("vdec", hd)), reads=psum.r(b) + pv.r(), writes=aux.r(n))
            banks.free(b)
        wbufs.free(wv_t)
        wg_t = wbufs.get()
        wgv = wg_t[:].rearrange("p (k n) -> p k n", k=8)
        load_w(wg_t, wgv, w_in_d[l, :, 3072:3584].rearrange("(k p) n -> p k n", p=128))
        for hd in range(4):
            gam = GAMMAS[hd]
            gC = float(np.float32(np.exp(np.float32(np.log1p(-np.float32(2.0) ** np.float32(-5.0 - hd))) * np.float32(128.0))))
            wb = wbufs.get()
            wq = wb[:].rearrange("p (k n) -> p k n", k=8)
            qb = 1536 + hd * 128
            kb = 2048 + hd * 128
            src = lambda c0, c1: w_in_d[l, :, c0:c1].rearrange("(k p) n -> p k n", p=128)
            load_w(wb, wq[:, :, 0:128], src(qb, qb + 128))
            load_w(wb, wq[:, :, 128:192], src(qb + 64, qb + 128))
            load_w(wb, wq[:, :, 192:256], src(qb, qb + 64))
            load_w(wb, wq[:, :, 256:384], src(kb, kb + 128))
            load_w(wb, wq[:, :, 384:448], src(kb + 64, kb + 128))
            load_w(wb, wq[:, :, 448:512], src(kb, kb + 64))
            for t in range(NT):
                rot = []
                for which in range(2):
                    ba = banks.get(); bb = banks.get()
                    proj(ba, wb, lambda k, o=which * 256: wq[:, k, o:o + 128], t)
                    proj(bb, wb, lambda k, o=which * 256 + 128: wq[:, k, o:o + 128], t)
                    t1 = ring.get(); t2 = ring.get(); qr = ring.get()
                    A("dve", OP("tensor_tensor", out=t1[:, 0:TT], in0=psum[:, ba, :], in1=rcos[:, tsl(t)], op=ALU.mult),
                      reads=psum.r(ba) + rcos.r(t), writes=t1.r())
                    A("dve", OP("tensor_tensor", out=t2[:, 0:TT], in0=psum[:, bb, :], in1=rsin[:, tsl(t)], op=ALU.mult),
                      reads=psum.r(bb) + rsin.r(t), writes=t2.r())
                    banks.free(ba, bb)
                    A("pool", OP("tensor_tensor", out=bf(qr), in0=t1[:, 0:TT], in1=t2[:, 0:TT], op=ALU.add),
                      reads=t1.r() + t2.r(), writes=qr.r())
                    ring.free(t1, t2)
                    rot.append(qr)
                qr, kr = rot
                if hd == 0 and t == 0:
                    dbg(f"qr{l}", qr, bf(qr), [128, TT], BF16)
                    dbg(f"kr{l}", kr, bf(kr), [128, TT], BF16)
                bt = banks.get()
                ktp = psum[:, bt, :].bitcast(BF16)
                for j in range(4):
                    A("pe", OP("transpose", out=ktp[:, j * 128:(j + 1) * 128], in_=bf(kr, 128, j * 128), identity=ident_b[:]),
                      reads=kr.r() + ident_b.r(), writes=psum.r(bt))
                ktm = ring.get()
                A("act", OP("activation", out=bf(ktm), in_=ktp[:, 0:TT], func=AF.Copy), reads=psum.r(bt), writes=ktm.r())
                banks.free(bt)
                bs = banks.get()
                for j in range(4):
                    A("pe", OP("matmul", out=psum[:, bs, j * 128:(j + 1) * 128], lhsT=bf(kr, 128, j * 128), rhs=bf(qr, 128, j * 128),
                                                                  start=True, stop=True), reads=kr.r() + qr.r(), writes=psum.r(bs))
                pT = ring.get()
                A("dve", OP("tensor_tensor", out=bf(pT).rearrange("p (j c) -> p j c", j=4), in0=psum[:, bs, :].rearrange("p (j c) -> p j c", j=4),
                                                          in1=maskT.unsqueeze(1).to_broadcast([128, 4, 128]), op=ALU.mult),
                  reads=psum.r(bs) + cst.r(), writes=pT.r())
                banks.free(bs)
                bkv = banks.get()
                for j in range(4):
                    n = t * 4 + j
                    A("pe", OP("matmul", out=psum[:, bkv, j * 128:(j + 1) * 128], lhsT=bf(ktm, 128, j * 128),
                                                                  rhs=vh[:, n, hd * 128:(hd + 1) * 128], start=True, stop=True),
                      reads=ktm.r() + aux.r(n), writes=psum.r(bkv))
                ring.free(ktm)
                bxo = banks.get()
                for j in range(4):
                    n = t * 4 + j
                    pb = pbs[n % 2]
                    if n > 0:
                        A("act", OP("activation", out=pb[:], in_=ust[:], func=AF.Identity, scale=float(gC * (128.0 ** -0.5))),
                          reads=ust.r(), writes=pb.r())
                    A("pe", OP("matmul", out=psum[:, bxo, j * 128:(j + 1) * 128], lhsT=vh[:, n, hd * 128:(hd + 1) * 128],
                                                                rhs=bf(pT, 128, j * 128), start=True, stop=(n == 0)),
                      reads=aux.r(n) + pT.r(), writes=psum.r(bxo))
                    if n > 0:
                        A("pe", OP("matmul", out=psum[:, bxo, j * 128:(j + 1) * 128], lhsT=pb[:], rhs=bf(qr, 128, j * 128),
                                                                      start=False, stop=True), reads=pb.r() + qr.r(), writes=psum.r(bxo))
                        A("dve", OP("scalar_tensor_tensor", out=ust[:], in0=ust[:], scalar=gC, in1=psum[:, bkv, j * 128:(j + 1) * 128],
                                                                       op0=ALU.mult, op1=ALU.add), reads=ust.r() + psum.r(bkv), writes=ust.r())
                    else:
                        A("dve", OP("tensor_copy", out=ust[:], in_=psum[:, bkv, j * 128:(j + 1) * 128]), reads=psum.r(bkv), writes=ust.r())
                banks.free(bkv)
                ring.free(pT, kr)
                oT = ring.get()
                A("dve", OP("tensor_tensor", out=oT[:, 0:TT].rearrange("p (j c) -> p j c", j=4), in0=psum[:, bxo, :].rearrange("p (j c) -> p j c", j=4),
                                                          in1=cst[:, C_QDEC + hd * 128:C_QDEC + (hd + 1) * 128].unsqueeze(1).to_broadcast([128, 4, 128]), op=ALU.mult),
                  reads=psum.r(bxo) + cst.r(), writes=oT.r())
                banks.free(bxo)
                ring.free(qr)
                ob = ring.get()
                A("act", OP("activation", out=bf(ob, TT, 0), in_=oT[:, 0:TT], func=AF.Copy), reads=oT.r(), writes=ob.r())
                A("act", OP("activation", out=bf(ob, TT, TT), in_=oT[:, 0:TT], func=AF.Square), reads=oT.r(), writes=ob.r())
                bm = banks.get(); bq = banks.get()
                A("pe", OP("matmul", out=psum[:, bm, :], lhsT=o128_b[:], rhs=bf(ob, TT, 0), start=True, stop=True),
                  reads=ob.r() + o128_b.r(), writes=psum.r(bm))
                A("pe", OP("matmul", out=psum[:, bq, :], lhsT=o128_b[:], rhs=bf(ob, TT, TT), start=True, stop=True),
                  reads=ob.r() + o128_b.r(), writes=psum.r(bq))
                ring.free(ob)
                m2 = ring.get()
                A("act", OP("activation", out=m2[:, 0:TT], in_=psum[:, bm, :], func=AF.Square), reads=psum.r(bm), writes=m2.r())
                A("dve", OP("tensor_tensor", out=m2[:, 0:TT], in0=psum[:, bq, :], in1=m2[:, 0:TT], op=ALU.subtract),
                  reads=psum.r(bq) + m2.r(), writes=m2.r())
                banks.free(bq)
                A("dve", OP("tensor_scalar", out=m2[:, 0:TT], in0=m2[:, 0:TT], scalar1=0.0, scalar2=None, op0=ALU.max), reads=m2.r(), writes=m2.r())
                A("act", OP("activation", out=m2[:, 0:TT], in_=m2[:, 0:TT], func=AF.Ln, bias=NORM_EPS), reads=m2.r(), writes=m2.r())
                A("act", OP("activation", out=m2[:, 0:TT], in_=m2[:, 0:TT], func=AF.Exp, scale=-0.5), reads=m2.r(), writes=m2.r())
                A("dve", OP("tensor_tensor", out=oT[:, 0:TT], in0=oT[:, 0:TT], in1=psum[:, bm, :], op=ALU.subtract),
                  reads=oT.r() + psum.r(bm), writes=oT.r())
                banks.free(bm)
                A("dve", OP("tensor_tensor", out=oT[:, 0:TT], in0=oT[:, 0:TT], in1=m2[:, 0:TT], op=ALU.mult),
                  reads=oT.r() + m2.r(), writes=oT.r())
                ring.free(m2)
                bg = banks.get()
                proj(bg, wg_t, lambda k: wgv[:, k, hd * 128:(hd + 1) * 128], t)
                sg = ring.get()
                A("act", OP("activation", out=sg[:, 0:TT], in_=psum[:, bg, :], func=AF.Silu), reads=psum.r(bg), writes=sg.r())
                banks.free(bg)
                A("dve", OP("scalar_tensor_tensor", out=mixed[:, hd, tsl(t)], in0=oT[:, 0:TT], scalar=PV(f"rnorm{l}", hd), in1=sg[:, 0:TT],
                                                                        op0=ALU.mult, op1=ALU.mult), reads=oT.r() + sg.r() + pv.r(), writes=mixed.r(hd * 4 + t))
                ring.free(oT, sg)
            wbufs.free(wb)
        wbufs.free(wg_t)
        dbg(f"yret{l}", mixed, mixed[:].rearrange("p c t -> p (c t)"), [128, 4 * SEQ], BF16)
        fw.pin = set()
        group_post(l, mixed, None, 2, wbufs, 512.0)
        dbg(f"hmix{l}", hT, hT[:].rearrange("p c t -> p (c t)"), [128, 8 * SEQ])
        for cm in reversed(phase):
            cm.__exit__(None, None, None)
        phase = []
        fw.barrier()
        if upto == "mix":
            break

        rmsnorm_to_xn(f"gffn{l}")
        actb = psb("actb", [128, 4, SEQ], BF16, nslots=16)
        wus = Pool([psb(f"wu{i}", [128, 8, 1024], BF16) for i in range(2)])
        wds = Pool([psb(f"wd{i}", [128, 4, 1024], BF16) for i in range(2)])
        fcar = psb("fcar", [128, 2, 2], F32)
        for grp in range(6):
            j0 = grp * 4
            wu = wus.get(); wd = wds.get()
            load_w(wu, wu[:, :, 0:512], w_up_d[l, :, j0 * 128:(j0 + 4) * 128].rearrange("(k p) n -> p k n", p=128))
            load_w(wu, wu[:, :, 512:1024], w_up_d[l, :, D_FF + j0 * 128:D_FF + (j0 + 4) * 128].rearrange("(k p) n -> p k n", p=128))
            load_w(wd, wd[:], w_down_d[l, j0 * 128:(j0 + 4) * 128, :].rearrange("(k p) n -> p k n", p=128))
            for jj in range(4):
                j = j0 + jj
                for t in range(NT):
                    outs = []
                    for which in range(2):
                        ch = j + 24 * which
                        b = banks.get()
                        proj(b, wu, lambda k, o=which * 512 + jj * 128: wu[:, k, o:o + 128], t)
                        vc = ring.get()
                        w0 = PV(f"fcw{l}", ch * 3 + 0); w1 = PV(f"fcw{l}", ch * 3 + 1); w2 = PV(f"fcw{l}", ch * 3 + 2)
                        A("act", OP("activation", out=vc[:, 0:TT], in_=psum[:, b, :], func=AF.Identity, scale=w2,
                                                                                  bias=PV(f"fcb{l}", ch)), reads=psum.r(b) + pv.r(), writes=vc.r())
                        A("dve", OP("scalar_tensor_tensor", out=vc[:, 1:TT], in0=psum[:, b, 0:TT - 1], scalar=w1, in1=vc[:, 1:TT],
                                                                                    op0=ALU.mult, op1=ALU.add), reads=psum.r(b) + vc.r() + pv.r(), writes=vc.r())
                        A("dve", OP("scalar_tensor_tensor", out=vc[:, 2:TT], in0=psum[:, b, 0:TT - 2], scalar=w0, in1=vc[:, 2:TT],
                                                                                    op0=ALU.mult, op1=ALU.add), reads=psum.r(b) + vc.r() + pv.r(), writes=vc.r())
                        if t > 0:
                            A("dve", OP("scalar_tensor_tensor", out=vc[:, 0:1], in0=fcar[:, which, 1:2], scalar=w1, in1=vc[:, 0:1],
                                                                                                op0=ALU.mult, op1=ALU.add), reads=fcar.r() + vc.r() + pv.r(), writes=vc.r())
                            A("dve", OP("scalar_tensor_tensor", out=vc[:, 0:2], in0=fcar[:, which, 0:2], scalar=w0, in1=vc[:, 0:2],
                                                                                                op0=ALU.mult, op1=ALU.add), reads=fcar.r() + vc.r() + pv.r(), writes=vc.r())
                        if t < NT - 1:
                            A("act", OP("activation", out=fcar[:, which, :], in_=psum[:, b, TT - 2:TT], func=AF.Copy),
                              reads=psum.r(b), writes=fcar.r())
                        banks.free(b)
                        outs.append(vc)
                    vc, gc = outs
                    A("act", OP("activation", out=gc[:, 0:TT], in_=gc[:, 0:TT], func=AF.Gelu_apprx_tanh), reads=gc.r(), writes=gc.r())
                    A("dve", OP("tensor_tensor", out=actb[:, jj, tsl(t)], in0=gc[:, 0:TT], in1=vc[:, 0:TT], op=ALU.mult),
                      reads=vc.r() + gc.r(), writes=actb.r(jj * 4 + t))
                    ring.free(vc, gc)
            for t in range(NT):
                for dc in range(8):
                    b = banks.get()
                    for k in range(4):
                        A("pe", OP("matmul", out=psum[:, b, :], lhsT=wd[:, k, dc * 128:(dc + 1) * 128], rhs=actb[:, k, tsl(t)],
                                                                    start=(k == 0), stop=(k == 3)), reads=wd.r() + actb.r(k * 4 + t), writes=psum.r(b))
                    A("dve", OP("tensor_tensor", out=hT[:, dc, tsl(t)], in0=hT[:, dc, tsl(t)], in1=psum[:, b, :], op=ALU.add),
                      reads=hs(dc, t) + psum.r(b), writes=hs(dc, t))
                    banks.free(b)
            wus.free(wu); wds.free(wd)
        dbg(f"hffn{l}", hT, hT[:].rearrange("p c t -> p (c t)"), [128, 8 * SEQ])
        for cm in reversed(phase):
            cm.__exit__(None, None, None)
        phase = []
        fw.barrier()

    for t in range(NT):
        b = banks.get()
        for c in range(8):
            sq = ring.get()
            A("act", OP("activation", out=bf(sq), in_=hT[:, c, tsl(t)], func=AF.Square), reads=hs(c, t), writes=sq.r())
            A("pe", OP("matmul", out=psum[:, b, :], lhsT=ones_b[:], rhs=bf(sq), start=(c == 0), stop=(c == 7)),
              reads=sq.r() + ones_b.r(), writes=psum.r(b))
            ring.free(sq)
        rs = ring.get()
        A("act", OP("activation", out=rs[:, 0:TT], in_=psum[:, b, :], func=AF.Ln, scale=1.0 / D_MODEL, bias=NORM_EPS), reads=psum.r(b), writes=rs.r())
        banks.free(b)
        A("act", OP("activation", out=rs[:, 0:TT], in_=rs[:, 0:TT], func=AF.Exp, scale=-0.5), reads=rs.r(), writes=rs.r())
        for c in range(8):
            ot = ring.get()
            A("dve", OP("scalar_tensor_tensor", out=ot[:, 0:TT], in0=hT[:, c, tsl(t)], scalar=PV("gfin", c), in1=rs[:, 0:TT],
                                                                         op0=ALU.mult, op1=ALU.mult), reads=hs(c, t) + rs.r() + pv.r(), writes=ot.r())
            ins = A("sp", OP("dma_start", out=outT_d[c * 128:(c + 1) * 128, tsl(t)], in_=ot[:, 0:TT]), reads=ot.r(), dma=True)
            out_dmas.append(ins)
            ring.free(ot)
        ring.free(rs)

    for cm in reversed(phase):
        cm.__exit__(None, None, None)
    fin = A("sp", None)
    for ins in out_dmas:
        fin.preds[ins] = True
    fw.emit()
    fw.close()
    return nc, dbg_out


def make_in_maps(inputs):
    inp = {k: np.asarray(v) for k, v in inputs.items()}
    pv = host_pv(inp)
    cst = host_consts()
    wbd, x1, x2, c1, c2 = host_struct(inp)
    shared = {
        "pv": pv, "cst": cst,
        "wbd": np.ascontiguousarray(wbd.reshape(DEPTH, 128, 1024)),
        "s5x1": x1, "s5x2": x2,
        "s5c1": np.ascontiguousarray(c1.reshape(DEPTH, 128, 512)),
        "s5c2": np.ascontiguousarray(c2.reshape(DEPTH, 128, 512)),
        "w_in": np.ascontiguousarray(inp["w_in"], dtype=np.float32),
        "s5_w_glu": np.ascontiguousarray(inp["s5_w_glu"], dtype=np.float32),
        "w_out": np.ascontiguousarray(inp["w_out"], dtype=np.float32),
        "w_up": np.ascontiguousarray(inp["w_up"], dtype=np.float32),
        "w_down": np.ascontiguousarray(inp["w_down"], dtype=np.float32),
    }
    maps = []
    for b in range(inp["x"].shape[0]):
        m = dict(shared)
        m["xT"] = np.ascontiguousarray(inp["x"][b].T.astype(np.float32))
        m["pos"] = np.ascontiguousarray(inp["positions"][b].astype(np.int32).reshape(1, SEQ))
        maps.append(m)
    return maps


_NC_CACHE = {}


def kernel(**inputs):
    if "nc" not in _NC_CACHE:
        _NC_CACHE["nc"] = build()[0]
    nc = _NC_CACHE["nc"]
    maps = make_in_maps(inputs)
    res = run_bass_kernel_spmd(nc, maps, core_ids=list(range(len(maps))))
    out = np.stack([np.ascontiguousarray(r["outT"].T) for r in res.results], axis=0)
    return out.astype(np.float32)
```

```python
import math
import os
from collections import deque
import numpy as np
import concourse.bass as bass
import concourse.mybir as mybir
from concourse.bass_utils import run_bass_kernel_spmd

F32 = mybir.dt.float32
BF16 = mybir.dt.bfloat16
I32 = mybir.dt.int32
AF = mybir.ActivationFunctionType
ALU = mybir.AluOpType

ENGS = ("pe", "act", "dve", "pool", "sp")
N_DMA_SEMS = 24

D_MODEL = 1024
SEQ = 2048
DEPTH = 2
TT = 512
NT = SEQ // TT
IN_WIDTH = 3584
D_FF = 3072
NORM_EPS = 1e-6
MAGIC = 12582912.0
TWO_PI_S = 6.2831850
GAMMAS = [1.0 - 2.0 ** (-5.0 - h) for h in range(4)]


class Reg:
    __slots__ = ("name", "w", "rds")

    def __init__(self, name):
        self.name = name
        self.w = None
        self.rds = []


class T:
    def __init__(self, h, name, nslots=1):
        self.h = h
        self.name = name
        self.regs = [Reg(f"{name}.{i}") for i in range(nslots)]

    def __getitem__(self, k):
        return self.h[k]

    def r(self, i=None, j=None):
        if i is None:
            return list(self.regs)
        if j is None:
            return [self.regs[i]]
        return self.regs[i:j]


class Ins:
    __slots__ = ("eng", "rec", "fn", "preds", "is_dma", "dma_sem", "dma_use", "cost", "lat", "seg",
                 "tset", "sched", "done", "rt", "pos", "waits", "needs_inc", "count", "waits_dma", "pin")

    def __init__(self, eng, rec, fn):
        self.eng = eng
        self.rec = rec
        self.fn = fn
        self.preds = {}
        self.is_dma = False
        self.dma_sem = None
        self.dma_use = None
        self.cost = 100.0
        self.lat = 0.0
        self.seg = 0
        self.tset = None
        self.sched = False
        self.done = 0.0
        self.rt = None
        self.pos = None
        self.waits = []
        self.needs_inc = False
        self.count = None
        self.pin = False


_ACT_GROUP = {}


def _act_group(func):
    if not _ACT_GROUP:
        _ACT_GROUP.update({AF.Exp: 1, AF.Ln: 1, AF.Gelu_apprx_tanh: 2, AF.Silu: 3, AF.Sin: 3, AF.Sigmoid: 4, AF.Sqrt: 5})
    return _ACT_GROUP.get(func)


def _free(ap):
    n = 1
    for d in ap.shape[1:]:
        n *= int(d)
    return n


def est_cost(eng, fn, is_dma):
    if fn is None:
        return 0.0, 0.0
    name, args, kw = fn
    if is_dma:
        out = kw["out"]
        nbytes = _free(out) * int(out.shape[0]) * 4
        return (1000.0 if eng == "pool" else 120.0), 2000.0 + nbytes / 150.0
    if eng == "pe":
        if name == "transpose":
            return 110.0, 0.0
        n = _free(kw["rhs"])
        return max(n, 64) / 1.9 + 10.0, 0.0
    out = kw.get("out")
    if out is None:
        out = args[0]
    n = _free(out)
    if eng == "act":
        return 150.0 + n / 1.2, 0.0
    if eng == "dve":
        if name == "tensor_tensor_scan":
            return 120.0 + 2.0 * n / 0.96, 0.0
        if name == "reciprocal":
            return 120.0 + 6.1 * n, 0.0
        if name in ("tensor_tensor", "scalar_tensor_tensor"):
            return 120.0 + n / 0.96, 0.0
        return 120.0 + n / 1.5, 0.0
    if eng == "pool":
        if name == "tensor_tensor":
            return 150.0 + 2.15 * n, 0.0
        return 150.0 + 1.2 * n, 0.0
    return 100.0, 0.0


class FW:
    SEM_LAT = 80.0
    WINDOW = int(os.environ.get('KW', '80'))
    WIN_ENG = {e: int(os.environ.get('KW_' + e.upper(), '0')) for e in ('pe', 'act', 'dve', 'pool', 'sp')}

    def __init__(self, nc):
        self.nc = nc
        self.all = []
        self.dma_rr = 0
        self.dma_rr_pool = 0
        self.dma_uses = [0] * N_DMA_SEMS
        self.dma_last = [None] * N_DMA_SEMS
        self.seg = 0
        self.seg_dma_uses = []
        self.pool_dmas = []
        self.pin = set()
        self.unpin_names = {'scalar_tensor_tensor', 'tensor_tensor'}
        self._stack = []

    def sb(self, name, shape, dtype, nslots=1):
        cm = self.nc.sbuf_tensor(name, list(shape), dtype)
        h = cm.__enter__()
        self._stack.append(cm)
        return T(h, name, nslots)

    def ps(self, name, shape, dtype, nslots=1):
        cm = self.nc.psum_tensor(name, list(shape), dtype)
        h = cm.__enter__()
        self._stack.append(cm)
        return T(h, name, nslots)

    @staticmethod
    def _add_pred(ins, p, kind):
        if p is None or p is ins:
            return
        if p.is_dma or ins.is_dma or p.eng != ins.eng:
            needs = True
        else:
            needs = (ins.eng != "pe")
        ins.preds[p] = ins.preds.get(p, False) or needs

    def op(self, eng, fn, reads=(), writes=(), dma=False):
        ins = Ins(eng, len(self.all), fn)
        ins.seg = self.seg
        ins.is_dma = dma
        ins.pin = (eng in self.pin) and not (fn is not None and fn[0] in self.unpin_names)
        self.all.append(ins)
        ins.cost, ins.lat = est_cost(eng, fn, dma)
        if eng == "act" and fn is not None and fn[0] == "activation":
            ins.tset = _act_group(fn[2].get("func"))
        if dma:
            half = N_DMA_SEMS // 2
            if eng == "pool":
                k = half + self.dma_rr_pool
                self.dma_rr_pool = (self.dma_rr_pool + 1) % half
            else:
                k = self.dma_rr
                self.dma_rr = (k + 1) % half
            self._add_pred(ins, self.dma_last[k], "SEM")
            self.dma_last[k] = ins
            self.dma_uses[k] += 1
            ins.dma_sem = k
            ins.dma_use = self.dma_uses[k]
            if eng == "pool":
                self.pool_dmas.append(ins)
                if len(self.pool_dmas) > 4:
                    self._add_pred(ins, self.pool_dmas[-5], "SEM")
        for r in reads:
            self._add_pred(ins, r.w, "RAW")
        for r in writes:
            self._add_pred(ins, r.w, "WAW")
            for p in r.rds:
                self._add_pred(ins, p, "WAR")
        for r in reads:
            r.rds.append(ins)
        for r in writes:
            r.w = ins
            r.rds = []
        return ins

    def barrier(self):
        self.seg_dma_uses.append(list(self.dma_uses))
        self.seg += 1

    def schedule(self):
        nseg = self.seg + 1
        final = {e: [] for e in ENGS}
        eng_free = {e: 0.0 for e in ENGS}
        cur_set = None
        bar_marks = []
        for sg in range(nseg):
            pending = {e: [i for i in self.all if i.seg == sg and i.eng == e] for e in ENGS}
            remaining = sum(len(v) for v in pending.values())
            while remaining:
                best = None
                best_t = None
                for e in ENGS:
                    lst = pending[e]
                    ef = eng_free[e]
                    lim = min(len(lst), self.WIN_ENG[e] or self.WINDOW)
                    if lim and lst[0].pin:
                        lim = 1
                    for wi in range(lim):
                        c = lst[wi]
                        if wi > 0 and c.pin:
                            break
                        if c.rt is None:
                            rt = 0.0
                            ok = True
                            for p, needs in c.preds.items():
                                if not p.sched:
                                    ok = False
                                    break
                                d = p.done + (self.SEM_LAT if needs else 0.0)
                                if d > rt:
                                    rt = d
                            if not ok:
                                continue
                            c.rt = rt
                        t = c.rt if c.rt > ef else ef
                        if e == "act" and c.tset is not None and c.tset != cur_set:
                            t += 1300.0
                        if best is None or t < best_t or (t == best_t and c.rec < best[1].rec):
                            best = (e, c, wi)
                            best_t = t
                        if t <= ef:
                            break
                assert best is not None, "scheduler deadlock"
                e, c, wi = best
                pending[e].pop(wi)
                remaining -= 1
                if e == "act" and c.tset is not None:
                    cur_set = c.tset
                end = best_t + c.cost
                eng_free[e] = end
                c.done = end + c.lat
                c.sched = True
                c.pos = len(final[e])
                final[e].append(c)
            if sg < nseg - 1:
                tmax = max(eng_free.values())
                dmax = max([i.done for i in self.all if i.seg == sg and i.is_dma] + [0.0])
                tmax = max(tmax, dmax)
                marks = {}
                for e in ENGS:
                    b = Ins(e, -1, None)
                    b.sched = True
                    b.pos = len(final[e])
                    final[e].append(b)
                    marks[e] = b
                    eng_free[e] = tmax
                bar_marks.append(marks)
        self.final = final
        self.bar_marks = bar_marks
        self.est_total = max(eng_free.values())

    def emit(self):
        nc = self.nc
        self.schedule()
        final = self.final
        for bi, marks in enumerate(self.bar_marks):
            for e, b in marks.items():
                for e2 in ENGS:
                    if e2 == e:
                        continue
                    pos2 = marks[e2].pos
                    for j in range(pos2 - 1, -1, -1):
                        p = final[e2][j]
                        if p.fn is not None and not p.is_dma:
                            b.preds[p] = True
                            break
                b.waits_dma = self.seg_dma_uses[bi]
        for e in ENGS:
            seen = {}
            for ins in final[e]:
                for p, needs in ins.preds.items():
                    if not needs:
                        assert p.eng == e and p.pos < ins.pos, "ordering edge violated"
                        continue
                    if p.is_dma:
                        key = f"d{p.dma_sem}"
                        v = p.dma_use
                    else:
                        key = p.eng
                        v = p.pos
                        if p.eng == e:
                            assert p.pos < ins.pos
                    if seen.get(key, -1) >= v:
                        continue
                    seen[key] = v
                    ins.waits.append((key, p))
                    if not p.is_dma:
                        p.needs_inc = True
                wd = getattr(ins, "waits_dma", None) if ins.fn is None else None
                if wd is not None:
                    for k, u in enumerate(wd):
                        if u > 0 and seen.get(f"d{k}", -1) < u:
                            seen[f"d{k}"] = u
                            ins.waits.append((f"d{k}", u))
        sem_cms = []
        sems = {}
        for e in list(ENGS) + [f"d{k}" for k in range(N_DMA_SEMS)]:
            cm = nc.semaphore(f"s_{e}")
            sems[e] = cm.__enter__()
            sem_cms.append(cm)
        for e in ENGS:
            c = 0
            for ins in final[e]:
                if ins.needs_inc:
                    c += 1
                    ins.count = c

        def run(eng_name, eng):
            for ins in final[eng_name]:
                for (key, p) in ins.waits:
                    if isinstance(p, int):
                        eng.wait_ge(sems[key], 16 * p)
                    elif p.is_dma:
                        eng.wait_ge(sems[key], 16 * p.dma_use)
                    else:
                        eng.wait_ge(sems[key], p.count)
                if ins.fn is None:
                    continue
                name, args, kw = ins.fn
                bi = getattr(eng, name)(*args, **kw)
                if ins.dma_sem is not None:
                    bi.then_inc(sems[f"d{ins.dma_sem}"], 16)
                elif ins.needs_inc:
                    bi.then_inc(sems[eng_name], 1)

        with nc.Block() as block:
            @block.tensor
            def _(eng):
                run("pe", eng)

            @block.scalar
            def _(eng):
                run("act", eng)

            @block.vector
            def _(eng):
                run("dve", eng)

            @block.gpsimd
            def _(eng):
                run("pool", eng)

            @block.sync
            def _(eng):
                run("sp", eng)
        for cm in reversed(sem_cms):
            cm.__exit__(None, None, None)

    def close(self):
        for cm in reversed(self._stack):
            cm.__exit__(None, None, None)
        self._stack = []


def OP(name, *args, **kw):
    return (name, args, kw)


class TV:
    def __init__(self, parent, slot, ap):
        self.parent = parent
        self.slot = slot
        self.ap = ap
        self.is_view = True

    def __getitem__(self, k):
        return self.ap[k]

    def r(self):
        return self.parent.r(self.slot)


class Pool:
    def __init__(self, items):
        self.free_ = deque(items)
        self.n = len(items)
        self.extra = []

    def get(self, halo=False):
        assert self.free_, "scratch pool exhausted"
        if not halo:
            return self.free_.popleft()
        for i, it in enumerate(self.free_):
            if not getattr(it, "is_view", False):
                del self.free_[i]
                return it
        raise AssertionError("no halo-capable scratch tile free")

    def free(self, *items):
        for it in items:
            self.free_.append(it)

    def enable_extra(self, views):
        self.extra = list(views)
        for v in views:
            self.free_.appendleft(v)

    def disable_extra(self):
        for v in self.extra:
            assert v in self.free_, "extra scratch view still in use"
            self.free_.remove(v)
        self.extra = []


def _chunked(v, nchunk):
    return np.ascontiguousarray(np.asarray(v, np.float32).reshape(nchunk, 128).T)


class PVLayout:
    def __init__(self):
        self.idx = {}
        self.n = 0

    def add(self, name, k):
        self.idx[name] = (self.n, k)
        self.n += k


def pv_layout():
    pl = PVLayout()
    for l in range(DEPTH):
        for name, k in (("gmix", 8), ("gffn", 8), ("lcw", 16), ("lcb", 4), ("lba", 4), ("lbx", 4),
                        ("llam", 4), ("lnorm", 4), ("s5d", 4), ("s5bg", 4), ("s5n", 4), ("rnorm", 4),
                        ("fcw", 144), ("fcb", 48), ("s5lr", 32), ("s5li", 32), ("s5ldt", 32)):
            pl.add(f"{name}{l}", k)
    for name, k in (("gfin", 8), ("invf", 1), ("sgn", 1), ("nsgn", 1), ("vdec", 4), ("rowmask", 8)):
        pl.add(name, k)
    return pl


PVL = pv_layout()
C_ID = 0
C_IOTA = 128
C_MASK = C_IOTA + 512
C_QDEC = C_MASK + 128
NCST = C_QDEC + 512


def host_consts():
    cst = np.zeros((128, NCST), np.float32)
    cst[:, C_ID:C_ID + 128] = np.eye(128, dtype=np.float32)
    cst[:, C_IOTA:C_IOTA + 512] = np.arange(512, dtype=np.float32)[None, :]
    m = np.arange(128)[:, None]
    c = np.arange(128)[None, :]
    cst[:, C_MASK:C_MASK + 128] = np.where(c >= m, np.float32(128.0 ** -0.5), np.float32(0.0))
    for h in range(4):
        lg = np.log1p(-np.float32(2.0) ** np.float32(-5.0 - h)).astype(np.float32)
        cst[:, C_QDEC + h * 128:C_QDEC + (h + 1) * 128] = np.exp(lg * (np.arange(128, dtype=np.float32) + 1.0))[None, :]
    return cst


def host_pv(inp):
    pv = np.zeros((128, PVL.n), np.float32)

    def put(name, arr):
        o, k = PVL.idx[name]
        assert arr.shape == (128, k), (name, arr.shape, k)
        pv[:, o:o + k] = arr

    for l in range(DEPTH):
        put(f"gmix{l}", _chunked(inp["norm_mix"][l], 8))
        put(f"gffn{l}", _chunked(inp["norm_ffn"][l], 8))
        cw = np.asarray(inp["lru_conv_w"][l], np.float32)
        put(f"lcw{l}", np.ascontiguousarray(cw.reshape(4, 4, 128).transpose(2, 1, 0).reshape(128, 16)))
        put(f"lcb{l}", _chunked(inp["lru_conv_b"][l], 4))
        put(f"lba{l}", _chunked(np.asarray(inp["lru_ba"][l]).reshape(512), 4))
        put(f"lbx{l}", _chunked(np.asarray(inp["lru_bx"][l]).reshape(512), 4))
        put(f"llam{l}", _chunked(inp["lru_lambda"][l], 4))
        put(f"lnorm{l}", _chunked(inp["lru_norm"][l], 4))
        put(f"s5d{l}", _chunked(inp["s5_d"][l], 4))
        put(f"s5bg{l}", _chunked(inp["s5_b_glu"][l], 4))
        put(f"s5n{l}", _chunked(inp["s5_norm"][l], 4))
        put(f"rnorm{l}", _chunked(inp["ret_norm"][l], 4))
        fw_ = np.asarray(inp["ffn_conv_w"][l], np.float32)
        put(f"fcw{l}", np.ascontiguousarray(fw_.reshape(3, 48, 128).transpose(2, 1, 0).reshape(128, 144)))
        put(f"fcb{l}", _chunked(inp["ffn_conv_b"][l], 48))
        lr = np.asarray(inp["s5_lambda_re"][l], np.float32).T
        li = np.asarray(inp["s5_lambda_im"][l], np.float32).T
        put(f"s5lr{l}", np.concatenate([lr, lr], 0))
        put(f"s5li{l}", np.concatenate([li, li], 0))
        put(f"s5ldt{l}", np.broadcast_to(np.asarray(inp["s5_log_dt"][l], np.float32)[None, :], (128, 32)).copy())
    put("gfin", _chunked(inp["norm_final"], 8))
    half = 64
    inv = (np.float32(10000.0) ** (-np.arange(half, dtype=np.float32) * np.float32(2.0) / np.float32(128.0))).astype(np.float32)
    put("invf", np.concatenate([inv, inv])[:, None])
    sgn = np.concatenate([-np.ones(64, np.float32), np.ones(64, np.float32)])[:, None]
    put("sgn", sgn)
    put("nsgn", -sgn)
    vd = np.zeros((128, 4), np.float32)
    for h in range(4):
        lg = np.log1p(-np.float32(2.0) ** np.float32(-5.0 - h)).astype(np.float32)
        vd[:, h] = np.exp(-lg * (np.arange(128, dtype=np.float32) + 1.0))
    put("vdec", vd)
    rm = np.zeros((128, 8), np.float32)
    for g in range(8):
        rm[16 * g:16 * g + 16, g] = 1.0
    put("rowmask", rm)
    return pv


def host_struct(inp):
    wbd = np.zeros((DEPTH, 128, 2, 4, 128), np.float32)
    for l in range(DEPTH):
        for which, nm in enumerate(("lru_wa", "lru_wx")):
            w = np.asarray(inp[nm][l], np.float32)
            for c in range(4):
                for hb in range(2):
                    wbd[l, hb * 64:(hb + 1) * 64, which, c, hb * 64:(hb + 1) * 64] = w[2 * c + hb]
    x1 = np.zeros((DEPTH, 128, 512), np.float32)
    x2 = np.zeros((DEPTH, 128, 512), np.float32)
    c1 = np.zeros((DEPTH, 128, 4, 128), np.float32)
    c2 = np.zeros((DEPTH, 128, 4, 128), np.float32)
    for l in range(DEPTH):
        br = np.asarray(inp["s5_b_re"][l], np.float32).transpose(1, 0, 2).reshape(64, 512)
        bi = np.asarray(inp["s5_b_im"][l], np.float32).transpose(1, 0, 2).reshape(64, 512)
        x1[l] = np.concatenate([br, bi], 0)
        x2[l] = np.concatenate([bi, br], 0)
        cr = np.asarray(inp["s5_c_re"][l], np.float32).reshape(4, 128, 64)
        ci = np.asarray(inp["s5_c_im"][l], np.float32).reshape(4, 128, 64)
        c1[l] = np.concatenate([cr, ci], 2).transpose(1, 0, 2)
        c2[l] = np.concatenate([ci, cr], 2).transpose(1, 0, 2)
    return wbd, x1, x2, c1, c2


def build(debug=(), upto="all", nlayers=DEPTH):
    nc = bass.Bass("TRN2", target_bir_lowering=False)
    fw = FW(nc)
    dram = {}

    def din(name, shape, dtype=F32):
        dram[name] = nc.dram_tensor(name, list(shape), dtype, kind="ExternalInput").ap()
        return dram[name]

    xT_d = din("xT", [D_MODEL, SEQ])
    pos_d = din("pos", [1, SEQ], I32)
    pv_d = din("pv", [128, PVL.n])
    cst_d = din("cst", [128, NCST])
    wbd_d = din("wbd", [DEPTH, 128, 2 * 4 * 128])
    x1_d = din("s5x1", [DEPTH, 128, 512])
    x2_d = din("s5x2", [DEPTH, 128, 512])
    c1_d = din("s5c1", [DEPTH, 128, 512])
    c2_d = din("s5c2", [DEPTH, 128, 512])
    w_in_d = din("w_in", [DEPTH, D_MODEL, IN_WIDTH])
    w_glu_d = din("s5_w_glu", [DEPTH, 512, 512])
    w_out_d = din("w_out", [DEPTH, 1536, D_MODEL])
    w_up_d = din("w_up", [DEPTH, D_MODEL, 2 * D_FF])
    w_down_d = din("w_down", [DEPTH, D_FF, D_MODEL])
    outT_d = nc.dram_tensor("outT", [D_MODEL, SEQ], F32, kind="ExternalOutput").ap()
    dbg_out = {}
    out_dmas = []

    A = fw.op

    hT = fw.sb("hT", [128, 8, SEQ], F32, nslots=32)
    xn = fw.sb("xn", [128, 8, SEQ], BF16, nslots=32)
    rcos = fw.sb("rcos", [128, SEQ], F32, nslots=4)
    rsin = fw.sb("rsin", [128, SEQ], F32, nslots=4)
    pv = fw.sb("pvs", [128, PVL.n], F32)
    cst = fw.sb("csts", [128, NCST], F32)
    ident_b = fw.sb("ident_b", [128, 128], BF16)
    ones_b = fw.sb("ones_b", [128, 128], BF16)
    o128_b = fw.sb("o128_b", [128, 128], BF16)
    NRING = 9
    ring = Pool([fw.sb(f"ring{i}", [128, 520], F32) for i in range(NRING)])
    psum = fw.ps("psum", [128, 8, 512], F32, nslots=8)
    banks = Pool(list(range(8)))
    rot_views = [TV(rcos, i, rcos[:, i * TT:(i + 1) * TT]) for i in range(4)] + [TV(rsin, i, rsin[:, i * TT:(i + 1) * TT]) for i in range(4)]

    def PV(name, c=0, n=1):
        o, k = PVL.idx[name]
        return pv[:, o + c:o + c + n]

    def hs(c, t):
        return hT.r(c * 4 + t)

    def xs(c, t):
        return xn.r(c * 4 + t)

    def tsl(t):
        return slice(t * TT, (t + 1) * TT)

    def bf(tile, n=TT, off=0):
        return tile[:].bitcast(BF16)[:, off:off + n]

    def dbg(name, tile, ap, shape, dtype=F32):
        if name not in debug:
            return
        d = nc.dram_tensor("dbg_" + name, list(shape), dtype, kind="ExternalOutput").ap()
        dbg_out[name] = d
        ins = A("sp", OP("dma_start", out=d, in_=ap), reads=tile.r(), dma=True)
        out_dmas.append(ins)

    A("sp", OP("dma_start", out=pv[:], in_=pv_d), writes=pv.r(), dma=True)
    A("sp", OP("dma_start", out=cst[:], in_=cst_d), writes=cst.r(), dma=True)
    for c in range(8):
        for t in range(NT):
            A("sp", OP("dma_start", out=hT[:, c, tsl(t)], in_=xT_d[c * 128:(c + 1) * 128, tsl(t)]),
              writes=hs(c, t), dma=True)
    A("dve", OP("memset", ones_b[:], 1.0), writes=ones_b.r())
    A("dve", OP("memset", o128_b[:], 1.0 / 128.0), writes=o128_b.r())
    A("act", OP("activation", out=ident_b[:], in_=cst[:, C_ID:C_ID + 128], func=AF.Copy),
      reads=cst.r(), writes=ident_b.r())
    ident_f = cst[:, C_ID:C_ID + 128]
    iota = cst[:, C_IOTA:C_IOTA + 512]
    maskT = cst[:, C_MASK:C_MASK + 128]

    def load_w(dst_tile, dst_ap, src_ap):
        return A("pool", OP("dma_start", out=dst_ap, in_=src_ap), writes=dst_tile.r(), dma=True)

    def rmsnorm_to_xn(gname):
        for t in range(NT):
            b = banks.get()
            for c in range(8):
                sq = ring.get()
                A("act", OP("activation", out=bf(sq), in_=hT[:, c, tsl(t)], func=AF.Square),
                  reads=hs(c, t), writes=sq.r())
                A("pe", OP("matmul", out=psum[:, b, :], lhsT=ones_b[:], rhs=bf(sq), start=(c == 0), stop=(c == 7)),
                  reads=sq.r() + ones_b.r(), writes=psum.r(b))
                ring.free(sq)
            rs = ring.get()
            A("act", OP("activation", out=rs[:, 0:TT], in_=psum[:, b, :], func=AF.Ln, scale=1.0 / D_MODEL, bias=NORM_EPS), reads=psum.r(b), writes=rs.r())
            banks.free(b)
            A("act", OP("activation", out=rs[:, 0:TT], in_=rs[:, 0:TT], func=AF.Exp, scale=-0.5), reads=rs.r(), writes=rs.r())
            for c in range(8):
                A("dve", OP("scalar_tensor_tensor", out=xn[:, c, tsl(t)], in0=hT[:, c, tsl(t)], scalar=PV(gname, c),
                                                                      in1=rs[:, 0:TT], op0=ALU.mult, op1=ALU.mult),
                  reads=hs(c, t) + rs.r() + pv.r(), writes=xs(c, t))
            ring.free(rs)

    def proj(b, wtile, w_ap_fn, t):
        for k in range(8):
            A("pe", OP("matmul", out=psum[:, b, :], lhsT=w_ap_fn(k), rhs=xn[:, k, tsl(t)], start=(k == 0), stop=(k == 7)),
              reads=wtile.r() + xs(k, t), writes=psum.r(b))

    def group_post(l, mixed, gname, grp, wbufs_pool, eps_div):
        wo = wbufs_pool.get()
        wo_v = wo[:].rearrange("p (k n) -> p k n", k=4)
        load_w(wo, wo_v, w_out_d[l, grp * 512:(grp + 1) * 512, :].rearrange("(k p) n -> p k n", p=128))
        for t in range(NT):
            if gname is not None:
                b = banks.get()
                for c in range(4):
                    sq = ring.get()
                    A("act", OP("activation", out=bf(sq), in_=mixed[:, c, tsl(t)], func=AF.Square),
                      reads=mixed.r(c * 4 + t), writes=sq.r())
                    A("pe", OP("matmul", out=psum[:, b, :], lhsT=ones_b[:], rhs=bf(sq), start=(c == 0), stop=(c == 3)),
                      reads=sq.r() + ones_b.r(), writes=psum.r(b))
                    ring.free(sq)
                rs = ring.get()
                A("act", OP("activation", out=rs[:, 0:TT], in_=psum[:, b, :], func=AF.Ln, scale=1.0 / 512.0, bias=NORM_EPS), reads=psum.r(b), writes=rs.r())
                banks.free(b)
                A("act", OP("activation", out=rs[:, 0:TT], in_=rs[:, 0:TT], func=AF.Exp, scale=-0.5), reads=rs.r(), writes=rs.r())
                for c in range(4):
                    A("dve", OP("scalar_tensor_tensor", out=mixed[:, c, tsl(t)], in0=mixed[:, c, tsl(t)],
                                                                          scalar=PV(gname + str(l), c), in1=rs[:, 0:TT],
                                                                          op0=ALU.mult, op1=ALU.mult),
                      reads=mixed.r(c * 4 + t) + rs.r() + pv.r(), writes=mixed.r(c * 4 + t))
                ring.free(rs)
            for dc in range(8):
                b = banks.get()
                for k in range(4):
                    A("pe", OP("matmul", out=psum[:, b, :], lhsT=wo_v[:, k, dc * 128:(dc + 1) * 128],
                                                           rhs=mixed[:, k, tsl(t)], start=(k == 0), stop=(k == 3)),
                      reads=wo.r() + mixed.r(k * 4 + t), writes=psum.r(b))
                A("dve", OP("tensor_tensor", out=hT[:, dc, tsl(t)], in0=hT[:, dc, tsl(t)], in1=psum[:, b, :], op=ALU.add),
                  reads=hs(dc, t) + psum.r(b), writes=hs(dc, t))
                banks.free(b)
        wbufs_pool.free(wo)

    def build_rot():
        for t in range(NT):
            pi_ = ring.get(); pf = ring.get(); k_ = ring.get()
            A("sp", OP("dma_start", out=pi_[:].bitcast(I32)[:, 0:TT],
                                                          in_=pos_d[0:1, tsl(t)].to_broadcast([128, TT])),
              writes=pi_.r(), dma=True)
            A("dve", OP("tensor_copy", out=pf[:, 0:TT], in_=pi_[:].bitcast(I32)[:, 0:TT]),
              reads=pi_.r(), writes=pf.r())
            A("dve", OP("tensor_scalar", out=pf[:, 0:TT], in0=pf[:, 0:TT], scalar1=PV("invf"), scalar2=None,
                                                      op0=ALU.mult), reads=pf.r() + pv.r(), writes=pf.r())
            A("dve", OP("tensor_scalar", out=k_[:, 0:TT], in0=pf[:, 0:TT], scalar1=1.0 / (2.0 * math.pi),
                                                             scalar2=MAGIC, op0=ALU.mult, op1=ALU.add),
              reads=pf.r(), writes=k_.r())
            A("dve", OP("tensor_scalar", out=k_[:, 0:TT], in0=k_[:, 0:TT], scalar1=MAGIC, scalar2=None,
                                                      op0=ALU.subtract), reads=k_.r(), writes=k_.r())
            C1 = 6.28125
            C2 = 2.0 * math.pi - 6.28125
            A("dve", OP("scalar_tensor_tensor", out=pf[:, 0:TT], in0=k_[:, 0:TT], scalar=-C1, in1=pf[:, 0:TT],
                                                                    op0=ALU.mult, op1=ALU.add), reads=pf.r() + k_.r(), writes=pf.r())
            A("dve", OP("scalar_tensor_tensor", out=pf[:, 0:TT], in0=k_[:, 0:TT], scalar=-C2, in1=pf[:, 0:TT],
                                                                    op0=ALU.mult, op1=ALU.add), reads=pf.r() + k_.r(), writes=pf.r())
            A("dve", OP("tensor_scalar", out=pf[:, 0:TT], in0=pf[:, 0:TT], scalar1=3.1415925, scalar2=-3.1415925,
                                                      op0=ALU.min, op1=ALU.max), reads=pf.r(), writes=pf.r())
            A("act", OP("activation", out=rsin[:, tsl(t)], in_=pf[:, 0:TT], func=AF.Sin, scale=PV("sgn")),
              reads=pf.r() + pv.r(), writes=rsin.r(t))
            A("act", OP("activation", out=k_[:, 0:TT], in_=pf[:, 0:TT], func=AF.Abs),
              reads=pf.r(), writes=k_.r())
            A("act", OP("activation", out=rcos[:, tsl(t)], in_=k_[:, 0:TT], func=AF.Sin, scale=-1.0,
                                                        bias=math.pi / 2.0 - 1e-6), reads=k_.r(), writes=rcos.r(t))
            ring.free(pi_, pf, k_)


    phase = []
    for l in range(nlayers):

        def psb(name, shape, dtype, nslots=1):
            cm = nc.sbuf_tensor(f"{name}_{l}", list(shape), dtype)
            h = cm.__enter__()
            phase.append(cm)
            return T(h, name, nslots)

        mixed = psb("mixed", [128, 4, SEQ], BF16, nslots=16)
        aux = psb("aux", [128, 8192], BF16, nslots=16)
        wbufs = Pool([psb(f"wbuf{i}", [128, 4096], BF16) for i in range(2)])
        wbd = psb("wbd", [128, 2, 4, 128], BF16)
        lpar = psb("lpar", [128, 16], F32)
        lcar = psb("lcar", [128, 4, 4], F32)
        s5p = psb("s5p", [128, 32, 8], F32)
        s5off = psb("s5off", [128, 32, 4], F32)
        bst1 = psb("bst1", [128, 512], BF16)
        bst2 = psb("bst2", [128, 512], BF16)
        w5pool = Pool([psb(f"w5_{i}", [128, 8, 128], BF16) for i in range(2)])
        lhs_b = psb("lhs_b", [128, 2, 8, 128], BF16)
        lhs_c = psb("lhs_c", [128, 2, 8, 128], BF16)
        zcar = psb("zcar", [128, 32], F32)
        ust = psb("ust", [128, 128], F32)
        pbs = [psb(f"pb{i}", [128, 128], BF16) for i in range(2)]

        load_w(wbd, wbd[:].rearrange("p a c n -> p (a c n)"), wbd_d[l])
        A("act", OP("activation", out=lpar[:, 0:4], in_=PV(f"llam{l}", 0, 4), func=AF.Exp, scale=-1.0), reads=pv.r(), writes=lpar.r())
        A("act", OP("activation", out=lpar[:, 0:4], in_=lpar[:, 0:4], func=AF.Ln, bias=1.0), reads=lpar.r(), writes=lpar.r())
        A("dve", OP("tensor_scalar", out=lpar[:, 4:8], in0=lpar[:, 0:4], scalar1=-4.0, scalar2=None, op0=ALU.mult), reads=lpar.r(), writes=lpar.r())
        A("dve", OP("tensor_scalar", out=lpar[:, 0:4], in0=lpar[:, 0:4], scalar1=-8.0, scalar2=None, op0=ALU.mult), reads=lpar.r(), writes=lpar.r())
        A("dve", OP("tensor_scalar", out=lpar[:, 8:12], in0=PV(f"lba{l}", 0, 4), scalar1=0.5, scalar2=None, op0=ALU.mult), reads=pv.r(), writes=lpar.r())
        A("dve", OP("tensor_scalar", out=lpar[:, 12:16], in0=PV(f"lbx{l}", 0, 4), scalar1=0.5, scalar2=None, op0=ALU.mult), reads=pv.r(), writes=lpar.r())
        A("dve", OP("memset", lcar[:], 0.0), writes=lcar.r())
        A("dve", OP("memset", zcar[:], 0.0), writes=zcar.r())

        S = lambda j: s5p[:, :, j]
        LR = PV(f"s5lr{l}", 0, 32)
        LI = PV(f"s5li{l}", 0, 32)
        R_, W_ = s5p.r(), s5p.r()
        A("act", OP("activation", out=S(6), in_=PV(f"s5ldt{l}", 0, 32), func=AF.Exp), reads=pv.r(), writes=W_)
        A("dve", OP("tensor_tensor", out=S(0), in0=LR, in1=S(6), op=ALU.mult), reads=R_ + pv.r(), writes=W_)
        A("act", OP("activation", out=S(0), in_=S(0), func=AF.Exp), reads=R_, writes=W_)
        A("dve", OP("tensor_tensor", out=S(7), in0=LI, in1=S(6), op=ALU.mult), reads=R_ + pv.r(), writes=W_)
        A("dve", OP("tensor_scalar", out=S(6), in0=S(7), scalar1=1.0 / (2.0 * math.pi), scalar2=MAGIC, op0=ALU.mult, op1=ALU.add), reads=R_, writes=W_)
        A("dve", OP("tensor_scalar", out=S(6), in0=S(6), scalar1=MAGIC, scalar2=None, op0=ALU.subtract), reads=R_, writes=W_)
        A("dve", OP("scalar_tensor_tensor", out=S(1), in0=S(7), scalar=1.0 / (2.0 * math.pi), in1=S(6), op0=ALU.mult, op1=ALU.subtract), reads=R_, writes=W_)
        A("act", OP("activation", out=S(7), in_=S(1), func=AF.Sin, scale=TWO_PI_S), reads=R_, writes=W_)
        A("act", OP("activation", out=S(6), in_=S(1), func=AF.Abs), reads=R_, writes=W_)
        A("act", OP("activation", out=S(6), in_=S(6), func=AF.Sin, scale=-TWO_PI_S, bias=math.pi / 2.0 - 1e-6), reads=R_, writes=W_)
        A("dve", OP("tensor_tensor", out=S(6), in0=S(6), in1=S(0), op=ALU.mult), reads=R_, writes=W_)
        A("dve", OP("tensor_scalar", out=S(6), in0=S(6), scalar1=-1.0, scalar2=None, op0=ALU.add), reads=R_, writes=W_)
        A("dve", OP("tensor_tensor", out=S(7), in0=S(7), in1=S(0), op=ALU.mult), reads=R_, writes=W_)
        A("dve", OP("tensor_tensor", out=S(5), in0=LR, in1=LR, op=ALU.mult), reads=R_ + pv.r(), writes=W_)
        A("dve", OP("tensor_tensor", out=S(4), in0=LI, in1=LI, op=ALU.mult), reads=R_ + pv.r(), writes=W_)
        A("dve", OP("tensor_tensor", out=S(5), in0=S(5), in1=S(4), op=ALU.add), reads=R_, writes=W_)
        A("dve", OP("reciprocal", out=S(5), in_=S(5)), reads=R_, writes=W_)
        A("dve", OP("tensor_tensor", out=S(2), in0=S(6), in1=LR, op=ALU.mult), reads=R_ + pv.r(), writes=W_)
        A("dve", OP("tensor_tensor", out=S(4), in0=S(7), in1=LI, op=ALU.mult), reads=R_ + pv.r(), writes=W_)
        A("dve", OP("tensor_tensor", out=S(2), in0=S(2), in1=S(4), op=ALU.add), reads=R_, writes=W_)
        A("dve", OP("tensor_tensor", out=S(2), in0=S(2), in1=S(5), op=ALU.mult), reads=R_, writes=W_)
        A("dve", OP("tensor_tensor", out=S(3), in0=S(7), in1=LR, op=ALU.mult), reads=R_ + pv.r(), writes=W_)
        A("dve", OP("tensor_tensor", out=S(4), in0=S(6), in1=LI, op=ALU.mult), reads=R_ + pv.r(), writes=W_)
        A("dve", OP("tensor_tensor", out=S(3), in0=S(3), in1=S(4), op=ALU.subtract), reads=R_, writes=W_)
        A("dve", OP("tensor_tensor", out=S(3), in0=S(3), in1=S(5), op=ALU.mult), reads=R_, writes=W_)
        A("dve", OP("tensor_copy", out=S(5), in_=S(3)), reads=R_, writes=W_)
        A("dve", OP("tensor_scalar", out=S(3), in0=S(3), scalar1=PV("sgn"), scalar2=None, op0=ALU.mult), reads=R_ + pv.r(), writes=W_)
        A("dve", OP("tensor_scalar", out=S(4), in0=S(2), scalar1=PV("nsgn"), scalar2=None, op0=ALU.mult), reads=R_ + pv.r(), writes=W_)
        for t in range(NT):
            A("dve", OP("tensor_scalar", out=s5off[:, :, t], in0=S(1), scalar1=float(TT * t), scalar2=MAGIC, op0=ALU.mult, op1=ALU.add),
              reads=R_, writes=s5off.r())
            A("dve", OP("tensor_scalar", out=s5off[:, :, t], in0=s5off[:, :, t], scalar1=MAGIC, scalar2=None, op0=ALU.subtract),
              reads=s5off.r(), writes=s5off.r())
            A("dve", OP("scalar_tensor_tensor", out=s5off[:, :, t], in0=S(1), scalar=float(TT * t), in1=s5off[:, :, t],
                                                           op0=ALU.mult, op1=ALU.subtract), reads=R_ + s5off.r(), writes=s5off.r())
        cn1 = ring.get(); cn2 = ring.get()
        A("sp", OP("dma_start", out=cn1[:, 0:512], in_=x1_d[l]), writes=cn1.r(), dma=True)
        A("sp", OP("dma_start", out=cn2[:, 0:512], in_=x2_d[l]), writes=cn2.r(), dma=True)
        v3 = lambda tl: tl[:, 0:512].rearrange("p (g c) -> p g c", c=16)
        bc = lambda j: s5p[:, :, j:j + 1].to_broadcast([128, 32, 16])
        t1 = ring.get(); t2 = ring.get()
        t1v = t1[:, 0:512].rearrange("p (g c) -> p g c", c=16)
        t2v = t2[:, 0:512].rearrange("p (g c) -> p g c", c=16)
        A("dve", OP("tensor_tensor", out=t1v, in0=v3(cn1), in1=bc(2), op=ALU.mult), reads=cn1.r() + R_, writes=t1.r())
        A("dve", OP("tensor_tensor", out=t2v, in0=v3(cn2), in1=bc(3), op=ALU.mult), reads=cn2.r() + R_, writes=t2.r())
        A("dve", OP("tensor_tensor", out=bst1[:], in0=t1[:, 0:512], in1=t2[:, 0:512], op=ALU.add), reads=t1.r() + t2.r(), writes=bst1.r())
        A("dve", OP("tensor_tensor", out=t1v, in0=v3(cn2), in1=bc(4), op=ALU.mult), reads=cn2.r() + R_, writes=t1.r())
        A("dve", OP("tensor_tensor", out=t2v, in0=v3(cn1), in1=bc(5), op=ALU.mult), reads=cn1.r() + R_, writes=t2.r())
        A("dve", OP("tensor_tensor", out=bst2[:], in0=t1[:, 0:512], in1=t2[:, 0:512], op=ALU.add), reads=t1.r() + t2.r(), writes=bst2.r())
        ring.free(t1, t2, cn1, cn2)
        A("pool", OP("memset", lhs_c[:].rearrange("p a g n -> p (a g n)"), 0.0), writes=lhs_c.r())
        dbg(f"bst1_{l}", bst1, bst1[:], [128, 512], BF16)
        dbg(f"s5p_{l}", s5p, s5p[:].rearrange("p g j -> p (g j)"), [128, 256])

        rmsnorm_to_xn(f"gmix{l}")
        dbg(f"xn{l}", xn, xn[:].rearrange("p c t -> p (c t)"), [128, 8 * SEQ], BF16)

        zbf = aux[:].rearrange("p (c t) -> p c t", c=4)
        def lru_chunk(c):
            wb = wbufs.get()
            wv_ = wb[:].rearrange("p (k n) -> p k n", k=8)
            load_w(wb, wv_[:, :, 0:128], w_in_d[l, :, c * 128:(c + 1) * 128].rearrange("(k p) n -> p k n", p=128))
            load_w(wb, wv_[:, :, 128:256], w_in_d[l, :, 512 + c * 128:512 + (c + 1) * 128].rearrange("(k p) n -> p k n", p=128))
            return wb, wv_

        def lru_unit(c, t, wb, wv_):
            bx = banks.get(); bg = banks.get()
            proj(bx, wb, lambda k: wv_[:, k, 0:128], t)
            proj(bg, wb, lambda k: wv_[:, k, 128:256], t)
            xp = ring.get(halo=True); gy = ring.get(); xc = ring.get(); xcb = ring.get()
            A("act", OP("activation", out=xp[:, 0:3], in_=lcar[:, c, 0:3], func=AF.Copy), reads=lcar.r(), writes=xp.r())
            A("act", OP("activation", out=xp[:, 3:3 + TT], in_=psum[:, bx, :], func=AF.Copy), reads=psum.r(bx), writes=xp.r())
            A("act", OP("activation", out=lcar[:, c, 0:3], in_=xp[:, TT:TT + 3], func=AF.Copy), reads=xp.r(), writes=lcar.r())
            A("act", OP("activation", out=gy[:, 0:TT], in_=psum[:, bg, :], func=AF.Gelu_apprx_tanh), reads=psum.r(bg), writes=gy.r())
            banks.free(bx, bg)
            A("act", OP("activation", out=xc[:, 0:TT], in_=xp[:, 3:3 + TT], func=AF.Identity,
                                                          scale=PV(f"lcw{l}", c * 4 + 3), bias=PV(f"lcb{l}", c)),
              reads=xp.r() + pv.r(), writes=xc.r())
            for k in range(3):
                A("dve", OP("scalar_tensor_tensor", out=xc[:, 0:TT], in0=xp[:, k:k + TT], scalar=PV(f"lcw{l}", c * 4 + k),
                                                                            in1=xc[:, 0:TT], op0=ALU.mult, op1=ALU.add),
                  reads=xp.r() + xc.r() + pv.r(), writes=xc.r())
            A("act", OP("activation", out=bf(xcb), in_=xc[:, 0:TT], func=AF.Copy), reads=xc.r(), writes=xcb.r())
            ring.free(xp)
            br_ = banks.get(); bi_ = banks.get()
            A("pe", OP("matmul", out=psum[:, br_, :], lhsT=wbd[:, 0, c, :], rhs=bf(xcb), start=True, stop=True),
              reads=wbd.r() + xcb.r(), writes=psum.r(br_))
            A("pe", OP("matmul", out=psum[:, bi_, :], lhsT=wbd[:, 1, c, :], rhs=bf(xcb), start=True, stop=True),
              reads=wbd.r() + xcb.r(), writes=psum.r(bi_))
            rr = ring.get(); ii = ring.get(); a2 = ring.get()
            A("act", OP("activation", out=rr[:, 0:TT], in_=psum[:, br_, :], func=AF.Tanh, scale=0.5, bias=lpar[:, 8 + c:9 + c]),
              reads=psum.r(br_) + lpar.r(), writes=rr.r())
            A("act", OP("activation", out=ii[:, 0:TT], in_=psum[:, bi_, :], func=AF.Tanh, scale=0.5, bias=lpar[:, 12 + c:13 + c]),
              reads=psum.r(bi_) + lpar.r(), writes=ii.r())
            banks.free(br_, bi_)
            ring.free(xcb)
            A("act", OP("activation", out=a2[:, 0:TT], in_=rr[:, 0:TT], func=AF.Exp, scale=lpar[:, c:c + 1], bias=lpar[:, c:c + 1]),
              reads=rr.r() + lpar.r(), writes=a2.r())
            A("act", OP("activation", out=rr[:, 0:TT], in_=rr[:, 0:TT], func=AF.Exp, scale=lpar[:, 4 + c:5 + c], bias=lpar[:, 4 + c:5 + c]),
              reads=rr.r() + lpar.r(), writes=rr.r())
            A("dve", OP("tensor_scalar", out=a2[:, 0:TT], in0=a2[:, 0:TT], scalar1=1.0, scalar2=-1e-30, op0=ALU.subtract, op1=ALU.min),
              reads=a2.r(), writes=a2.r())
            A("act", OP("activation", out=a2[:, 0:TT], in_=a2[:, 0:TT], func=AF.Ln, scale=-1.0), reads=a2.r(), writes=a2.r())
            A("act", OP("activation", out=a2[:, 0:TT], in_=a2[:, 0:TT], func=AF.Exp, scale=0.5), reads=a2.r(), writes=a2.r())
            A("dve", OP("scalar_tensor_tensor", out=ii[:, 0:TT], in0=ii[:, 0:TT], scalar=1.0, in1=xc[:, 0:TT], op0=ALU.add, op1=ALU.mult),
              reads=ii.r() + xc.r(), writes=ii.r())
            A("dve", OP("scalar_tensor_tensor", out=ii[:, 0:TT], in0=ii[:, 0:TT], scalar=0.5, in1=a2[:, 0:TT], op0=ALU.mult, op1=ALU.mult),
              reads=ii.r() + a2.r(), writes=ii.r())
            hh = xc
            A("dve", OP("tensor_tensor_scan", out=hh[:, 0:TT], data0=rr[:, 0:TT], data1=ii[:, 0:TT],
                                                                        initial=lcar[:, c, 3:4], op0=ALU.mult, op1=ALU.add),
              reads=rr.r() + ii.r() + lcar.r(), writes=hh.r())
            A("act", OP("activation", out=lcar[:, c, 3:4], in_=hh[:, TT - 1:TT], func=AF.Copy), reads=hh.r(), writes=lcar.r())
            A("dve", OP("tensor_tensor", out=mixed[:, c, tsl(t)], in0=hh[:, 0:TT], in1=gy[:, 0:TT], op=ALU.mult),
              reads=hh.r() + gy.r(), writes=mixed.r(c * 4 + t))
            ring.free(rr, ii, a2, xc, gy)

        def s5_chunk(c):
            wb = w5pool.get()
            wv_ = wb
            load_w(wb, wv_[:, :, 0:128], w_in_d[l, :, 1024 + c * 128:1024 + (c + 1) * 128].rearrange("(k p) n -> p k n", p=128))
            for which, bst in enumerate((bst1, bst2)):
                b = banks.get()
                tpv = psum[:, b, :].bitcast(BF16)
                A("pe", OP("transpose", out=tpv[:, 0:128], in_=bst[:, c * 128:(c + 1) * 128], identity=ident_b[:]),
                  reads=bst.r() + ident_b.r(), writes=psum.r(b))
                tb = ring.get()
                A("act", OP("activation", out=bf(tb, 128), in_=tpv[:, 0:128], func=AF.Copy), reads=psum.r(b), writes=tb.r())
                banks.free(b)
                for g in range(8):
                    A("dve", OP("tensor_scalar", out=lhs_b[:, which, g, :], in0=bf(tb, 128), scalar1=PV("rowmask", g),
                                                                                scalar2=None, op0=ALU.mult),
                      reads=tb.r() + pv.r(), writes=lhs_b.r())
                ring.free(tb)
            for which, (cd, sname) in enumerate(((c1_d, "nsgn"), (c2_d, None))):
                cn = ring.get()
                A("sp", OP("dma_start", out=cn[:, 0:128], in_=cd[l, :, c * 128:(c + 1) * 128]), writes=cn.r(), dma=True)
                b = banks.get()
                A("pe", OP("transpose", out=psum[:, b, 0:128], in_=cn[:, 0:128], identity=ident_f),
                  reads=cn.r() + cst.r(), writes=psum.r(b))
                ring.free(cn)
                for g in range(8):
                    if sname is not None:
                        A("dve", OP("tensor_scalar", out=lhs_c[:, which, g, g * 16:(g + 1) * 16], in0=psum[:, b, g * 16:(g + 1) * 16],
                                                                                  scalar1=PV("nsgn"), scalar2=None, op0=ALU.mult),
                          reads=psum.r(b) + pv.r(), writes=lhs_c.r())
                    else:
                        A("dve", OP("tensor_scalar", out=lhs_c[:, which, g, g * 16:(g + 1) * 16], in0=psum[:, b, g * 16:(g + 1) * 16],
                                                                                  scalar1=-1.0, scalar2=None, op0=ALU.mult),
                          reads=psum.r(b), writes=lhs_c.r())
                banks.free(b)
            return wb, wv_

        def s5_unit(c, t, wb, wv_):
            bu = banks.get()
            proj(bu, wb, lambda k: wv_[:, k, 0:128], t)
            uf = ring.get(); ub = ring.get()
            A("act", OP("activation", out=uf[:, 0:TT], in_=psum[:, bu, :], func=AF.Copy), reads=psum.r(bu), writes=uf.r())
            A("act", OP("activation", out=bf(ub), in_=psum[:, bu, :], func=AF.Copy), reads=psum.r(bu), writes=ub.r())
            banks.free(bu)
            by = banks.get()
            for g in range(8):
                G = c * 8 + g
                b1 = banks.get(); b2 = banks.get()
                A("pe", OP("matmul", out=psum[:, b1, :], lhsT=lhs_b[:, 0, g, :], rhs=bf(ub), start=True, stop=True),
                  reads=lhs_b.r() + ub.r(), writes=psum.r(b1))
                A("pe", OP("matmul", out=psum[:, b2, :], lhsT=lhs_b[:, 1, g, :], rhs=bf(ub), start=True, stop=True),
                  reads=lhs_b.r() + ub.r(), writes=psum.r(b2))
                uu = ring.get(); kk = ring.get(); tc_ = ring.get(); ts_ = ring.get()
                A("act", OP("activation", out=uu[:, 0:TT], in_=iota, func=AF.Identity, scale=s5p[:, G, 1:2], bias=s5off[:, G, t:t + 1]), reads=cst.r() + s5p.r() + s5off.r(), writes=uu.r())
                A("dve", OP("tensor_scalar", out=kk[:, 0:TT], in0=uu[:, 0:TT], scalar1=MAGIC, scalar2=MAGIC,
                                                                  op0=ALU.add, op1=ALU.subtract), reads=uu.r(), writes=kk.r())
                A("pool", OP("tensor_tensor", out=uu[:, 0:TT], in0=uu[:, 0:TT], in1=kk[:, 0:TT], op=ALU.subtract),
                  reads=uu.r() + kk.r(), writes=uu.r())
                A("act", OP("activation", out=ts_[:, 0:TT], in_=uu[:, 0:TT], func=AF.Sin, scale=TWO_PI_S), reads=uu.r(), writes=ts_.r())
                A("act", OP("activation", out=kk[:, 0:TT], in_=uu[:, 0:TT], func=AF.Abs), reads=uu.r(), writes=kk.r())
                A("act", OP("activation", out=tc_[:, 0:TT], in_=kk[:, 0:TT], func=AF.Sin, scale=-TWO_PI_S, bias=math.pi / 2.0 - 1e-6),
                  reads=kk.r(), writes=tc_.r())
                m1 = uu; m2 = kk
                A("dve", OP("tensor_tensor", out=m1[:, 0:TT], in0=psum[:, b1, :], in1=tc_[:, 0:TT], op=ALU.mult),
                  reads=psum.r(b1) + tc_.r(), writes=m1.r())
                A("dve", OP("tensor_tensor", out=m2[:, 0:TT], in0=psum[:, b2, :], in1=ts_[:, 0:TT], op=ALU.mult),
                  reads=psum.r(b2) + ts_.r(), writes=m2.r())
                banks.free(b1, b2)
                A("dve", OP("tensor_tensor", out=m1[:, 0:TT], in0=m1[:, 0:TT], in1=m2[:, 0:TT], op=ALU.add), reads=m1.r() + m2.r(), writes=m1.r())
                zz = m2
                A("dve", OP("tensor_tensor_scan", out=zz[:, 0:TT], data0=s5p[:, G, 0:1].to_broadcast([128, TT]), data1=m1[:, 0:TT],
                                                                          initial=zcar[:, G:G + 1], op0=ALU.mult, op1=ALU.add),
                  reads=m1.r() + s5p.r() + zcar.r(), writes=zz.r())
                A("act", OP("activation", out=zcar[:, G:G + 1], in_=zz[:, TT - 1:TT], func=AF.Copy), reads=zz.r(), writes=zcar.r())
                w12 = m1
                A("pool", OP("tensor_tensor", out=bf(w12, TT, 0), in0=zz[:, 0:TT], in1=tc_[:, 0:TT], op=ALU.mult),
                  reads=zz.r() + tc_.r(), writes=w12.r())
                A("pool", OP("tensor_tensor", out=bf(w12, TT, TT), in0=zz[:, 0:TT], in1=ts_[:, 0:TT], op=ALU.mult),
                  reads=zz.r() + ts_.r(), writes=w12.r())
                A("pe", OP("matmul", out=psum[:, by, :], lhsT=lhs_c[:, 0, g, :], rhs=bf(w12, TT, 0), start=(g == 0), stop=False),
                  reads=lhs_c.r() + w12.r(), writes=psum.r(by))
                A("pe", OP("matmul", out=psum[:, by, :], lhsT=lhs_c[:, 1, g, :], rhs=bf(w12, TT, TT), start=False, stop=(g == 7)),
                  reads=lhs_c.r() + w12.r(), writes=psum.r(by))
                ring.free(uu, kk, tc_, ts_)
            A("dve", OP("scalar_tensor_tensor", out=uf[:, 0:TT], in0=uf[:, 0:TT], scalar=PV(f"s5d{l}", c), in1=psum[:, by, :],
                                                             op0=ALU.mult, op1=ALU.add), reads=uf.r() + psum.r(by) + pv.r(), writes=uf.r())
            banks.free(by)
            A("act", OP("activation", out=zbf[:, c, tsl(t)], in_=uf[:, 0:TT], func=AF.Gelu_apprx_tanh), reads=uf.r(), writes=aux.r(c * 4 + t))
            ring.free(uf, ub)

        ring.enable_extra(rot_views)
        for c in range(4):
            lwb = lru_chunk(c)
            swb = s5_chunk(c)
            for t in range(NT):
                s5_unit(c, t, *swb)
                lru_unit(c, t, *lwb)
            wbufs.free(lwb[0])
            w5pool.free(swb[0])
        dbg(f"ylru_raw{l}", mixed, mixed[:].rearrange("p c t -> p (c t)"), [128, 4 * SEQ], BF16)
        if upto == "lru_raw":
            break
        group_post(l, mixed, "lnorm", 0, wbufs, 512.0)
        dbg(f"ylru{l}", mixed, mixed[:].rearrange("p c t -> p (c t)"), [128, 4 * SEQ], BF16)
        if upto == "lru":
            break

        dbg(f"s5z{l}", aux, aux[:], [128, 4 * SEQ], BF16)
        wg_t = wbufs.get()
        wgl = wg_t[:, 0:2048].rearrange("p (k n) -> p k n", k=4)
        load_w(wg_t, wgl, w_glu_d[l].rearrange("(k p) n -> p k n", p=128))
        for t in range(NT):
            for oc in range(4):
                b = banks.get()
                for k in range(4):
                    A("pe", OP("matmul", out=psum[:, b, :], lhsT=wgl[:, k, oc * 128:(oc + 1) * 128], rhs=zbf[:, k, tsl(t)],
                                                                start=(k == 0), stop=(k == 3)), reads=wg_t.r() + aux.r(k * 4 + t), writes=psum.r(b))
                sg = ring.get()
                A("act", OP("activation", out=sg[:, 0:TT], in_=psum[:, b, :], func=AF.Sigmoid, bias=PV(f"s5bg{l}", oc)),
                  reads=psum.r(b) + pv.r(), writes=sg.r())
                banks.free(b)
                A("dve", OP("tensor_tensor", out=mixed[:, oc, tsl(t)], in0=zbf[:, oc, tsl(t)], in1=sg[:, 0:TT], op=ALU.mult),
                  reads=sg.r() + aux.r(oc * 4 + t), writes=mixed.r(oc * 4 + t))
                ring.free(sg)
        wbufs.free(wg_t)
        group_post(l, mixed, "s5n", 1, wbufs, 512.0)
        dbg(f"ys5{l}", mixed, mixed[:].rearrange("p c t -> p (c t)"), [128, 4 * SEQ], BF16)
        if upto == "s5":
            break

        ring.disable_extra()
        build_rot()
        if l == 0:
            dbg("rcos", rcos, rcos[:], [128, SEQ])
            dbg("rsin", rsin, rsin[:], [128, SEQ])
        fw.pin = set(os.environ.get('KPIN', 'dve').split(',')) - {''}
        vh = aux[:].rearrange("p (n e) -> p n e", n=16)
        wv_t = wbufs.get()
        wvv = wv_t[:].rearrange("p (k n) -> p k n", k=8)
        load_w(wv_t, wvv, w_in_d[l, :, 2560:3072].rearrange("(k p) n -> p k n", p=128))
        for n in range(16):
            b = banks.get()
            t = n // 4
            for k in range(8):
                A("pe", OP("matmul", out=psum[:, b, :], lhsT=xn[:, k, n * 128:(n + 1) * 128], rhs=wvv[:, k, :],
                                                          start=(k == 0), stop=(k == 7)), reads=wv_t.r() + xs(k, t), writes=psum.r(b))
            for hd in range(4):
                A("act", OP("activation", out=vh[:, n, hd * 128:(hd + 1) * 128], in_=psum[:, b, hd * 128:(hd + 1) * 128],
                                                                 func=AF.Identity, scale=PV("vdec", hd)), reads=psum.r(b) + pv.r(), writes=aux.r(n))
            banks.free(b)
        wbufs.free(wv_t)
        wg_t = wbufs.get()
        wgv = wg_t[:].rearrange("p (k n) -> p k n", k=8)
        load_w(wg_t, wgv, w_in_d[l, :, 3072:3584].rearrange("(k p) n -> p k n", p=128))
        for hd in range(4):
            gam = GAMMAS[hd]
            gC = float(np.float32(np.exp(np.float32(np.log1p(-np.float32(2.0) ** np.float32(-5.0 - hd))) * np.float32(128.0))))
            wb = wbufs.get()
            wq = wb[:].rearrange("p (k n) -> p k n", k=8)
            qb = 1536 + hd * 128
            kb = 2048 + hd * 128
            src = lambda c0, c1: w_in_d[l, :, c0:c1].rearrange("(k p) n -> p k n", p=128)
            load_w(wb, wq[:, :, 0:128], src(qb, qb + 128))
            load_w(wb, wq[:, :, 128:192], src(qb + 64, qb + 128))
            load_w(wb, wq[:, :, 192:256], src(qb, qb + 64))
            load_w(wb, wq[:, :, 256:384], src(kb, kb + 128))
            load_w(wb, wq[:, :, 384:448], src(kb + 64, kb + 128))
            load_w(wb, wq[:, :, 448:512], src(kb, kb + 64))
            for t in range(NT):
                rot = []
                for which in range(2):
                    ba = banks.get(); bb = banks.get()
                    proj(ba, wb, lambda k, o=which * 256: wq[:, k, o:o + 128], t)
                    proj(bb, wb, lambda k, o=which * 256 + 128: wq[:, k, o:o + 128], t)
                    t1 = ring.get(); t2 = ring.get(); qr = ring.get()
                    A("dve", OP("tensor_tensor", out=t1[:, 0:TT], in0=psum[:, ba, :], in1=rcos[:, tsl(t)], op=ALU.mult),
                      reads=psum.r(ba) + rcos.r(t), writes=t1.r())
                    A("dve", OP("tensor_tensor", out=t2[:, 0:TT], in0=psum[:, bb, :], in1=rsin[:, tsl(t)], op=ALU.mult),
                      reads=psum.r(bb) + rsin.r(t), writes=t2.r())
                    banks.free(ba, bb)
                    A("pool", OP("tensor_tensor", out=bf(qr), in0=t1[:, 0:TT], in1=t2[:, 0:TT], op=ALU.add),
                      reads=t1.r() + t2.r(), writes=qr.r())
                    ring.free(t1, t2)
                    rot.append(qr)
                qr, kr = rot
                if hd == 0 and t == 0:
                    dbg(f"qr{l}", qr, bf(qr), [128, TT], BF16)
                    dbg(f"kr{l}", kr, bf(kr), [128, TT], BF16)
                bt = banks.get()
                ktp = psum[:, bt, :].bitcast(BF16)
                for j in range(4):
                    A("pe", OP("transpose", out=ktp[:, j * 128:(j + 1) * 128], in_=bf(kr, 128, j * 128), identity=ident_b[:]),
                      reads=kr.r() + ident_b.r(), writes=psum.r(bt))
                ktm = ring.get()
                A("act", OP("activation", out=bf(ktm), in_=ktp[:, 0:TT], func=AF.Copy), reads=psum.r(bt), writes=ktm.r())
                banks.free(bt)
                bs = banks.get()
                for j in range(4):
                    A("pe", OP("matmul", out=psum[:, bs, j * 128:(j + 1) * 128], lhsT=bf(kr, 128, j * 128), rhs=bf(qr, 128, j * 128),
                                                                  start=True, stop=True), reads=kr.r() + qr.r(), writes=psum.r(bs))
                pT = ring.get()
                A("dve", OP("tensor_tensor", out=bf(pT).rearrange("p (j c) -> p j c", j=4), in0=psum[:, bs, :].rearrange("p (j c) -> p j c", j=4),
                                                          in1=maskT.unsqueeze(1).to_broadcast([128, 4, 128]), op=ALU.mult),
                  reads=psum.r(bs) + cst.r(), writes=pT.r())
                banks.free(bs)
                bkv = banks.get()
                for j in range(4):
                    n = t * 4 + j
                    A("pe", OP("matmul", out=psum[:, bkv, j * 128:(j + 1) * 128], lhsT=bf(ktm, 128, j * 128),
                                                                  rhs=vh[:, n, hd * 128:(hd + 1) * 128], start=True, stop=True),
                      reads=ktm.r() + aux.r(n), writes=psum.r(bkv))
                ring.free(ktm)
                bxo = banks.get()
                for j in range(4):
                    n = t * 4 + j
                    pb = pbs[n % 2]
                    if n > 0:
                        A("act", OP("activation", out=pb[:], in_=ust[:], func=AF.Identity, scale=float(gC * (128.0 ** -0.5))),
                          reads=ust.r(), writes=pb.r())
                    A("pe", OP("matmul", out=psum[:, bxo, j * 128:(j + 1) * 128], lhsT=vh[:, n, hd * 128:(hd + 1) * 128],
                                                                rhs=bf(pT, 128, j * 128), start=True, stop=(n == 0)),
                      reads=aux.r(n) + pT.r(), writes=psum.r(bxo))
                    if n > 0:
                        A("pe", OP("matmul", out=psum[:, bxo, j * 128:(j + 1) * 128], lhsT=pb[:], rhs=bf(qr, 128, j * 128),
                                                                      start=False, stop=True), reads=pb.r() + qr.r(), writes=psum.r(bxo))
                        A("dve", OP("scalar_tensor_tensor", out=ust[:], in0=ust[:], scalar=gC, in1=psum[:, bkv, j * 128:(j + 1) * 128],
                                                                       op0=ALU.mult, op1=ALU.add), reads=ust.r() + psum.r(bkv), writes=ust.r())
                    else:
                        A("dve", OP("tensor_copy", out=ust[:], in_=psum[:, bkv, j * 128:(j + 1) * 128]), reads=psum.r(bkv), writes=ust.r())
                banks.free(bkv)
                ring.free(pT, kr)
                oT = ring.get()
                A("dve", OP("tensor_tensor", out=oT[:, 0:TT].rearrange("p (j c) -> p j c", j=4), in0=psum[:, bxo, :].rearrange("p (j c) -> p j c", j=4),
                                                          in1=cst[:, C_QDEC + hd * 128:C_QDEC + (hd + 1) * 128].unsqueeze(1).to_broadcast([128, 4, 128]), op=ALU.mult),
                  reads=psum.r(bxo) + cst.r(), writes=oT.r())
                banks.free(bxo)
                ring.free(qr)
                ob = ring.get()
                A("act", OP("activation", out=bf(ob, TT, 0), in_=oT[:, 0:TT], func=AF.Copy), reads=oT.r(), writes=ob.r())
                A("act", OP("activation", out=bf(ob, TT, TT), in_=oT[:, 0:TT], func=AF.Square), reads=oT.r(), writes=ob.r())
                bm = banks.get(); bq = banks.get()
                A("pe", OP("matmul", out=psum[:, bm, :], lhsT=o128_b[:], rhs=bf(ob, TT, 0), start=True, stop=True),
                  reads=ob.r() + o128_b.r(), writes=psum.r(bm))
                A("pe", OP("matmul", out=psum[:, bq, :], lhsT=o128_b[:], rhs=bf(ob, TT, TT), start=True, stop=True),
                  reads=ob.r() + o128_b.r(), writes=psum.r(bq))
                ring.free(ob)
                m2 = ring.get()
                A("act", OP("activation", out=m2[:, 0:TT], in_=psum[:, bm, :], func=AF.Square), reads=psum.r(bm), writes=m2.r())
                A("dve", OP("tensor_tensor", out=m2[:, 0:TT], in0=psum[:, bq, :], in1=m2[:, 0:TT], op=ALU.subtract),
                  reads=psum.r(bq) + m2.r(), writes=m2.r())
                banks.free(bq)
                A("dve", OP("tensor_scalar", out=m2[:, 0:TT], in0=m2[:, 0:TT], scalar1=0.0, scalar2=None, op0=ALU.max), reads=m2.r(), writes=m2.r())
                A("act", OP("activation", out=m2[:, 0:TT], in_=m2[:, 0:TT], func=AF.Ln, bias=NORM_EPS), reads=m2.r(), writes=m2.r())
                A("act", OP("activation", out=m2[:, 0:TT], in_=m2[:, 0:TT], func=AF.Exp, scale=-0.5), reads=m2.r(), writes=m2.r())
                A("dve", OP("tensor_tensor", out=oT[:, 0:TT], in0=oT[:, 0:TT], in1=psum[:, bm, :], op=ALU.subtract),
                  reads=oT.r() + psum.r(bm), writes=oT.r())
                banks.free(bm)
                A("dve", OP("tensor_tensor", out=oT[:, 0:TT], in0=oT[:, 0:TT], in1=m2[:, 0:TT], op=ALU.mult),
                  reads=oT.r() + m2.r(), writes=oT.r())
                ring.free(m2)
                bg = banks.get()
                proj(bg, wg_t, lambda k: wgv[:, k, hd * 128:(hd + 1) * 128], t)
                sg = ring.get()
                A("act", OP("activation", out=sg[:, 0:TT], in_=psum[:, bg, :], func=AF.Silu), reads=psum.r(bg), writes=sg.r())
                banks.free(bg)
                A("dve", OP("scalar_tensor_tensor", out=mixed[:, hd, tsl(t)], in0=oT[:, 0:TT], scalar=PV(f"rnorm{l}", hd), in1=sg[:, 0:TT],
                                                                        op0=ALU.mult, op1=ALU.mult), reads=oT.r() + sg.r() + pv.r(), writes=mixed.r(hd * 4 + t))
                ring.free(oT, sg)
            wbufs.free(wb)
        wbufs.free(wg_t)
        dbg(f"yret{l}", mixed, mixed[:].rearrange("p c t -> p (c t)"), [128, 4 * SEQ], BF16)
        fw.pin = set()
        group_post(l, mixed, None, 2, wbufs, 512.0)
        dbg(f"hmix{l}", hT, hT[:].rearrange("p c t -> p (c t)"), [128, 8 * SEQ])
        for cm in reversed(phase):
            cm.__exit__(None, None, None)
        phase = []
        fw.barrier()
        if upto == "mix":
            break

        rmsnorm_to_xn(f"gffn{l}")
        actb = psb("actb", [128, 4, SEQ], BF16, nslots=16)
        wus = Pool([psb(f"wu{i}", [128, 8, 1024], BF16) for i in range(2)])
        wds = Pool([psb(f"wd{i}", [128, 4, 1024], BF16) for i in range(2)])
        fcar = psb("fcar", [128, 4, 2, 2], F32, nslots=4)

        def ffn_load(grp):
            j0 = grp * 4
            wu = wus.get(); wd = wds.get()
            load_w(wu, wu[:, :, 0:512], w_up_d[l, :, j0 * 128:(j0 + 4) * 128].rearrange("(k p) n -> p k n", p=128))
            load_w(wu, wu[:, :, 512:1024], w_up_d[l, :, D_FF + j0 * 128:D_FF + (j0 + 4) * 128].rearrange("(k p) n -> p k n", p=128))
            load_w(wd, wd[:], w_down_d[l, j0 * 128:(j0 + 4) * 128, :].rearrange("(k p) n -> p k n", p=128))
            return wu, wd

        def ffn_up(grp, jj, t, wu):
            j = grp * 4 + jj
            outs = []
            for which in range(2):
                ch = j + 24 * which
                b = banks.get()
                proj(b, wu, lambda k, o=which * 512 + jj * 128: wu[:, k, o:o + 128], t)
                vc = ring.get()
                w0 = PV(f"fcw{l}", ch * 3 + 0); w1 = PV(f"fcw{l}", ch * 3 + 1); w2 = PV(f"fcw{l}", ch * 3 + 2)
                A("act", OP("activation", out=vc[:, 0:TT], in_=psum[:, b, :], func=AF.Identity, scale=w2, bias=PV(f"fcb{l}", ch)),
                  reads=psum.r(b) + pv.r(), writes=vc.r())
                A("dve", OP("scalar_tensor_tensor", out=vc[:, 1:TT], in0=psum[:, b, 0:TT - 1], scalar=w1, in1=vc[:, 1:TT],
                            op0=ALU.mult, op1=ALU.add), reads=psum.r(b) + vc.r() + pv.r(), writes=vc.r())
                A("dve", OP("scalar_tensor_tensor", out=vc[:, 2:TT], in0=psum[:, b, 0:TT - 2], scalar=w0, in1=vc[:, 2:TT],
                            op0=ALU.mult, op1=ALU.add), reads=psum.r(b) + vc.r() + pv.r(), writes=vc.r())
                if t > 0:
                    A("dve", OP("scalar_tensor_tensor", out=vc[:, 0:1], in0=fcar[:, jj, which, 1:2], scalar=w1, in1=vc[:, 0:1],
                                op0=ALU.mult, op1=ALU.add), reads=fcar.r(jj) + vc.r() + pv.r(), writes=vc.r())
                    A("dve", OP("scalar_tensor_tensor", out=vc[:, 0:2], in0=fcar[:, jj, which, 0:2], scalar=w0, in1=vc[:, 0:2],
                                op0=ALU.mult, op1=ALU.add), reads=fcar.r(jj) + vc.r() + pv.r(), writes=vc.r())
                if t < NT - 1:
                    A("act", OP("activation", out=fcar[:, jj, which, :], in_=psum[:, b, TT - 2:TT], func=AF.Copy),
                      reads=psum.r(b), writes=fcar.r(jj))
                banks.free(b)
                outs.append(vc)
            vc, gc = outs
            A("act", OP("activation", out=gc[:, 0:TT], in_=gc[:, 0:TT], func=AF.Gelu_apprx_tanh), reads=gc.r(), writes=gc.r())
            A("dve", OP("tensor_tensor", out=actb[:, jj, tsl(t)], in0=gc[:, 0:TT], in1=vc[:, 0:TT], op=ALU.mult),
              reads=vc.r() + gc.r(), writes=actb.r(jj * 4 + t))
            ring.free(vc, gc)

        def ffn_down(t, wd):
            for dc in range(8):
                b = banks.get()
                for k in range(4):
                    A("pe", OP("matmul", out=psum[:, b, :], lhsT=wd[:, k, dc * 128:(dc + 1) * 128], rhs=actb[:, k, tsl(t)],
                               start=(k == 0), stop=(k == 3)), reads=wd.r() + actb.r(k * 4 + t), writes=psum.r(b))
                A("dve", OP("tensor_tensor", out=hT[:, dc, tsl(t)], in0=hT[:, dc, tsl(t)], in1=psum[:, b, :], op=ALU.add),
                  reads=hs(dc, t) + psum.r(b), writes=hs(dc, t))
                banks.free(b)

        cur = ffn_load(0)
        for t in range(NT):
            for jj in range(4):
                ffn_up(0, jj, t, cur[0])
        for grp in range(6):
            nxt = ffn_load(grp + 1) if grp + 1 < 6 else None
            for t in range(NT):
                ffn_down(t, cur[1])
                if nxt is not None:
                    for jj in range(4):
                        ffn_up(grp + 1, jj, t, nxt[0])
            wus.free(cur[0]); wds.free(cur[1])
            cur = nxt
        dbg(f"hffn{l}", hT, hT[:].rearrange("p c t -> p (c t)"), [128, 8 * SEQ])
        for cm in reversed(phase):
            cm.__exit__(None, None, None)
        phase = []
        fw.barrier()

    for t in range(NT):
        b = banks.get()
        for c in range(8):
            sq = ring.get()
            A("act", OP("activation", out=bf(sq), in_=hT[:, c, tsl(t)], func=AF.Square), reads=hs(c, t), writes=sq.r())
            A("pe", OP("matmul", out=psum[:, b, :], lhsT=ones_b[:], rhs=bf(sq), start=(c == 0), stop=(c == 7)),
              reads=sq.r() + ones_b.r(), writes=psum.r(b))
            ring.free(sq)
        rs = ring.get()
        A("act", OP("activation", out=rs[:, 0:TT], in_=psum[:, b, :], func=AF.Ln, scale=1.0 / D_MODEL, bias=NORM_EPS), reads=psum.r(b), writes=rs.r())
        banks.free(b)
        A("act", OP("activation", out=rs[:, 0:TT], in_=rs[:, 0:TT], func=AF.Exp, scale=-0.5), reads=rs.r(), writes=rs.r())
        for c in range(8):
            ot = ring.get()
            A("dve", OP("scalar_tensor_tensor", out=ot[:, 0:TT], in0=hT[:, c, tsl(t)], scalar=PV("gfin", c), in1=rs[:, 0:TT],
                                                                         op0=ALU.mult, op1=ALU.mult), reads=hs(c, t) + rs.r() + pv.r(), writes=ot.r())
            ins = A("sp", OP("dma_start", out=outT_d[c * 128:(c + 1) * 128, tsl(t)], in_=ot[:, 0:TT]), reads=ot.r(), dma=True)
            out_dmas.append(ins)
            ring.free(ot)
        ring.free(rs)

    for cm in reversed(phase):
        cm.__exit__(None, None, None)
    fin = A("sp", None)
    for ins in out_dmas:
        fin.preds[ins] = True
    fw.emit()
    fw.close()
    return nc, dbg_out


def make_in_maps(inputs):
    inp = {k: np.asarray(v) for k, v in inputs.items()}
    pv = host_pv(inp)
    cst = host_consts()
    wbd, x1, x2, c1, c2 = host_struct(inp)
    shared = {
        "pv": pv, "cst": cst,
        "wbd": np.ascontiguousarray(wbd.reshape(DEPTH, 128, 1024)),
        "s5x1": x1, "s5x2": x2,
        "s5c1": np.ascontiguousarray(c1.reshape(DEPTH, 128, 512)),
        "s5c2": np.ascontiguousarray(c2.reshape(DEPTH, 128, 512)),
        "w_in": np.ascontiguousarray(inp["w_in"], dtype=np.float32),
        "s5_w_glu": np.ascontiguousarray(inp["s5_w_glu"], dtype=np.float32),
        "w_out": np.ascontiguousarray(inp["w_out"], dtype=np.float32),
        "w_up": np.ascontiguousarray(inp["w_up"], dtype=np.float32),
        "w_down": np.ascontiguousarray(inp["w_down"], dtype=np.float32),
    }
    maps = []
    for b in range(inp["x"].shape[0]):
        m = dict(shared)
        m["xT"] = np.ascontiguousarray(inp["x"][b].T.astype(np.float32))
        m["pos"] = np.ascontiguousarray(inp["positions"][b].astype(np.int32).reshape(1, SEQ))
        maps.append(m)
    return maps


_NC_CACHE = {}


def kernel(**inputs):
    if "nc" not in _NC_CACHE:
        _NC_CACHE["nc"] = build()[0]
    nc = _NC_CACHE["nc"]
    maps = make_in_maps(inputs)
    res = run_bass_kernel_spmd(nc, maps, core_ids=list(range(len(maps))))
    out = np.stack([np.ascontiguousarray(r["outT"].T) for r in res.results], axis=0)
    return out.astype(np.float32)
```

```python
import math
import os
from collections import deque
import numpy as np
import concourse.bass as bass
import concourse.mybir as mybir
from concourse.bass_utils import run_bass_kernel_spmd

F32 = mybir.dt.float32
BF16 = mybir.dt.bfloat16
I32 = mybir.dt.int32
AF = mybir.ActivationFunctionType
ALU = mybir.AluOpType

ENGS = ("pe", "act", "dve", "pool", "sp")
N_DMA_SEMS = 24

D_MODEL = 1024
SEQ = 2048
DEPTH = 2
TT = 512
NT = SEQ // TT
IN_WIDTH = 3584
D_FF = 3072
NORM_EPS = 1e-6
MAGIC = 12582912.0
TWO_PI_S = 6.2831850
GAMMAS = [1.0 - 2.0 ** (-5.0 - h) for h in range(4)]


class Reg:
    __slots__ = ("name", "w", "rds")

    def __init__(self, name):
        self.name = name
        self.w = None
        self.rds = []


class T:
    def __init__(self, h, name, nslots=1):
        self.h = h
        self.name = name
        self.regs = [Reg(f"{name}.{i}") for i in range(nslots)]

    def __getitem__(self, k):
        return self.h[k]

    def r(self, i=None, j=None):
        if i is None:
            return list(self.regs)
        if j is None:
            return [self.regs[i]]
        return self.regs[i:j]


class Ins:
    __slots__ = ("eng", "rec", "fn", "preds", "is_dma", "dma_sem", "dma_use", "cost", "lat", "seg",
                 "tset", "sched", "done", "rt", "pos", "waits", "needs_inc", "count", "waits_dma", "pin")

    def __init__(self, eng, rec, fn):
        self.eng = eng
        self.rec = rec
        self.fn = fn
        self.preds = {}
        self.is_dma = False
        self.dma_sem = None
        self.dma_use = None
        self.cost = 100.0
        self.lat = 0.0
        self.seg = 0
        self.tset = None
        self.sched = False
        self.done = 0.0
        self.rt = None
        self.pos = None
        self.waits = []
        self.needs_inc = False
        self.count = None
        self.pin = False


_ACT_GROUP = {}


def _act_group(func):
    if not _ACT_GROUP:
        _ACT_GROUP.update({AF.Exp: 1, AF.Ln: 1, AF.Gelu_apprx_tanh: 2, AF.Silu: 3, AF.Sin: 3, AF.Sigmoid: 4, AF.Sqrt: 5})
    return _ACT_GROUP.get(func)


def _free(ap):
    n = 1
    for d in ap.shape[1:]:
        n *= int(d)
    return n


def est_cost(eng, fn, is_dma):
    if fn is None:
        return 0.0, 0.0
    name, args, kw = fn
    if is_dma:
        out = kw["out"]
        nbytes = _free(out) * int(out.shape[0]) * 4
        return (1000.0 if eng == "pool" else 120.0), 2000.0 + nbytes / 150.0
    if eng == "pe":
        if name == "transpose":
            return 110.0, 0.0
        n = _free(kw["rhs"])
        return max(n, 64) / 1.9 + 10.0, 0.0
    out = kw.get("out")
    if out is None:
        out = args[0]
    n = _free(out)
    if eng == "act":
        return 150.0 + n / 1.2, 0.0
    if eng == "dve":
        if name == "tensor_tensor_scan":
            return 120.0 + 2.0 * n / 0.96, 0.0
        if name == "reciprocal":
            return 120.0 + 6.1 * n, 0.0
        if name in ("tensor_tensor", "scalar_tensor_tensor"):
            return 120.0 + n / 0.96, 0.0
        return 120.0 + n / 1.5, 0.0
    if eng == "pool":
        if name == "tensor_tensor":
            return 150.0 + 2.15 * n, 0.0
        return 150.0 + 1.2 * n, 0.0
    return 100.0, 0.0


class FW:
    SEM_LAT = 500.0
    WINDOW = int(os.environ.get('KW', '80'))
    WIN_ENG = {e: int(os.environ.get('KW_' + e.upper(), '0')) for e in ('pe', 'act', 'dve', 'pool', 'sp')}

    def __init__(self, nc):
        self.nc = nc
        self.all = []
        self.dma_rr = 0
        self.dma_rr_pool = 0
        self.dma_uses = [0] * N_DMA_SEMS
        self.dma_last = [None] * N_DMA_SEMS
        self.seg = 0
        self.seg_dma_uses = []
        self.pool_dmas = []
        self.pin = set()
        self.unpin_names = {'scalar_tensor_tensor', 'tensor_tensor'}
        self._stack = []

    def sb(self, name, shape, dtype, nslots=1):
        cm = self.nc.sbuf_tensor(name, list(shape), dtype)
        h = cm.__enter__()
        self._stack.append(cm)
        return T(h, name, nslots)

    def ps(self, name, shape, dtype, nslots=1):
        cm = self.nc.psum_tensor(name, list(shape), dtype)
        h = cm.__enter__()
        self._stack.append(cm)
        return T(h, name, nslots)

    @staticmethod
    def _add_pred(ins, p, kind):
        if p is None or p is ins:
            return
        if p.is_dma or ins.is_dma or p.eng != ins.eng:
            needs = True
        else:
            needs = (ins.eng != "pe")
        ins.preds[p] = ins.preds.get(p, False) or needs

    def op(self, eng, fn, reads=(), writes=(), dma=False):
        ins = Ins(eng, len(self.all), fn)
        ins.seg = self.seg
        ins.is_dma = dma
        ins.pin = (eng in self.pin) and not (fn is not None and fn[0] in self.unpin_names)
        self.all.append(ins)
        ins.cost, ins.lat = est_cost(eng, fn, dma)
        if eng == "act" and fn is not None and fn[0] == "activation":
            ins.tset = _act_group(fn[2].get("func"))
        if dma:
            half = N_DMA_SEMS // 2
            if eng == "pool":
                k = half + self.dma_rr_pool
                self.dma_rr_pool = (self.dma_rr_pool + 1) % half
            else:
                k = self.dma_rr
                self.dma_rr = (k + 1) % half
            self._add_pred(ins, self.dma_last[k], "SEM")
            self.dma_last[k] = ins
            self.dma_uses[k] += 1
            ins.dma_sem = k
            ins.dma_use = self.dma_uses[k]
            if eng == "pool":
                self.pool_dmas.append(ins)
                if len(self.pool_dmas) > 4:
                    self._add_pred(ins, self.pool_dmas[-5], "SEM")
        for r in reads:
            self._add_pred(ins, r.w, "RAW")
        for r in writes:
            self._add_pred(ins, r.w, "WAW")
            for p in r.rds:
                self._add_pred(ins, p, "WAR")
        for r in reads:
            r.rds.append(ins)
        for r in writes:
            r.w = ins
            r.rds = []
        return ins

    def barrier(self):
        self.seg_dma_uses.append(list(self.dma_uses))
        self.seg += 1

    def schedule(self):
        nseg = self.seg + 1
        final = {e: [] for e in ENGS}
        eng_free = {e: 0.0 for e in ENGS}
        cur_set = None
        bar_marks = []
        for sg in range(nseg):
            pending = {e: [i for i in self.all if i.seg == sg and i.eng == e] for e in ENGS}
            remaining = sum(len(v) for v in pending.values())
            while remaining:
                best = None
                best_t = None
                for e in ENGS:
                    lst = pending[e]
                    ef = eng_free[e]
                    lim = min(len(lst), self.WIN_ENG[e] or self.WINDOW)
                    if lim and lst[0].pin:
                        lim = 1
                    for wi in range(lim):
                        c = lst[wi]
                        if wi > 0 and c.pin:
                            break
                        if c.rt is None:
                            rt = 0.0
                            ok = True
                            for p, needs in c.preds.items():
                                if not p.sched:
                                    ok = False
                                    break
                                d = p.done + (self.SEM_LAT if needs else 0.0)
                                if d > rt:
                                    rt = d
                            if not ok:
                                continue
                            c.rt = rt
                        t = c.rt if c.rt > ef else ef
                        if e == "act" and c.tset is not None and c.tset != cur_set:
                            t += 1300.0
                        if best is None or t < best_t or (t == best_t and c.rec < best[1].rec):
                            best = (e, c, wi)
                            best_t = t
                        if t <= ef:
                            break
                assert best is not None, "scheduler deadlock"
                e, c, wi = best
                pending[e].pop(wi)
                remaining -= 1
                if e == "act" and c.tset is not None:
                    cur_set = c.tset
                end = best_t + c.cost
                eng_free[e] = end
                c.done = end + c.lat
                c.sched = True
                c.pos = len(final[e])
                final[e].append(c)
            if sg < nseg - 1:
                tmax = max(eng_free.values())
                dmax = max([i.done for i in self.all if i.seg == sg and i.is_dma] + [0.0])
                tmax = max(tmax, dmax)
                marks = {}
                for e in ENGS:
                    b = Ins(e, -1, None)
                    b.sched = True
                    b.pos = len(final[e])
                    final[e].append(b)
                    marks[e] = b
                    eng_free[e] = tmax
                bar_marks.append(marks)
        self.final = final
        self.bar_marks = bar_marks
        self.est_total = max(eng_free.values())

    def emit(self):
        nc = self.nc
        self.schedule()
        final = self.final
        for bi, marks in enumerate(self.bar_marks):
            for e, b in marks.items():
                for e2 in ENGS:
                    if e2 == e:
                        continue
                    pos2 = marks[e2].pos
                    for j in range(pos2 - 1, -1, -1):
                        p = final[e2][j]
                        if p.fn is not None and not p.is_dma:
                            b.preds[p] = True
                            break
                b.waits_dma = self.seg_dma_uses[bi]
        for e in ENGS:
            seen = {}
            for ins in final[e]:
                for p, needs in ins.preds.items():
                    if not needs:
                        assert p.eng == e and p.pos < ins.pos, "ordering edge violated"
                        continue
                    if p.is_dma:
                        key = f"d{p.dma_sem}"
                        v = p.dma_use
                    else:
                        key = p.eng
                        v = p.pos
                        if p.eng == e:
                            assert p.pos < ins.pos
                    if seen.get(key, -1) >= v:
                        continue
                    seen[key] = v
                    ins.waits.append((key, p))
                    if not p.is_dma:
                        p.needs_inc = True
                wd = getattr(ins, "waits_dma", None) if ins.fn is None else None
                if wd is not None:
                    for k, u in enumerate(wd):
                        if u > 0 and seen.get(f"d{k}", -1) < u:
                            seen[f"d{k}"] = u
                            ins.waits.append((f"d{k}", u))
        sem_cms = []
        sems = {}
        for e in list(ENGS) + [f"d{k}" for k in range(N_DMA_SEMS)]:
            cm = nc.semaphore(f"s_{e}")
            sems[e] = cm.__enter__()
            sem_cms.append(cm)
        for e in ENGS:
            c = 0
            for ins in final[e]:
                if ins.needs_inc:
                    c += 1
                    ins.count = c

        def run(eng_name, eng):
            for ins in final[eng_name]:
                for (key, p) in ins.waits:
                    if isinstance(p, int):
                        eng.wait_ge(sems[key], 16 * p)
                    elif p.is_dma:
                        eng.wait_ge(sems[key], 16 * p.dma_use)
                    else:
                        eng.wait_ge(sems[key], p.count)
                if ins.fn is None:
                    continue
                name, args, kw = ins.fn
                bi = getattr(eng, name)(*args, **kw)
                if ins.dma_sem is not None:
                    bi.then_inc(sems[f"d{ins.dma_sem}"], 16)
                elif ins.needs_inc:
                    bi.then_inc(sems[eng_name], 1)

        with nc.Block() as block:
            @block.tensor
            def _(eng):
                run("pe", eng)

            @block.scalar
            def _(eng):
                run("act", eng)

            @block.vector
            def _(eng):
                run("dve", eng)

            @block.gpsimd
            def _(eng):
                run("pool", eng)

            @block.sync
            def _(eng):
                run("sp", eng)
        for cm in reversed(sem_cms):
            cm.__exit__(None, None, None)

    def close(self):
        for cm in reversed(self._stack):
            cm.__exit__(None, None, None)
        self._stack = []


def OP(name, *args, **kw):
    return (name, args, kw)


class TV:
    def __init__(self, parent, slot, ap):
        self.parent = parent
        self.slot = slot
        self.ap = ap
        self.is_view = True

    def __getitem__(self, k):
        return self.ap[k]

    def r(self):
        return self.parent.r(self.slot)


class Pool:
    def __init__(self, items):
        self.free_ = deque(items)
        self.n = len(items)
        self.extra = []

    def get(self, halo=False):
        assert self.free_, "scratch pool exhausted"
        if not halo:
            return self.free_.popleft()
        for i, it in enumerate(self.free_):
            if not getattr(it, "is_view", False):
                del self.free_[i]
                return it
        raise AssertionError("no halo-capable scratch tile free")

    def free(self, *items):
        for it in items:
            self.free_.append(it)

    def enable_extra(self, views):
        self.extra = list(views)
        for v in views:
            self.free_.appendleft(v)

    def disable_extra(self):
        for v in self.extra:
            assert v in self.free_, "extra scratch view still in use"
            self.free_.remove(v)
        self.extra = []


def _chunked(v, nchunk):
    return np.ascontiguousarray(np.asarray(v, np.float32).reshape(nchunk, 128).T)


class PVLayout:
    def __init__(self):
        self.idx = {}
        self.n = 0

    def add(self, name, k):
        self.idx[name] = (self.n, k)
        self.n += k


def pv_layout():
    pl = PVLayout()
    for l in range(DEPTH):
        for name, k in (("gmix", 8), ("gffn", 8), ("lcw", 16), ("lcb", 4), ("lba", 4), ("lbx", 4),
                        ("llam", 4), ("lnorm", 4), ("s5d", 4), ("s5bg", 4), ("s5n", 4), ("rnorm", 4),
                        ("fcw", 144), ("fcb", 48), ("s5lr", 32), ("s5li", 32), ("s5ldt", 32)):
            pl.add(f"{name}{l}", k)
    for name, k in (("gfin", 8), ("invf", 1), ("sgn", 1), ("nsgn", 1), ("vdec", 4), ("rowmask", 8)):
        pl.add(name, k)
    return pl


PVL = pv_layout()
C_ID = 0
C_IOTA = 128
C_MASK = C_IOTA + 512
C_QDEC = C_MASK + 128
NCST = C_QDEC + 512


def host_consts():
    cst = np.zeros((128, NCST), np.float32)
    cst[:, C_ID:C_ID + 128] = np.eye(128, dtype=np.float32)
    cst[:, C_IOTA:C_IOTA + 512] = np.arange(512, dtype=np.float32)[None, :]
    m = np.arange(128)[:, None]
    c = np.arange(128)[None, :]
    cst[:, C_MASK:C_MASK + 128] = np.where(c >= m, np.float32(128.0 ** -0.5), np.float32(0.0))
    for h in range(4):
        lg = np.log1p(-np.float32(2.0) ** np.float32(-5.0 - h)).astype(np.float32)
        cst[:, C_QDEC + h * 128:C_QDEC + (h + 1) * 128] = np.exp(lg * (np.arange(128, dtype=np.float32) + 1.0))[None, :]
    return cst


def host_pv(inp):
    pv = np.zeros((128, PVL.n), np.float32)

    def put(name, arr):
        o, k = PVL.idx[name]
        assert arr.shape == (128, k), (name, arr.shape, k)
        pv[:, o:o + k] = arr

    for l in range(DEPTH):
        put(f"gmix{l}", _chunked(inp["norm_mix"][l], 8))
        put(f"gffn{l}", _chunked(inp["norm_ffn"][l], 8))
        cw = np.asarray(inp["lru_conv_w"][l], np.float32)
        put(f"lcw{l}", np.ascontiguousarray(cw.reshape(4, 4, 128).transpose(2, 1, 0).reshape(128, 16)))
        put(f"lcb{l}", _chunked(inp["lru_conv_b"][l], 4))
        put(f"lba{l}", _chunked(np.asarray(inp["lru_ba"][l]).reshape(512), 4))
        put(f"lbx{l}", _chunked(np.asarray(inp["lru_bx"][l]).reshape(512), 4))
        put(f"llam{l}", _chunked(inp["lru_lambda"][l], 4))
        put(f"lnorm{l}", _chunked(inp["lru_norm"][l], 4))
        put(f"s5d{l}", _chunked(inp["s5_d"][l], 4))
        put(f"s5bg{l}", _chunked(inp["s5_b_glu"][l], 4))
        put(f"s5n{l}", _chunked(inp["s5_norm"][l], 4))
        put(f"rnorm{l}", _chunked(inp["ret_norm"][l], 4))
        fw_ = np.asarray(inp["ffn_conv_w"][l], np.float32)
        put(f"fcw{l}", np.ascontiguousarray(fw_.reshape(3, 48, 128).transpose(2, 1, 0).reshape(128, 144)))
        put(f"fcb{l}", _chunked(inp["ffn_conv_b"][l], 48))
        lr = np.asarray(inp["s5_lambda_re"][l], np.float32).T
        li = np.asarray(inp["s5_lambda_im"][l], np.float32).T
        put(f"s5lr{l}", np.concatenate([lr, lr], 0))
        put(f"s5li{l}", np.concatenate([li, li], 0))
        put(f"s5ldt{l}", np.broadcast_to(np.asarray(inp["s5_log_dt"][l], np.float32)[None, :], (128, 32)).copy())
    put("gfin", _chunked(inp["norm_final"], 8))
    half = 64
    inv = (np.float32(10000.0) ** (-np.arange(half, dtype=np.float32) * np.float32(2.0) / np.float32(128.0))).astype(np.float32)
    put("invf", np.concatenate([inv, inv])[:, None])
    sgn = np.concatenate([-np.ones(64, np.float32), np.ones(64, np.float32)])[:, None]
    put("sgn", sgn)
    put("nsgn", -sgn)
    vd = np.zeros((128, 4), np.float32)
    for h in range(4):
        lg = np.log1p(-np.float32(2.0) ** np.float32(-5.0 - h)).astype(np.float32)
        vd[:, h] = np.exp(-lg * (np.arange(128, dtype=np.float32) + 1.0))
    put("vdec", vd)
    rm = np.zeros((128, 8), np.float32)
    for g in range(8):
        rm[16 * g:16 * g + 16, g] = 1.0
    put("rowmask", rm)
    return pv


def host_struct(inp):
    wbd = np.zeros((DEPTH, 128, 2, 4, 128), np.float32)
    for l in range(DEPTH):
        for which, nm in enumerate(("lru_wa", "lru_wx")):
            w = np.asarray(inp[nm][l], np.float32)
            for c in range(4):
                for hb in range(2):
                    wbd[l, hb * 64:(hb + 1) * 64, which, c, hb * 64:(hb + 1) * 64] = w[2 * c + hb]
    x1 = np.zeros((DEPTH, 128, 512), np.float32)
    x2 = np.zeros((DEPTH, 128, 512), np.float32)
    c1 = np.zeros((DEPTH, 128, 4, 128), np.float32)
    c2 = np.zeros((DEPTH, 128, 4, 128), np.float32)
    for l in range(DEPTH):
        br = np.asarray(inp["s5_b_re"][l], np.float32).transpose(1, 0, 2).reshape(64, 512)
        bi = np.asarray(inp["s5_b_im"][l], np.float32).transpose(1, 0, 2).reshape(64, 512)
        x1[l] = np.concatenate([br, bi], 0)
        x2[l] = np.concatenate([bi, br], 0)
        cr = np.asarray(inp["s5_c_re"][l], np.float32).reshape(4, 128, 64)
        ci = np.asarray(inp["s5_c_im"][l], np.float32).reshape(4, 128, 64)
        c1[l] = np.concatenate([cr, ci], 2).transpose(1, 0, 2)
        c2[l] = np.concatenate([ci, cr], 2).transpose(1, 0, 2)
    return wbd, x1, x2, c1, c2


def build(debug=(), upto="all", nlayers=DEPTH):
    nc = bass.Bass("TRN2", target_bir_lowering=False)
    fw = FW(nc)
    dram = {}

    def din(name, shape, dtype=F32):
        dram[name] = nc.dram_tensor(name, list(shape), dtype, kind="ExternalInput").ap()
        return dram[name]

    xT_d = din("xT", [D_MODEL, SEQ])
    pos_d = din("pos", [1, SEQ], I32)
    pv_d = din("pv", [128, PVL.n])
    cst_d = din("cst", [128, NCST])
    wbd_d = din("wbd", [DEPTH, 128, 2 * 4 * 128])
    x1_d = din("s5x1", [DEPTH, 128, 512])
    x2_d = din("s5x2", [DEPTH, 128, 512])
    c1_d = din("s5c1", [DEPTH, 128, 512])
    c2_d = din("s5c2", [DEPTH, 128, 512])
    w_in_d = din("w_in", [DEPTH, D_MODEL, IN_WIDTH])
    w_glu_d = din("s5_w_glu", [DEPTH, 512, 512])
    w_out_d = din("w_out", [DEPTH, 1536, D_MODEL])
    w_up_d = din("w_up", [DEPTH, D_MODEL, 2 * D_FF])
    w_down_d = din("w_down", [DEPTH, D_FF, D_MODEL])
    outT_d = nc.dram_tensor("outT", [D_MODEL, SEQ], F32, kind="ExternalOutput").ap()
    dbg_out = {}
    out_dmas = []

    A = fw.op

    hT = fw.sb("hT", [128, 8, SEQ], F32, nslots=32)
    xn = fw.sb("xn", [128, 8, SEQ], BF16, nslots=32)
    rcos = fw.sb("rcos", [128, SEQ], F32, nslots=4)
    rsin = fw.sb("rsin", [128, SEQ], F32, nslots=4)
    pv = fw.sb("pvs", [128, PVL.n], F32)
    cst = fw.sb("csts", [128, NCST], F32)
    ident_b = fw.sb("ident_b", [128, 128], BF16)
    ones_b = fw.sb("ones_b", [128, 128], BF16)
    o128_b = fw.sb("o128_b", [128, 128], BF16)
    NRING = 9
    ring = Pool([fw.sb(f"ring{i}", [128, 520], F32) for i in range(NRING)])
    psum = fw.ps("psum", [128, 8, 512], F32, nslots=8)
    banks = Pool(list(range(8)))
    rot_views = [TV(rcos, i, rcos[:, i * TT:(i + 1) * TT]) for i in range(4)] + [TV(rsin, i, rsin[:, i * TT:(i + 1) * TT]) for i in range(4)]

    def PV(name, c=0, n=1):
        o, k = PVL.idx[name]
        return pv[:, o + c:o + c + n]

    def hs(c, t):
        return hT.r(c * 4 + t)

    def xs(c, t):
        return xn.r(c * 4 + t)

    def tsl(t):
        return slice(t * TT, (t + 1) * TT)

    def bf(tile, n=TT, off=0):
        return tile[:].bitcast(BF16)[:, off:off + n]

    def dbg(name, tile, ap, shape, dtype=F32):
        if name not in debug:
            return
        d = nc.dram_tensor("dbg_" + name, list(shape), dtype, kind="ExternalOutput").ap()
        dbg_out[name] = d
        ins = A("sp", OP("dma_start", out=d, in_=ap), reads=tile.r(), dma=True)
        out_dmas.append(ins)

    A("sp", OP("dma_start", out=pv[:], in_=pv_d), writes=pv.r(), dma=True)
    A("sp", OP("dma_start", out=cst[:], in_=cst_d), writes=cst.r(), dma=True)
    for c in range(8):
        for t in range(NT):
            A("sp", OP("dma_start", out=hT[:, c, tsl(t)], in_=xT_d[c * 128:(c + 1) * 128, tsl(t)]),
              writes=hs(c, t), dma=True)
    A("dve", OP("memset", ones_b[:], 1.0), writes=ones_b.r())
    A("dve", OP("memset", o128_b[:], 1.0 / 128.0), writes=o128_b.r())
    A("act", OP("activation", out=ident_b[:], in_=cst[:, C_ID:C_ID + 128], func=AF.Copy),
      reads=cst.r(), writes=ident_b.r())
    ident_f = cst[:, C_ID:C_ID + 128]
    iota = cst[:, C_IOTA:C_IOTA + 512]
    maskT = cst[:, C_MASK:C_MASK + 128]

    def load_w(dst_tile, dst_ap, src_ap):
        return A("pool", OP("dma_start", out=dst_ap, in_=src_ap), writes=dst_tile.r(), dma=True)

    def rmsnorm_to_xn(gname):
        for t in range(NT):
            b = banks.get()
            for c in range(8):
                sq = ring.get()
                A("act", OP("activation", out=bf(sq), in_=hT[:, c, tsl(t)], func=AF.Square),
                  reads=hs(c, t), writes=sq.r())
                A("pe", OP("matmul", out=psum[:, b, :], lhsT=ones_b[:], rhs=bf(sq), start=(c == 0), stop=(c == 7)),
                  reads=sq.r() + ones_b.r(), writes=psum.r(b))
                ring.free(sq)
            rs = ring.get()
            A("act", OP("activation", out=rs[:, 0:TT], in_=psum[:, b, :], func=AF.Ln, scale=1.0 / D_MODEL, bias=NORM_EPS), reads=psum.r(b), writes=rs.r())
            banks.free(b)
            A("act", OP("activation", out=rs[:, 0:TT], in_=rs[:, 0:TT], func=AF.Exp, scale=-0.5), reads=rs.r(), writes=rs.r())
            for c in range(8):
                A("dve", OP("scalar_tensor_tensor", out=xn[:, c, tsl(t)], in0=hT[:, c, tsl(t)], scalar=PV(gname, c),
                                                                      in1=rs[:, 0:TT], op0=ALU.mult, op1=ALU.mult),
                  reads=hs(c, t) + rs.r() + pv.r(), writes=xs(c, t))
            ring.free(rs)

    def proj(b, wtile, w_ap_fn, t):
        for k in range(8):
            A("pe", OP("matmul", out=psum[:, b, :], lhsT=w_ap_fn(k), rhs=xn[:, k, tsl(t)], start=(k == 0), stop=(k == 7)),
              reads=wtile.r() + xs(k, t), writes=psum.r(b))

    def group_post(l, mixed, gname, grp, wbufs_pool, eps_div):
        wo = wbufs_pool.get()
        wo_v = wo[:].rearrange("p (k n) -> p k n", k=4)
        load_w(wo, wo_v, w_out_d[l, grp * 512:(grp + 1) * 512, :].rearrange("(k p) n -> p k n", p=128))
        for t in range(NT):
            if gname is not None:
                b = banks.get()
                for c in range(4):
                    sq = ring.get()
                    A("act", OP("activation", out=bf(sq), in_=mixed[:, c, tsl(t)], func=AF.Square),
                      reads=mixed.r(c * 4 + t), writes=sq.r())
                    A("pe", OP("matmul", out=psum[:, b, :], lhsT=ones_b[:], rhs=bf(sq), start=(c == 0), stop=(c == 3)),
                      reads=sq.r() + ones_b.r(), writes=psum.r(b))
                    ring.free(sq)
                rs = ring.get()
                A("act", OP("activation", out=rs[:, 0:TT], in_=psum[:, b, :], func=AF.Ln, scale=1.0 / 512.0, bias=NORM_EPS), reads=psum.r(b), writes=rs.r())
                banks.free(b)
                A("act", OP("activation", out=rs[:, 0:TT], in_=rs[:, 0:TT], func=AF.Exp, scale=-0.5), reads=rs.r(), writes=rs.r())
                for c in range(4):
                    A("dve", OP("scalar_tensor_tensor", out=mixed[:, c, tsl(t)], in0=mixed[:, c, tsl(t)],
                                                                          scalar=PV(gname + str(l), c), in1=rs[:, 0:TT],
                                                                          op0=ALU.mult, op1=ALU.mult),
                      reads=mixed.r(c * 4 + t) + rs.r() + pv.r(), writes=mixed.r(c * 4 + t))
                ring.free(rs)
            for dc in range(8):
                b = banks.get()
                for k in range(4):
                    A("pe", OP("matmul", out=psum[:, b, :], lhsT=wo_v[:, k, dc * 128:(dc + 1) * 128],
                                                           rhs=mixed[:, k, tsl(t)], start=(k == 0), stop=(k == 3)),
                      reads=wo.r() + mixed.r(k * 4 + t), writes=psum.r(b))
                A("dve", OP("tensor_tensor", out=hT[:, dc, tsl(t)], in0=hT[:, dc, tsl(t)], in1=psum[:, b, :], op=ALU.add),
                  reads=hs(dc, t) + psum.r(b), writes=hs(dc, t))
                banks.free(b)
        wbufs_pool.free(wo)

    def build_rot():
        for t in range(NT):
            pi_ = ring.get(); pf = ring.get(); k_ = ring.get()
            A("sp", OP("dma_start", out=pi_[:].bitcast(I32)[:, 0:TT],
                                                          in_=pos_d[0:1, tsl(t)].to_broadcast([128, TT])),
              writes=pi_.r(), dma=True)
            A("dve", OP("tensor_copy", out=pf[:, 0:TT], in_=pi_[:].bitcast(I32)[:, 0:TT]),
              reads=pi_.r(), writes=pf.r())
            A("dve", OP("tensor_scalar", out=pf[:, 0:TT], in0=pf[:, 0:TT], scalar1=PV("invf"), scalar2=None,
                                                      op0=ALU.mult), reads=pf.r() + pv.r(), writes=pf.r())
            A("dve", OP("tensor_scalar", out=k_[:, 0:TT], in0=pf[:, 0:TT], scalar1=1.0 / (2.0 * math.pi),
                                                             scalar2=MAGIC, op0=ALU.mult, op1=ALU.add),
              reads=pf.r(), writes=k_.r())
            A("dve", OP("tensor_scalar", out=k_[:, 0:TT], in0=k_[:, 0:TT], scalar1=MAGIC, scalar2=None,
                                                      op0=ALU.subtract), reads=k_.r(), writes=k_.r())
            C1 = 6.28125
            C2 = 2.0 * math.pi - 6.28125
            A("dve", OP("scalar_tensor_tensor", out=pf[:, 0:TT], in0=k_[:, 0:TT], scalar=-C1, in1=pf[:, 0:TT],
                                                                    op0=ALU.mult, op1=ALU.add), reads=pf.r() + k_.r(), writes=pf.r())
            A("dve", OP("scalar_tensor_tensor", out=pf[:, 0:TT], in0=k_[:, 0:TT], scalar=-C2, in1=pf[:, 0:TT],
                                                                    op0=ALU.mult, op1=ALU.add), reads=pf.r() + k_.r(), writes=pf.r())
            A("dve", OP("tensor_scalar", out=pf[:, 0:TT], in0=pf[:, 0:TT], scalar1=3.1415925, scalar2=-3.1415925,
                                                      op0=ALU.min, op1=ALU.max), reads=pf.r(), writes=pf.r())
            A("act", OP("activation", out=rsin[:, tsl(t)], in_=pf[:, 0:TT], func=AF.Sin, scale=PV("sgn")),
              reads=pf.r() + pv.r(), writes=rsin.r(t))
            A("act", OP("activation", out=k_[:, 0:TT], in_=pf[:, 0:TT], func=AF.Abs),
              reads=pf.r(), writes=k_.r())
            A("act", OP("activation", out=rcos[:, tsl(t)], in_=k_[:, 0:TT], func=AF.Sin, scale=-1.0,
                                                        bias=math.pi / 2.0 - 1e-6), reads=k_.r(), writes=rcos.r(t))
            ring.free(pi_, pf, k_)


    phase = []
    for l in range(nlayers):

        def psb(name, shape, dtype, nslots=1):
            cm = nc.sbuf_tensor(f"{name}_{l}", list(shape), dtype)
            h = cm.__enter__()
            phase.append(cm)
            return T(h, name, nslots)

        mixed = psb("mixed", [128, 4, SEQ], BF16, nslots=16)
        aux = psb("aux", [128, 8192], BF16, nslots=16)
        wbufs = Pool([psb(f"wbuf{i}", [128, 4096], BF16) for i in range(2)])
        wbd = psb("wbd", [128, 2, 4, 128], BF16)
        lpar = psb("lpar", [128, 16], F32)
        lcar = psb("lcar", [128, 4, 4], F32)
        s5p = psb("s5p", [128, 32, 8], F32)
        s5off = psb("s5off", [128, 32, 4], F32)
        bst1 = psb("bst1", [128, 512], BF16)
        bst2 = psb("bst2", [128, 512], BF16)
        w5pool = Pool([psb(f"w5_{i}", [128, 8, 128], BF16) for i in range(2)])
        lhs_b = psb("lhs_b", [128, 2, 8, 128], BF16)
        lhs_c = psb("lhs_c", [128, 2, 8, 128], BF16)
        zcar = psb("zcar", [128, 32], F32)
        ust = psb("ust", [128, 128], F32)
        pbs = [psb(f"pb{i}", [128, 128], BF16) for i in range(2)]

        load_w(wbd, wbd[:].rearrange("p a c n -> p (a c n)"), wbd_d[l])
        A("act", OP("activation", out=lpar[:, 0:4], in_=PV(f"llam{l}", 0, 4), func=AF.Exp, scale=-1.0), reads=pv.r(), writes=lpar.r())
        A("act", OP("activation", out=lpar[:, 0:4], in_=lpar[:, 0:4], func=AF.Ln, bias=1.0), reads=lpar.r(), writes=lpar.r())
        A("dve", OP("tensor_scalar", out=lpar[:, 4:8], in0=lpar[:, 0:4], scalar1=-4.0, scalar2=None, op0=ALU.mult), reads=lpar.r(), writes=lpar.r())
        A("dve", OP("tensor_scalar", out=lpar[:, 0:4], in0=lpar[:, 0:4], scalar1=-8.0, scalar2=None, op0=ALU.mult), reads=lpar.r(), writes=lpar.r())
        A("dve", OP("tensor_scalar", out=lpar[:, 8:12], in0=PV(f"lba{l}", 0, 4), scalar1=0.5, scalar2=None, op0=ALU.mult), reads=pv.r(), writes=lpar.r())
        A("dve", OP("tensor_scalar", out=lpar[:, 12:16], in0=PV(f"lbx{l}", 0, 4), scalar1=0.5, scalar2=None, op0=ALU.mult), reads=pv.r(), writes=lpar.r())
        A("dve", OP("memset", lcar[:], 0.0), writes=lcar.r())
        A("dve", OP("memset", zcar[:], 0.0), writes=zcar.r())

        S = lambda j: s5p[:, :, j]
        LR = PV(f"s5lr{l}", 0, 32)
        LI = PV(f"s5li{l}", 0, 32)
        R_, W_ = s5p.r(), s5p.r()
        A("act", OP("activation", out=S(6), in_=PV(f"s5ldt{l}", 0, 32), func=AF.Exp), reads=pv.r(), writes=W_)
        A("dve", OP("tensor_tensor", out=S(0), in0=LR, in1=S(6), op=ALU.mult), reads=R_ + pv.r(), writes=W_)
        A("act", OP("activation", out=S(0), in_=S(0), func=AF.Exp), reads=R_, writes=W_)
        A("dve", OP("tensor_tensor", out=S(7), in0=LI, in1=S(6), op=ALU.mult), reads=R_ + pv.r(), writes=W_)
        A("dve", OP("tensor_scalar", out=S(6), in0=S(7), scalar1=1.0 / (2.0 * math.pi), scalar2=MAGIC, op0=ALU.mult, op1=ALU.add), reads=R_, writes=W_)
        A("dve", OP("tensor_scalar", out=S(6), in0=S(6), scalar1=MAGIC, scalar2=None, op0=ALU.subtract), reads=R_, writes=W_)
        A("dve", OP("scalar_tensor_tensor", out=S(1), in0=S(7), scalar=1.0 / (2.0 * math.pi), in1=S(6), op0=ALU.mult, op1=ALU.subtract), reads=R_, writes=W_)
        A("act", OP("activation", out=S(7), in_=S(1), func=AF.Sin, scale=TWO_PI_S), reads=R_, writes=W_)
        A("act", OP("activation", out=S(6), in_=S(1), func=AF.Abs), reads=R_, writes=W_)
        A("act", OP("activation", out=S(6), in_=S(6), func=AF.Sin, scale=-TWO_PI_S, bias=math.pi / 2.0 - 1e-6), reads=R_, writes=W_)
        A("dve", OP("tensor_tensor", out=S(6), in0=S(6), in1=S(0), op=ALU.mult), reads=R_, writes=W_)
        A("dve", OP("tensor_scalar", out=S(6), in0=S(6), scalar1=-1.0, scalar2=None, op0=ALU.add), reads=R_, writes=W_)
        A("dve", OP("tensor_tensor", out=S(7), in0=S(7), in1=S(0), op=ALU.mult), reads=R_, writes=W_)
        A("dve", OP("tensor_tensor", out=S(5), in0=LR, in1=LR, op=ALU.mult), reads=R_ + pv.r(), writes=W_)
        A("dve", OP("tensor_tensor", out=S(4), in0=LI, in1=LI, op=ALU.mult), reads=R_ + pv.r(), writes=W_)
        A("dve", OP("tensor_tensor", out=S(5), in0=S(5), in1=S(4), op=ALU.add), reads=R_, writes=W_)
        A("dve", OP("reciprocal", out=S(5), in_=S(5)), reads=R_, writes=W_)
        A("dve", OP("tensor_tensor", out=S(2), in0=S(6), in1=LR, op=ALU.mult), reads=R_ + pv.r(), writes=W_)
        A("dve", OP("tensor_tensor", out=S(4), in0=S(7), in1=LI, op=ALU.mult), reads=R_ + pv.r(), writes=W_)
        A("dve", OP("tensor_tensor", out=S(2), in0=S(2), in1=S(4), op=ALU.add), reads=R_, writes=W_)
        A("dve", OP("tensor_tensor", out=S(2), in0=S(2), in1=S(5), op=ALU.mult), reads=R_, writes=W_)
        A("dve", OP("tensor_tensor", out=S(3), in0=S(7), in1=LR, op=ALU.mult), reads=R_ + pv.r(), writes=W_)
        A("dve", OP("tensor_tensor", out=S(4), in0=S(6), in1=LI, op=ALU.mult), reads=R_ + pv.r(), writes=W_)
        A("dve", OP("tensor_tensor", out=S(3), in0=S(3), in1=S(4), op=ALU.subtract), reads=R_, writes=W_)
        A("dve", OP("tensor_tensor", out=S(3), in0=S(3), in1=S(5), op=ALU.mult), reads=R_, writes=W_)
        A("dve", OP("tensor_copy", out=S(5), in_=S(3)), reads=R_, writes=W_)
        A("dve", OP("tensor_scalar", out=S(3), in0=S(3), scalar1=PV("sgn"), scalar2=None, op0=ALU.mult), reads=R_ + pv.r(), writes=W_)
        A("dve", OP("tensor_scalar", out=S(4), in0=S(2), scalar1=PV("nsgn"), scalar2=None, op0=ALU.mult), reads=R_ + pv.r(), writes=W_)
        for t in range(NT):
            A("dve", OP("tensor_scalar", out=s5off[:, :, t], in0=S(1), scalar1=float(TT * t), scalar2=MAGIC, op0=ALU.mult, op1=ALU.add),
              reads=R_, writes=s5off.r())
            A("dve", OP("tensor_scalar", out=s5off[:, :, t], in0=s5off[:, :, t], scalar1=MAGIC, scalar2=None, op0=ALU.subtract),
              reads=s5off.r(), writes=s5off.r())
            A("dve", OP("scalar_tensor_tensor", out=s5off[:, :, t], in0=S(1), scalar=float(TT * t), in1=s5off[:, :, t],
                                                           op0=ALU.mult, op1=ALU.subtract), reads=R_ + s5off.r(), writes=s5off.r())
        cn1 = ring.get(); cn2 = ring.get()
        A("sp", OP("dma_start", out=cn1[:, 0:512], in_=x1_d[l]), writes=cn1.r(), dma=True)
        A("sp", OP("dma_start", out=cn2[:, 0:512], in_=x2_d[l]), writes=cn2.r(), dma=True)
        v3 = lambda tl: tl[:, 0:512].rearrange("p (g c) -> p g c", c=16)
        bc = lambda j: s5p[:, :, j:j + 1].to_broadcast([128, 32, 16])
        t1 = ring.get(); t2 = ring.get()
        t1v = t1[:, 0:512].rearrange("p (g c) -> p g c", c=16)
        t2v = t2[:, 0:512].rearrange("p (g c) -> p g c", c=16)
        A("dve", OP("tensor_tensor", out=t1v, in0=v3(cn1), in1=bc(2), op=ALU.mult), reads=cn1.r() + R_, writes=t1.r())
        A("dve", OP("tensor_tensor", out=t2v, in0=v3(cn2), in1=bc(3), op=ALU.mult), reads=cn2.r() + R_, writes=t2.r())
        A("dve", OP("tensor_tensor", out=bst1[:], in0=t1[:, 0:512], in1=t2[:, 0:512], op=ALU.add), reads=t1.r() + t2.r(), writes=bst1.r())
        A("dve", OP("tensor_tensor", out=t1v, in0=v3(cn2), in1=bc(4), op=ALU.mult), reads=cn2.r() + R_, writes=t1.r())
        A("dve", OP("tensor_tensor", out=t2v, in0=v3(cn1), in1=bc(5), op=ALU.mult), reads=cn1.r() + R_, writes=t2.r())
        A("dve", OP("tensor_tensor", out=bst2[:], in0=t1[:, 0:512], in1=t2[:, 0:512], op=ALU.add), reads=t1.r() + t2.r(), writes=bst2.r())
        ring.free(t1, t2, cn1, cn2)
        A("pool", OP("memset", lhs_c[:].rearrange("p a g n -> p (a g n)"), 0.0), writes=lhs_c.r())
        dbg(f"bst1_{l}", bst1, bst1[:], [128, 512], BF16)
        dbg(f"s5p_{l}", s5p, s5p[:].rearrange("p g j -> p (g j)"), [128, 256])

        rmsnorm_to_xn(f"gmix{l}")
        dbg(f"xn{l}", xn, xn[:].rearrange("p c t -> p (c t)"), [128, 8 * SEQ], BF16)

        zbf = aux[:].rearrange("p (c t) -> p c t", c=4)
        def lru_chunk(c):
            wb = wbufs.get()
            wv_ = wb[:].rearrange("p (k n) -> p k n", k=8)
            load_w(wb, wv_[:, :, 0:128], w_in_d[l, :, c * 128:(c + 1) * 128].rearrange("(k p) n -> p k n", p=128))
            load_w(wb, wv_[:, :, 128:256], w_in_d[l, :, 512 + c * 128:512 + (c + 1) * 128].rearrange("(k p) n -> p k n", p=128))
            return wb, wv_

        def lru_unit(c, t, wb, wv_):
            bx = banks.get(); bg = banks.get()
            proj(bx, wb, lambda k: wv_[:, k, 0:128], t)
            proj(bg, wb, lambda k: wv_[:, k, 128:256], t)
            xp = ring.get(halo=True); gy = ring.get(); xc = ring.get(); xcb = ring.get()
            A("act", OP("activation", out=xp[:, 0:3], in_=lcar[:, c, 0:3], func=AF.Copy), reads=lcar.r(), writes=xp.r())
            A("act", OP("activation", out=xp[:, 3:3 + TT], in_=psum[:, bx, :], func=AF.Copy), reads=psum.r(bx), writes=xp.r())
            A("act", OP("activation", out=lcar[:, c, 0:3], in_=xp[:, TT:TT + 3], func=AF.Copy), reads=xp.r(), writes=lcar.r())
            A("act", OP("activation", out=gy[:, 0:TT], in_=psum[:, bg, :], func=AF.Gelu_apprx_tanh), reads=psum.r(bg), writes=gy.r())
            banks.free(bx, bg)
            A("act", OP("activation", out=xc[:, 0:TT], in_=xp[:, 3:3 + TT], func=AF.Identity,
                                                          scale=PV(f"lcw{l}", c * 4 + 3), bias=PV(f"lcb{l}", c)),
              reads=xp.r() + pv.r(), writes=xc.r())
            for k in range(3):
                A("dve", OP("scalar_tensor_tensor", out=xc[:, 0:TT], in0=xp[:, k:k + TT], scalar=PV(f"lcw{l}", c * 4 + k),
                                                                            in1=xc[:, 0:TT], op0=ALU.mult, op1=ALU.add),
                  reads=xp.r() + xc.r() + pv.r(), writes=xc.r())
            A("act", OP("activation", out=bf(xcb), in_=xc[:, 0:TT], func=AF.Copy), reads=xc.r(), writes=xcb.r())
            ring.free(xp)
            br_ = banks.get(); bi_ = banks.get()
            A("pe", OP("matmul", out=psum[:, br_, :], lhsT=wbd[:, 0, c, :], rhs=bf(xcb), start=True, stop=True),
              reads=wbd.r() + xcb.r(), writes=psum.r(br_))
            A("pe", OP("matmul", out=psum[:, bi_, :], lhsT=wbd[:, 1, c, :], rhs=bf(xcb), start=True, stop=True),
              reads=wbd.r() + xcb.r(), writes=psum.r(bi_))
            rr = ring.get(); ii = ring.get(); a2 = ring.get()
            A("act", OP("activation", out=rr[:, 0:TT], in_=psum[:, br_, :], func=AF.Tanh, scale=0.5, bias=lpar[:, 8 + c:9 + c]),
              reads=psum.r(br_) + lpar.r(), writes=rr.r())
            A("act", OP("activation", out=ii[:, 0:TT], in_=psum[:, bi_, :], func=AF.Tanh, scale=0.5, bias=lpar[:, 12 + c:13 + c]),
              reads=psum.r(bi_) + lpar.r(), writes=ii.r())
            banks.free(br_, bi_)
            ring.free(xcb)
            A("act", OP("activation", out=a2[:, 0:TT], in_=rr[:, 0:TT], func=AF.Exp, scale=lpar[:, c:c + 1], bias=lpar[:, c:c + 1]),
              reads=rr.r() + lpar.r(), writes=a2.r())
            A("act", OP("activation", out=rr[:, 0:TT], in_=rr[:, 0:TT], func=AF.Exp, scale=lpar[:, 4 + c:5 + c], bias=lpar[:, 4 + c:5 + c]),
              reads=rr.r() + lpar.r(), writes=rr.r())
            A("dve", OP("tensor_scalar", out=a2[:, 0:TT], in0=a2[:, 0:TT], scalar1=1.0, scalar2=-1e-30, op0=ALU.subtract, op1=ALU.min),
              reads=a2.r(), writes=a2.r())
            A("act", OP("activation", out=a2[:, 0:TT], in_=a2[:, 0:TT], func=AF.Ln, scale=-1.0), reads=a2.r(), writes=a2.r())
            A("act", OP("activation", out=a2[:, 0:TT], in_=a2[:, 0:TT], func=AF.Exp, scale=0.5), reads=a2.r(), writes=a2.r())
            A("dve", OP("scalar_tensor_tensor", out=ii[:, 0:TT], in0=ii[:, 0:TT], scalar=1.0, in1=xc[:, 0:TT], op0=ALU.add, op1=ALU.mult),
              reads=ii.r() + xc.r(), writes=ii.r())
            A("dve", OP("scalar_tensor_tensor", out=ii[:, 0:TT], in0=ii[:, 0:TT], scalar=0.5, in1=a2[:, 0:TT], op0=ALU.mult, op1=ALU.mult),
              reads=ii.r() + a2.r(), writes=ii.r())
            hh = xc
            A("dve", OP("tensor_tensor_scan", out=hh[:, 0:TT], data0=rr[:, 0:TT], data1=ii[:, 0:TT],
                                                                        initial=lcar[:, c, 3:4], op0=ALU.mult, op1=ALU.add),
              reads=rr.r() + ii.r() + lcar.r(), writes=hh.r())
            A("act", OP("activation", out=lcar[:, c, 3:4], in_=hh[:, TT - 1:TT], func=AF.Copy), reads=hh.r(), writes=lcar.r())
            A("dve", OP("tensor_tensor", out=mixed[:, c, tsl(t)], in0=hh[:, 0:TT], in1=gy[:, 0:TT], op=ALU.mult),
              reads=hh.r() + gy.r(), writes=mixed.r(c * 4 + t))
            ring.free(rr, ii, a2, xc, gy)

        def s5_chunk(c):
            wb = w5pool.get()
            wv_ = wb
            load_w(wb, wv_[:, :, 0:128], w_in_d[l, :, 1024 + c * 128:1024 + (c + 1) * 128].rearrange("(k p) n -> p k n", p=128))
            for which, bst in enumerate((bst1, bst2)):
                b = banks.get()
                tpv = psum[:, b, :].bitcast(BF16)
                A("pe", OP("transpose", out=tpv[:, 0:128], in_=bst[:, c * 128:(c + 1) * 128], identity=ident_b[:]),
                  reads=bst.r() + ident_b.r(), writes=psum.r(b))
                tb = ring.get()
                A("act", OP("activation", out=bf(tb, 128), in_=tpv[:, 0:128], func=AF.Copy), reads=psum.r(b), writes=tb.r())
                banks.free(b)
                for g in range(8):
                    A("dve", OP("tensor_scalar", out=lhs_b[:, which, g, :], in0=bf(tb, 128), scalar1=PV("rowmask", g),
                                                                                scalar2=None, op0=ALU.mult),
                      reads=tb.r() + pv.r(), writes=lhs_b.r())
                ring.free(tb)
            for which, (cd, sname) in enumerate(((c1_d, "nsgn"), (c2_d, None))):
                cn = ring.get()
                A("sp", OP("dma_start", out=cn[:, 0:128], in_=cd[l, :, c * 128:(c + 1) * 128]), writes=cn.r(), dma=True)
                b = banks.get()
                A("pe", OP("transpose", out=psum[:, b, 0:128], in_=cn[:, 0:128], identity=ident_f),
                  reads=cn.r() + cst.r(), writes=psum.r(b))
                ring.free(cn)
                for g in range(8):
                    if sname is not None:
                        A("dve", OP("tensor_scalar", out=lhs_c[:, which, g, g * 16:(g + 1) * 16], in0=psum[:, b, g * 16:(g + 1) * 16],
                                                                                  scalar1=PV("nsgn"), scalar2=None, op0=ALU.mult),
                          reads=psum.r(b) + pv.r(), writes=lhs_c.r())
                    else:
                        A("dve", OP("tensor_scalar", out=lhs_c[:, which, g, g * 16:(g + 1) * 16], in0=psum[:, b, g * 16:(g + 1) * 16],
                                                                                  scalar1=-1.0, scalar2=None, op0=ALU.mult),
                          reads=psum.r(b), writes=lhs_c.r())
                banks.free(b)
            return wb, wv_

        def s5_unit(c, t, wb, wv_):
            bu = banks.get()
            proj(bu, wb, lambda k: wv_[:, k, 0:128], t)
            uf = ring.get(); ub = ring.get()
            A("act", OP("activation", out=uf[:, 0:TT], in_=psum[:, bu, :], func=AF.Copy), reads=psum.r(bu), writes=uf.r())
            A("act", OP("activation", out=bf(ub), in_=psum[:, bu, :], func=AF.Copy), reads=psum.r(bu), writes=ub.r())
            banks.free(bu)
            by = banks.get()
            for g in range(8):
                G = c * 8 + g
                b1 = banks.get(); b2 = banks.get()
                A("pe", OP("matmul", out=psum[:, b1, :], lhsT=lhs_b[:, 0, g, :], rhs=bf(ub), start=True, stop=True),
                  reads=lhs_b.r() + ub.r(), writes=psum.r(b1))
                A("pe", OP("matmul", out=psum[:, b2, :], lhsT=lhs_b[:, 1, g, :], rhs=bf(ub), start=True, stop=True),
                  reads=lhs_b.r() + ub.r(), writes=psum.r(b2))
                uu = ring.get(); kk = ring.get(); tc_ = ring.get(); ts_ = ring.get()
                A("act", OP("activation", out=uu[:, 0:TT], in_=iota, func=AF.Identity, scale=s5p[:, G, 1:2], bias=s5off[:, G, t:t + 1]), reads=cst.r() + s5p.r() + s5off.r(), writes=uu.r())
                A("dve", OP("tensor_scalar", out=kk[:, 0:TT], in0=uu[:, 0:TT], scalar1=MAGIC, scalar2=MAGIC,
                                                                  op0=ALU.add, op1=ALU.subtract), reads=uu.r(), writes=kk.r())
                A("pool", OP("tensor_tensor", out=uu[:, 0:TT], in0=uu[:, 0:TT], in1=kk[:, 0:TT], op=ALU.subtract),
                  reads=uu.r() + kk.r(), writes=uu.r())
                A("act", OP("activation", out=ts_[:, 0:TT], in_=uu[:, 0:TT], func=AF.Sin, scale=TWO_PI_S), reads=uu.r(), writes=ts_.r())
                A("act", OP("activation", out=kk[:, 0:TT], in_=uu[:, 0:TT], func=AF.Abs), reads=uu.r(), writes=kk.r())
                A("act", OP("activation", out=tc_[:, 0:TT], in_=kk[:, 0:TT], func=AF.Sin, scale=-TWO_PI_S, bias=math.pi / 2.0 - 1e-6),
                  reads=kk.r(), writes=tc_.r())
                m1 = uu; m2 = kk
                A("dve", OP("tensor_tensor", out=m1[:, 0:TT], in0=psum[:, b1, :], in1=tc_[:, 0:TT], op=ALU.mult),
                  reads=psum.r(b1) + tc_.r(), writes=m1.r())
                A("dve", OP("tensor_tensor", out=m2[:, 0:TT], in0=psum[:, b2, :], in1=ts_[:, 0:TT], op=ALU.mult),
                  reads=psum.r(b2) + ts_.r(), writes=m2.r())
                banks.free(b1, b2)
                A("dve", OP("tensor_tensor", out=m1[:, 0:TT], in0=m1[:, 0:TT], in1=m2[:, 0:TT], op=ALU.add), reads=m1.r() + m2.r(), writes=m1.r())
                zz = m2
                A("dve", OP("tensor_tensor_scan", out=zz[:, 0:TT], data0=s5p[:, G, 0:1].to_broadcast([128, TT]), data1=m1[:, 0:TT],
                                                                          initial=zcar[:, G:G + 1], op0=ALU.mult, op1=ALU.add),
                  reads=m1.r() + s5p.r() + zcar.r(), writes=zz.r())
                A("act", OP("activation", out=zcar[:, G:G + 1], in_=zz[:, TT - 1:TT], func=AF.Copy), reads=zz.r(), writes=zcar.r())
                w12 = m1
                A("pool", OP("tensor_tensor", out=bf(w12, TT, 0), in0=zz[:, 0:TT], in1=tc_[:, 0:TT], op=ALU.mult),
                  reads=zz.r() + tc_.r(), writes=w12.r())
                A("pool", OP("tensor_tensor", out=bf(w12, TT, TT), in0=zz[:, 0:TT], in1=ts_[:, 0:TT], op=ALU.mult),
                  reads=zz.r() + ts_.r(), writes=w12.r())
                A("pe", OP("matmul", out=psum[:, by, :], lhsT=lhs_c[:, 0, g, :], rhs=bf(w12, TT, 0), start=(g == 0), stop=False),
                  reads=lhs_c.r() + w12.r(), writes=psum.r(by))
                A("pe", OP("matmul", out=psum[:, by, :], lhsT=lhs_c[:, 1, g, :], rhs=bf(w12, TT, TT), start=False, stop=(g == 7)),
                  reads=lhs_c.r() + w12.r(), writes=psum.r(by))
                ring.free(uu, kk, tc_, ts_)
            A("dve", OP("scalar_tensor_tensor", out=uf[:, 0:TT], in0=uf[:, 0:TT], scalar=PV(f"s5d{l}", c), in1=psum[:, by, :],
                                                             op0=ALU.mult, op1=ALU.add), reads=uf.r() + psum.r(by) + pv.r(), writes=uf.r())
            banks.free(by)
            A("act", OP("activation", out=zbf[:, c, tsl(t)], in_=uf[:, 0:TT], func=AF.Gelu_apprx_tanh), reads=uf.r(), writes=aux.r(c * 4 + t))
            ring.free(uf, ub)

        ring.enable_extra(rot_views)
        for c in range(4):
            lwb = lru_chunk(c)
            swb = s5_chunk(c)
            for t in range(NT):
                s5_unit(c, t, *swb)
                lru_unit(c, t, *lwb)
            wbufs.free(lwb[0])
            w5pool.free(swb[0])
        dbg(f"ylru_raw{l}", mixed, mixed[:].rearrange("p c t -> p (c t)"), [128, 4 * SEQ], BF16)
        if upto == "lru_raw":
            break
        group_post(l, mixed, "lnorm", 0, wbufs, 512.0)
        dbg(f"ylru{l}", mixed, mixed[:].rearrange("p c t -> p (c t)"), [128, 4 * SEQ], BF16)
        if upto == "lru":
            break

        dbg(f"s5z{l}", aux, aux[:], [128, 4 * SEQ], BF16)
        wg_t = wbufs.get()
        wgl = wg_t[:, 0:2048].rearrange("p (k n) -> p k n", k=4)
        load_w(wg_t, wgl, w_glu_d[l].rearrange("(k p) n -> p k n", p=128))
        for t in range(NT):
            for oc in range(4):
                b = banks.get()
                for k in range(4):
                    A("pe", OP("matmul", out=psum[:, b, :], lhsT=wgl[:, k, oc * 128:(oc + 1) * 128], rhs=zbf[:, k, tsl(t)],
                                                                start=(k == 0), stop=(k == 3)), reads=wg_t.r() + aux.r(k * 4 + t), writes=psum.r(b))
                sg = ring.get()
                A("act", OP("activation", out=sg[:, 0:TT], in_=psum[:, b, :], func=AF.Sigmoid, bias=PV(f"s5bg{l}", oc)),
                  reads=psum.r(b) + pv.r(), writes=sg.r())
                banks.free(b)
                A("dve", OP("tensor_tensor", out=mixed[:, oc, tsl(t)], in0=zbf[:, oc, tsl(t)], in1=sg[:, 0:TT], op=ALU.mult),
                  reads=sg.r() + aux.r(oc * 4 + t), writes=mixed.r(oc * 4 + t))
                ring.free(sg)
        wbufs.free(wg_t)
        group_post(l, mixed, "s5n", 1, wbufs, 512.0)
        dbg(f"ys5{l}", mixed, mixed[:].rearrange("p c t -> p (c t)"), [128, 4 * SEQ], BF16)
        if upto == "s5":
            break

        ring.disable_extra()
        build_rot()
        if l == 0:
            dbg("rcos", rcos, rcos[:], [128, SEQ])
            dbg("rsin", rsin, rsin[:], [128, SEQ])
        fw.pin = set(os.environ.get('KPIN', 'dve').split(',')) - {''}
        vh = aux[:].rearrange("p (n e) -> p n e", n=16)
        wv_t = wbufs.get()
        wvv = wv_t[:].rearrange("p (k n) -> p k n", k=8)
        load_w(wv_t, wvv, w_in_d[l, :, 2560:3072].rearrange("(k p) n -> p k n", p=128))
        for n in range(16):
            b = banks.get()
            t = n // 4
            for k in range(8):
                A("pe", OP("matmul", out=psum[:, b, :], lhsT=xn[:, k, n * 128:(n + 1) * 128], rhs=wvv[:, k, :],
                                                          start=(k == 0), stop=(k == 7)), reads=wv_t.r() + xs(k, t), writes=psum.r(b))
            for hd in range(4):
                A("act", OP("activation", out=vh[:, n, hd * 128:(hd + 1) * 128], in_=psum[:, b, hd * 128:(hd + 1) * 128],
                                                                 func=AF.Identity, scale=PV("vdec", hd)), reads=psum.r(b) + pv.r(), writes=aux.r(n))
            banks.free(b)
        wbufs.free(wv_t)
        wg_t = wbufs.get()
        wgv = wg_t[:].rearrange("p (k n) -> p k n", k=8)
        load_w(wg_t, wgv, w_in_d[l, :, 3072:3584].rearrange("(k p) n -> p k n", p=128))
        for hd in range(4):
            gam = GAMMAS[hd]
            gC = float(np.float32(np.exp(np.float32(np.log1p(-np.float32(2.0) ** np.float32(-5.0 - hd))) * np.float32(128.0))))
            wb = wbufs.get()
            wq = wb[:].rearrange("p (k n) -> p k n", k=8)
            qb = 1536 + hd * 128
            kb = 2048 + hd * 128
            src = lambda c0, c1: w_in_d[l, :, c0:c1].rearrange("(k p) n -> p k n", p=128)
            load_w(wb, wq[:, :, 0:128], src(qb, qb + 128))
            load_w(wb, wq[:, :, 128:192], src(qb + 64, qb + 128))
            load_w(wb, wq[:, :, 192:256], src(qb, qb + 64))
            load_w(wb, wq[:, :, 256:384], src(kb, kb + 128))
            load_w(wb, wq[:, :, 384:448], src(kb + 64, kb + 128))
            load_w(wb, wq[:, :, 448:512], src(kb, kb + 64))
            for t in range(NT):
                rot = []
                for which in range(2):
                    ba = banks.get(); bb = banks.get()
                    proj(ba, wb, lambda k, o=which * 256: wq[:, k, o:o + 128], t)
                    proj(bb, wb, lambda k, o=which * 256 + 128: wq[:, k, o:o + 128], t)
                    t1 = ring.get(); t2 = ring.get(); qr = ring.get()
                    A("dve", OP("tensor_tensor", out=t1[:, 0:TT], in0=psum[:, ba, :], in1=rcos[:, tsl(t)], op=ALU.mult),
                      reads=psum.r(ba) + rcos.r(t), writes=t1.r())
                    A("dve", OP("tensor_tensor", out=t2[:, 0:TT], in0=psum[:, bb, :], in1=rsin[:, tsl(t)], op=ALU.mult),
                      reads=psum.r(bb) + rsin.r(t), writes=t2.r())
                    banks.free(ba, bb)
                    A("pool", OP("tensor_tensor", out=bf(qr), in0=t1[:, 0:TT], in1=t2[:, 0:TT], op=ALU.add),
                      reads=t1.r() + t2.r(), writes=qr.r())
                    ring.free(t1, t2)
                    rot.append(qr)
                qr, kr = rot
                if hd == 0 and t == 0:
                    dbg(f"qr{l}", qr, bf(qr), [128, TT], BF16)
                    dbg(f"kr{l}", kr, bf(kr), [128, TT], BF16)
                bt = banks.get()
                ktp = psum[:, bt, :].bitcast(BF16)
                for j in range(4):
                    A("pe", OP("transpose", out=ktp[:, j * 128:(j + 1) * 128], in_=bf(kr, 128, j * 128), identity=ident_b[:]),
                      reads=kr.r() + ident_b.r(), writes=psum.r(bt))
                ktm = ring.get()
                A("act", OP("activation", out=bf(ktm), in_=ktp[:, 0:TT], func=AF.Copy), reads=psum.r(bt), writes=ktm.r())
                banks.free(bt)
                bs = banks.get()
                for j in range(4):
                    A("pe", OP("matmul", out=psum[:, bs, j * 128:(j + 1) * 128], lhsT=bf(kr, 128, j * 128), rhs=bf(qr, 128, j * 128),
                                                                  start=True, stop=True), reads=kr.r() + qr.r(), writes=psum.r(bs))
                pT = ring.get()
                A("dve", OP("tensor_tensor", out=bf(pT).rearrange("p (j c) -> p j c", j=4), in0=psum[:, bs, :].rearrange("p (j c) -> p j c", j=4),
                                                          in1=maskT.unsqueeze(1).to_broadcast([128, 4, 128]), op=ALU.mult),
                  reads=psum.r(bs) + cst.r(), writes=pT.r())
                banks.free(bs)
                bkv = banks.get()
                for j in range(4):
                    n = t * 4 + j
                    A("pe", OP("matmul", out=psum[:, bkv, j * 128:(j + 1) * 128], lhsT=bf(ktm, 128, j * 128),
                                                                  rhs=vh[:, n, hd * 128:(hd + 1) * 128], start=True, stop=True),
                      reads=ktm.r() + aux.r(n), writes=psum.r(bkv))
                ring.free(ktm)
                bxo = banks.get()
                for j in range(4):
                    n = t * 4 + j
                    pb = pbs[n % 2]
                    if n > 0:
                        A("act", OP("activation", out=pb[:], in_=ust[:], func=AF.Identity, scale=float(gC * (128.0 ** -0.5))),
                          reads=ust.r(), writes=pb.r())
                    A("pe", OP("matmul", out=psum[:, bxo, j * 128:(j + 1) * 128], lhsT=vh[:, n, hd * 128:(hd + 1) * 128],
                                                                rhs=bf(pT, 128, j * 128), start=True, stop=(n == 0)),
                      reads=aux.r(n) + pT.r(), writes=psum.r(bxo))
                    if n > 0:
                        A("pe", OP("matmul", out=psum[:, bxo, j * 128:(j + 1) * 128], lhsT=pb[:], rhs=bf(qr, 128, j * 128),
                                                                      start=False, stop=True), reads=pb.r() + qr.r(), writes=psum.r(bxo))
                        A("dve", OP("scalar_tensor_tensor", out=ust[:], in0=ust[:], scalar=gC, in1=psum[:, bkv, j * 128:(j + 1) * 128],
                                                                       op0=ALU.mult, op1=ALU.add), reads=ust.r() + psum.r(bkv), writes=ust.r())
                    else:
                        A("dve", OP("tensor_copy", out=ust[:], in_=psum[:, bkv, j * 128:(j + 1) * 128]), reads=psum.r(bkv), writes=ust.r())
                banks.free(bkv)
                ring.free(pT, kr)
                oT = ring.get()
                A("dve", OP("tensor_tensor", out=oT[:, 0:TT].rearrange("p (j c) -> p j c", j=4), in0=psum[:, bxo, :].rearrange("p (j c) -> p j c", j=4),
                                                          in1=cst[:, C_QDEC + hd * 128:C_QDEC + (hd + 1) * 128].unsqueeze(1).to_broadcast([128, 4, 128]), op=ALU.mult),
                  reads=psum.r(bxo) + cst.r(), writes=oT.r())
                banks.free(bxo)
                ring.free(qr)
                ob = ring.get()
                A("act", OP("activation", out=bf(ob, TT, 0), in_=oT[:, 0:TT], func=AF.Copy), reads=oT.r(), writes=ob.r())
                A("act", OP("activation", out=bf(ob, TT, TT), in_=oT[:, 0:TT], func=AF.Square), reads=oT.r(), writes=ob.r())
                bm = banks.get(); bq = banks.get()
                A("pe", OP("matmul", out=psum[:, bm, :], lhsT=o128_b[:], rhs=bf(ob, TT, 0), start=True, stop=True),
                  reads=ob.r() + o128_b.r(), writes=psum.r(bm))
                A("pe", OP("matmul", out=psum[:, bq, :], lhsT=o128_b[:], rhs=bf(ob, TT, TT), start=True, stop=True),
                  reads=ob.r() + o128_b.r(), writes=psum.r(bq))
                ring.free(ob)
                m2 = ring.get()
                A("act", OP("activation", out=m2[:, 0:TT], in_=psum[:, bm, :], func=AF.Square), reads=psum.r(bm), writes=m2.r())
                A("dve", OP("tensor_tensor", out=m2[:, 0:TT], in0=psum[:, bq, :], in1=m2[:, 0:TT], op=ALU.subtract),
                  reads=psum.r(bq) + m2.r(), writes=m2.r())
                banks.free(bq)
                A("dve", OP("tensor_scalar", out=m2[:, 0:TT], in0=m2[:, 0:TT], scalar1=0.0, scalar2=None, op0=ALU.max), reads=m2.r(), writes=m2.r())
                A("act", OP("activation", out=m2[:, 0:TT], in_=m2[:, 0:TT], func=AF.Ln, bias=NORM_EPS), reads=m2.r(), writes=m2.r())
                A("act", OP("activation", out=m2[:, 0:TT], in_=m2[:, 0:TT], func=AF.Exp, scale=-0.5), reads=m2.r(), writes=m2.r())
                A("dve", OP("tensor_tensor", out=oT[:, 0:TT], in0=oT[:, 0:TT], in1=psum[:, bm, :], op=ALU.subtract),
                  reads=oT.r() + psum.r(bm), writes=oT.r())
                banks.free(bm)
                A("dve", OP("tensor_tensor", out=oT[:, 0:TT], in0=oT[:, 0:TT], in1=m2[:, 0:TT], op=ALU.mult),
                  reads=oT.r() + m2.r(), writes=oT.r())
                ring.free(m2)
                bg = banks.get()
                proj(bg, wg_t, lambda k: wgv[:, k, hd * 128:(hd + 1) * 128], t)
                sg = ring.get()
                A("act", OP("activation", out=sg[:, 0:TT], in_=psum[:, bg, :], func=AF.Silu), reads=psum.r(bg), writes=sg.r())
                banks.free(bg)
                A("dve", OP("scalar_tensor_tensor", out=mixed[:, hd, tsl(t)], in0=oT[:, 0:TT], scalar=PV(f"rnorm{l}", hd), in1=sg[:, 0:TT],
                                                                        op0=ALU.mult, op1=ALU.mult), reads=oT.r() + sg.r() + pv.r(), writes=mixed.r(hd * 4 + t))
                ring.free(oT, sg)
            wbufs.free(wb)
        wbufs.free(wg_t)
        dbg(f"yret{l}", mixed, mixed[:].rearrange("p c t -> p (c t)"), [128, 4 * SEQ], BF16)
        fw.pin = set()
        group_post(l, mixed, None, 2, wbufs, 512.0)
        dbg(f"hmix{l}", hT, hT[:].rearrange("p c t -> p (c t)"), [128, 8 * SEQ])
        for cm in reversed(phase):
            cm.__exit__(None, None, None)
        phase = []
        fw.barrier()
        if upto == "mix":
            break

        rmsnorm_to_xn(f"gffn{l}")
        actb = psb("actb", [128, 4, SEQ], BF16, nslots=16)
        wus = Pool([psb(f"wu{i}", [128, 8, 1024], BF16) for i in range(2)])
        wds = Pool([psb(f"wd{i}", [128, 4, 1024], BF16) for i in range(2)])
        fcar = psb("fcar", [128, 4, 2, 2], F32, nslots=4)

        def ffn_load(grp):
            j0 = grp * 4
            wu = wus.get(); wd = wds.get()
            load_w(wu, wu[:, :, 0:512], w_up_d[l, :, j0 * 128:(j0 + 4) * 128].rearrange("(k p) n -> p k n", p=128))
            load_w(wu, wu[:, :, 512:1024], w_up_d[l, :, D_FF + j0 * 128:D_FF + (j0 + 4) * 128].rearrange("(k p) n -> p k n", p=128))
            load_w(wd, wd[:], w_down_d[l, j0 * 128:(j0 + 4) * 128, :].rearrange("(k p) n -> p k n", p=128))
            return wu, wd

        def ffn_up(grp, jj, t, wu):
            j = grp * 4 + jj
            outs = []
            for which in range(2):
                ch = j + 24 * which
                b = banks.get()
                proj(b, wu, lambda k, o=which * 512 + jj * 128: wu[:, k, o:o + 128], t)
                vc = ring.get()
                w0 = PV(f"fcw{l}", ch * 3 + 0); w1 = PV(f"fcw{l}", ch * 3 + 1); w2 = PV(f"fcw{l}", ch * 3 + 2)
                A("act", OP("activation", out=vc[:, 0:TT], in_=psum[:, b, :], func=AF.Identity, scale=w2, bias=PV(f"fcb{l}", ch)),
                  reads=psum.r(b) + pv.r(), writes=vc.r())
                A("dve", OP("scalar_tensor_tensor", out=vc[:, 1:TT], in0=psum[:, b, 0:TT - 1], scalar=w1, in1=vc[:, 1:TT],
                            op0=ALU.mult, op1=ALU.add), reads=psum.r(b) + vc.r() + pv.r(), writes=vc.r())
                A("dve", OP("scalar_tensor_tensor", out=vc[:, 2:TT], in0=psum[:, b, 0:TT - 2], scalar=w0, in1=vc[:, 2:TT],
                            op0=ALU.mult, op1=ALU.add), reads=psum.r(b) + vc.r() + pv.r(), writes=vc.r())
                if t > 0:
                    A("dve", OP("scalar_tensor_tensor", out=vc[:, 0:1], in0=fcar[:, jj, which, 1:2], scalar=w1, in1=vc[:, 0:1],
                                op0=ALU.mult, op1=ALU.add), reads=fcar.r(jj) + vc.r() + pv.r(), writes=vc.r())
                    A("dve", OP("scalar_tensor_tensor", out=vc[:, 0:2], in0=fcar[:, jj, which, 0:2], scalar=w0, in1=vc[:, 0:2],
                                op0=ALU.mult, op1=ALU.add), reads=fcar.r(jj) + vc.r() + pv.r(), writes=vc.r())
                if t < NT - 1:
                    A("act", OP("activation", out=fcar[:, jj, which, :], in_=psum[:, b, TT - 2:TT], func=AF.Copy),
                      reads=psum.r(b), writes=fcar.r(jj))
                banks.free(b)
                outs.append(vc)
            vc, gc = outs
            A("act", OP("activation", out=gc[:, 0:TT], in_=gc[:, 0:TT], func=AF.Gelu_apprx_tanh), reads=gc.r(), writes=gc.r())
            A("dve", OP("tensor_tensor", out=actb[:, jj, tsl(t)], in0=gc[:, 0:TT], in1=vc[:, 0:TT], op=ALU.mult),
              reads=vc.r() + gc.r(), writes=actb.r(jj * 4 + t))
            ring.free(vc, gc)

        def ffn_down(t, wd):
            for dc in range(8):
                b = banks.get()
                for k in range(4):
                    A("pe", OP("matmul", out=psum[:, b, :], lhsT=wd[:, k, dc * 128:(dc + 1) * 128], rhs=actb[:, k, tsl(t)],
                               start=(k == 0), stop=(k == 3)), reads=wd.r() + actb.r(k * 4 + t), writes=psum.r(b))
                A("dve", OP("tensor_tensor", out=hT[:, dc, tsl(t)], in0=hT[:, dc, tsl(t)], in1=psum[:, b, :], op=ALU.add),
                  reads=hs(dc, t) + psum.r(b), writes=hs(dc, t))
                banks.free(b)

        cur = ffn_load(0)
        for t in range(NT):
            for jj in range(4):
                ffn_up(0, jj, t, cur[0])
        for grp in range(6):
            nxt = ffn_load(grp + 1) if grp + 1 < 6 else None
            for t in range(NT):
                ffn_down(t, cur[1])
                if nxt is not None:
                    for jj in range(4):
                        ffn_up(grp + 1, jj, t, nxt[0])
            wus.free(cur[0]); wds.free(cur[1])
            cur = nxt
        dbg(f"hffn{l}", hT, hT[:].rearrange("p c t -> p (c t)"), [128, 8 * SEQ])
        for cm in reversed(phase):
            cm.__exit__(None, None, None)
        phase = []
        fw.barrier()

    for t in range(NT):
        b = banks.get()
        for c in range(8):
            sq = ring.get()
            A("act", OP("activation", out=bf(sq), in_=hT[:, c, tsl(t)], func=AF.Square), reads=hs(c, t), writes=sq.r())
            A("pe", OP("matmul", out=psum[:, b, :], lhsT=ones_b[:], rhs=bf(sq), start=(c == 0), stop=(c == 7)),
              reads=sq.r() + ones_b.r(), writes=psum.r(b))
            ring.free(sq)
        rs = ring.get()
        A("act", OP("activation", out=rs[:, 0:TT], in_=psum[:, b, :], func=AF.Ln, scale=1.0 / D_MODEL, bias=NORM_EPS), reads=psum.r(b), writes=rs.r())
        banks.free(b)
        A("act", OP("activation", out=rs[:, 0:TT], in_=rs[:, 0:TT], func=AF.Exp, scale=-0.5), reads=rs.r(), writes=rs.r())
        for c in range(8):
            ot = ring.get()
            A("dve", OP("scalar_tensor_tensor", out=ot[:, 0:TT], in0=hT[:, c, tsl(t)], scalar=PV("gfin", c), in1=rs[:, 0:TT],
                                                                         op0=ALU.mult, op1=ALU.mult), reads=hs(c, t) + rs.r() + pv.r(), writes=ot.r())
            ins = A("sp", OP("dma_start", out=outT_d[c * 128:(c + 1) * 128, tsl(t)], in_=ot[:, 0:TT]), reads=ot.r(), dma=True)
            out_dmas.append(ins)
            ring.free(ot)
        ring.free(rs)

    for cm in reversed(phase):
        cm.__exit__(None, None, None)
    fin = A("sp", None)
    for ins in out_dmas:
        fin.preds[ins] = True
    fw.emit()
    fw.close()
    return nc, dbg_out


def make_in_maps(inputs):
    inp = {k: np.asarray(v) for k, v in inputs.items()}
    pv = host_pv(inp)
    cst = host_consts()
    wbd, x1, x2, c1, c2 = host_struct(inp)
    shared = {
        "pv": pv, "cst": cst,
        "wbd": np.ascontiguousarray(wbd.reshape(DEPTH, 128, 1024)),
        "s5x1": x1, "s5x2": x2,
        "s5c1": np.ascontiguousarray(c1.reshape(DEPTH, 128, 512)),
        "s5c2": np.ascontiguousarray(c2.reshape(DEPTH, 128, 512)),
        "w_in": np.ascontiguousarray(inp["w_in"], dtype=np.float32),
        "s5_w_glu": np.ascontiguousarray(inp["s5_w_glu"], dtype=np.float32),
        "w_out": np.ascontiguousarray(inp["w_out"], dtype=np.float32),
        "w_up": np.ascontiguousarray(inp["w_up"], dtype=np.float32),
        "w_down": np.ascontiguousarray(inp["w_down"], dtype=np.float32),
    }
    maps = []
    for b in range(inp["x"].shape[0]):
        m = dict(shared)
        m["xT"] = np.ascontiguousarray(inp["x"][b].T.astype(np.float32))
        m["pos"] = np.ascontiguousarray(inp["positions"][b].astype(np.int32).reshape(1, SEQ))
        maps.append(m)
    return maps


_NC_CACHE = {}


def kernel(**inputs):
    if "nc" not in _NC_CACHE:
        _NC_CACHE["nc"] = build()[0]
    nc = _NC_CACHE["nc"]
    maps = make_in_maps(inputs)
    res = run_bass_kernel_spmd(nc, maps, core_ids=list(range(len(maps))))
    out = np.stack([np.ascontiguousarray(r["outT"].T) for r in res.results], axis=0)
    return out.astype(np.float32)
```

```python
import math
import os
from collections import deque
import numpy as np
import concourse.bass as bass
import concourse.mybir as mybir
from concourse.bass_utils import run_bass_kernel_spmd

F32 = mybir.dt.float32
BF16 = mybir.dt.bfloat16
I32 = mybir.dt.int32
AF = mybir.ActivationFunctionType
ALU = mybir.AluOpType

ENGS = ("pe", "act", "dve", "pool", "sp")
N_DMA_SEMS = 24

D_MODEL = 1024
SEQ = 2048
DEPTH = 2
TT = 512
NT = SEQ // TT
IN_WIDTH = 3584
D_FF = 3072
NORM_EPS = 1e-6
MAGIC = 12582912.0
TWO_PI_S = 6.2831850
GAMMAS = [1.0 - 2.0 ** (-5.0 - h) for h in range(4)]


class Reg:
    __slots__ = ("name", "w", "rds")

    def __init__(self, name):
        self.name = name
        self.w = None
        self.rds = []


class T:
    def __init__(self, h, name, nslots=1):
        self.h = h
        self.name = name
        self.regs = [Reg(f"{name}.{i}") for i in range(nslots)]

    def __getitem__(self, k):
        return self.h[k]

    def r(self, i=None, j=None):
        if i is None:
            return list(self.regs)
        if j is None:
            return [self.regs[i]]
        return self.regs[i:j]


class Ins:
    __slots__ = ("eng", "rec", "fn", "preds", "is_dma", "dma_sem", "dma_use", "cost", "lat", "seg",
                 "tset", "sched", "done", "rt", "pos", "waits", "needs_inc", "count", "waits_dma", "pin")

    def __init__(self, eng, rec, fn):
        self.eng = eng
        self.rec = rec
        self.fn = fn
        self.preds = {}
        self.is_dma = False
        self.dma_sem = None
        self.dma_use = None
        self.cost = 100.0
        self.lat = 0.0
        self.seg = 0
        self.tset = None
        self.sched = False
        self.done = 0.0
        self.rt = None
        self.pos = None
        self.waits = []
        self.needs_inc = False
        self.count = None
        self.pin = False


_ACT_GROUP = {}


def _act_group(func):
    if not _ACT_GROUP:
        _ACT_GROUP.update({AF.Exp: 1, AF.Ln: 1, AF.Gelu_apprx_tanh: 2, AF.Silu: 3, AF.Sin: 3, AF.Sigmoid: 4, AF.Sqrt: 5})
    return _ACT_GROUP.get(func)


def _free(ap):
    n = 1
    for d in ap.shape[1:]:
        n *= int(d)
    return n


def est_cost(eng, fn, is_dma):
    if fn is None:
        return 0.0, 0.0
    name, args, kw = fn
    if is_dma:
        out = kw["out"]
        nbytes = _free(out) * int(out.shape[0]) * 4
        return (1000.0 if eng == "pool" else 120.0), 2000.0 + nbytes / 150.0
    if eng == "pe":
        if name == "transpose":
            return 110.0, 0.0
        n = _free(kw["rhs"])
        return max(n, 64) / 1.9 + 10.0, 0.0
    out = kw.get("out")
    if out is None:
        out = args[0]
    n = _free(out)
    if eng == "act":
        return 150.0 + n / 1.2, 0.0
    if eng == "dve":
        if name == "tensor_tensor_scan":
            return 120.0 + 2.0 * n / 0.96, 0.0
        if name == "reciprocal":
            return 120.0 + 6.1 * n, 0.0
        if name in ("tensor_tensor", "scalar_tensor_tensor"):
            return 120.0 + n / 0.96, 0.0
        return 120.0 + n / 1.5, 0.0
    if eng == "pool":
        if name == "tensor_tensor":
            return 150.0 + 2.15 * n, 0.0
        return 150.0 + 1.2 * n, 0.0
    return 100.0, 0.0


class FW:
    SEM_LAT = 500.0
    WINDOW = int(os.environ.get('KW', '80'))
    WIN_ENG = {e: int(os.environ.get('KW_' + e.upper(), '0')) for e in ('pe', 'act', 'dve', 'pool', 'sp')}

    def __init__(self, nc):
        self.nc = nc
        self.all = []
        self.dma_rr = 0
        self.dma_rr_pool = 0
        self.dma_uses = [0] * N_DMA_SEMS
        self.dma_last = [None] * N_DMA_SEMS
        self.seg = 0
        self.seg_dma_uses = []
        self.pool_dmas = []
        self.pin = set()
        self.unpin_names = {'scalar_tensor_tensor', 'tensor_tensor'}
        self._stack = []

    def sb(self, name, shape, dtype, nslots=1):
        cm = self.nc.sbuf_tensor(name, list(shape), dtype)
        h = cm.__enter__()
        self._stack.append(cm)
        return T(h, name, nslots)

    def ps(self, name, shape, dtype, nslots=1):
        cm = self.nc.psum_tensor(name, list(shape), dtype)
        h = cm.__enter__()
        self._stack.append(cm)
        return T(h, name, nslots)

    @staticmethod
    def _add_pred(ins, p, kind):
        if p is None or p is ins:
            return
        if p.is_dma or ins.is_dma or p.eng != ins.eng:
            needs = True
        else:
            needs = (ins.eng != "pe")
        ins.preds[p] = ins.preds.get(p, False) or needs

    def op(self, eng, fn, reads=(), writes=(), dma=False):
        ins = Ins(eng, len(self.all), fn)
        ins.seg = self.seg
        ins.is_dma = dma
        ins.pin = (eng in self.pin) and not (fn is not None and fn[0] in self.unpin_names)
        self.all.append(ins)
        ins.cost, ins.lat = est_cost(eng, fn, dma)
        if eng == "act" and fn is not None and fn[0] == "activation":
            ins.tset = _act_group(fn[2].get("func"))
        if dma:
            half = N_DMA_SEMS // 2
            if eng == "pool":
                k = half + self.dma_rr_pool
                self.dma_rr_pool = (self.dma_rr_pool + 1) % half
            else:
                k = self.dma_rr
                self.dma_rr = (k + 1) % half
            self._add_pred(ins, self.dma_last[k], "SEM")
            self.dma_last[k] = ins
            self.dma_uses[k] += 1
            ins.dma_sem = k
            ins.dma_use = self.dma_uses[k]
            if eng == "pool":
                self.pool_dmas.append(ins)
                if len(self.pool_dmas) > 4:
                    self._add_pred(ins, self.pool_dmas[-5], "SEM")
        for r in reads:
            self._add_pred(ins, r.w, "RAW")
        for r in writes:
            self._add_pred(ins, r.w, "WAW")
            for p in r.rds:
                self._add_pred(ins, p, "WAR")
        for r in reads:
            r.rds.append(ins)
        for r in writes:
            r.w = ins
            r.rds = []
        return ins

    def barrier(self):
        self.seg_dma_uses.append(list(self.dma_uses))
        self.seg += 1

    def schedule(self):
        nseg = self.seg + 1
        final = {e: [] for e in ENGS}
        eng_free = {e: 0.0 for e in ENGS}
        cur_set = None
        bar_marks = []
        for sg in range(nseg):
            pending = {e: [i for i in self.all if i.seg == sg and i.eng == e] for e in ENGS}
            remaining = sum(len(v) for v in pending.values())
            while remaining:
                best = None
                best_t = None
                for e in ENGS:
                    lst = pending[e]
                    ef = eng_free[e]
                    lim = min(len(lst), self.WIN_ENG[e] or self.WINDOW)
                    if lim and lst[0].pin:
                        lim = 1
                    for wi in range(lim):
                        c = lst[wi]
                        if wi > 0 and c.pin:
                            break
                        if c.rt is None:
                            rt = 0.0
                            ok = True
                            for p, needs in c.preds.items():
                                if not p.sched:
                                    ok = False
                                    break
                                d = p.done + (self.SEM_LAT if needs else 0.0)
                                if d > rt:
                                    rt = d
                            if not ok:
                                continue
                            c.rt = rt
                        t = c.rt if c.rt > ef else ef
                        if e == "act" and c.tset is not None and c.tset != cur_set:
                            t += 1300.0
                        if best is None or t < best_t or (t == best_t and c.rec < best[1].rec):
                            best = (e, c, wi)
                            best_t = t
                        if t <= ef:
                            break
                assert best is not None, "scheduler deadlock"
                e, c, wi = best
                pending[e].pop(wi)
                remaining -= 1
                if e == "act" and c.tset is not None:
                    cur_set = c.tset
                end = best_t + c.cost
                eng_free[e] = end
                c.done = end + c.lat
                c.sched = True
                c.pos = len(final[e])
                final[e].append(c)
            if sg < nseg - 1:
                tmax = max(eng_free.values())
                dmax = max([i.done for i in self.all if i.seg == sg and i.is_dma] + [0.0])
                tmax = max(tmax, dmax)
                marks = {}
                for e in ENGS:
                    b = Ins(e, -1, None)
                    b.sched = True
                    b.pos = len(final[e])
                    final[e].append(b)
                    marks[e] = b
                    eng_free[e] = tmax
                bar_marks.append(marks)
        self.final = final
        self.bar_marks = bar_marks
        self.est_total = max(eng_free.values())

    def emit(self):
        nc = self.nc
        self.schedule()
        final = self.final
        for bi, marks in enumerate(self.bar_marks):
            for e, b in marks.items():
                for e2 in ENGS:
                    if e2 == e:
                        continue
                    pos2 = marks[e2].pos
                    for j in range(pos2 - 1, -1, -1):
                        p = final[e2][j]
                        if p.fn is not None and not p.is_dma:
                            b.preds[p] = True
                            break
                b.waits_dma = self.seg_dma_uses[bi]
        for e in ENGS:
            seen = {}
            for ins in final[e]:
                for p, needs in ins.preds.items():
                    if not needs:
                        assert p.eng == e and p.pos < ins.pos, "ordering edge violated"
                        continue
                    if p.is_dma:
                        key = f"d{p.dma_sem}"
                        v = p.dma_use
                    else:
                        key = p.eng
                        v = p.pos
                        if p.eng == e:
                            assert p.pos < ins.pos
                    if seen.get(key, -1) >= v:
                        continue
                    seen[key] = v
                    ins.waits.append((key, p))
                    if not p.is_dma:
                        p.needs_inc = True
                wd = getattr(ins, "waits_dma", None) if ins.fn is None else None
                if wd is not None:
                    for k, u in enumerate(wd):
                        if u > 0 and seen.get(f"d{k}", -1) < u:
                            seen[f"d{k}"] = u
                            ins.waits.append((f"d{k}", u))
        sem_cms = []
        sems = {}
        for e in list(ENGS) + [f"d{k}" for k in range(N_DMA_SEMS)]:
            cm = nc.semaphore(f"s_{e}")
            sems[e] = cm.__enter__()
            sem_cms.append(cm)
        for e in ENGS:
            c = 0
            for ins in final[e]:
                if ins.needs_inc:
                    c += 1
                    ins.count = c

        def run(eng_name, eng):
            for ins in final[eng_name]:
                for (key, p) in ins.waits:
                    if isinstance(p, int):
                        eng.wait_ge(sems[key], 16 * p)
                    elif p.is_dma:
                        eng.wait_ge(sems[key], 16 * p.dma_use)
                    else:
                        eng.wait_ge(sems[key], p.count)
                if ins.fn is None:
                    continue
                name, args, kw = ins.fn
                bi = getattr(eng, name)(*args, **kw)
                if ins.dma_sem is not None:
                    bi.then_inc(sems[f"d{ins.dma_sem}"], 16)
                elif ins.needs_inc:
                    bi.then_inc(sems[eng_name], 1)

        with nc.Block() as block:
            @block.tensor
            def _(eng):
                run("pe", eng)

            @block.scalar
            def _(eng):
                run("act", eng)

            @block.vector
            def _(eng):
                run("dve", eng)

            @block.gpsimd
            def _(eng):
                run("pool", eng)

            @block.sync
            def _(eng):
                run("sp", eng)
        for cm in reversed(sem_cms):
            cm.__exit__(None, None, None)

    def close(self):
        for cm in reversed(self._stack):
            cm.__exit__(None, None, None)
        self._stack = []


def OP(name, *args, **kw):
    return (name, args, kw)


class TV:
    def __init__(self, parent, slot, ap):
        self.parent = parent
        self.slot = slot
        self.ap = ap
        self.is_view = True

    def __getitem__(self, k):
        return self.ap[k]

    def r(self):
        return self.parent.r(self.slot)


class Pool:
    def __init__(self, items):
        self.free_ = deque(items)
        self.n = len(items)
        self.extra = []

    def get(self, halo=False):
        assert self.free_, "scratch pool exhausted"
        if not halo:
            return self.free_.popleft()
        for i, it in enumerate(self.free_):
            if not getattr(it, "is_view", False):
                del self.free_[i]
                return it
        raise AssertionError("no halo-capable scratch tile free")

    def free(self, *items):
        for it in items:
            self.free_.append(it)

    def enable_extra(self, views):
        self.extra = list(views)
        for v in views:
            self.free_.appendleft(v)

    def disable_extra(self):
        for v in self.extra:
            assert v in self.free_, "extra scratch view still in use"
            self.free_.remove(v)
        self.extra = []


def _chunked(v, nchunk):
    return np.ascontiguousarray(np.asarray(v, np.float32).reshape(nchunk, 128).T)


class PVLayout:
    def __init__(self):
        self.idx = {}
        self.n = 0

    def add(self, name, k):
        self.idx[name] = (self.n, k)
        self.n += k


def pv_layout():
    pl = PVLayout()
    for l in range(DEPTH):
        for name, k in (("gmix", 8), ("gffn", 8), ("lcw", 16), ("lcb", 4), ("lba", 4), ("lbx", 4),
                        ("llam", 4), ("lnorm", 4), ("s5d", 4), ("s5bg", 4), ("s5n", 4), ("rnorm", 4),
                        ("fcw", 144), ("fcb", 48), ("s5lr", 32), ("s5li", 32), ("s5ldt", 32)):
            pl.add(f"{name}{l}", k)
    for name, k in (("gfin", 8), ("invf", 1), ("sgn", 1), ("nsgn", 1), ("vdec", 4), ("rowmask", 8)):
        pl.add(name, k)
    return pl


PVL = pv_layout()
C_ID = 0
C_IOTA = 128
C_MASK = C_IOTA + 512
C_QDEC = C_MASK + 128
NCST = C_QDEC + 512


def host_consts():
    cst = np.zeros((128, NCST), np.float32)
    cst[:, C_ID:C_ID + 128] = np.eye(128, dtype=np.float32)
    cst[:, C_IOTA:C_IOTA + 512] = np.arange(512, dtype=np.float32)[None, :]
    m = np.arange(128)[:, None]
    c = np.arange(128)[None, :]
    cst[:, C_MASK:C_MASK + 128] = np.where(c >= m, np.float32(128.0 ** -0.5), np.float32(0.0))
    for h in range(4):
        lg = np.log1p(-np.float32(2.0) ** np.float32(-5.0 - h)).astype(np.float32)
        cst[:, C_QDEC + h * 128:C_QDEC + (h + 1) * 128] = np.exp(lg * (np.arange(128, dtype=np.float32) + 1.0))[None, :]
    return cst


def host_pv(inp):
    pv = np.zeros((128, PVL.n), np.float32)

    def put(name, arr):
        o, k = PVL.idx[name]
        assert arr.shape == (128, k), (name, arr.shape, k)
        pv[:, o:o + k] = arr

    for l in range(DEPTH):
        put(f"gmix{l}", _chunked(inp["norm_mix"][l], 8))
        put(f"gffn{l}", _chunked(inp["norm_ffn"][l], 8))
        cw = np.asarray(inp["lru_conv_w"][l], np.float32)
        put(f"lcw{l}", np.ascontiguousarray(cw.reshape(4, 4, 128).transpose(2, 1, 0).reshape(128, 16)))
        put(f"lcb{l}", _chunked(inp["lru_conv_b"][l], 4))
        put(f"lba{l}", _chunked(np.asarray(inp["lru_ba"][l]).reshape(512), 4))
        put(f"lbx{l}", _chunked(np.asarray(inp["lru_bx"][l]).reshape(512), 4))
        put(f"llam{l}", _chunked(inp["lru_lambda"][l], 4))
        put(f"lnorm{l}", _chunked(inp["lru_norm"][l], 4))
        put(f"s5d{l}", _chunked(inp["s5_d"][l], 4))
        put(f"s5bg{l}", _chunked(inp["s5_b_glu"][l], 4))
        put(f"s5n{l}", _chunked(inp["s5_norm"][l], 4))
        put(f"rnorm{l}", _chunked(inp["ret_norm"][l], 4))
        fw_ = np.asarray(inp["ffn_conv_w"][l], np.float32)
        put(f"fcw{l}", np.ascontiguousarray(fw_.reshape(3, 48, 128).transpose(2, 1, 0).reshape(128, 144)))
        put(f"fcb{l}", _chunked(inp["ffn_conv_b"][l], 48))
        lr = np.asarray(inp["s5_lambda_re"][l], np.float32).T
        li = np.asarray(inp["s5_lambda_im"][l], np.float32).T
        put(f"s5lr{l}", np.concatenate([lr, lr], 0))
        put(f"s5li{l}", np.concatenate([li, li], 0))
        put(f"s5ldt{l}", np.broadcast_to(np.asarray(inp["s5_log_dt"][l], np.float32)[None, :], (128, 32)).copy())
    put("gfin", _chunked(inp["norm_final"], 8))
    half = 64
    inv = (np.float32(10000.0) ** (-np.arange(half, dtype=np.float32) * np.float32(2.0) / np.float32(128.0))).astype(np.float32)
    put("invf", np.concatenate([inv, inv])[:, None])
    sgn = np.concatenate([-np.ones(64, np.float32), np.ones(64, np.float32)])[:, None]
    put("sgn", sgn)
    put("nsgn", -sgn)
    vd = np.zeros((128, 4), np.float32)
    for h in range(4):
        lg = np.log1p(-np.float32(2.0) ** np.float32(-5.0 - h)).astype(np.float32)
        vd[:, h] = np.exp(-lg * (np.arange(128, dtype=np.float32) + 1.0))
    put("vdec", vd)
    rm = np.zeros((128, 8), np.float32)
    for g in range(8):
        rm[16 * g:16 * g + 16, g] = 1.0
    put("rowmask", rm)
    return pv


def host_struct(inp):
    wbd = np.zeros((DEPTH, 128, 2, 4, 128), np.float32)
    for l in range(DEPTH):
        for which, nm in enumerate(("lru_wa", "lru_wx")):
            w = np.asarray(inp[nm][l], np.float32)
            for c in range(4):
                for hb in range(2):
                    wbd[l, hb * 64:(hb + 1) * 64, which, c, hb * 64:(hb + 1) * 64] = w[2 * c + hb]
    x1 = np.zeros((DEPTH, 128, 512), np.float32)
    x2 = np.zeros((DEPTH, 128, 512), np.float32)
    c1 = np.zeros((DEPTH, 128, 4, 128), np.float32)
    c2 = np.zeros((DEPTH, 128, 4, 128), np.float32)
    for l in range(DEPTH):
        br = np.asarray(inp["s5_b_re"][l], np.float32).transpose(1, 0, 2).reshape(64, 512)
        bi = np.asarray(inp["s5_b_im"][l], np.float32).transpose(1, 0, 2).reshape(64, 512)
        x1[l] = np.concatenate([br, bi], 0)
        x2[l] = np.concatenate([bi, br], 0)
        cr = np.asarray(inp["s5_c_re"][l], np.float32).reshape(4, 128, 64)
        ci = np.asarray(inp["s5_c_im"][l], np.float32).reshape(4, 128, 64)
        c1[l] = np.concatenate([cr, ci], 2).transpose(1, 0, 2)
        c2[l] = np.concatenate([ci, cr], 2).transpose(1, 0, 2)
    return wbd, x1, x2, c1, c2


def build(debug=(), upto="all", nlayers=DEPTH):
    nc = bass.Bass("TRN2", target_bir_lowering=False)
    fw = FW(nc)
    dram = {}

    def din(name, shape, dtype=F32):
        dram[name] = nc.dram_tensor(name, list(shape), dtype, kind="ExternalInput").ap()
        return dram[name]

    xT_d = din("xT", [D_MODEL, SEQ])
    pos_d = din("pos", [1, SEQ], I32)
    pv_d = din("pv", [128, PVL.n])
    cst_d = din("cst", [128, NCST])
    wbd_d = din("wbd", [DEPTH, 128, 2 * 4 * 128])
    x1_d = din("s5x1", [DEPTH, 128, 512])
    x2_d = din("s5x2", [DEPTH, 128, 512])
    c1_d = din("s5c1", [DEPTH, 128, 512])
    c2_d = din("s5c2", [DEPTH, 128, 512])
    w_in_d = din("w_in", [DEPTH, D_MODEL, IN_WIDTH])
    w_glu_d = din("s5_w_glu", [DEPTH, 512, 512])
    w_out_d = din("w_out", [DEPTH, 1536, D_MODEL])
    w_up_d = din("w_up", [DEPTH, D_MODEL, 2 * D_FF])
    w_down_d = din("w_down", [DEPTH, D_FF, D_MODEL])
    outT_d = nc.dram_tensor("outT", [D_MODEL, SEQ], F32, kind="ExternalOutput").ap()
    dbg_out = {}
    out_dmas = []

    A = fw.op

    hT = fw.sb("hT", [128, 8, SEQ], F32, nslots=32)
    xn = fw.sb("xn", [128, 8, SEQ], BF16, nslots=32)
    rcos = fw.sb("rcos", [128, SEQ], F32, nslots=4)
    rsin = fw.sb("rsin", [128, SEQ], F32, nslots=4)
    pv = fw.sb("pvs", [128, PVL.n], F32)
    cst = fw.sb("csts", [128, NCST], F32)
    ident_b = fw.sb("ident_b", [128, 128], BF16)
    ones_b = fw.sb("ones_b", [128, 128], BF16)
    o128_b = fw.sb("o128_b", [128, 128], BF16)
    NRING = 9
    ring = Pool([fw.sb(f"ring{i}", [128, 520], F32) for i in range(NRING)])
    psum = fw.ps("psum", [128, 8, 512], F32, nslots=8)
    banks = Pool(list(range(8)))
    rot_views = [TV(rcos, i, rcos[:, i * TT:(i + 1) * TT]) for i in range(4)] + [TV(rsin, i, rsin[:, i * TT:(i + 1) * TT]) for i in range(4)]

    def PV(name, c=0, n=1):
        o, k = PVL.idx[name]
        return pv[:, o + c:o + c + n]

    def hs(c, t):
        return hT.r(c * 4 + t)

    def xs(c, t):
        return xn.r(c * 4 + t)

    def tsl(t):
        return slice(t * TT, (t + 1) * TT)

    def bf(tile, n=TT, off=0):
        return tile[:].bitcast(BF16)[:, off:off + n]

    def dbg(name, tile, ap, shape, dtype=F32):
        if name not in debug:
            return
        d = nc.dram_tensor("dbg_" + name, list(shape), dtype, kind="ExternalOutput").ap()
        dbg_out[name] = d
        ins = A("sp", OP("dma_start", out=d, in_=ap), reads=tile.r(), dma=True)
        out_dmas.append(ins)

    A("sp", OP("dma_start", out=pv[:], in_=pv_d), writes=pv.r(), dma=True)
    A("sp", OP("dma_start", out=cst[:], in_=cst_d), writes=cst.r(), dma=True)
    for c in range(8):
        for t in range(NT):
            A("sp", OP("dma_start", out=hT[:, c, tsl(t)], in_=xT_d[c * 128:(c + 1) * 128, tsl(t)]),
              writes=hs(c, t), dma=True)
    A("dve", OP("memset", ones_b[:], 1.0), writes=ones_b.r())
    A("dve", OP("memset", o128_b[:], 1.0 / 128.0), writes=o128_b.r())
    A("act", OP("activation", out=ident_b[:], in_=cst[:, C_ID:C_ID + 128], func=AF.Copy),
      reads=cst.r(), writes=ident_b.r())
    ident_f = cst[:, C_ID:C_ID + 128]
    iota = cst[:, C_IOTA:C_IOTA + 512]
    maskT = cst[:, C_MASK:C_MASK + 128]

    def load_w(dst_tile, dst_ap, src_ap):
        return A("pool", OP("dma_start", out=dst_ap, in_=src_ap), writes=dst_tile.r(), dma=True)

    def rmsnorm_to_xn(gname):
        for t in range(NT):
            b = banks.get()
            for c in range(8):
                sq = ring.get()
                A("act", OP("activation", out=bf(sq), in_=hT[:, c, tsl(t)], func=AF.Square),
                  reads=hs(c, t), writes=sq.r())
                A("pe", OP("matmul", out=psum[:, b, :], lhsT=ones_b[:], rhs=bf(sq), start=(c == 0), stop=(c == 7)),
                  reads=sq.r() + ones_b.r(), writes=psum.r(b))
                ring.free(sq)
            rs = ring.get()
            A("act", OP("activation", out=rs[:, 0:TT], in_=psum[:, b, :], func=AF.Ln, scale=1.0 / D_MODEL, bias=NORM_EPS), reads=psum.r(b), writes=rs.r())
            banks.free(b)
            A("act", OP("activation", out=rs[:, 0:TT], in_=rs[:, 0:TT], func=AF.Exp, scale=-0.5), reads=rs.r(), writes=rs.r())
            for c in range(8):
                A("dve", OP("scalar_tensor_tensor", out=xn[:, c, tsl(t)], in0=hT[:, c, tsl(t)], scalar=PV(gname, c),
                                                                      in1=rs[:, 0:TT], op0=ALU.mult, op1=ALU.mult),
                  reads=hs(c, t) + rs.r() + pv.r(), writes=xs(c, t))
            ring.free(rs)

    def proj(b, wtile, w_ap_fn, t):
        for k in range(8):
            A("pe", OP("matmul", out=psum[:, b, :], lhsT=w_ap_fn(k), rhs=xn[:, k, tsl(t)], start=(k == 0), stop=(k == 7)),
              reads=wtile.r() + xs(k, t), writes=psum.r(b))

    def group_post(l, mixed, gname, grp, wbufs_pool, eps_div):
        wo = wbufs_pool.get()
        wo_v = wo[:].rearrange("p (k n) -> p k n", k=4)
        load_w(wo, wo_v, w_out_d[l, grp * 512:(grp + 1) * 512, :].rearrange("(k p) n -> p k n", p=128))
        for t in range(NT):
            if gname is not None:
                b = banks.get()
                for c in range(4):
                    sq = ring.get()
                    A("act", OP("activation", out=bf(sq), in_=mixed[:, c, tsl(t)], func=AF.Square),
                      reads=mixed.r(c * 4 + t), writes=sq.r())
                    A("pe", OP("matmul", out=psum[:, b, :], lhsT=ones_b[:], rhs=bf(sq), start=(c == 0), stop=(c == 3)),
                      reads=sq.r() + ones_b.r(), writes=psum.r(b))
                    ring.free(sq)
                rs = ring.get()
                A("act", OP("activation", out=rs[:, 0:TT], in_=psum[:, b, :], func=AF.Ln, scale=1.0 / 512.0, bias=NORM_EPS), reads=psum.r(b), writes=rs.r())
                banks.free(b)
                A("act", OP("activation", out=rs[:, 0:TT], in_=rs[:, 0:TT], func=AF.Exp, scale=-0.5), reads=rs.r(), writes=rs.r())
                for c in range(4):
                    A("dve", OP("scalar_tensor_tensor", out=mixed[:, c, tsl(t)], in0=mixed[:, c, tsl(t)],
                                                                          scalar=PV(gname + str(l), c), in1=rs[:, 0:TT],
                                                                          op0=ALU.mult, op1=ALU.mult),
                      reads=mixed.r(c * 4 + t) + rs.r() + pv.r(), writes=mixed.r(c * 4 + t))
                ring.free(rs)
            for dc in range(8):
                b = banks.get()
                for k in range(4):
                    A("pe", OP("matmul", out=psum[:, b, :], lhsT=wo_v[:, k, dc * 128:(dc + 1) * 128],
                                                           rhs=mixed[:, k, tsl(t)], start=(k == 0), stop=(k == 3)),
                      reads=wo.r() + mixed.r(k * 4 + t), writes=psum.r(b))
                A("dve", OP("tensor_tensor", out=hT[:, dc, tsl(t)], in0=hT[:, dc, tsl(t)], in1=psum[:, b, :], op=ALU.add),
                  reads=hs(dc, t) + psum.r(b), writes=hs(dc, t))
                banks.free(b)
        wbufs_pool.free(wo)

    def build_rot():
        for t in range(NT):
            pi_ = ring.get(); pf = ring.get(); k_ = ring.get()
            A("sp", OP("dma_start", out=pi_[:].bitcast(I32)[:, 0:TT],
                                                          in_=pos_d[0:1, tsl(t)].to_broadcast([128, TT])),
              writes=pi_.r(), dma=True)
            A("dve", OP("tensor_copy", out=pf[:, 0:TT], in_=pi_[:].bitcast(I32)[:, 0:TT]),
              reads=pi_.r(), writes=pf.r())
            A("dve", OP("tensor_scalar", out=pf[:, 0:TT], in0=pf[:, 0:TT], scalar1=PV("invf"), scalar2=None,
                                                      op0=ALU.mult), reads=pf.r() + pv.r(), writes=pf.r())
            A("dve", OP("tensor_scalar", out=k_[:, 0:TT], in0=pf[:, 0:TT], scalar1=1.0 / (2.0 * math.pi),
                                                             scalar2=MAGIC, op0=ALU.mult, op1=ALU.add),
              reads=pf.r(), writes=k_.r())
            A("dve", OP("tensor_scalar", out=k_[:, 0:TT], in0=k_[:, 0:TT], scalar1=MAGIC, scalar2=None,
                                                      op0=ALU.subtract), reads=k_.r(), writes=k_.r())
            C1 = 6.28125
            C2 = 2.0 * math.pi - 6.28125
            A("dve", OP("scalar_tensor_tensor", out=pf[:, 0:TT], in0=k_[:, 0:TT], scalar=-C1, in1=pf[:, 0:TT],
                                                                    op0=ALU.mult, op1=ALU.add), reads=pf.r() + k_.r(), writes=pf.r())
            A("dve", OP("scalar_tensor_tensor", out=pf[:, 0:TT], in0=k_[:, 0:TT], scalar=-C2, in1=pf[:, 0:TT],
                                                                    op0=ALU.mult, op1=ALU.add), reads=pf.r() + k_.r(), writes=pf.r())
            A("dve", OP("tensor_scalar", out=pf[:, 0:TT], in0=pf[:, 0:TT], scalar1=3.1415925, scalar2=-3.1415925,
                                                      op0=ALU.min, op1=ALU.max), reads=pf.r(), writes=pf.r())
            A("act", OP("activation", out=rsin[:, tsl(t)], in_=pf[:, 0:TT], func=AF.Sin, scale=PV("sgn")),
              reads=pf.r() + pv.r(), writes=rsin.r(t))
            A("act", OP("activation", out=k_[:, 0:TT], in_=pf[:, 0:TT], func=AF.Abs),
              reads=pf.r(), writes=k_.r())
            A("act", OP("activation", out=rcos[:, tsl(t)], in_=k_[:, 0:TT], func=AF.Sin, scale=-1.0,
                                                        bias=math.pi / 2.0 - 1e-6), reads=k_.r(), writes=rcos.r(t))
            ring.free(pi_, pf, k_)


    phase = []
    for l in range(nlayers):

        def psb(name, shape, dtype, nslots=1):
            cm = nc.sbuf_tensor(f"{name}_{l}", list(shape), dtype)
            h = cm.__enter__()
            phase.append(cm)
            return T(h, name, nslots)

        mixed = psb("mixed", [128, 4, SEQ], BF16, nslots=16)
        aux = psb("aux", [128, 8192], BF16, nslots=16)
        wbufs = Pool([psb(f"wbuf{i}", [128, 4096], BF16) for i in range(2)])
        wbd = psb("wbd", [128, 2, 4, 128], BF16)
        lpar = psb("lpar", [128, 16], F32)
        lcar = psb("lcar", [128, 4, 4], F32)
        s5p = psb("s5p", [128, 32, 8], F32)
        s5off = psb("s5off", [128, 32, 4], F32)
        bst1 = psb("bst1", [128, 512], BF16)
        bst2 = psb("bst2", [128, 512], BF16)
        w5tiles = [psb(f"w5_{i}", [128, 8, 128], BF16) for i in range(2)]
        w5pool = Pool(list(w5tiles))
        lhs_b = psb("lhs_b", [128, 2, 8, 128], BF16, nslots=2)
        lhs_c = psb("lhs_c", [128, 2, 8, 128], BF16, nslots=2)
        zcar = psb("zcar", [128, 32], F32)
        ust = psb("ust", [128, 128], F32)
        pbs = [psb(f"pb{i}", [128, 128], BF16) for i in range(2)]
        ret_views = []
        for tl_ in (lhs_b, lhs_c):
            f32v = tl_[:].rearrange("p a g n -> p (a g n)").bitcast(F32)
            ret_views += [TV(tl_, 0, f32v[:, 0:TT]), TV(tl_, 1, f32v[:, TT:2 * TT])]
        for tl_ in w5tiles:
            ret_views.append(TV(tl_, 0, tl_[:].rearrange("p k n -> p (k n)").bitcast(F32)))

        load_w(wbd, wbd[:].rearrange("p a c n -> p (a c n)"), wbd_d[l])
        A("act", OP("activation", out=lpar[:, 0:4], in_=PV(f"llam{l}", 0, 4), func=AF.Exp, scale=-1.0), reads=pv.r(), writes=lpar.r())
        A("act", OP("activation", out=lpar[:, 0:4], in_=lpar[:, 0:4], func=AF.Ln, bias=1.0), reads=lpar.r(), writes=lpar.r())
        A("dve", OP("tensor_scalar", out=lpar[:, 4:8], in0=lpar[:, 0:4], scalar1=-4.0, scalar2=None, op0=ALU.mult), reads=lpar.r(), writes=lpar.r())
        A("dve", OP("tensor_scalar", out=lpar[:, 0:4], in0=lpar[:, 0:4], scalar1=-8.0, scalar2=None, op0=ALU.mult), reads=lpar.r(), writes=lpar.r())
        A("dve", OP("tensor_scalar", out=lpar[:, 8:12], in0=PV(f"lba{l}", 0, 4), scalar1=0.5, scalar2=None, op0=ALU.mult), reads=pv.r(), writes=lpar.r())
        A("dve", OP("tensor_scalar", out=lpar[:, 12:16], in0=PV(f"lbx{l}", 0, 4), scalar1=0.5, scalar2=None, op0=ALU.mult), reads=pv.r(), writes=lpar.r())
        A("dve", OP("memset", lcar[:], 0.0), writes=lcar.r())
        A("dve", OP("memset", zcar[:], 0.0), writes=zcar.r())

        S = lambda j: s5p[:, :, j]
        LR = PV(f"s5lr{l}", 0, 32)
        LI = PV(f"s5li{l}", 0, 32)
        R_, W_ = s5p.r(), s5p.r()
        A("act", OP("activation", out=S(6), in_=PV(f"s5ldt{l}", 0, 32), func=AF.Exp), reads=pv.r(), writes=W_)
        A("dve", OP("tensor_tensor", out=S(0), in0=LR, in1=S(6), op=ALU.mult), reads=R_ + pv.r(), writes=W_)
        A("act", OP("activation", out=S(0), in_=S(0), func=AF.Exp), reads=R_, writes=W_)
        A("dve", OP("tensor_tensor", out=S(7), in0=LI, in1=S(6), op=ALU.mult), reads=R_ + pv.r(), writes=W_)
        A("dve", OP("tensor_scalar", out=S(6), in0=S(7), scalar1=1.0 / (2.0 * math.pi), scalar2=MAGIC, op0=ALU.mult, op1=ALU.add), reads=R_, writes=W_)
        A("dve", OP("tensor_scalar", out=S(6), in0=S(6), scalar1=MAGIC, scalar2=None, op0=ALU.subtract), reads=R_, writes=W_)
        A("dve", OP("scalar_tensor_tensor", out=S(1), in0=S(7), scalar=1.0 / (2.0 * math.pi), in1=S(6), op0=ALU.mult, op1=ALU.subtract), reads=R_, writes=W_)
        A("act", OP("activation", out=S(7), in_=S(1), func=AF.Sin, scale=TWO_PI_S), reads=R_, writes=W_)
        A("act", OP("activation", out=S(6), in_=S(1), func=AF.Abs), reads=R_, writes=W_)
        A("act", OP("activation", out=S(6), in_=S(6), func=AF.Sin, scale=-TWO_PI_S, bias=math.pi / 2.0 - 1e-6), reads=R_, writes=W_)
        A("dve", OP("tensor_tensor", out=S(6), in0=S(6), in1=S(0), op=ALU.mult), reads=R_, writes=W_)
        A("dve", OP("tensor_scalar", out=S(6), in0=S(6), scalar1=-1.0, scalar2=None, op0=ALU.add), reads=R_, writes=W_)
        A("dve", OP("tensor_tensor", out=S(7), in0=S(7), in1=S(0), op=ALU.mult), reads=R_, writes=W_)
        A("dve", OP("tensor_tensor", out=S(5), in0=LR, in1=LR, op=ALU.mult), reads=R_ + pv.r(), writes=W_)
        A("dve", OP("tensor_tensor", out=S(4), in0=LI, in1=LI, op=ALU.mult), reads=R_ + pv.r(), writes=W_)
        A("dve", OP("tensor_tensor", out=S(5), in0=S(5), in1=S(4), op=ALU.add), reads=R_, writes=W_)
        A("dve", OP("reciprocal", out=S(5), in_=S(5)), reads=R_, writes=W_)
        A("dve", OP("tensor_tensor", out=S(2), in0=S(6), in1=LR, op=ALU.mult), reads=R_ + pv.r(), writes=W_)
        A("dve", OP("tensor_tensor", out=S(4), in0=S(7), in1=LI, op=ALU.mult), reads=R_ + pv.r(), writes=W_)
        A("dve", OP("tensor_tensor", out=S(2), in0=S(2), in1=S(4), op=ALU.add), reads=R_, writes=W_)
        A("dve", OP("tensor_tensor", out=S(2), in0=S(2), in1=S(5), op=ALU.mult), reads=R_, writes=W_)
        A("dve", OP("tensor_tensor", out=S(3), in0=S(7), in1=LR, op=ALU.mult), reads=R_ + pv.r(), writes=W_)
        A("dve", OP("tensor_tensor", out=S(4), in0=S(6), in1=LI, op=ALU.mult), reads=R_ + pv.r(), writes=W_)
        A("dve", OP("tensor_tensor", out=S(3), in0=S(3), in1=S(4), op=ALU.subtract), reads=R_, writes=W_)
        A("dve", OP("tensor_tensor", out=S(3), in0=S(3), in1=S(5), op=ALU.mult), reads=R_, writes=W_)
        A("dve", OP("tensor_copy", out=S(5), in_=S(3)), reads=R_, writes=W_)
        A("dve", OP("tensor_scalar", out=S(3), in0=S(3), scalar1=PV("sgn"), scalar2=None, op0=ALU.mult), reads=R_ + pv.r(), writes=W_)
        A("dve", OP("tensor_scalar", out=S(4), in0=S(2), scalar1=PV("nsgn"), scalar2=None, op0=ALU.mult), reads=R_ + pv.r(), writes=W_)
        for t in range(NT):
            A("dve", OP("tensor_scalar", out=s5off[:, :, t], in0=S(1), scalar1=float(TT * t), scalar2=MAGIC, op0=ALU.mult, op1=ALU.add),
              reads=R_, writes=s5off.r())
            A("dve", OP("tensor_scalar", out=s5off[:, :, t], in0=s5off[:, :, t], scalar1=MAGIC, scalar2=None, op0=ALU.subtract),
              reads=s5off.r(), writes=s5off.r())
            A("dve", OP("scalar_tensor_tensor", out=s5off[:, :, t], in0=S(1), scalar=float(TT * t), in1=s5off[:, :, t],
                                                           op0=ALU.mult, op1=ALU.subtract), reads=R_ + s5off.r(), writes=s5off.r())
        cn1 = ring.get(); cn2 = ring.get()
        A("sp", OP("dma_start", out=cn1[:, 0:512], in_=x1_d[l]), writes=cn1.r(), dma=True)
        A("sp", OP("dma_start", out=cn2[:, 0:512], in_=x2_d[l]), writes=cn2.r(), dma=True)
        v3 = lambda tl: tl[:, 0:512].rearrange("p (g c) -> p g c", c=16)
        bc = lambda j: s5p[:, :, j:j + 1].to_broadcast([128, 32, 16])
        t1 = ring.get(); t2 = ring.get()
        t1v = t1[:, 0:512].rearrange("p (g c) -> p g c", c=16)
        t2v = t2[:, 0:512].rearrange("p (g c) -> p g c", c=16)
        A("dve", OP("tensor_tensor", out=t1v, in0=v3(cn1), in1=bc(2), op=ALU.mult), reads=cn1.r() + R_, writes=t1.r())
        A("dve", OP("tensor_tensor", out=t2v, in0=v3(cn2), in1=bc(3), op=ALU.mult), reads=cn2.r() + R_, writes=t2.r())
        A("dve", OP("tensor_tensor", out=bst1[:], in0=t1[:, 0:512], in1=t2[:, 0:512], op=ALU.add), reads=t1.r() + t2.r(), writes=bst1.r())
        A("dve", OP("tensor_tensor", out=t1v, in0=v3(cn2), in1=bc(4), op=ALU.mult), reads=cn2.r() + R_, writes=t1.r())
        A("dve", OP("tensor_tensor", out=t2v, in0=v3(cn1), in1=bc(5), op=ALU.mult), reads=cn1.r() + R_, writes=t2.r())
        A("dve", OP("tensor_tensor", out=bst2[:], in0=t1[:, 0:512], in1=t2[:, 0:512], op=ALU.add), reads=t1.r() + t2.r(), writes=bst2.r())
        ring.free(t1, t2, cn1, cn2)
        A("pool", OP("memset", lhs_c[:].rearrange("p a g n -> p (a g n)"), 0.0), writes=lhs_c.r())
        dbg(f"bst1_{l}", bst1, bst1[:], [128, 512], BF16)
        dbg(f"s5p_{l}", s5p, s5p[:].rearrange("p g j -> p (g j)"), [128, 256])

        rmsnorm_to_xn(f"gmix{l}")
        dbg(f"xn{l}", xn, xn[:].rearrange("p c t -> p (c t)"), [128, 8 * SEQ], BF16)

        zbf = aux[:].rearrange("p (c t) -> p c t", c=4)
        def lru_chunk(c):
            wb = wbufs.get()
            wv_ = wb[:].rearrange("p (k n) -> p k n", k=8)
            load_w(wb, wv_[:, :, 0:128], w_in_d[l, :, c * 128:(c + 1) * 128].rearrange("(k p) n -> p k n", p=128))
            load_w(wb, wv_[:, :, 128:256], w_in_d[l, :, 512 + c * 128:512 + (c + 1) * 128].rearrange("(k p) n -> p k n", p=128))
            return wb, wv_

        def lru_unit(c, t, wb, wv_):
            bx = banks.get(); bg = banks.get()
            proj(bx, wb, lambda k: wv_[:, k, 0:128], t)
            proj(bg, wb, lambda k: wv_[:, k, 128:256], t)
            xp = ring.get(halo=True); gy = ring.get(); xc = ring.get(); xcb = ring.get()
            A("act", OP("activation", out=xp[:, 0:3], in_=lcar[:, c, 0:3], func=AF.Copy), reads=lcar.r(), writes=xp.r())
            A("act", OP("activation", out=xp[:, 3:3 + TT], in_=psum[:, bx, :], func=AF.Copy), reads=psum.r(bx), writes=xp.r())
            A("act", OP("activation", out=lcar[:, c, 0:3], in_=xp[:, TT:TT + 3], func=AF.Copy), reads=xp.r(), writes=lcar.r())
            A("act", OP("activation", out=gy[:, 0:TT], in_=psum[:, bg, :], func=AF.Gelu_apprx_tanh), reads=psum.r(bg), writes=gy.r())
            banks.free(bx, bg)
            A("act", OP("activation", out=xc[:, 0:TT], in_=xp[:, 3:3 + TT], func=AF.Identity,
                                                          scale=PV(f"lcw{l}", c * 4 + 3), bias=PV(f"lcb{l}", c)),
              reads=xp.r() + pv.r(), writes=xc.r())
            for k in range(3):
                A("dve", OP("scalar_tensor_tensor", out=xc[:, 0:TT], in0=xp[:, k:k + TT], scalar=PV(f"lcw{l}", c * 4 + k),
                                                                            in1=xc[:, 0:TT], op0=ALU.mult, op1=ALU.add),
                  reads=xp.r() + xc.r() + pv.r(), writes=xc.r())
            A("act", OP("activation", out=bf(xcb), in_=xc[:, 0:TT], func=AF.Copy), reads=xc.r(), writes=xcb.r())
            ring.free(xp)
            br_ = banks.get(); bi_ = banks.get()
            A("pe", OP("matmul", out=psum[:, br_, :], lhsT=wbd[:, 0, c, :], rhs=bf(xcb), start=True, stop=True),
              reads=wbd.r() + xcb.r(), writes=psum.r(br_))
            A("pe", OP("matmul", out=psum[:, bi_, :], lhsT=wbd[:, 1, c, :], rhs=bf(xcb), start=True, stop=True),
              reads=wbd.r() + xcb.r(), writes=psum.r(bi_))
            rr = ring.get(); ii = ring.get(); a2 = ring.get()
            A("act", OP("activation", out=rr[:, 0:TT], in_=psum[:, br_, :], func=AF.Tanh, scale=0.5, bias=lpar[:, 8 + c:9 + c]),
              reads=psum.r(br_) + lpar.r(), writes=rr.r())
            A("act", OP("activation", out=ii[:, 0:TT], in_=psum[:, bi_, :], func=AF.Tanh, scale=0.5, bias=lpar[:, 12 + c:13 + c]),
              reads=psum.r(bi_) + lpar.r(), writes=ii.r())
            banks.free(br_, bi_)
            ring.free(xcb)
            A("act", OP("activation", out=a2[:, 0:TT], in_=rr[:, 0:TT], func=AF.Exp, scale=lpar[:, c:c + 1], bias=lpar[:, c:c + 1]),
              reads=rr.r() + lpar.r(), writes=a2.r())
            A("act", OP("activation", out=rr[:, 0:TT], in_=rr[:, 0:TT], func=AF.Exp, scale=lpar[:, 4 + c:5 + c], bias=lpar[:, 4 + c:5 + c]),
              reads=rr.r() + lpar.r(), writes=rr.r())
            A("dve", OP("tensor_scalar", out=a2[:, 0:TT], in0=a2[:, 0:TT], scalar1=1.0, scalar2=-1e-30, op0=ALU.subtract, op1=ALU.min),
              reads=a2.r(), writes=a2.r())
            A("act", OP("activation", out=a2[:, 0:TT], in_=a2[:, 0:TT], func=AF.Ln, scale=-1.0), reads=a2.r(), writes=a2.r())
            A("act", OP("activation", out=a2[:, 0:TT], in_=a2[:, 0:TT], func=AF.Exp, scale=0.5), reads=a2.r(), writes=a2.r())
            A("dve", OP("scalar_tensor_tensor", out=ii[:, 0:TT], in0=ii[:, 0:TT], scalar=1.0, in1=xc[:, 0:TT], op0=ALU.add, op1=ALU.mult),
              reads=ii.r() + xc.r(), writes=ii.r())
            A("dve", OP("scalar_tensor_tensor", out=ii[:, 0:TT], in0=ii[:, 0:TT], scalar=0.5, in1=a2[:, 0:TT], op0=ALU.mult, op1=ALU.mult),
              reads=ii.r() + a2.r(), writes=ii.r())
            hh = xc
            A("dve", OP("tensor_tensor_scan", out=hh[:, 0:TT], data0=rr[:, 0:TT], data1=ii[:, 0:TT],
                                                                        initial=lcar[:, c, 3:4], op0=ALU.mult, op1=ALU.add),
              reads=rr.r() + ii.r() + lcar.r(), writes=hh.r())
            A("act", OP("activation", out=lcar[:, c, 3:4], in_=hh[:, TT - 1:TT], func=AF.Copy), reads=hh.r(), writes=lcar.r())
            A("dve", OP("tensor_tensor", out=mixed[:, c, tsl(t)], in0=hh[:, 0:TT], in1=gy[:, 0:TT], op=ALU.mult),
              reads=hh.r() + gy.r(), writes=mixed.r(c * 4 + t))
            ring.free(rr, ii, a2, xc, gy)

        def s5_chunk(c):
            wb = w5pool.get()
            wv_ = wb
            load_w(wb, wv_[:, :, 0:128], w_in_d[l, :, 1024 + c * 128:1024 + (c + 1) * 128].rearrange("(k p) n -> p k n", p=128))
            for which, bst in enumerate((bst1, bst2)):
                b = banks.get()
                tpv = psum[:, b, :].bitcast(BF16)
                A("pe", OP("transpose", out=tpv[:, 0:128], in_=bst[:, c * 128:(c + 1) * 128], identity=ident_b[:]),
                  reads=bst.r() + ident_b.r(), writes=psum.r(b))
                tb = ring.get()
                A("act", OP("activation", out=bf(tb, 128), in_=tpv[:, 0:128], func=AF.Copy), reads=psum.r(b), writes=tb.r())
                banks.free(b)
                for g in range(8):
                    A("dve", OP("tensor_scalar", out=lhs_b[:, which, g, :], in0=bf(tb, 128), scalar1=PV("rowmask", g),
                                                                                scalar2=None, op0=ALU.mult),
                      reads=tb.r() + pv.r(), writes=lhs_b.r())
                ring.free(tb)
            for which, (cd, sname) in enumerate(((c1_d, "nsgn"), (c2_d, None))):
                cn = ring.get()
                A("sp", OP("dma_start", out=cn[:, 0:128], in_=cd[l, :, c * 128:(c + 1) * 128]), writes=cn.r(), dma=True)
                b = banks.get()
                A("pe", OP("transpose", out=psum[:, b, 0:128], in_=cn[:, 0:128], identity=ident_f),
                  reads=cn.r() + cst.r(), writes=psum.r(b))
                ring.free(cn)
                for g in range(8):
                    if sname is not None:
                        A("dve", OP("tensor_scalar", out=lhs_c[:, which, g, g * 16:(g + 1) * 16], in0=psum[:, b, g * 16:(g + 1) * 16],
                                                                                  scalar1=PV("nsgn"), scalar2=None, op0=ALU.mult),
                          reads=psum.r(b) + pv.r(), writes=lhs_c.r())
                    else:
                        A("dve", OP("tensor_scalar", out=lhs_c[:, which, g, g * 16:(g + 1) * 16], in0=psum[:, b, g * 16:(g + 1) * 16],
                                                                                  scalar1=-1.0, scalar2=None, op0=ALU.mult),
                          reads=psum.r(b), writes=lhs_c.r())
                banks.free(b)
            return wb, wv_

        def s5_unit(c, t, wb, wv_):
            bu = banks.get()
            proj(bu, wb, lambda k: wv_[:, k, 0:128], t)
            uf = ring.get(); ub = ring.get()
            A("act", OP("activation", out=uf[:, 0:TT], in_=psum[:, bu, :], func=AF.Copy), reads=psum.r(bu), writes=uf.r())
            A("act", OP("activation", out=bf(ub), in_=psum[:, bu, :], func=AF.Copy), reads=psum.r(bu), writes=ub.r())
            banks.free(bu)
            by = banks.get()
            for g in range(8):
                G = c * 8 + g
                b1 = banks.get(); b2 = banks.get()
                A("pe", OP("matmul", out=psum[:, b1, :], lhsT=lhs_b[:, 0, g, :], rhs=bf(ub), start=True, stop=True),
                  reads=lhs_b.r() + ub.r(), writes=psum.r(b1))
                A("pe", OP("matmul", out=psum[:, b2, :], lhsT=lhs_b[:, 1, g, :], rhs=bf(ub), start=True, stop=True),
                  reads=lhs_b.r() + ub.r(), writes=psum.r(b2))
                uu = ring.get(); kk = ring.get(); tc_ = ring.get(); ts_ = ring.get()
                A("act", OP("activation", out=uu[:, 0:TT], in_=iota, func=AF.Identity, scale=s5p[:, G, 1:2], bias=s5off[:, G, t:t + 1]), reads=cst.r() + s5p.r() + s5off.r(), writes=uu.r())
                A("dve", OP("tensor_scalar", out=kk[:, 0:TT], in0=uu[:, 0:TT], scalar1=MAGIC, scalar2=MAGIC,
                                                                  op0=ALU.add, op1=ALU.subtract), reads=uu.r(), writes=kk.r())
                A("pool", OP("tensor_tensor", out=uu[:, 0:TT], in0=uu[:, 0:TT], in1=kk[:, 0:TT], op=ALU.subtract),
                  reads=uu.r() + kk.r(), writes=uu.r())
                A("act", OP("activation", out=ts_[:, 0:TT], in_=uu[:, 0:TT], func=AF.Sin, scale=TWO_PI_S), reads=uu.r(), writes=ts_.r())
                A("act", OP("activation", out=kk[:, 0:TT], in_=uu[:, 0:TT], func=AF.Abs), reads=uu.r(), writes=kk.r())
                A("act", OP("activation", out=tc_[:, 0:TT], in_=kk[:, 0:TT], func=AF.Sin, scale=-TWO_PI_S, bias=math.pi / 2.0 - 1e-6),
                  reads=kk.r(), writes=tc_.r())
                m1 = uu; m2 = kk
                A("dve", OP("tensor_tensor", out=m1[:, 0:TT], in0=psum[:, b1, :], in1=tc_[:, 0:TT], op=ALU.mult),
                  reads=psum.r(b1) + tc_.r(), writes=m1.r())
                A("dve", OP("tensor_tensor", out=m2[:, 0:TT], in0=psum[:, b2, :], in1=ts_[:, 0:TT], op=ALU.mult),
                  reads=psum.r(b2) + ts_.r(), writes=m2.r())
                banks.free(b1, b2)
                A("dve", OP("tensor_tensor", out=m1[:, 0:TT], in0=m1[:, 0:TT], in1=m2[:, 0:TT], op=ALU.add), reads=m1.r() + m2.r(), writes=m1.r())
                zz = m2
                A("dve", OP("tensor_tensor_scan", out=zz[:, 0:TT], data0=s5p[:, G, 0:1].to_broadcast([128, TT]), data1=m1[:, 0:TT],
                                                                          initial=zcar[:, G:G + 1], op0=ALU.mult, op1=ALU.add),
                  reads=m1.r() + s5p.r() + zcar.r(), writes=zz.r())
                A("act", OP("activation", out=zcar[:, G:G + 1], in_=zz[:, TT - 1:TT], func=AF.Copy), reads=zz.r(), writes=zcar.r())
                w12 = m1
                A("pool", OP("tensor_tensor", out=bf(w12, TT, 0), in0=zz[:, 0:TT], in1=tc_[:, 0:TT], op=ALU.mult),
                  reads=zz.r() + tc_.r(), writes=w12.r())
                A("pool", OP("tensor_tensor", out=bf(w12, TT, TT), in0=zz[:, 0:TT], in1=ts_[:, 0:TT], op=ALU.mult),
                  reads=zz.r() + ts_.r(), writes=w12.r())
                A("pe", OP("matmul", out=psum[:, by, :], lhsT=lhs_c[:, 0, g, :], rhs=bf(w12, TT, 0), start=(g == 0), stop=False),
                  reads=lhs_c.r() + w12.r(), writes=psum.r(by))
                A("pe", OP("matmul", out=psum[:, by, :], lhsT=lhs_c[:, 1, g, :], rhs=bf(w12, TT, TT), start=False, stop=(g == 7)),
                  reads=lhs_c.r() + w12.r(), writes=psum.r(by))
                ring.free(uu, kk, tc_, ts_)
            A("dve", OP("scalar_tensor_tensor", out=uf[:, 0:TT], in0=uf[:, 0:TT], scalar=PV(f"s5d{l}", c), in1=psum[:, by, :],
                                                             op0=ALU.mult, op1=ALU.add), reads=uf.r() + psum.r(by) + pv.r(), writes=uf.r())
            banks.free(by)
            A("act", OP("activation", out=zbf[:, c, tsl(t)], in_=uf[:, 0:TT], func=AF.Gelu_apprx_tanh), reads=uf.r(), writes=aux.r(c * 4 + t))
            ring.free(uf, ub)

        ring.enable_extra(rot_views)
        for c in range(4):
            lwb = lru_chunk(c)
            swb = s5_chunk(c)
            for t in range(NT):
                s5_unit(c, t, *swb)
                lru_unit(c, t, *lwb)
            wbufs.free(lwb[0])
            w5pool.free(swb[0])
        dbg(f"ylru_raw{l}", mixed, mixed[:].rearrange("p c t -> p (c t)"), [128, 4 * SEQ], BF16)
        if upto == "lru_raw":
            break
        group_post(l, mixed, "lnorm", 0, wbufs, 512.0)
        dbg(f"ylru{l}", mixed, mixed[:].rearrange("p c t -> p (c t)"), [128, 4 * SEQ], BF16)
        if upto == "lru":
            break

        dbg(f"s5z{l}", aux, aux[:], [128, 4 * SEQ], BF16)
        wg_t = wbufs.get()
        wgl = wg_t[:, 0:2048].rearrange("p (k n) -> p k n", k=4)
        load_w(wg_t, wgl, w_glu_d[l].rearrange("(k p) n -> p k n", p=128))
        for t in range(NT):
            for oc in range(4):
                b = banks.get()
                for k in range(4):
                    A("pe", OP("matmul", out=psum[:, b, :], lhsT=wgl[:, k, oc * 128:(oc + 1) * 128], rhs=zbf[:, k, tsl(t)],
                                                                start=(k == 0), stop=(k == 3)), reads=wg_t.r() + aux.r(k * 4 + t), writes=psum.r(b))
                sg = ring.get()
                A("act", OP("activation", out=sg[:, 0:TT], in_=psum[:, b, :], func=AF.Sigmoid, bias=PV(f"s5bg{l}", oc)),
                  reads=psum.r(b) + pv.r(), writes=sg.r())
                banks.free(b)
                A("dve", OP("tensor_tensor", out=mixed[:, oc, tsl(t)], in0=zbf[:, oc, tsl(t)], in1=sg[:, 0:TT], op=ALU.mult),
                  reads=sg.r() + aux.r(oc * 4 + t), writes=mixed.r(oc * 4 + t))
                ring.free(sg)
        wbufs.free(wg_t)
        group_post(l, mixed, "s5n", 1, wbufs, 512.0)
        dbg(f"ys5{l}", mixed, mixed[:].rearrange("p c t -> p (c t)"), [128, 4 * SEQ], BF16)
        if upto == "s5":
            break

        ring.disable_extra()
        build_rot()
        if l == 0:
            dbg("rcos", rcos, rcos[:], [128, SEQ])
            dbg("rsin", rsin, rsin[:], [128, SEQ])
        ring.enable_extra(ret_views)
        fw.pin = set(os.environ.get('KPIN', 'dve').split(',')) - {''}
        vh = aux[:].rearrange("p (n e) -> p n e", n=16)
        wv_t = wbufs.get()
        wvv = wv_t[:].rearrange("p (k n) -> p k n", k=8)
        load_w(wv_t, wvv, w_in_d[l, :, 2560:3072].rearrange("(k p) n -> p k n", p=128))
        for n in range(16):
            b = banks.get()
            t = n // 4
            for k in range(8):
                A("pe", OP("matmul", out=psum[:, b, :], lhsT=xn[:, k, n * 128:(n + 1) * 128], rhs=wvv[:, k, :],
                                                          start=(k == 0), stop=(k == 7)), reads=wv_t.r() + xs(k, t), writes=psum.r(b))
            for hd in range(4):
                A("act", OP("activation", out=vh[:, n, hd * 128:(hd + 1) * 128], in_=psum[:, b, hd * 128:(hd + 1) * 128],
                                                                 func=AF.Identity, scale=PV("vdec", hd)), reads=psum.r(b) + pv.r(), writes=aux.r(n))
            banks.free(b)
        wbufs.free(wv_t)
        wg_t = wbufs.get()
        wgv = wg_t[:].rearrange("p (k n) -> p k n", k=8)
        load_w(wg_t, wgv, w_in_d[l, :, 3072:3584].rearrange("(k p) n -> p k n", p=128))
        def ret_weights(hd):
            wb = wbufs.get()
            wq = wb[:].rearrange("p (k n) -> p k n", k=8)
            qb = 1536 + hd * 128
            kb = 2048 + hd * 128
            src = lambda c0, c1: w_in_d[l, :, c0:c1].rearrange("(k p) n -> p k n", p=128)
            load_w(wb, wq[:, :, 0:128], src(qb, qb + 128))
            load_w(wb, wq[:, :, 128:192], src(qb + 64, qb + 128))
            load_w(wb, wq[:, :, 192:256], src(qb, qb + 64))
            load_w(wb, wq[:, :, 256:384], src(kb, kb + 128))
            load_w(wb, wq[:, :, 384:448], src(kb + 64, kb + 128))
            load_w(wb, wq[:, :, 448:512], src(kb, kb + 64))
            return wb, wq

        def ret_A(hd, t, wb, wq):
            rot = []
            for which in range(2):
                ba = banks.get(); bb = banks.get()
                proj(ba, wb, lambda k, o=which * 256: wq[:, k, o:o + 128], t)
                proj(bb, wb, lambda k, o=which * 256 + 128: wq[:, k, o:o + 128], t)
                t1 = ring.get(); t2 = ring.get(); qr = ring.get()
                A("dve", OP("tensor_tensor", out=t1[:, 0:TT], in0=psum[:, ba, :], in1=rcos[:, tsl(t)], op=ALU.mult),
                  reads=psum.r(ba) + rcos.r(t), writes=t1.r())
                A("dve", OP("tensor_tensor", out=t2[:, 0:TT], in0=psum[:, bb, :], in1=rsin[:, tsl(t)], op=ALU.mult),
                  reads=psum.r(bb) + rsin.r(t), writes=t2.r())
                banks.free(ba, bb)
                A("pool", OP("tensor_tensor", out=bf(qr), in0=t1[:, 0:TT], in1=t2[:, 0:TT], op=ALU.add),
                  reads=t1.r() + t2.r(), writes=qr.r())
                ring.free(t1, t2)
                rot.append(qr)
            qr, kr = rot
            if hd == 0 and t == 0:
                dbg(f"qr{l}", qr, bf(qr), [128, TT], BF16)
                dbg(f"kr{l}", kr, bf(kr), [128, TT], BF16)
            bt = banks.get()
            ktp = psum[:, bt, :].bitcast(BF16)
            for j in range(4):
                A("pe", OP("transpose", out=ktp[:, j * 128:(j + 1) * 128], in_=bf(kr, 128, j * 128), identity=ident_b[:]),
                  reads=kr.r() + ident_b.r(), writes=psum.r(bt))
            ktm = ring.get()
            A("act", OP("activation", out=bf(ktm), in_=ktp[:, 0:TT], func=AF.Copy), reads=psum.r(bt), writes=ktm.r())
            banks.free(bt)
            bs = banks.get()
            for j in range(4):
                A("pe", OP("matmul", out=psum[:, bs, j * 128:(j + 1) * 128], lhsT=bf(kr, 128, j * 128), rhs=bf(qr, 128, j * 128),
                                                              start=True, stop=True), reads=kr.r() + qr.r(), writes=psum.r(bs))
            pT = ring.get()
            A("dve", OP("tensor_tensor", out=bf(pT).rearrange("p (j c) -> p j c", j=4), in0=psum[:, bs, :].rearrange("p (j c) -> p j c", j=4),
                                                      in1=maskT.unsqueeze(1).to_broadcast([128, 4, 128]), op=ALU.mult),
              reads=psum.r(bs) + cst.r(), writes=pT.r())
            banks.free(bs)
            bkv = banks.get()
            for j in range(4):
                n = t * 4 + j
                A("pe", OP("matmul", out=psum[:, bkv, j * 128:(j + 1) * 128], lhsT=bf(ktm, 128, j * 128),
                                                              rhs=vh[:, n, hd * 128:(hd + 1) * 128], start=True, stop=True),
                  reads=ktm.r() + aux.r(n), writes=psum.r(bkv))
            ring.free(ktm)
            ring.free(kr)
            bg = banks.get()
            proj(bg, wg_t, lambda k: wgv[:, k, hd * 128:(hd + 1) * 128], t)
            sg = ring.get()
            A("act", OP("activation", out=sg[:, 0:TT], in_=psum[:, bg, :], func=AF.Silu), reads=psum.r(bg), writes=sg.r())
            banks.free(bg)
            return qr, pT, bkv, sg

        def ret_B(hd, t, qr, pT, bkv, sg):
            gC = float(np.float32(np.exp(np.float32(np.log1p(-np.float32(2.0) ** np.float32(-5.0 - hd))) * np.float32(128.0))))
            bxo = banks.get()
            for j in range(4):
                n = t * 4 + j
                pb = pbs[n % 2]
                if n > 0:
                    A("act", OP("activation", out=pb[:], in_=ust[:], func=AF.Identity, scale=float(gC * (128.0 ** -0.5))),
                      reads=ust.r(), writes=pb.r())
                A("pe", OP("matmul", out=psum[:, bxo, j * 128:(j + 1) * 128], lhsT=vh[:, n, hd * 128:(hd + 1) * 128],
                                                            rhs=bf(pT, 128, j * 128), start=True, stop=(n == 0)),
                  reads=aux.r(n) + pT.r(), writes=psum.r(bxo))
                if n > 0:
                    A("pe", OP("matmul", out=psum[:, bxo, j * 128:(j + 1) * 128], lhsT=pb[:], rhs=bf(qr, 128, j * 128),
                                                                  start=False, stop=True), reads=pb.r() + qr.r(), writes=psum.r(bxo))
                    A("dve", OP("scalar_tensor_tensor", out=ust[:], in0=ust[:], scalar=gC, in1=psum[:, bkv, j * 128:(j + 1) * 128],
                                                                   op0=ALU.mult, op1=ALU.add), reads=ust.r() + psum.r(bkv), writes=ust.r())
                else:
                    A("dve", OP("tensor_copy", out=ust[:], in_=psum[:, bkv, j * 128:(j + 1) * 128]), reads=psum.r(bkv), writes=ust.r())
            banks.free(bkv)
            ring.free(pT)
            oT = ring.get()
            A("dve", OP("tensor_tensor", out=oT[:, 0:TT].rearrange("p (j c) -> p j c", j=4), in0=psum[:, bxo, :].rearrange("p (j c) -> p j c", j=4),
                                                      in1=cst[:, C_QDEC + hd * 128:C_QDEC + (hd + 1) * 128].unsqueeze(1).to_broadcast([128, 4, 128]), op=ALU.mult),
              reads=psum.r(bxo) + cst.r(), writes=oT.r())
            banks.free(bxo)
            ring.free(qr)
            ob = ring.get()
            A("act", OP("activation", out=bf(ob, TT, 0), in_=oT[:, 0:TT], func=AF.Copy), reads=oT.r(), writes=ob.r())
            A("act", OP("activation", out=bf(ob, TT, TT), in_=oT[:, 0:TT], func=AF.Square), reads=oT.r(), writes=ob.r())
            bm = banks.get(); bq = banks.get()
            A("pe", OP("matmul", out=psum[:, bm, :], lhsT=o128_b[:], rhs=bf(ob, TT, 0), start=True, stop=True),
              reads=ob.r() + o128_b.r(), writes=psum.r(bm))
            A("pe", OP("matmul", out=psum[:, bq, :], lhsT=o128_b[:], rhs=bf(ob, TT, TT), start=True, stop=True),
              reads=ob.r() + o128_b.r(), writes=psum.r(bq))
            ring.free(ob)
            m2 = ring.get()
            A("act", OP("activation", out=m2[:, 0:TT], in_=psum[:, bm, :], func=AF.Square), reads=psum.r(bm), writes=m2.r())
            A("dve", OP("tensor_tensor", out=m2[:, 0:TT], in0=psum[:, bq, :], in1=m2[:, 0:TT], op=ALU.subtract),
              reads=psum.r(bq) + m2.r(), writes=m2.r())
            banks.free(bq)
            A("dve", OP("tensor_scalar", out=m2[:, 0:TT], in0=m2[:, 0:TT], scalar1=0.0, scalar2=None, op0=ALU.max), reads=m2.r(), writes=m2.r())
            A("act", OP("activation", out=m2[:, 0:TT], in_=m2[:, 0:TT], func=AF.Ln, bias=NORM_EPS), reads=m2.r(), writes=m2.r())
            A("act", OP("activation", out=m2[:, 0:TT], in_=m2[:, 0:TT], func=AF.Exp, scale=-0.5), reads=m2.r(), writes=m2.r())
            A("dve", OP("tensor_tensor", out=oT[:, 0:TT], in0=oT[:, 0:TT], in1=psum[:, bm, :], op=ALU.subtract),
              reads=oT.r() + psum.r(bm), writes=oT.r())
            banks.free(bm)
            A("dve", OP("tensor_tensor", out=oT[:, 0:TT], in0=oT[:, 0:TT], in1=m2[:, 0:TT], op=ALU.mult),
              reads=oT.r() + m2.r(), writes=oT.r())
            ring.free(m2)
            A("dve", OP("scalar_tensor_tensor", out=mixed[:, hd, tsl(t)], in0=oT[:, 0:TT], scalar=PV(f"rnorm{l}", hd), in1=sg[:, 0:TT],
                                                                    op0=ALU.mult, op1=ALU.mult), reads=oT.r() + sg.r() + pv.r(), writes=mixed.r(hd * 4 + t))
            ring.free(oT, sg)

        pend = None
        for hd in range(4):
            wbq = ret_weights(hd)
            for t in range(NT):
                a_out = ret_A(hd, t, *wbq)
                if pend is not None:
                    ret_B(*pend)
                pend = (hd, t) + a_out
            wbufs.free(wbq[0])
        ret_B(*pend)
        wbufs.free(wg_t)
        dbg(f"yret{l}", mixed, mixed[:].rearrange("p c t -> p (c t)"), [128, 4 * SEQ], BF16)
        fw.pin = set()
        group_post(l, mixed, None, 2, wbufs, 512.0)
        ring.disable_extra()
        dbg(f"hmix{l}", hT, hT[:].rearrange("p c t -> p (c t)"), [128, 8 * SEQ])
        for cm in reversed(phase):
            cm.__exit__(None, None, None)
        phase = []
        fw.barrier()
        if upto == "mix":
            break

        rmsnorm_to_xn(f"gffn{l}")
        actb = psb("actb", [128, 4, SEQ], BF16, nslots=16)
        wus = Pool([psb(f"wu{i}", [128, 8, 1024], BF16) for i in range(2)])
        wds = Pool([psb(f"wd{i}", [128, 4, 1024], BF16) for i in range(2)])
        fcar = psb("fcar", [128, 4, 2, 2], F32, nslots=4)

        def ffn_load(grp):
            j0 = grp * 4
            wu = wus.get(); wd = wds.get()
            load_w(wu, wu[:, :, 0:512], w_up_d[l, :, j0 * 128:(j0 + 4) * 128].rearrange("(k p) n -> p k n", p=128))
            load_w(wu, wu[:, :, 512:1024], w_up_d[l, :, D_FF + j0 * 128:D_FF + (j0 + 4) * 128].rearrange("(k p) n -> p k n", p=128))
            load_w(wd, wd[:], w_down_d[l, j0 * 128:(j0 + 4) * 128, :].rearrange("(k p) n -> p k n", p=128))
            return wu, wd

        def ffn_up(grp, jj, t, wu):
            j = grp * 4 + jj
            outs = []
            for which in range(2):
                ch = j + 24 * which
                b = banks.get()
                proj(b, wu, lambda k, o=which * 512 + jj * 128: wu[:, k, o:o + 128], t)
                vc = ring.get()
                w0 = PV(f"fcw{l}", ch * 3 + 0); w1 = PV(f"fcw{l}", ch * 3 + 1); w2 = PV(f"fcw{l}", ch * 3 + 2)
                A("act", OP("activation", out=vc[:, 0:TT], in_=psum[:, b, :], func=AF.Identity, scale=w2, bias=PV(f"fcb{l}", ch)),
                  reads=psum.r(b) + pv.r(), writes=vc.r())
                A("dve", OP("scalar_tensor_tensor", out=vc[:, 1:TT], in0=psum[:, b, 0:TT - 1], scalar=w1, in1=vc[:, 1:TT],
                            op0=ALU.mult, op1=ALU.add), reads=psum.r(b) + vc.r() + pv.r(), writes=vc.r())
                A("dve", OP("scalar_tensor_tensor", out=vc[:, 2:TT], in0=psum[:, b, 0:TT - 2], scalar=w0, in1=vc[:, 2:TT],
                            op0=ALU.mult, op1=ALU.add), reads=psum.r(b) + vc.r() + pv.r(), writes=vc.r())
                if t > 0:
                    A("dve", OP("scalar_tensor_tensor", out=vc[:, 0:1], in0=fcar[:, jj, which, 1:2], scalar=w1, in1=vc[:, 0:1],
                                op0=ALU.mult, op1=ALU.add), reads=fcar.r(jj) + vc.r() + pv.r(), writes=vc.r())
                    A("dve", OP("scalar_tensor_tensor", out=vc[:, 0:2], in0=fcar[:, jj, which, 0:2], scalar=w0, in1=vc[:, 0:2],
                                op0=ALU.mult, op1=ALU.add), reads=fcar.r(jj) + vc.r() + pv.r(), writes=vc.r())
                if t < NT - 1:
                    A("act", OP("activation", out=fcar[:, jj, which, :], in_=psum[:, b, TT - 2:TT], func=AF.Copy),
                      reads=psum.r(b), writes=fcar.r(jj))
                banks.free(b)
                outs.append(vc)
            vc, gc = outs
            A("act", OP("activation", out=gc[:, 0:TT], in_=gc[:, 0:TT], func=AF.Gelu_apprx_tanh), reads=gc.r(), writes=gc.r())
            A("dve", OP("tensor_tensor", out=actb[:, jj, tsl(t)], in0=gc[:, 0:TT], in1=vc[:, 0:TT], op=ALU.mult),
              reads=vc.r() + gc.r(), writes=actb.r(jj * 4 + t))
            ring.free(vc, gc)

        def ffn_down(t, wd):
            for dc in range(8):
                b = banks.get()
                for k in range(4):
                    A("pe", OP("matmul", out=psum[:, b, :], lhsT=wd[:, k, dc * 128:(dc + 1) * 128], rhs=actb[:, k, tsl(t)],
                               start=(k == 0), stop=(k == 3)), reads=wd.r() + actb.r(k * 4 + t), writes=psum.r(b))
                A("dve", OP("tensor_tensor", out=hT[:, dc, tsl(t)], in0=hT[:, dc, tsl(t)], in1=psum[:, b, :], op=ALU.add),
                  reads=hs(dc, t) + psum.r(b), writes=hs(dc, t))
                banks.free(b)

        cur = ffn_load(0)
        for t in range(NT):
            for jj in range(4):
                ffn_up(0, jj, t, cur[0])
        for grp in range(6):
            nxt = ffn_load(grp + 1) if grp + 1 < 6 else None
            for t in range(NT):
                ffn_down(t, cur[1])
                if nxt is not None:
                    for jj in range(4):
                        ffn_up(grp + 1, jj, t, nxt[0])
            wus.free(cur[0]); wds.free(cur[1])
            cur = nxt
        dbg(f"hffn{l}", hT, hT[:].rearrange("p c t -> p (c t)"), [128, 8 * SEQ])
        for cm in reversed(phase):
            cm.__exit__(None, None, None)
        phase = []
        fw.barrier()

    for t in range(NT):
        b = banks.get()
        for c in range(8):
            sq = ring.get()
            A("act", OP("activation", out=bf(sq), in_=hT[:, c, tsl(t)], func=AF.Square), reads=hs(c, t), writes=sq.r())
            A("pe", OP("matmul", out=psum[:, b, :], lhsT=ones_b[:], rhs=bf(sq), start=(c == 0), stop=(c == 7)),
              reads=sq.r() + ones_b.r(), writes=psum.r(b))
            ring.free(sq)
        rs = ring.get()
        A("act", OP("activation", out=rs[:, 0:TT], in_=psum[:, b, :], func=AF.Ln, scale=1.0 / D_MODEL, bias=NORM_EPS), reads=psum.r(b), writes=rs.r())
        banks.free(b)
        A("act", OP("activation", out=rs[:, 0:TT], in_=rs[:, 0:TT], func=AF.Exp, scale=-0.5), reads=rs.r(), writes=rs.r())
        for c in range(8):
            ot = ring.get()
            A("dve", OP("scalar_tensor_tensor", out=ot[:, 0:TT], in0=hT[:, c, tsl(t)], scalar=PV("gfin", c), in1=rs[:, 0:TT],
                                                                         op0=ALU.mult, op1=ALU.mult), reads=hs(c, t) + rs.r() + pv.r(), writes=ot.r())
            ins = A("sp", OP("dma_start", out=outT_d[c * 128:(c + 1) * 128, tsl(t)], in_=ot[:, 0:TT]), reads=ot.r(), dma=True)
            out_dmas.append(ins)
            ring.free(ot)
        ring.free(rs)

    for cm in reversed(phase):
        cm.__exit__(None, None, None)
    fin = A("sp", None)
    for ins in out_dmas:
        fin.preds[ins] = True
    fw.emit()
    fw.close()
    return nc, dbg_out


def make_in_maps(inputs):
    inp = {k: np.asarray(v) for k, v in inputs.items()}
    pv = host_pv(inp)
    cst = host_consts()
    wbd, x1, x2, c1, c2 = host_struct(inp)
    shared = {
        "pv": pv, "cst": cst,
        "wbd": np.ascontiguousarray(wbd.reshape(DEPTH, 128, 1024)),
        "s5x1": x1, "s5x2": x2,
        "s5c1": np.ascontiguousarray(c1.reshape(DEPTH, 128, 512)),
        "s5c2": np.ascontiguousarray(c2.reshape(DEPTH, 128, 512)),
        "w_in": np.ascontiguousarray(inp["w_in"], dtype=np.float32),
        "s5_w_glu": np.ascontiguousarray(inp["s5_w_glu"], dtype=np.float32),
        "w_out": np.ascontiguousarray(inp["w_out"], dtype=np.float32),
        "w_up": np.ascontiguousarray(inp["w_up"], dtype=np.float32),
        "w_down": np.ascontiguousarray(inp["w_down"], dtype=np.float32),
    }
    maps = []
    for b in range(inp["x"].shape[0]):
        m = dict(shared)
        m["xT"] = np.ascontiguousarray(inp["x"][b].T.astype(np.float32))
        m["pos"] = np.ascontiguousarray(inp["positions"][b].astype(np.int32).reshape(1, SEQ))
        maps.append(m)
    return maps


_NC_CACHE = {}


def kernel(**inputs):
    if "nc" not in _NC_CACHE:
        _NC_CACHE["nc"] = build()[0]
    nc = _NC_CACHE["nc"]
    maps = make_in_maps(inputs)
    res = run_bass_kernel_spmd(nc, maps, core_ids=list(range(len(maps))))
    out = np.stack([np.ascontiguousarray(r["outT"].T) for r in res.results], axis=0)
    return out.astype(np.float32)
```

```python
import math
import os
from collections import deque
import numpy as np
import concourse.bass as bass
import concourse.mybir as mybir
from concourse.bass_utils import run_bass_kernel_spmd

F32 = mybir.dt.float32
BF16 = mybir.dt.bfloat16
I32 = mybir.dt.int32
AF = mybir.ActivationFunctionType
ALU = mybir.AluOpType

ENGS = ("pe", "act", "dve", "pool", "sp")
N_DMA_SEMS = 24

D_MODEL = 1024
SEQ = 2048
DEPTH = 2
TT = 512
NT = SEQ // TT
IN_WIDTH = 3584
D_FF = 3072
NORM_EPS = 1e-6
MAGIC = 12582912.0
TWO_PI_S = 6.2831850
GAMMAS = [1.0 - 2.0 ** (-5.0 - h) for h in range(4)]


class Reg:
    __slots__ = ("name", "w", "rds")

    def __init__(self, name):
        self.name = name
        self.w = None
        self.rds = []


class T:
    def __init__(self, h, name, nslots=1):
        self.h = h
        self.name = name
        self.regs = [Reg(f"{name}.{i}") for i in range(nslots)]

    def __getitem__(self, k):
        return self.h[k]

    def r(self, i=None, j=None):
        if i is None:
            return list(self.regs)
        if j is None:
            return [self.regs[i]]
        return self.regs[i:j]


class Ins:
    __slots__ = ("eng", "rec", "fn", "preds", "is_dma", "dma_sem", "dma_use", "cost", "lat", "seg",
                 "tset", "sched", "done", "rt", "pos", "waits", "needs_inc", "count", "waits_dma", "pin")

    def __init__(self, eng, rec, fn):
        self.eng = eng
        self.rec = rec
        self.fn = fn
        self.preds = {}
        self.is_dma = False
        self.dma_sem = None
        self.dma_use = None
        self.cost = 100.0
        self.lat = 0.0
        self.seg = 0
        self.tset = None
        self.sched = False
        self.done = 0.0
        self.rt = None
        self.pos = None
        self.waits = []
        self.needs_inc = False
        self.count = None
        self.pin = False


_ACT_GROUP = {}


def _act_group(func):
    if not _ACT_GROUP:
        _ACT_GROUP.update({AF.Exp: 1, AF.Ln: 1, AF.Gelu_apprx_tanh: 2, AF.Silu: 3, AF.Sin: 3, AF.Sigmoid: 4, AF.Sqrt: 5})
    return _ACT_GROUP.get(func)


def _free(ap):
    n = 1
    for d in ap.shape[1:]:
        n *= int(d)
    return n


def est_cost(eng, fn, is_dma):
    if fn is None:
        return 0.0, 0.0
    name, args, kw = fn
    if is_dma:
        out = kw["out"]
        nbytes = _free(out) * int(out.shape[0]) * 4
        return (1000.0 if eng == "pool" else 120.0), 2000.0 + nbytes / 150.0
    if eng == "pe":
        if name == "transpose":
            return 110.0, 0.0
        n = _free(kw["rhs"])
        return max(n, 64) / 1.9 + 10.0, 0.0
    out = kw.get("out")
    if out is None:
        out = args[0]
    n = _free(out)
    if eng == "act":
        return 150.0 + n / 1.2, 0.0
    if eng == "dve":
        if name == "tensor_tensor_scan":
            return 120.0 + 2.0 * n / 0.96, 0.0
        if name == "reciprocal":
            return 120.0 + 6.1 * n, 0.0
        if name in ("tensor_tensor", "scalar_tensor_tensor"):
            return 120.0 + n / 0.96, 0.0
        return 120.0 + n / 1.5, 0.0
    if eng == "pool":
        if name == "tensor_tensor":
            return 150.0 + 2.15 * n, 0.0
        return 150.0 + 1.2 * n, 0.0
    return 100.0, 0.0


class FW:
    SEM_LAT = 500.0
    WINDOW = int(os.environ.get('KW', '80'))
    WIN_ENG = {e: int(os.environ.get('KW_' + e.upper(), '0')) for e in ('pe', 'act', 'dve', 'pool', 'sp')}

    def __init__(self, nc):
        self.nc = nc
        self.all = []
        self.dma_rr = 0
        self.dma_rr_pool = 0
        self.dma_uses = [0] * N_DMA_SEMS
        self.dma_last = [None] * N_DMA_SEMS
        self.seg = 0
        self.seg_dma_uses = []
        self.pool_dmas = []
        self.pin = set()
        self.unpin_names = {'scalar_tensor_tensor', 'tensor_tensor'}
        self._stack = []

    def sb(self, name, shape, dtype, nslots=1):
        cm = self.nc.sbuf_tensor(name, list(shape), dtype)
        h = cm.__enter__()
        self._stack.append(cm)
        return T(h, name, nslots)

    def ps(self, name, shape, dtype, nslots=1):
        cm = self.nc.psum_tensor(name, list(shape), dtype)
        h = cm.__enter__()
        self._stack.append(cm)
        return T(h, name, nslots)

    @staticmethod
    def _add_pred(ins, p, kind):
        if p is None or p is ins:
            return
        if p.is_dma or ins.is_dma or p.eng != ins.eng:
            needs = True
        else:
            needs = (ins.eng != "pe")
        ins.preds[p] = ins.preds.get(p, False) or needs

    def op(self, eng, fn, reads=(), writes=(), dma=False):
        ins = Ins(eng, len(self.all), fn)
        ins.seg = self.seg
        ins.is_dma = dma
        ins.pin = (eng in self.pin) and not (fn is not None and fn[0] in self.unpin_names)
        self.all.append(ins)
        ins.cost, ins.lat = est_cost(eng, fn, dma)
        if eng == "act" and fn is not None and fn[0] == "activation":
            ins.tset = _act_group(fn[2].get("func"))
        if dma:
            half = N_DMA_SEMS // 2
            if eng == "pool":
                k = half + self.dma_rr_pool
                self.dma_rr_pool = (self.dma_rr_pool + 1) % half
            else:
                k = self.dma_rr
                self.dma_rr = (k + 1) % half
            self._add_pred(ins, self.dma_last[k], "SEM")
            self.dma_last[k] = ins
            self.dma_uses[k] += 1
            ins.dma_sem = k
            ins.dma_use = self.dma_uses[k]
            if eng == "pool":
                self.pool_dmas.append(ins)
                if len(self.pool_dmas) > 4:
                    self._add_pred(ins, self.pool_dmas[-5], "SEM")
        for r in reads:
            self._add_pred(ins, r.w, "RAW")
        for r in writes:
            self._add_pred(ins, r.w, "WAW")
            for p in r.rds:
                self._add_pred(ins, p, "WAR")
        for r in reads:
            r.rds.append(ins)
        for r in writes:
            r.w = ins
            r.rds = []
        return ins

    def barrier(self):
        self.seg_dma_uses.append(list(self.dma_uses))
        self.seg += 1

    def schedule(self):
        nseg = self.seg + 1
        final = {e: [] for e in ENGS}
        eng_free = {e: 0.0 for e in ENGS}
        cur_set = None
        bar_marks = []
        for sg in range(nseg):
            pending = {e: [i for i in self.all if i.seg == sg and i.eng == e] for e in ENGS}
            remaining = sum(len(v) for v in pending.values())
            while remaining:
                best = None
                best_t = None
                for e in ENGS:
                    lst = pending[e]
                    ef = eng_free[e]
                    lim = min(len(lst), self.WIN_ENG[e] or self.WINDOW)
                    if lim and lst[0].pin:
                        lim = 1
                    for wi in range(lim):
                        c = lst[wi]
                        if wi > 0 and c.pin:
                            break
                        if c.rt is None:
                            rt = 0.0
                            ok = True
                            for p, needs in c.preds.items():
                                if not p.sched:
                                    ok = False
                                    break
                                d = p.done + (self.SEM_LAT if needs else 0.0)
                                if d > rt:
                                    rt = d
                            if not ok:
                                continue
                            c.rt = rt
                        t = c.rt if c.rt > ef else ef
                        if e == "act" and c.tset is not None and c.tset != cur_set:
                            t += 1300.0
                        if best is None or t < best_t or (t == best_t and c.rec < best[1].rec):
                            best = (e, c, wi)
                            best_t = t
                        if t <= ef:
                            break
                assert best is not None, "scheduler deadlock"
                e, c, wi = best
                pending[e].pop(wi)
                remaining -= 1
                if e == "act" and c.tset is not None:
                    cur_set = c.tset
                end = best_t + c.cost
                eng_free[e] = end
                c.done = end + c.lat
                c.sched = True
                c.pos = len(final[e])
                final[e].append(c)
            if sg < nseg - 1:
                tmax = max(eng_free.values())
                dmax = max([i.done for i in self.all if i.seg == sg and i.is_dma] + [0.0])
                tmax = max(tmax, dmax)
                marks = {}
                for e in ENGS:
                    b = Ins(e, -1, None)
                    b.sched = True
                    b.pos = len(final[e])
                    final[e].append(b)
                    marks[e] = b
                    eng_free[e] = tmax
                bar_marks.append(marks)
        self.final = final
        self.bar_marks = bar_marks
        self.est_total = max(eng_free.values())

    def emit(self):
        nc = self.nc
        self.schedule()
        final = self.final
        for bi, marks in enumerate(self.bar_marks):
            for e, b in marks.items():
                for e2 in ENGS:
                    if e2 == e:
                        continue
                    pos2 = marks[e2].pos
                    for j in range(pos2 - 1, -1, -1):
                        p = final[e2][j]
                        if p.fn is not None and not p.is_dma:
                            b.preds[p] = True
                            break
                b.waits_dma = self.seg_dma_uses[bi]
        for e in ENGS:
            seen = {}
            for ins in final[e]:
                latest = {}
                for p, needs in ins.preds.items():
                    if not needs:
                        assert p.eng == e and p.pos < ins.pos, "ordering edge violated"
                        continue
                    if p.is_dma:
                        key = f"d{p.dma_sem}"
                        v = p.dma_use
                    else:
                        key = p.eng
                        v = p.pos
                        if p.eng == e:
                            assert p.pos < ins.pos
                    cur = latest.get(key)
                    if cur is None or v > cur[0]:
                        latest[key] = (v, p)
                for key, (v, p) in latest.items():
                    if seen.get(key, -1) >= v:
                        continue
                    seen[key] = v
                    ins.waits.append((key, p))
                    if not p.is_dma:
                        p.needs_inc = True
                wd = getattr(ins, "waits_dma", None) if ins.fn is None else None
                if wd is not None:
                    for k, u in enumerate(wd):
                        if u > 0 and seen.get(f"d{k}", -1) < u:
                            seen[f"d{k}"] = u
                            ins.waits.append((f"d{k}", u))
        sem_cms = []
        sems = {}
        for e in list(ENGS) + [f"d{k}" for k in range(N_DMA_SEMS)]:
            cm = nc.semaphore(f"s_{e}")
            sems[e] = cm.__enter__()
            sem_cms.append(cm)
        for e in ENGS:
            c = 0
            for ins in final[e]:
                if ins.needs_inc:
                    c += 1
                    ins.count = c

        def run(eng_name, eng):
            for ins in final[eng_name]:
                for (key, p) in ins.waits:
                    if isinstance(p, int):
                        eng.wait_ge(sems[key], 16 * p)
                    elif p.is_dma:
                        eng.wait_ge(sems[key], 16 * p.dma_use)
                    else:
                        eng.wait_ge(sems[key], p.count)
                if ins.fn is None:
                    continue
                name, args, kw = ins.fn
                bi = getattr(eng, name)(*args, **kw)
                if ins.dma_sem is not None:
                    bi.then_inc(sems[f"d{ins.dma_sem}"], 16)
                elif ins.needs_inc:
                    bi.then_inc(sems[eng_name], 1)

        with nc.Block() as block:
            @block.tensor
            def _(eng):
                run("pe", eng)

            @block.scalar
            def _(eng):
                run("act", eng)

            @block.vector
            def _(eng):
                run("dve", eng)

            @block.gpsimd
            def _(eng):
                run("pool", eng)

            @block.sync
            def _(eng):
                run("sp", eng)
        for cm in reversed(sem_cms):
            cm.__exit__(None, None, None)

    def close(self):
        for cm in reversed(self._stack):
            cm.__exit__(None, None, None)
        self._stack = []


def OP(name, *args, **kw):
    return (name, args, kw)


class TV:
    def __init__(self, parent, slot, ap):
        self.parent = parent
        self.slot = slot
        self.ap = ap
        self.is_view = True

    def __getitem__(self, k):
        return self.ap[k]

    def r(self):
        return self.parent.r(self.slot)


class Pool:
    def __init__(self, items):
        self.free_ = deque(items)
        self.n = len(items)
        self.extra = []

    def get(self, halo=False):
        assert self.free_, "scratch pool exhausted"
        if not halo:
            return self.free_.popleft()
        for i, it in enumerate(self.free_):
            if not getattr(it, "is_view", False):
                del self.free_[i]
                return it
        raise AssertionError("no halo-capable scratch tile free")

    def free(self, *items):
        for it in items:
            self.free_.append(it)

    def enable_extra(self, views):
        self.extra = list(views)
        for v in views:
            self.free_.appendleft(v)

    def disable_extra(self):
        for v in self.extra:
            assert v in self.free_, "extra scratch view still in use"
            self.free_.remove(v)
        self.extra = []


def _chunked(v, nchunk):
    return np.ascontiguousarray(np.asarray(v, np.float32).reshape(nchunk, 128).T)


class PVLayout:
    def __init__(self):
        self.idx = {}
        self.n = 0

    def add(self, name, k):
        self.idx[name] = (self.n, k)
        self.n += k


def pv_layout():
    pl = PVLayout()
    for l in range(DEPTH):
        for name, k in (("gmix", 8), ("gffn", 8), ("lcw", 16), ("lcb", 4), ("lba", 4), ("lbx", 4),
                        ("llam", 4), ("lnorm", 4), ("s5d", 4), ("s5bg", 4), ("s5n", 4), ("rnorm", 4),
                        ("fcw", 144), ("fcb", 48), ("s5lr", 32), ("s5li", 32), ("s5ldt", 32)):
            pl.add(f"{name}{l}", k)
    for name, k in (("gfin", 8), ("invf", 1), ("sgn", 1), ("nsgn", 1), ("vdec", 4), ("rowmask", 8)):
        pl.add(name, k)
    return pl


PVL = pv_layout()
C_ID = 0
C_IOTA = 128
C_MASK = C_IOTA + 512
C_QDEC = C_MASK + 128
NCST = C_QDEC + 512


def host_consts():
    cst = np.zeros((128, NCST), np.float32)
    cst[:, C_ID:C_ID + 128] = np.eye(128, dtype=np.float32)
    cst[:, C_IOTA:C_IOTA + 512] = np.arange(512, dtype=np.float32)[None, :]
    m = np.arange(128)[:, None]
    c = np.arange(128)[None, :]
    cst[:, C_MASK:C_MASK + 128] = np.where(c >= m, np.float32(128.0 ** -0.5), np.float32(0.0))
    for h in range(4):
        lg = np.log1p(-np.float32(2.0) ** np.float32(-5.0 - h)).astype(np.float32)
        cst[:, C_QDEC + h * 128:C_QDEC + (h + 1) * 128] = np.exp(lg * (np.arange(128, dtype=np.float32) + 1.0))[None, :]
    return cst


def host_pv(inp):
    pv = np.zeros((128, PVL.n), np.float32)

    def put(name, arr):
        o, k = PVL.idx[name]
        assert arr.shape == (128, k), (name, arr.shape, k)
        pv[:, o:o + k] = arr

    for l in range(DEPTH):
        put(f"gmix{l}", _chunked(inp["norm_mix"][l], 8))
        put(f"gffn{l}", _chunked(inp["norm_ffn"][l], 8))
        cw = np.asarray(inp["lru_conv_w"][l], np.float32)
        put(f"lcw{l}", np.ascontiguousarray(cw.reshape(4, 4, 128).transpose(2, 1, 0).reshape(128, 16)))
        put(f"lcb{l}", _chunked(inp["lru_conv_b"][l], 4))
        put(f"lba{l}", _chunked(np.asarray(inp["lru_ba"][l]).reshape(512), 4))
        put(f"lbx{l}", _chunked(np.asarray(inp["lru_bx"][l]).reshape(512), 4))
        put(f"llam{l}", _chunked(inp["lru_lambda"][l], 4))
        put(f"lnorm{l}", _chunked(inp["lru_norm"][l], 4))
        put(f"s5d{l}", _chunked(inp["s5_d"][l], 4))
        put(f"s5bg{l}", _chunked(inp["s5_b_glu"][l], 4))
        put(f"s5n{l}", _chunked(inp["s5_norm"][l], 4))
        put(f"rnorm{l}", _chunked(inp["ret_norm"][l], 4))
        fw_ = np.asarray(inp["ffn_conv_w"][l], np.float32)
        put(f"fcw{l}", np.ascontiguousarray(fw_.reshape(3, 48, 128).transpose(2, 1, 0).reshape(128, 144)))
        put(f"fcb{l}", _chunked(inp["ffn_conv_b"][l], 48))
        lr = np.asarray(inp["s5_lambda_re"][l], np.float32).T
        li = np.asarray(inp["s5_lambda_im"][l], np.float32).T
        put(f"s5lr{l}", np.concatenate([lr, lr], 0))
        put(f"s5li{l}", np.concatenate([li, li], 0))
        put(f"s5ldt{l}", np.broadcast_to(np.asarray(inp["s5_log_dt"][l], np.float32)[None, :], (128, 32)).copy())
    put("gfin", _chunked(inp["norm_final"], 8))
    half = 64
    inv = (np.float32(10000.0) ** (-np.arange(half, dtype=np.float32) * np.float32(2.0) / np.float32(128.0))).astype(np.float32)
    put("invf", np.concatenate([inv, inv])[:, None])
    sgn = np.concatenate([-np.ones(64, np.float32), np.ones(64, np.float32)])[:, None]
    put("sgn", sgn)
    put("nsgn", -sgn)
    vd = np.zeros((128, 4), np.float32)
    for h in range(4):
        lg = np.log1p(-np.float32(2.0) ** np.float32(-5.0 - h)).astype(np.float32)
        vd[:, h] = np.exp(-lg * (np.arange(128, dtype=np.float32) + 1.0))
    put("vdec", vd)
    rm = np.zeros((128, 8), np.float32)
    for g in range(8):
        rm[16 * g:16 * g + 16, g] = 1.0
    put("rowmask", rm)
    return pv


def host_struct(inp):
    wbd = np.zeros((DEPTH, 128, 2, 4, 128), np.float32)
    for l in range(DEPTH):
        for which, nm in enumerate(("lru_wa", "lru_wx")):
            w = np.asarray(inp[nm][l], np.float32)
            for c in range(4):
                for hb in range(2):
                    wbd[l, hb * 64:(hb + 1) * 64, which, c, hb * 64:(hb + 1) * 64] = w[2 * c + hb]
    x1 = np.zeros((DEPTH, 128, 512), np.float32)
    x2 = np.zeros((DEPTH, 128, 512), np.float32)
    c1 = np.zeros((DEPTH, 128, 4, 128), np.float32)
    c2 = np.zeros((DEPTH, 128, 4, 128), np.float32)
    for l in range(DEPTH):
        br = np.asarray(inp["s5_b_re"][l], np.float32).transpose(1, 0, 2).reshape(64, 512)
        bi = np.asarray(inp["s5_b_im"][l], np.float32).transpose(1, 0, 2).reshape(64, 512)
        x1[l] = np.concatenate([br, bi], 0)
        x2[l] = np.concatenate([bi, br], 0)
        cr = np.asarray(inp["s5_c_re"][l], np.float32).reshape(4, 128, 64)
        ci = np.asarray(inp["s5_c_im"][l], np.float32).reshape(4, 128, 64)
        c1[l] = np.concatenate([cr, ci], 2).transpose(1, 0, 2)
        c2[l] = np.concatenate([ci, cr], 2).transpose(1, 0, 2)
    return wbd, x1, x2, c1, c2


def build(debug=(), upto="all", nlayers=DEPTH):
    nc = bass.Bass("TRN2", target_bir_lowering=False)
    fw = FW(nc)
    dram = {}

    def din(name, shape, dtype=F32):
        dram[name] = nc.dram_tensor(name, list(shape), dtype, kind="ExternalInput").ap()
        return dram[name]

    xT_d = din("xT", [D_MODEL, SEQ])
    pos_d = din("pos", [1, SEQ], I32)
    pv_d = din("pv", [128, PVL.n])
    cst_d = din("cst", [128, NCST])
    wbd_d = din("wbd", [DEPTH, 128, 2 * 4 * 128])
    x1_d = din("s5x1", [DEPTH, 128, 512])
    x2_d = din("s5x2", [DEPTH, 128, 512])
    c1_d = din("s5c1", [DEPTH, 128, 512])
    c2_d = din("s5c2", [DEPTH, 128, 512])
    w_in_d = din("w_in", [DEPTH, D_MODEL, IN_WIDTH])
    w_glu_d = din("s5_w_glu", [DEPTH, 512, 512])
    w_out_d = din("w_out", [DEPTH, 1536, D_MODEL])
    w_up_d = din("w_up", [DEPTH, D_MODEL, 2 * D_FF])
    w_down_d = din("w_down", [DEPTH, D_FF, D_MODEL])
    outT_d = nc.dram_tensor("outT", [D_MODEL, SEQ], F32, kind="ExternalOutput").ap()
    dbg_out = {}
    out_dmas = []

    A = fw.op

    hT = fw.sb("hT", [128, 8, SEQ], F32, nslots=32)
    xn = fw.sb("xn", [128, 8, SEQ], BF16, nslots=32)
    rcos = fw.sb("rcos", [128, SEQ], F32, nslots=4)
    rsin = fw.sb("rsin", [128, SEQ], F32, nslots=4)
    pv = fw.sb("pvs", [128, PVL.n], F32)
    cst = fw.sb("csts", [128, NCST], F32)
    ident_b = fw.sb("ident_b", [128, 128], BF16)
    ones_b = fw.sb("ones_b", [128, 128], BF16)
    o128_b = fw.sb("o128_b", [128, 128], BF16)
    NRING = 9
    ring = Pool([fw.sb(f"ring{i}", [128, 520], F32) for i in range(NRING)])
    psum = fw.ps("psum", [128, 8, 512], F32, nslots=8)
    banks = Pool(list(range(8)))
    rot_views = [TV(rcos, i, rcos[:, i * TT:(i + 1) * TT]) for i in range(4)] + [TV(rsin, i, rsin[:, i * TT:(i + 1) * TT]) for i in range(4)]

    def PV(name, c=0, n=1):
        o, k = PVL.idx[name]
        return pv[:, o + c:o + c + n]

    def hs(c, t):
        return hT.r(c * 4 + t)

    def xs(c, t):
        return xn.r(c * 4 + t)

    def tsl(t):
        return slice(t * TT, (t + 1) * TT)

    def bf(tile, n=TT, off=0):
        return tile[:].bitcast(BF16)[:, off:off + n]

    def dbg(name, tile, ap, shape, dtype=F32):
        if name not in debug:
            return
        d = nc.dram_tensor("dbg_" + name, list(shape), dtype, kind="ExternalOutput").ap()
        dbg_out[name] = d
        ins = A("sp", OP("dma_start", out=d, in_=ap), reads=tile.r(), dma=True)
        out_dmas.append(ins)

    A("sp", OP("dma_start", out=pv[:], in_=pv_d), writes=pv.r(), dma=True)
    A("sp", OP("dma_start", out=cst[:], in_=cst_d), writes=cst.r(), dma=True)
    for c in range(8):
        for t in range(NT):
            A("sp", OP("dma_start", out=hT[:, c, tsl(t)], in_=xT_d[c * 128:(c + 1) * 128, tsl(t)]),
              writes=hs(c, t), dma=True)
    A("dve", OP("memset", ones_b[:], 1.0), writes=ones_b.r())
    A("dve", OP("memset", o128_b[:], 1.0 / 128.0), writes=o128_b.r())
    A("act", OP("activation", out=ident_b[:], in_=cst[:, C_ID:C_ID + 128], func=AF.Copy),
      reads=cst.r(), writes=ident_b.r())
    ident_f = cst[:, C_ID:C_ID + 128]
    iota = cst[:, C_IOTA:C_IOTA + 512]
    maskT = cst[:, C_MASK:C_MASK + 128]

    def load_w(dst_tile, dst_ap, src_ap):
        return A("pool", OP("dma_start", out=dst_ap, in_=src_ap), writes=dst_tile.r(), dma=True)

    def rmsnorm_to_xn(gname):
        for t in range(NT):
            b = banks.get()
            for c in range(8):
                sq = ring.get()
                A("act", OP("activation", out=bf(sq), in_=hT[:, c, tsl(t)], func=AF.Square),
                  reads=hs(c, t), writes=sq.r())
                A("pe", OP("matmul", out=psum[:, b, :], lhsT=ones_b[:], rhs=bf(sq), start=(c == 0), stop=(c == 7)),
                  reads=sq.r() + ones_b.r(), writes=psum.r(b))
                ring.free(sq)
            rs = ring.get()
            A("act", OP("activation", out=rs[:, 0:TT], in_=psum[:, b, :], func=AF.Ln, scale=1.0 / D_MODEL, bias=NORM_EPS), reads=psum.r(b), writes=rs.r())
            banks.free(b)
            A("act", OP("activation", out=rs[:, 0:TT], in_=rs[:, 0:TT], func=AF.Exp, scale=-0.5), reads=rs.r(), writes=rs.r())
            for c in range(8):
                A("dve", OP("scalar_tensor_tensor", out=xn[:, c, tsl(t)], in0=hT[:, c, tsl(t)], scalar=PV(gname, c),
                                                                      in1=rs[:, 0:TT], op0=ALU.mult, op1=ALU.mult),
                  reads=hs(c, t) + rs.r() + pv.r(), writes=xs(c, t))
            ring.free(rs)

    def proj(b, wtile, w_ap_fn, t):
        for k in range(8):
            A("pe", OP("matmul", out=psum[:, b, :], lhsT=w_ap_fn(k), rhs=xn[:, k, tsl(t)], start=(k == 0), stop=(k == 7)),
              reads=wtile.r() + xs(k, t), writes=psum.r(b))

    def group_post(l, mixed, gname, grp, wbufs_pool, eps_div):
        wo = wbufs_pool.get()
        wo_v = wo[:].rearrange("p (k n) -> p k n", k=4)
        load_w(wo, wo_v, w_out_d[l, grp * 512:(grp + 1) * 512, :].rearrange("(k p) n -> p k n", p=128))
        for t in range(NT):
            if gname is not None:
                b = banks.get()
                for c in range(4):
                    sq = ring.get()
                    A("act", OP("activation", out=bf(sq), in_=mixed[:, c, tsl(t)], func=AF.Square),
                      reads=mixed.r(c * 4 + t), writes=sq.r())
                    A("pe", OP("matmul", out=psum[:, b, :], lhsT=ones_b[:], rhs=bf(sq), start=(c == 0), stop=(c == 3)),
                      reads=sq.r() + ones_b.r(), writes=psum.r(b))
                    ring.free(sq)
                rs = ring.get()
                A("act", OP("activation", out=rs[:, 0:TT], in_=psum[:, b, :], func=AF.Ln, scale=1.0 / 512.0, bias=NORM_EPS), reads=psum.r(b), writes=rs.r())
                banks.free(b)
                A("act", OP("activation", out=rs[:, 0:TT], in_=rs[:, 0:TT], func=AF.Exp, scale=-0.5), reads=rs.r(), writes=rs.r())
                for c in range(4):
                    A("dve", OP("scalar_tensor_tensor", out=mixed[:, c, tsl(t)], in0=mixed[:, c, tsl(t)],
                                                                          scalar=PV(gname + str(l), c), in1=rs[:, 0:TT],
                                                                          op0=ALU.mult, op1=ALU.mult),
                      reads=mixed.r(c * 4 + t) + rs.r() + pv.r(), writes=mixed.r(c * 4 + t))
                ring.free(rs)
            for dc in range(8):
                b = banks.get()
                for k in range(4):
                    A("pe", OP("matmul", out=psum[:, b, :], lhsT=wo_v[:, k, dc * 128:(dc + 1) * 128],
                                                           rhs=mixed[:, k, tsl(t)], start=(k == 0), stop=(k == 3)),
                      reads=wo.r() + mixed.r(k * 4 + t), writes=psum.r(b))
                A("dve", OP("tensor_tensor", out=hT[:, dc, tsl(t)], in0=hT[:, dc, tsl(t)], in1=psum[:, b, :], op=ALU.add),
                  reads=hs(dc, t) + psum.r(b), writes=hs(dc, t))
                banks.free(b)
        wbufs_pool.free(wo)

    def build_rot():
        for t in range(NT):
            pi_ = ring.get(); pf = ring.get(); k_ = ring.get()
            A("sp", OP("dma_start", out=pi_[:].bitcast(I32)[:, 0:TT],
                                                          in_=pos_d[0:1, tsl(t)].to_broadcast([128, TT])),
              writes=pi_.r(), dma=True)
            A("dve", OP("tensor_copy", out=pf[:, 0:TT], in_=pi_[:].bitcast(I32)[:, 0:TT]),
              reads=pi_.r(), writes=pf.r())
            A("dve", OP("tensor_scalar", out=pf[:, 0:TT], in0=pf[:, 0:TT], scalar1=PV("invf"), scalar2=None,
                                                      op0=ALU.mult), reads=pf.r() + pv.r(), writes=pf.r())
            A("dve", OP("tensor_scalar", out=k_[:, 0:TT], in0=pf[:, 0:TT], scalar1=1.0 / (2.0 * math.pi),
                                                             scalar2=MAGIC, op0=ALU.mult, op1=ALU.add),
              reads=pf.r(), writes=k_.r())
            A("dve", OP("tensor_scalar", out=k_[:, 0:TT], in0=k_[:, 0:TT], scalar1=MAGIC, scalar2=None,
                                                      op0=ALU.subtract), reads=k_.r(), writes=k_.r())
            C1 = 6.28125
            C2 = 2.0 * math.pi - 6.28125
            A("dve", OP("scalar_tensor_tensor", out=pf[:, 0:TT], in0=k_[:, 0:TT], scalar=-C1, in1=pf[:, 0:TT],
                                                                    op0=ALU.mult, op1=ALU.add), reads=pf.r() + k_.r(), writes=pf.r())
            A("dve", OP("scalar_tensor_tensor", out=pf[:, 0:TT], in0=k_[:, 0:TT], scalar=-C2, in1=pf[:, 0:TT],
                                                                    op0=ALU.mult, op1=ALU.add), reads=pf.r() + k_.r(), writes=pf.r())
            A("dve", OP("tensor_scalar", out=pf[:, 0:TT], in0=pf[:, 0:TT], scalar1=3.1415925, scalar2=-3.1415925,
                                                      op0=ALU.min, op1=ALU.max), reads=pf.r(), writes=pf.r())
            A("act", OP("activation", out=rsin[:, tsl(t)], in_=pf[:, 0:TT], func=AF.Sin, scale=PV("sgn")),
              reads=pf.r() + pv.r(), writes=rsin.r(t))
            A("act", OP("activation", out=k_[:, 0:TT], in_=pf[:, 0:TT], func=AF.Abs),
              reads=pf.r(), writes=k_.r())
            A("act", OP("activation", out=rcos[:, tsl(t)], in_=k_[:, 0:TT], func=AF.Sin, scale=-1.0,
                                                        bias=math.pi / 2.0 - 1e-6), reads=k_.r(), writes=rcos.r(t))
            ring.free(pi_, pf, k_)


    phase = []
    for l in range(nlayers):

        def psb(name, shape, dtype, nslots=1):
            cm = nc.sbuf_tensor(f"{name}_{l}", list(shape), dtype)
            h = cm.__enter__()
            phase.append(cm)
            return T(h, name, nslots)

        mixed = psb("mixed", [128, 4, SEQ], BF16, nslots=16)
        aux = psb("aux", [128, 8192], BF16, nslots=16)
        wbufs = Pool([psb(f"wbuf{i}", [128, 4096], BF16) for i in range(2)])
        wbd = psb("wbd", [128, 2, 4, 128], BF16)
        lpar = psb("lpar", [128, 16], F32)
        lcar = psb("lcar", [128, 4, 4], F32)
        s5p = psb("s5p", [128, 32, 8], F32)
        s5off = psb("s5off", [128, 32, 4], F32)
        bst1 = psb("bst1", [128, 512], BF16)
        bst2 = psb("bst2", [128, 512], BF16)
        w5tiles = [psb(f"w5_{i}", [128, 8, 128], BF16) for i in range(2)]
        w5pool = Pool(list(w5tiles))
        lhs_b = psb("lhs_b", [128, 2, 8, 128], BF16, nslots=2)
        lhs_c = psb("lhs_c", [128, 2, 8, 128], BF16, nslots=2)
        zcar = psb("zcar", [128, 32], F32)
        ust = psb("ust", [128, 128], F32)
        pbs = [psb(f"pb{i}", [128, 128], BF16) for i in range(2)]
        ret_views = []
        for tl_ in (lhs_b, lhs_c):
            f32v = tl_[:].rearrange("p a g n -> p (a g n)").bitcast(F32)
            ret_views += [TV(tl_, 0, f32v[:, 0:TT]), TV(tl_, 1, f32v[:, TT:2 * TT])]
        for tl_ in w5tiles:
            ret_views.append(TV(tl_, 0, tl_[:].rearrange("p k n -> p (k n)").bitcast(F32)))

        load_w(wbd, wbd[:].rearrange("p a c n -> p (a c n)"), wbd_d[l])
        A("act", OP("activation", out=lpar[:, 0:4], in_=PV(f"llam{l}", 0, 4), func=AF.Exp, scale=-1.0), reads=pv.r(), writes=lpar.r())
        A("act", OP("activation", out=lpar[:, 0:4], in_=lpar[:, 0:4], func=AF.Ln, bias=1.0), reads=lpar.r(), writes=lpar.r())
        A("dve", OP("tensor_scalar", out=lpar[:, 4:8], in0=lpar[:, 0:4], scalar1=-4.0, scalar2=None, op0=ALU.mult), reads=lpar.r(), writes=lpar.r())
        A("dve", OP("tensor_scalar", out=lpar[:, 0:4], in0=lpar[:, 0:4], scalar1=-8.0, scalar2=None, op0=ALU.mult), reads=lpar.r(), writes=lpar.r())
        A("dve", OP("tensor_scalar", out=lpar[:, 8:12], in0=PV(f"lba{l}", 0, 4), scalar1=0.5, scalar2=None, op0=ALU.mult), reads=pv.r(), writes=lpar.r())
        A("dve", OP("tensor_scalar", out=lpar[:, 12:16], in0=PV(f"lbx{l}", 0, 4), scalar1=0.5, scalar2=None, op0=ALU.mult), reads=pv.r(), writes=lpar.r())
        A("dve", OP("memset", lcar[:], 0.0), writes=lcar.r())
        A("dve", OP("memset", zcar[:], 0.0), writes=zcar.r())

        S = lambda j: s5p[:, :, j]
        LR = PV(f"s5lr{l}", 0, 32)
        LI = PV(f"s5li{l}", 0, 32)
        R_, W_ = s5p.r(), s5p.r()
        A("act", OP("activation", out=S(6), in_=PV(f"s5ldt{l}", 0, 32), func=AF.Exp), reads=pv.r(), writes=W_)
        A("dve", OP("tensor_tensor", out=S(0), in0=LR, in1=S(6), op=ALU.mult), reads=R_ + pv.r(), writes=W_)
        A("act", OP("activation", out=S(0), in_=S(0), func=AF.Exp), reads=R_, writes=W_)
        A("dve", OP("tensor_tensor", out=S(7), in0=LI, in1=S(6), op=ALU.mult), reads=R_ + pv.r(), writes=W_)
        A("dve", OP("tensor_scalar", out=S(6), in0=S(7), scalar1=1.0 / (2.0 * math.pi), scalar2=MAGIC, op0=ALU.mult, op1=ALU.add), reads=R_, writes=W_)
        A("dve", OP("tensor_scalar", out=S(6), in0=S(6), scalar1=MAGIC, scalar2=None, op0=ALU.subtract), reads=R_, writes=W_)
        A("dve", OP("scalar_tensor_tensor", out=S(1), in0=S(7), scalar=1.0 / (2.0 * math.pi), in1=S(6), op0=ALU.mult, op1=ALU.subtract), reads=R_, writes=W_)
        A("act", OP("activation", out=S(7), in_=S(1), func=AF.Sin, scale=TWO_PI_S), reads=R_, writes=W_)
        A("act", OP("activation", out=S(6), in_=S(1), func=AF.Abs), reads=R_, writes=W_)
        A("act", OP("activation", out=S(6), in_=S(6), func=AF.Sin, scale=-TWO_PI_S, bias=math.pi / 2.0 - 1e-6), reads=R_, writes=W_)
        A("dve", OP("tensor_tensor", out=S(6), in0=S(6), in1=S(0), op=ALU.mult), reads=R_, writes=W_)
        A("dve", OP("tensor_scalar", out=S(6), in0=S(6), scalar1=-1.0, scalar2=None, op0=ALU.add), reads=R_, writes=W_)
        A("dve", OP("tensor_tensor", out=S(7), in0=S(7), in1=S(0), op=ALU.mult), reads=R_, writes=W_)
        A("dve", OP("tensor_tensor", out=S(5), in0=LR, in1=LR, op=ALU.mult), reads=R_ + pv.r(), writes=W_)
        A("dve", OP("tensor_tensor", out=S(4), in0=LI, in1=LI, op=ALU.mult), reads=R_ + pv.r(), writes=W_)
        A("dve", OP("tensor_tensor", out=S(5), in0=S(5), in1=S(4), op=ALU.add), reads=R_, writes=W_)
        A("dve", OP("reciprocal", out=S(5), in_=S(5)), reads=R_, writes=W_)
        A("dve", OP("tensor_tensor", out=S(2), in0=S(6), in1=LR, op=ALU.mult), reads=R_ + pv.r(), writes=W_)
        A("dve", OP("tensor_tensor", out=S(4), in0=S(7), in1=LI, op=ALU.mult), reads=R_ + pv.r(), writes=W_)
        A("dve", OP("tensor_tensor", out=S(2), in0=S(2), in1=S(4), op=ALU.add), reads=R_, writes=W_)
        A("dve", OP("tensor_tensor", out=S(2), in0=S(2), in1=S(5), op=ALU.mult), reads=R_, writes=W_)
        A("dve", OP("tensor_tensor", out=S(3), in0=S(7), in1=LR, op=ALU.mult), reads=R_ + pv.r(), writes=W_)
        A("dve", OP("tensor_tensor", out=S(4), in0=S(6), in1=LI, op=ALU.mult), reads=R_ + pv.r(), writes=W_)
        A("dve", OP("tensor_tensor", out=S(3), in0=S(3), in1=S(4), op=ALU.subtract), reads=R_, writes=W_)
        A("dve", OP("tensor_tensor", out=S(3), in0=S(3), in1=S(5), op=ALU.mult), reads=R_, writes=W_)
        A("dve", OP("tensor_copy", out=S(5), in_=S(3)), reads=R_, writes=W_)
        A("dve", OP("tensor_scalar", out=S(3), in0=S(3), scalar1=PV("sgn"), scalar2=None, op0=ALU.mult), reads=R_ + pv.r(), writes=W_)
        A("dve", OP("tensor_scalar", out=S(4), in0=S(2), scalar1=PV("nsgn"), scalar2=None, op0=ALU.mult), reads=R_ + pv.r(), writes=W_)
        for t in range(NT):
            A("dve", OP("tensor_scalar", out=s5off[:, :, t], in0=S(1), scalar1=float(TT * t), scalar2=MAGIC, op0=ALU.mult, op1=ALU.add),
              reads=R_, writes=s5off.r())
            A("dve", OP("tensor_scalar", out=s5off[:, :, t], in0=s5off[:, :, t], scalar1=MAGIC, scalar2=None, op0=ALU.subtract),
              reads=s5off.r(), writes=s5off.r())
            A("dve", OP("scalar_tensor_tensor", out=s5off[:, :, t], in0=S(1), scalar=float(TT * t), in1=s5off[:, :, t],
                                                           op0=ALU.mult, op1=ALU.subtract), reads=R_ + s5off.r(), writes=s5off.r())
        cn1 = ring.get(); cn2 = ring.get()
        A("sp", OP("dma_start", out=cn1[:, 0:512], in_=x1_d[l]), writes=cn1.r(), dma=True)
        A("sp", OP("dma_start", out=cn2[:, 0:512], in_=x2_d[l]), writes=cn2.r(), dma=True)
        v3 = lambda tl: tl[:, 0:512].rearrange("p (g c) -> p g c", c=16)
        bc = lambda j: s5p[:, :, j:j + 1].to_broadcast([128, 32, 16])
        t1 = ring.get(); t2 = ring.get()
        t1v = t1[:, 0:512].rearrange("p (g c) -> p g c", c=16)
        t2v = t2[:, 0:512].rearrange("p (g c) -> p g c", c=16)
        A("dve", OP("tensor_tensor", out=t1v, in0=v3(cn1), in1=bc(2), op=ALU.mult), reads=cn1.r() + R_, writes=t1.r())
        A("dve", OP("tensor_tensor", out=t2v, in0=v3(cn2), in1=bc(3), op=ALU.mult), reads=cn2.r() + R_, writes=t2.r())
        A("dve", OP("tensor_tensor", out=bst1[:], in0=t1[:, 0:512], in1=t2[:, 0:512], op=ALU.add), reads=t1.r() + t2.r(), writes=bst1.r())
        A("dve", OP("tensor_tensor", out=t1v, in0=v3(cn2), in1=bc(4), op=ALU.mult), reads=cn2.r() + R_, writes=t1.r())
        A("dve", OP("tensor_tensor", out=t2v, in0=v3(cn1), in1=bc(5), op=ALU.mult), reads=cn1.r() + R_, writes=t2.r())
        A("dve", OP("tensor_tensor", out=bst2[:], in0=t1[:, 0:512], in1=t2[:, 0:512], op=ALU.add), reads=t1.r() + t2.r(), writes=bst2.r())
        ring.free(t1, t2, cn1, cn2)
        A("pool", OP("memset", lhs_c[:].rearrange("p a g n -> p (a g n)"), 0.0), writes=lhs_c.r())
        dbg(f"bst1_{l}", bst1, bst1[:], [128, 512], BF16)
        dbg(f"s5p_{l}", s5p, s5p[:].rearrange("p g j -> p (g j)"), [128, 256])

        rmsnorm_to_xn(f"gmix{l}")
        dbg(f"xn{l}", xn, xn[:].rearrange("p c t -> p (c t)"), [128, 8 * SEQ], BF16)

        zbf = aux[:].rearrange("p (c t) -> p c t", c=4)
        def lru_chunk(c):
            wb = wbufs.get()
            wv_ = wb[:].rearrange("p (k n) -> p k n", k=8)
            load_w(wb, wv_[:, :, 0:128], w_in_d[l, :, c * 128:(c + 1) * 128].rearrange("(k p) n -> p k n", p=128))
            load_w(wb, wv_[:, :, 128:256], w_in_d[l, :, 512 + c * 128:512 + (c + 1) * 128].rearrange("(k p) n -> p k n", p=128))
            return wb, wv_

        def lru_unit(c, t, wb, wv_):
            bx = banks.get(); bg = banks.get()
            proj(bx, wb, lambda k: wv_[:, k, 0:128], t)
            proj(bg, wb, lambda k: wv_[:, k, 128:256], t)
            xp = ring.get(halo=True); gy = ring.get(); xc = ring.get(); xcb = ring.get()
            A("act", OP("activation", out=xp[:, 0:3], in_=lcar[:, c, 0:3], func=AF.Copy), reads=lcar.r(), writes=xp.r())
            A("act", OP("activation", out=xp[:, 3:3 + TT], in_=psum[:, bx, :], func=AF.Copy), reads=psum.r(bx), writes=xp.r())
            A("act", OP("activation", out=lcar[:, c, 0:3], in_=xp[:, TT:TT + 3], func=AF.Copy), reads=xp.r(), writes=lcar.r())
            A("act", OP("activation", out=gy[:, 0:TT], in_=psum[:, bg, :], func=AF.Gelu_apprx_tanh), reads=psum.r(bg), writes=gy.r())
            banks.free(bx, bg)
            A("act", OP("activation", out=xc[:, 0:TT], in_=xp[:, 3:3 + TT], func=AF.Identity,
                                                          scale=PV(f"lcw{l}", c * 4 + 3), bias=PV(f"lcb{l}", c)),
              reads=xp.r() + pv.r(), writes=xc.r())
            for k in range(3):
                A("dve", OP("scalar_tensor_tensor", out=xc[:, 0:TT], in0=xp[:, k:k + TT], scalar=PV(f"lcw{l}", c * 4 + k),
                                                                            in1=xc[:, 0:TT], op0=ALU.mult, op1=ALU.add),
                  reads=xp.r() + xc.r() + pv.r(), writes=xc.r())
            A("act", OP("activation", out=bf(xcb), in_=xc[:, 0:TT], func=AF.Copy), reads=xc.r(), writes=xcb.r())
            ring.free(xp)
            br_ = banks.get(); bi_ = banks.get()
            A("pe", OP("matmul", out=psum[:, br_, :], lhsT=wbd[:, 0, c, :], rhs=bf(xcb), start=True, stop=True),
              reads=wbd.r() + xcb.r(), writes=psum.r(br_))
            A("pe", OP("matmul", out=psum[:, bi_, :], lhsT=wbd[:, 1, c, :], rhs=bf(xcb), start=True, stop=True),
              reads=wbd.r() + xcb.r(), writes=psum.r(bi_))
            rr = ring.get(); ii = ring.get(); a2 = ring.get()
            A("act", OP("activation", out=rr[:, 0:TT], in_=psum[:, br_, :], func=AF.Tanh, scale=0.5, bias=lpar[:, 8 + c:9 + c]),
              reads=psum.r(br_) + lpar.r(), writes=rr.r())
            A("act", OP("activation", out=ii[:, 0:TT], in_=psum[:, bi_, :], func=AF.Tanh, scale=0.5, bias=lpar[:, 12 + c:13 + c]),
              reads=psum.r(bi_) + lpar.r(), writes=ii.r())
            banks.free(br_, bi_)
            ring.free(xcb)
            A("act", OP("activation", out=a2[:, 0:TT], in_=rr[:, 0:TT], func=AF.Exp, scale=lpar[:, c:c + 1], bias=lpar[:, c:c + 1]),
              reads=rr.r() + lpar.r(), writes=a2.r())
            A("act", OP("activation", out=rr[:, 0:TT], in_=rr[:, 0:TT], func=AF.Exp, scale=lpar[:, 4 + c:5 + c], bias=lpar[:, 4 + c:5 + c]),
              reads=rr.r() + lpar.r(), writes=rr.r())
            A("dve", OP("tensor_scalar", out=a2[:, 0:TT], in0=a2[:, 0:TT], scalar1=1.0, scalar2=-1e-30, op0=ALU.subtract, op1=ALU.min),
              reads=a2.r(), writes=a2.r())
            A("act", OP("activation", out=a2[:, 0:TT], in_=a2[:, 0:TT], func=AF.Ln, scale=-1.0), reads=a2.r(), writes=a2.r())
            A("act", OP("activation", out=a2[:, 0:TT], in_=a2[:, 0:TT], func=AF.Exp, scale=0.5), reads=a2.r(), writes=a2.r())
            A("dve", OP("scalar_tensor_tensor", out=ii[:, 0:TT], in0=ii[:, 0:TT], scalar=1.0, in1=xc[:, 0:TT], op0=ALU.add, op1=ALU.mult),
              reads=ii.r() + xc.r(), writes=ii.r())
            A("dve", OP("scalar_tensor_tensor", out=ii[:, 0:TT], in0=ii[:, 0:TT], scalar=0.5, in1=a2[:, 0:TT], op0=ALU.mult, op1=ALU.mult),
              reads=ii.r() + a2.r(), writes=ii.r())
            hh = xc
            A("dve", OP("tensor_tensor_scan", out=hh[:, 0:TT], data0=rr[:, 0:TT], data1=ii[:, 0:TT],
                                                                        initial=lcar[:, c, 3:4], op0=ALU.mult, op1=ALU.add),
              reads=rr.r() + ii.r() + lcar.r(), writes=hh.r())
            A("act", OP("activation", out=lcar[:, c, 3:4], in_=hh[:, TT - 1:TT], func=AF.Copy), reads=hh.r(), writes=lcar.r())
            A("dve", OP("tensor_tensor", out=mixed[:, c, tsl(t)], in0=hh[:, 0:TT], in1=gy[:, 0:TT], op=ALU.mult),
              reads=hh.r() + gy.r(), writes=mixed.r(c * 4 + t))
            ring.free(rr, ii, a2, xc, gy)

        def s5_chunk(c):
            wb = w5pool.get()
            wv_ = wb
            load_w(wb, wv_[:, :, 0:128], w_in_d[l, :, 1024 + c * 128:1024 + (c + 1) * 128].rearrange("(k p) n -> p k n", p=128))
            for which, bst in enumerate((bst1, bst2)):
                b = banks.get()
                tpv = psum[:, b, :].bitcast(BF16)
                A("pe", OP("transpose", out=tpv[:, 0:128], in_=bst[:, c * 128:(c + 1) * 128], identity=ident_b[:]),
                  reads=bst.r() + ident_b.r(), writes=psum.r(b))
                tb = ring.get()
                A("act", OP("activation", out=bf(tb, 128), in_=tpv[:, 0:128], func=AF.Copy), reads=psum.r(b), writes=tb.r())
                banks.free(b)
                for g in range(8):
                    A("dve", OP("tensor_scalar", out=lhs_b[:, which, g, :], in0=bf(tb, 128), scalar1=PV("rowmask", g),
                                                                                scalar2=None, op0=ALU.mult),
                      reads=tb.r() + pv.r(), writes=lhs_b.r())
                ring.free(tb)
            for which, (cd, sname) in enumerate(((c1_d, "nsgn"), (c2_d, None))):
                cn = ring.get()
                A("sp", OP("dma_start", out=cn[:, 0:128], in_=cd[l, :, c * 128:(c + 1) * 128]), writes=cn.r(), dma=True)
                b = banks.get()
                A("pe", OP("transpose", out=psum[:, b, 0:128], in_=cn[:, 0:128], identity=ident_f),
                  reads=cn.r() + cst.r(), writes=psum.r(b))
                ring.free(cn)
                for g in range(8):
                    if sname is not None:
                        A("dve", OP("tensor_scalar", out=lhs_c[:, which, g, g * 16:(g + 1) * 16], in0=psum[:, b, g * 16:(g + 1) * 16],
                                                                                  scalar1=PV("nsgn"), scalar2=None, op0=ALU.mult),
                          reads=psum.r(b) + pv.r(), writes=lhs_c.r())
                    else:
                        A("dve", OP("tensor_scalar", out=lhs_c[:, which, g, g * 16:(g + 1) * 16], in0=psum[:, b, g * 16:(g + 1) * 16],
                                                                                  scalar1=-1.0, scalar2=None, op0=ALU.mult),
                          reads=psum.r(b), writes=lhs_c.r())
                banks.free(b)
            return wb, wv_

        def s5_unit(c, t, wb, wv_):
            bu = banks.get()
            proj(bu, wb, lambda k: wv_[:, k, 0:128], t)
            uf = ring.get(); ub = ring.get()
            A("act", OP("activation", out=uf[:, 0:TT], in_=psum[:, bu, :], func=AF.Copy), reads=psum.r(bu), writes=uf.r())
            A("act", OP("activation", out=bf(ub), in_=psum[:, bu, :], func=AF.Copy), reads=psum.r(bu), writes=ub.r())
            banks.free(bu)
            by = banks.get()
            for g in range(8):
                G = c * 8 + g
                b1 = banks.get(); b2 = banks.get()
                A("pe", OP("matmul", out=psum[:, b1, :], lhsT=lhs_b[:, 0, g, :], rhs=bf(ub), start=True, stop=True),
                  reads=lhs_b.r() + ub.r(), writes=psum.r(b1))
                A("pe", OP("matmul", out=psum[:, b2, :], lhsT=lhs_b[:, 1, g, :], rhs=bf(ub), start=True, stop=True),
                  reads=lhs_b.r() + ub.r(), writes=psum.r(b2))
                uu = ring.get(); kk = ring.get(); tc_ = ring.get(); ts_ = ring.get()
                A("act", OP("activation", out=uu[:, 0:TT], in_=iota, func=AF.Identity, scale=s5p[:, G, 1:2], bias=s5off[:, G, t:t + 1]), reads=cst.r() + s5p.r() + s5off.r(), writes=uu.r())
                A("dve", OP("tensor_scalar", out=kk[:, 0:TT], in0=uu[:, 0:TT], scalar1=MAGIC, scalar2=MAGIC,
                                                                  op0=ALU.add, op1=ALU.subtract), reads=uu.r(), writes=kk.r())
                A("pool", OP("tensor_tensor", out=uu[:, 0:TT], in0=uu[:, 0:TT], in1=kk[:, 0:TT], op=ALU.subtract),
                  reads=uu.r() + kk.r(), writes=uu.r())
                A("act", OP("activation", out=ts_[:, 0:TT], in_=uu[:, 0:TT], func=AF.Sin, scale=TWO_PI_S), reads=uu.r(), writes=ts_.r())
                A("act", OP("activation", out=kk[:, 0:TT], in_=uu[:, 0:TT], func=AF.Abs), reads=uu.r(), writes=kk.r())
                A("act", OP("activation", out=tc_[:, 0:TT], in_=kk[:, 0:TT], func=AF.Sin, scale=-TWO_PI_S, bias=math.pi / 2.0 - 1e-6),
                  reads=kk.r(), writes=tc_.r())
                m1 = uu; m2 = kk
                A("dve", OP("tensor_tensor", out=m1[:, 0:TT], in0=psum[:, b1, :], in1=tc_[:, 0:TT], op=ALU.mult),
                  reads=psum.r(b1) + tc_.r(), writes=m1.r())
                A("dve", OP("tensor_tensor", out=m2[:, 0:TT], in0=psum[:, b2, :], in1=ts_[:, 0:TT], op=ALU.mult),
                  reads=psum.r(b2) + ts_.r(), writes=m2.r())
                banks.free(b1, b2)
                A("dve", OP("tensor_tensor", out=m1[:, 0:TT], in0=m1[:, 0:TT], in1=m2[:, 0:TT], op=ALU.add), reads=m1.r() + m2.r(), writes=m1.r())
                zz = m2
                A("dve", OP("tensor_tensor_scan", out=zz[:, 0:TT], data0=s5p[:, G, 0:1].to_broadcast([128, TT]), data1=m1[:, 0:TT],
                                                                          initial=zcar[:, G:G + 1], op0=ALU.mult, op1=ALU.add),
                  reads=m1.r() + s5p.r() + zcar.r(), writes=zz.r())
                A("act", OP("activation", out=zcar[:, G:G + 1], in_=zz[:, TT - 1:TT], func=AF.Copy), reads=zz.r(), writes=zcar.r())
                w12 = m1
                A("pool", OP("tensor_tensor", out=bf(w12, TT, 0), in0=zz[:, 0:TT], in1=tc_[:, 0:TT], op=ALU.mult),
                  reads=zz.r() + tc_.r(), writes=w12.r())
                A("pool", OP("tensor_tensor", out=bf(w12, TT, TT), in0=zz[:, 0:TT], in1=ts_[:, 0:TT], op=ALU.mult),
                  reads=zz.r() + ts_.r(), writes=w12.r())
                A("pe", OP("matmul", out=psum[:, by, :], lhsT=lhs_c[:, 0, g, :], rhs=bf(w12, TT, 0), start=(g == 0), stop=False),
                  reads=lhs_c.r() + w12.r(), writes=psum.r(by))
                A("pe", OP("matmul", out=psum[:, by, :], lhsT=lhs_c[:, 1, g, :], rhs=bf(w12, TT, TT), start=False, stop=(g == 7)),
                  reads=lhs_c.r() + w12.r(), writes=psum.r(by))
                ring.free(uu, kk, tc_, ts_)
            A("dve", OP("scalar_tensor_tensor", out=uf[:, 0:TT], in0=uf[:, 0:TT], scalar=PV(f"s5d{l}", c), in1=psum[:, by, :],
                                                             op0=ALU.mult, op1=ALU.add), reads=uf.r() + psum.r(by) + pv.r(), writes=uf.r())
            banks.free(by)
            A("act", OP("activation", out=zbf[:, c, tsl(t)], in_=uf[:, 0:TT], func=AF.Gelu_apprx_tanh), reads=uf.r(), writes=aux.r(c * 4 + t))
            ring.free(uf, ub)

        ring.enable_extra(rot_views)
        for c in range(4):
            lwb = lru_chunk(c)
            swb = s5_chunk(c)
            for t in range(NT):
                s5_unit(c, t, *swb)
                lru_unit(c, t, *lwb)
            wbufs.free(lwb[0])
            w5pool.free(swb[0])
        dbg(f"ylru_raw{l}", mixed, mixed[:].rearrange("p c t -> p (c t)"), [128, 4 * SEQ], BF16)
        if upto == "lru_raw":
            break
        group_post(l, mixed, "lnorm", 0, wbufs, 512.0)
        dbg(f"ylru{l}", mixed, mixed[:].rearrange("p c t -> p (c t)"), [128, 4 * SEQ], BF16)
        if upto == "lru":
            break

        dbg(f"s5z{l}", aux, aux[:], [128, 4 * SEQ], BF16)
        wg_t = wbufs.get()
        wgl = wg_t[:, 0:2048].rearrange("p (k n) -> p k n", k=4)
        load_w(wg_t, wgl, w_glu_d[l].rearrange("(k p) n -> p k n", p=128))
        for t in range(NT):
            for oc in range(4):
                b = banks.get()
                for k in range(4):
                    A("pe", OP("matmul", out=psum[:, b, :], lhsT=wgl[:, k, oc * 128:(oc + 1) * 128], rhs=zbf[:, k, tsl(t)],
                                                                start=(k == 0), stop=(k == 3)), reads=wg_t.r() + aux.r(k * 4 + t), writes=psum.r(b))
                sg = ring.get()
                A("act", OP("activation", out=sg[:, 0:TT], in_=psum[:, b, :], func=AF.Sigmoid, bias=PV(f"s5bg{l}", oc)),
                  reads=psum.r(b) + pv.r(), writes=sg.r())
                banks.free(b)
                A("dve", OP("tensor_tensor", out=mixed[:, oc, tsl(t)], in0=zbf[:, oc, tsl(t)], in1=sg[:, 0:TT], op=ALU.mult),
                  reads=sg.r() + aux.r(oc * 4 + t), writes=mixed.r(oc * 4 + t))
                ring.free(sg)
        wbufs.free(wg_t)
        group_post(l, mixed, "s5n", 1, wbufs, 512.0)
        dbg(f"ys5{l}", mixed, mixed[:].rearrange("p c t -> p (c t)"), [128, 4 * SEQ], BF16)
        if upto == "s5":
            break

        ring.disable_extra()
        build_rot()
        if l == 0:
            dbg("rcos", rcos, rcos[:], [128, SEQ])
            dbg("rsin", rsin, rsin[:], [128, SEQ])
        ring.enable_extra(ret_views)
        fw.pin = set(os.environ.get('KPIN', 'dve').split(',')) - {''}
        vh = aux[:].rearrange("p (n e) -> p n e", n=16)
        wv_t = wbufs.get()
        wvv = wv_t[:].rearrange("p (k n) -> p k n", k=8)
        load_w(wv_t, wvv, w_in_d[l, :, 2560:3072].rearrange("(k p) n -> p k n", p=128))
        for n in range(16):
            b = banks.get()
            t = n // 4
            for k in range(8):
                A("pe", OP("matmul", out=psum[:, b, :], lhsT=xn[:, k, n * 128:(n + 1) * 128], rhs=wvv[:, k, :],
                                                          start=(k == 0), stop=(k == 7)), reads=wv_t.r() + xs(k, t), writes=psum.r(b))
            for hd in range(4):
                A("act", OP("activation", out=vh[:, n, hd * 128:(hd + 1) * 128], in_=psum[:, b, hd * 128:(hd + 1) * 128],
                                                                 func=AF.Identity, scale=PV("vdec", hd)), reads=psum.r(b) + pv.r(), writes=aux.r(n))
            banks.free(b)
        wbufs.free(wv_t)
        wg_t = wbufs.get()
        wgv = wg_t[:].rearrange("p (k n) -> p k n", k=8)
        load_w(wg_t, wgv, w_in_d[l, :, 3072:3584].rearrange("(k p) n -> p k n", p=128))
        def ret_weights(hd):
            wb = wbufs.get()
            wq = wb[:].rearrange("p (k n) -> p k n", k=8)
            qb = 1536 + hd * 128
            kb = 2048 + hd * 128
            src = lambda c0, c1: w_in_d[l, :, c0:c1].rearrange("(k p) n -> p k n", p=128)
            load_w(wb, wq[:, :, 0:128], src(qb, qb + 128))
            load_w(wb, wq[:, :, 128:192], src(qb + 64, qb + 128))
            load_w(wb, wq[:, :, 192:256], src(qb, qb + 64))
            load_w(wb, wq[:, :, 256:384], src(kb, kb + 128))
            load_w(wb, wq[:, :, 384:448], src(kb + 64, kb + 128))
            load_w(wb, wq[:, :, 448:512], src(kb, kb + 64))
            return wb, wq

        def ret_A(hd, t, wb, wq):
            rot = []
            for which in range(2):
                ba = banks.get(); bb = banks.get()
                proj(ba, wb, lambda k, o=which * 256: wq[:, k, o:o + 128], t)
                proj(bb, wb, lambda k, o=which * 256 + 128: wq[:, k, o:o + 128], t)
                t1 = ring.get(); t2 = ring.get(); qr = ring.get()
                A("dve", OP("tensor_tensor", out=t1[:, 0:TT], in0=psum[:, ba, :], in1=rcos[:, tsl(t)], op=ALU.mult),
                  reads=psum.r(ba) + rcos.r(t), writes=t1.r())
                A("dve", OP("tensor_tensor", out=t2[:, 0:TT], in0=psum[:, bb, :], in1=rsin[:, tsl(t)], op=ALU.mult),
                  reads=psum.r(bb) + rsin.r(t), writes=t2.r())
                banks.free(ba, bb)
                A("pool", OP("tensor_tensor", out=bf(qr), in0=t1[:, 0:TT], in1=t2[:, 0:TT], op=ALU.add),
                  reads=t1.r() + t2.r(), writes=qr.r())
                ring.free(t1, t2)
                rot.append(qr)
            qr, kr = rot
            if hd == 0 and t == 0:
                dbg(f"qr{l}", qr, bf(qr), [128, TT], BF16)
                dbg(f"kr{l}", kr, bf(kr), [128, TT], BF16)
            bt = banks.get()
            ktp = psum[:, bt, :].bitcast(BF16)
            for j in range(4):
                A("pe", OP("transpose", out=ktp[:, j * 128:(j + 1) * 128], in_=bf(kr, 128, j * 128), identity=ident_b[:]),
                  reads=kr.r() + ident_b.r(), writes=psum.r(bt))
            ktm = ring.get()
            A("act", OP("activation", out=bf(ktm), in_=ktp[:, 0:TT], func=AF.Copy), reads=psum.r(bt), writes=ktm.r())
            banks.free(bt)
            bs = banks.get()
            for j in range(4):
                A("pe", OP("matmul", out=psum[:, bs, j * 128:(j + 1) * 128], lhsT=bf(kr, 128, j * 128), rhs=bf(qr, 128, j * 128),
                                                              start=True, stop=True), reads=kr.r() + qr.r(), writes=psum.r(bs))
            pT = ring.get()
            A("dve", OP("tensor_tensor", out=bf(pT).rearrange("p (j c) -> p j c", j=4), in0=psum[:, bs, :].rearrange("p (j c) -> p j c", j=4),
                                                      in1=maskT.unsqueeze(1).to_broadcast([128, 4, 128]), op=ALU.mult),
              reads=psum.r(bs) + cst.r(), writes=pT.r())
            banks.free(bs)
            bkv = banks.get()
            for j in range(4):
                n = t * 4 + j
                A("pe", OP("matmul", out=psum[:, bkv, j * 128:(j + 1) * 128], lhsT=bf(ktm, 128, j * 128),
                                                              rhs=vh[:, n, hd * 128:(hd + 1) * 128], start=True, stop=True),
                  reads=ktm.r() + aux.r(n), writes=psum.r(bkv))
            ring.free(ktm)
            ring.free(kr)
            bg = banks.get()
            proj(bg, wg_t, lambda k: wgv[:, k, hd * 128:(hd + 1) * 128], t)
            sg = ring.get()
            A("act", OP("activation", out=sg[:, 0:TT], in_=psum[:, bg, :], func=AF.Silu), reads=psum.r(bg), writes=sg.r())
            banks.free(bg)
            return qr, pT, bkv, sg

        def ret_B(hd, t, qr, pT, bkv, sg):
            gC = float(np.float32(np.exp(np.float32(np.log1p(-np.float32(2.0) ** np.float32(-5.0 - hd))) * np.float32(128.0))))
            bxo = banks.get()
            for j in range(4):
                n = t * 4 + j
                pb = pbs[n % 2]
                if n > 0:
                    A("act", OP("activation", out=pb[:], in_=ust[:], func=AF.Identity, scale=float(gC * (128.0 ** -0.5))),
                      reads=ust.r(), writes=pb.r())
                A("pe", OP("matmul", out=psum[:, bxo, j * 128:(j + 1) * 128], lhsT=vh[:, n, hd * 128:(hd + 1) * 128],
                                                            rhs=bf(pT, 128, j * 128), start=True, stop=(n == 0)),
                  reads=aux.r(n) + pT.r(), writes=psum.r(bxo))
                if n > 0:
                    A("pe", OP("matmul", out=psum[:, bxo, j * 128:(j + 1) * 128], lhsT=pb[:], rhs=bf(qr, 128, j * 128),
                                                                  start=False, stop=True), reads=pb.r() + qr.r(), writes=psum.r(bxo))
                    A("dve", OP("scalar_tensor_tensor", out=ust[:], in0=ust[:], scalar=gC, in1=psum[:, bkv, j * 128:(j + 1) * 128],
                                                                   op0=ALU.mult, op1=ALU.add), reads=ust.r() + psum.r(bkv), writes=ust.r())
                else:
                    A("dve", OP("tensor_copy", out=ust[:], in_=psum[:, bkv, j * 128:(j + 1) * 128]), reads=psum.r(bkv), writes=ust.r())
            banks.free(bkv)
            ring.free(pT)
            oT = ring.get()
            A("dve", OP("tensor_tensor", out=oT[:, 0:TT].rearrange("p (j c) -> p j c", j=4), in0=psum[:, bxo, :].rearrange("p (j c) -> p j c", j=4),
                                                      in1=cst[:, C_QDEC + hd * 128:C_QDEC + (hd + 1) * 128].unsqueeze(1).to_broadcast([128, 4, 128]), op=ALU.mult),
              reads=psum.r(bxo) + cst.r(), writes=oT.r())
            banks.free(bxo)
            ring.free(qr)
            ob = ring.get()
            A("act", OP("activation", out=bf(ob, TT, 0), in_=oT[:, 0:TT], func=AF.Copy), reads=oT.r(), writes=ob.r())
            A("act", OP("activation", out=bf(ob, TT, TT), in_=oT[:, 0:TT], func=AF.Square), reads=oT.r(), writes=ob.r())
            bm = banks.get(); bq = banks.get()
            A("pe", OP("matmul", out=psum[:, bm, :], lhsT=o128_b[:], rhs=bf(ob, TT, 0), start=True, stop=True),
              reads=ob.r() + o128_b.r(), writes=psum.r(bm))
            A("pe", OP("matmul", out=psum[:, bq, :], lhsT=o128_b[:], rhs=bf(ob, TT, TT), start=True, stop=True),
              reads=ob.r() + o128_b.r(), writes=psum.r(bq))
            ring.free(ob)
            m2 = ring.get()
            A("act", OP("activation", out=m2[:, 0:TT], in_=psum[:, bm, :], func=AF.Square), reads=psum.r(bm), writes=m2.r())
            A("dve", OP("tensor_tensor", out=m2[:, 0:TT], in0=psum[:, bq, :], in1=m2[:, 0:TT], op=ALU.subtract),
              reads=psum.r(bq) + m2.r(), writes=m2.r())
            banks.free(bq)
            A("dve", OP("tensor_scalar", out=m2[:, 0:TT], in0=m2[:, 0:TT], scalar1=0.0, scalar2=None, op0=ALU.max), reads=m2.r(), writes=m2.r())
            A("act", OP("activation", out=m2[:, 0:TT], in_=m2[:, 0:TT], func=AF.Ln, bias=NORM_EPS), reads=m2.r(), writes=m2.r())
            A("act", OP("activation", out=m2[:, 0:TT], in_=m2[:, 0:TT], func=AF.Exp, scale=-0.5), reads=m2.r(), writes=m2.r())
            A("dve", OP("tensor_tensor", out=oT[:, 0:TT], in0=oT[:, 0:TT], in1=psum[:, bm, :], op=ALU.subtract),
              reads=oT.r() + psum.r(bm), writes=oT.r())
            banks.free(bm)
            A("dve", OP("tensor_tensor", out=oT[:, 0:TT], in0=oT[:, 0:TT], in1=m2[:, 0:TT], op=ALU.mult),
              reads=oT.r() + m2.r(), writes=oT.r())
            ring.free(m2)
            A("dve", OP("scalar_tensor_tensor", out=mixed[:, hd, tsl(t)], in0=oT[:, 0:TT], scalar=PV(f"rnorm{l}", hd), in1=sg[:, 0:TT],
                                                                    op0=ALU.mult, op1=ALU.mult), reads=oT.r() + sg.r() + pv.r(), writes=mixed.r(hd * 4 + t))
            ring.free(oT, sg)

        pend = None
        for hd in range(4):
            wbq = ret_weights(hd)
            for t in range(NT):
                a_out = ret_A(hd, t, *wbq)
                if pend is not None:
                    ret_B(*pend)
                pend = (hd, t) + a_out
            wbufs.free(wbq[0])
        ret_B(*pend)
        wbufs.free(wg_t)
        dbg(f"yret{l}", mixed, mixed[:].rearrange("p c t -> p (c t)"), [128, 4 * SEQ], BF16)
        fw.pin = set()
        group_post(l, mixed, None, 2, wbufs, 512.0)
        ring.disable_extra()
        dbg(f"hmix{l}", hT, hT[:].rearrange("p c t -> p (c t)"), [128, 8 * SEQ])
        for cm in reversed(phase):
            cm.__exit__(None, None, None)
        phase = []
        fw.barrier()
        if upto == "mix":
            break

        rmsnorm_to_xn(f"gffn{l}")
        actb = psb("actb", [128, 4, SEQ], BF16, nslots=16)
        wus = Pool([psb(f"wu{i}", [128, 8, 1024], BF16) for i in range(2)])
        wds = Pool([psb(f"wd{i}", [128, 4, 1024], BF16) for i in range(2)])
        fcar = psb("fcar", [128, 4, 2, 2], F32, nslots=4)

        def ffn_load(grp):
            j0 = grp * 4
            wu = wus.get(); wd = wds.get()
            load_w(wu, wu[:, :, 0:512], w_up_d[l, :, j0 * 128:(j0 + 4) * 128].rearrange("(k p) n -> p k n", p=128))
            load_w(wu, wu[:, :, 512:1024], w_up_d[l, :, D_FF + j0 * 128:D_FF + (j0 + 4) * 128].rearrange("(k p) n -> p k n", p=128))
            load_w(wd, wd[:], w_down_d[l, j0 * 128:(j0 + 4) * 128, :].rearrange("(k p) n -> p k n", p=128))
            return wu, wd

        def ffn_up(grp, jj, t, wu):
            j = grp * 4 + jj
            outs = []
            for which in range(2):
                ch = j + 24 * which
                b = banks.get()
                proj(b, wu, lambda k, o=which * 512 + jj * 128: wu[:, k, o:o + 128], t)
                vc = ring.get()
                w0 = PV(f"fcw{l}", ch * 3 + 0); w1 = PV(f"fcw{l}", ch * 3 + 1); w2 = PV(f"fcw{l}", ch * 3 + 2)
                A("act", OP("activation", out=vc[:, 0:TT], in_=psum[:, b, :], func=AF.Identity, scale=w2, bias=PV(f"fcb{l}", ch)),
                  reads=psum.r(b) + pv.r(), writes=vc.r())
                A("dve", OP("scalar_tensor_tensor", out=vc[:, 1:TT], in0=psum[:, b, 0:TT - 1], scalar=w1, in1=vc[:, 1:TT],
                            op0=ALU.mult, op1=ALU.add), reads=psum.r(b) + vc.r() + pv.r(), writes=vc.r())
                A("dve", OP("scalar_tensor_tensor", out=vc[:, 2:TT], in0=psum[:, b, 0:TT - 2], scalar=w0, in1=vc[:, 2:TT],
                            op0=ALU.mult, op1=ALU.add), reads=psum.r(b) + vc.r() + pv.r(), writes=vc.r())
                if t > 0:
                    A("dve", OP("scalar_tensor_tensor", out=vc[:, 0:1], in0=fcar[:, jj, which, 1:2], scalar=w1, in1=vc[:, 0:1],
                                op0=ALU.mult, op1=ALU.add), reads=fcar.r(jj) + vc.r() + pv.r(), writes=vc.r())
                    A("dve", OP("scalar_tensor_tensor", out=vc[:, 0:2], in0=fcar[:, jj, which, 0:2], scalar=w0, in1=vc[:, 0:2],
                                op0=ALU.mult, op1=ALU.add), reads=fcar.r(jj) + vc.r() + pv.r(), writes=vc.r())
                if t < NT - 1:
                    A("act", OP("activation", out=fcar[:, jj, which, :], in_=psum[:, b, TT - 2:TT], func=AF.Copy),
                      reads=psum.r(b), writes=fcar.r(jj))
                banks.free(b)
                outs.append(vc)
            vc, gc = outs
            A("act", OP("activation", out=gc[:, 0:TT], in_=gc[:, 0:TT], func=AF.Gelu_apprx_tanh), reads=gc.r(), writes=gc.r())
            A("dve", OP("tensor_tensor", out=actb[:, jj, tsl(t)], in0=gc[:, 0:TT], in1=vc[:, 0:TT], op=ALU.mult),
              reads=vc.r() + gc.r(), writes=actb.r(jj * 4 + t))
            ring.free(vc, gc)

        def ffn_down(t, wd):
            for dc in range(8):
                b = banks.get()
                for k in range(4):
                    A("pe", OP("matmul", out=psum[:, b, :], lhsT=wd[:, k, dc * 128:(dc + 1) * 128], rhs=actb[:, k, tsl(t)],
                               start=(k == 0), stop=(k == 3)), reads=wd.r() + actb.r(k * 4 + t), writes=psum.r(b))
                A("dve", OP("tensor_tensor", out=hT[:, dc, tsl(t)], in0=hT[:, dc, tsl(t)], in1=psum[:, b, :], op=ALU.add),
                  reads=hs(dc, t) + psum.r(b), writes=hs(dc, t))
                banks.free(b)

        cur = ffn_load(0)
        for t in range(NT):
            for jj in range(4):
                ffn_up(0, jj, t, cur[0])
        for grp in range(6):
            nxt = ffn_load(grp + 1) if grp + 1 < 6 else None
            for t in range(NT):
                ffn_down(t, cur[1])
                if nxt is not None:
                    for jj in range(4):
                        ffn_up(grp + 1, jj, t, nxt[0])
            wus.free(cur[0]); wds.free(cur[1])
            cur = nxt
        dbg(f"hffn{l}", hT, hT[:].rearrange("p c t -> p (c t)"), [128, 8 * SEQ])
        for cm in reversed(phase):
            cm.__exit__(None, None, None)
        phase = []
        fw.barrier()

    for t in range(NT):
        b = banks.get()
        for c in range(8):
            sq = ring.get()
            A("act", OP("activation", out=bf(sq), in_=hT[:, c, tsl(t)], func=AF.Square), reads=hs(c, t), writes=sq.r())
            A("pe", OP("matmul", out=psum[:, b, :], lhsT=ones_b[:], rhs=bf(sq), start=(c == 0), stop=(c == 7)),
              reads=sq.r() + ones_b.r(), writes=psum.r(b))
            ring.free(sq)
        rs = ring.get()
        A("act", OP("activation", out=rs[:, 0:TT], in_=psum[:, b, :], func=AF.Ln, scale=1.0 / D_MODEL, bias=NORM_EPS), reads=psum.r(b), writes=rs.r())
        banks.free(b)
        A("act", OP("activation", out=rs[:, 0:TT], in_=rs[:, 0:TT], func=AF.Exp, scale=-0.5), reads=rs.r(), writes=rs.r())
        for c in range(8):
            ot = ring.get()
            A("dve", OP("scalar_tensor_tensor", out=ot[:, 0:TT], in0=hT[:, c, tsl(t)], scalar=PV("gfin", c), in1=rs[:, 0:TT],
                                                                         op0=ALU.mult, op1=ALU.mult), reads=hs(c, t) + rs.r() + pv.r(), writes=ot.r())
            ins = A("sp", OP("dma_start", out=outT_d[c * 128:(c + 1) * 128, tsl(t)], in_=ot[:, 0:TT]), reads=ot.r(), dma=True)
            out_dmas.append(ins)
            ring.free(ot)
        ring.free(rs)

    for cm in reversed(phase):
        cm.__exit__(None, None, None)
    fin = A("sp", None)
    for ins in out_dmas:
        fin.preds[ins] = True
    fw.emit()
    fw.close()
    return nc, dbg_out


def make_in_maps(inputs):
    inp = {k: np.asarray(v) for k, v in inputs.items()}
    pv = host_pv(inp)
    cst = host_consts()
    wbd, x1, x2, c1, c2 = host_struct(inp)
    shared = {
        "pv": pv, "cst": cst,
        "wbd": np.ascontiguousarray(wbd.reshape(DEPTH, 128, 1024)),
        "s5x1": x1, "s5x2": x2,
        "s5c1": np.ascontiguousarray(c1.reshape(DEPTH, 128, 512)),
        "s5c2": np.ascontiguousarray(c2.reshape(DEPTH, 128, 512)),
        "w_in": np.ascontiguousarray(inp["w_in"], dtype=np.float32),
        "s5_w_glu": np.ascontiguousarray(inp["s5_w_glu"], dtype=np.float32),
        "w_out": np.ascontiguousarray(inp["w_out"], dtype=np.float32),
        "w_up": np.ascontiguousarray(inp["w_up"], dtype=np.float32),
        "w_down": np.ascontiguousarray(inp["w_down"], dtype=np.float32),
    }
    maps = []
    for b in range(inp["x"].shape[0]):
        m = dict(shared)
        m["xT"] = np.ascontiguousarray(inp["x"][b].T.astype(np.float32))
        m["pos"] = np.ascontiguousarray(inp["positions"][b].astype(np.int32).reshape(1, SEQ))
        maps.append(m)
    return maps


_NC_CACHE = {}


def kernel(**inputs):
    if "nc" not in _NC_CACHE:
        _NC_CACHE["nc"] = build()[0]
    nc = _NC_CACHE["nc"]
    maps = make_in_maps(inputs)
    res = run_bass_kernel_spmd(nc, maps, core_ids=list(range(len(maps))))
    out = np.stack([np.ascontiguousarray(r["outT"].T) for r in res.results], axis=0)
    return out.astype(np.float32)
```
